# Optimizing a Trainium2 kernel written in Bass

```python
import jax, jax.numpy as jnp
from jax import lax
import numpy as np

D_MODEL = 1024
BATCH = 2
SEQ = 8192
DEPTH = 2
DEC_BATCH = 128
DEC_SEQ = 4
PAST_LEN = 8192
PAGE_SIZE = 128

N_A = DEPTH // 2
N_B = DEPTH - N_A
N_META = 16
D_FF = 2816
CONV_W = 3
HEAD_DIM = 64
N_HEADS = D_MODEL // HEAD_DIM
N_KV = 4
GROUP = N_HEADS // N_KV
WINDOW = 128
BLOCK = WINDOW
ROPE_THETA = 10000.0
EPS = 1e-6
NEG = -1e30

kernel_name = "yoco_shortconv_swa_sink_decoder_step"


def rmsnorm(x, g):
    x32 = x.astype(jnp.float32)
    y = x32 * lax.rsqrt(jnp.mean(x32 * x32, axis=-1, keepdims=True) + EPS) * g.astype(jnp.float32)
    return y.astype(x.dtype)


def swiglu(h, w_in, w_out):
    gate, up = jnp.split(h @ w_in, 2, axis=-1)
    return (jax.nn.silu(gate) * up) @ w_out


def short_conv_mixer(h, prev_u, w_in, kern, w_out):
    T = h.shape[1]
    b, c, z = jnp.split(h @ w_in, 3, axis=-1)
    u = c * z
    u_full = jnp.concatenate([prev_u.astype(u.dtype), u], axis=1)
    conv = kern[0] * u_full[:, 0:T]
    for j in range(1, CONV_W):
        conv = conv + kern[j] * u_full[:, j:j + T]
    y = (b * conv) @ w_out
    return y, u_full[:, -(CONV_W - 1):]


def rope(x, pos):
    inv = 1.0 / (ROPE_THETA ** (jnp.arange(0, HEAD_DIM, 2, dtype=jnp.float32) / HEAD_DIM))
    ang = pos.astype(jnp.float32)[:, None] * inv[None, :]
    cos = jnp.cos(ang)[None, :, None, :]
    sin = jnp.sin(ang)[None, :, None, :]
    x32 = x.astype(jnp.float32)
    x1, x2 = jnp.split(x32, 2, axis=-1)
    out = jnp.concatenate([x1 * cos - x2 * sin, x2 * cos + x1 * sin], axis=-1)
    return out.astype(x.dtype)


def shared_kv(x, pos, g, w_kv):
    B, T, _ = x.shape
    k, v = jnp.split(rmsnorm(x, g) @ w_kv, 2, axis=-1)
    k = rope(k.reshape(B, T, N_KV, HEAD_DIM), pos)
    return k, v.reshape(B, T, N_KV, HEAD_DIM)


def query(h, w_q, pos):
    B, T, _ = h.shape
    return rope((h @ w_q).reshape(B, T, N_HEADS, HEAD_DIM), pos)


def sink_softmax(scores, mask, sinks):
    s = jnp.where(mask, scores, NEG)
    sk = sinks.astype(jnp.float32).reshape(N_KV, GROUP)[:, :, None, None]
    m = jnp.maximum(jnp.max(s, axis=-1, keepdims=True), sk)
    p = jnp.exp(s - m)
    denom = jnp.sum(p, axis=-1, keepdims=True) + jnp.exp(sk - m)
    return p / denom


def window_attn_prompt(q, k, v, sinks):
    B, L = q.shape[0], q.shape[1]
    P = (-L) % BLOCK
    Lp = L + P
    NB = Lp // BLOCK
    padw = ((0, 0), (P, 0), (0, 0), (0, 0))
    qb = jnp.pad(q, padw).reshape(B, NB, BLOCK, N_KV, GROUP, HEAD_DIM)
    kb = jnp.pad(k, padw).reshape(B, NB, BLOCK, N_KV, HEAD_DIM)
    vb = jnp.pad(v, padw).reshape(B, NB, BLOCK, N_KV, HEAD_DIM)
    kband = jnp.concatenate([jnp.concatenate([jnp.zeros_like(kb[:, :1]), kb[:, :-1]], axis=1), kb], axis=2)
    vband = jnp.concatenate([jnp.concatenate([jnp.zeros_like(vb[:, :1]), vb[:, :-1]], axis=1), vb], axis=2)
    blk = jnp.arange(NB, dtype=jnp.int32)[:, None]
    qpos = blk * BLOCK + jnp.arange(BLOCK, dtype=jnp.int32)[None, :] - P
    kpos = (blk - 1) * BLOCK + jnp.arange(2 * BLOCK, dtype=jnp.int32)[None, :] - P
    dist = qpos[:, :, None] - kpos[:, None, :]
    mask = (kpos[:, None, :] >= 0) & (dist >= 0) & (dist <= WINDOW)
    scores = jnp.einsum('bnqkgd,bnskd->bnkgqs', qb, kband).astype(jnp.float32) * (HEAD_DIM ** -0.5)
    p = sink_softmax(scores, mask[None, :, None, None], sinks)
    out = jnp.einsum('bnkgqs,bnskd->bnqkgd', p.astype(v.dtype), vband)
    return out.reshape(B, Lp, N_HEADS * HEAD_DIM)[:, P:]


def window_attn_sample(q, k_all, v_all, qpos, kpos, sinks):
    B, T = q.shape[0], q.shape[1]
    qg = q.reshape(B, T, N_KV, GROUP, HEAD_DIM)
    dist = qpos[:, None] - kpos[None, :]
    mask = (kpos[None, :] >= 0) & (dist >= 0) & (dist <= WINDOW)
    scores = jnp.einsum('btkgd,bskd->bkgts', qg, k_all).astype(jnp.float32) * (HEAD_DIM ** -0.5)
    p = sink_softmax(scores, mask, sinks)
    out = jnp.einsum('bkgts,bskd->btkgd', p.astype(v_all.dtype), v_all)
    return out.reshape(B, T, N_HEADS * HEAD_DIM)


def setup_inputs(seed: int = 0) -> dict:
    key = jax.random.key(seed)
    ks = jax.random.split(key, 20)
    f32 = jnp.float32
    w_keep = min(WINDOW, PAST_LEN)
    nrm = lambda k, shape, s: jax.random.normal(k, shape, f32) * s
    return {
        "x_prompt": nrm(ks[0], (BATCH, SEQ, D_MODEL), 1.0),
        "x_sample": nrm(ks[1], (DEC_BATCH, DEC_SEQ, D_MODEL), 1.0),
        "state_conv": nrm(ks[2], (N_A, DEC_BATCH, CONV_W - 1, D_MODEL), 1.0),
        "cache_k": nrm(ks[3], (DEC_BATCH, w_keep, N_KV, HEAD_DIM), 1.0),
        "cache_v": nrm(ks[4], (DEC_BATCH, w_keep, N_KV, HEAD_DIM), 1.0),
        "meta_tokens": nrm(ks[5], (N_META, D_MODEL), 1.0),
        "norm_g": 1.0 + nrm(ks[6], (DEPTH, 3, D_MODEL), 0.02),
        "ffn_w_in": nrm(ks[7], (DEPTH, 2, D_MODEL, 2 * D_FF), D_MODEL ** -0.5),
        "ffn_w_out": nrm(ks[8], (DEPTH, 2, D_FF, D_MODEL), D_FF ** -0.5),
        "conv_w_in": nrm(ks[9], (N_A, D_MODEL, 3 * D_MODEL), D_MODEL ** -0.5),
        "conv_kernel": nrm(ks[10], (N_A, CONV_W, D_MODEL), CONV_W ** -0.5),
        "conv_w_out": nrm(ks[11], (N_A, D_MODEL, D_MODEL), D_MODEL ** -0.5),
        "kv_norm_g": 1.0 + nrm(ks[12], (D_MODEL,), 0.02),
        "w_kv": nrm(ks[13], (D_MODEL, 2 * N_KV * HEAD_DIM), D_MODEL ** -0.5),
        "w_q": nrm(ks[14], (N_B, D_MODEL, N_HEADS * HEAD_DIM), D_MODEL ** -0.5),
        "w_o": nrm(ks[15], (N_B, N_HEADS * HEAD_DIM, D_MODEL), (N_HEADS * HEAD_DIM) ** -0.5),
        "sinks": nrm(ks[16], (N_B, N_HEADS), 0.5),
        "final_norm_g": 1.0 + nrm(ks[17], (D_MODEL,), 0.02),
    }


def reference(x_prompt, x_sample, state_conv, cache_k, cache_v, meta_tokens, norm_g,
              ffn_w_in, ffn_w_out, conv_w_in, conv_kernel, conv_w_out, kv_norm_g, w_kv,
              w_q, w_o, sinks, final_norm_g):
    B, Bd = x_prompt.shape[0], x_sample.shape[0]
    L = x_prompt.shape[1] + N_META
    T = x_sample.shape[1]
    w_keep_s = cache_k.shape[1]
    w_keep_p = min(WINDOW, L)

    meta = jnp.broadcast_to(meta_tokens[None].astype(x_prompt.dtype), (B, N_META, D_MODEL))
    xp = jnp.concatenate([meta, x_prompt], axis=1)
    xs = x_sample
    pos_p = jnp.arange(L, dtype=jnp.int32)
    pos_s = PAST_LEN + jnp.arange(T, dtype=jnp.int32)
    kpos_s = jnp.concatenate([PAST_LEN - w_keep_s + jnp.arange(w_keep_s, dtype=jnp.int32), pos_s])

    conv_p, conv_s = [], []
    for layer in range(DEPTH):
        if layer == N_A:
            kp, vp = shared_kv(xp, pos_p, kv_norm_g, w_kv)
            ks_new, vs_new = shared_kv(xs, pos_s, kv_norm_g, w_kv)
            ks_all = jnp.concatenate([cache_k.astype(ks_new.dtype), ks_new], axis=1)
            vs_all = jnp.concatenate([cache_v.astype(vs_new.dtype), vs_new], axis=1)
            new_k_p, new_v_p = kp[:, L - w_keep_p:], vp[:, L - w_keep_p:]
            new_k_s, new_v_s = ks_all[:, -w_keep_s:], vs_all[:, -w_keep_s:]
        g = norm_g[layer]
        xp = xp + 0.5 * swiglu(rmsnorm(xp, g[0]), ffn_w_in[layer, 0], ffn_w_out[layer, 0])
        xs = xs + 0.5 * swiglu(rmsnorm(xs, g[0]), ffn_w_in[layer, 0], ffn_w_out[layer, 0])
        hp, hs = rmsnorm(xp, g[1]), rmsnorm(xs, g[1])
        if layer < N_A:
            a = layer
            zero_prev = jnp.zeros((B, CONV_W - 1, D_MODEL), hp.dtype)
            yp, sp = short_conv_mixer(hp, zero_prev, conv_w_in[a], conv_kernel[a], conv_w_out[a])
            ys, ss = short_conv_mixer(hs, state_conv[a], conv_w_in[a], conv_kernel[a], conv_w_out[a])
            conv_p.append(sp)
            conv_s.append(ss)
        else:
            bi = layer - N_A
            qp = query(hp, w_q[bi], pos_p)
            yp = window_attn_prompt(qp, kp, vp, sinks[bi]) @ w_o[bi]
            qs = query(hs, w_q[bi], pos_s)
            ys = window_attn_sample(qs, ks_all, vs_all, pos_s, kpos_s, sinks[bi]) @ w_o[bi]
        xp = xp + yp
        xs = xs + ys
        xp = xp + 0.5 * swiglu(rmsnorm(xp, g[2]), ffn_w_in[layer, 1], ffn_w_out[layer, 1])
        xs = xs + 0.5 * swiglu(rmsnorm(xs, g[2]), ffn_w_in[layer, 1], ffn_w_out[layer, 1])

    y_prompt = rmsnorm(xp, final_norm_g)[:, N_META:]
    y_sample = rmsnorm(xs, final_norm_g)
    new_state_conv_p = jnp.stack(conv_p, axis=0)
    new_state_conv_s = jnp.stack(conv_s, axis=0)
    return (y_prompt, y_sample, new_state_conv_p, new_state_conv_s, new_k_p, new_v_p, new_k_s, new_v_s)
```

```python
import contextlib
import numpy as np
import concourse.bass as bass
import concourse.mybir as mybir
from concourse.alu_op_type import AluOpType as ALU
from concourse.bass_utils import run_bass_kernel_spmd

F32 = mybir.dt.float32
BF16 = mybir.dt.bfloat16
AF = mybir.ActivationFunctionType

D = 1024
DC = 8
DFF = 2816
FCH = 22
HALO = 130
OWN = 2052
NP_ = HALO + OWN
NS = 64
NT = NP_ + NS
NSEQ = 16
EPS = 1e-6
DEBUG = {}

ENGS = ("pe", "act", "dve", "pool", "sp")


class Ev:
    __slots__ = ("kind", "eng", "sem", "count", "needed", "pos")

    def __init__(self, kind, eng):
        self.kind = kind
        self.eng = eng
        self.sem = None
        self.count = None
        self.needed = False
        self.pos = None


class Sched:
    def __init__(self, nc):
        self.nc = nc
        self.q = {e: [] for e in ENGS}
        self.dma_sems = {}
        self.eng_sem = {}

    def _reduce(self, waits, eng):
        best = {}
        for w in waits:
            if w is None:
                continue
            if w.kind == 'c':
                k = ('c', w.eng)
                if k not in best or w.pos > best[k].pos:
                    best[k] = w
            else:
                k = ('d', w.sem)
                if k not in best or w.count > best[k].count:
                    best[k] = w
        ws = list(best.values())
        for w in ws:
            w.needed = True
        return ws

    def op(self, eng, fn, waits=()):
        ev = Ev('c', eng)
        ev.pos = len(self.q[eng])
        self.q[eng].append((fn, self._reduce(waits, eng), ev))
        return ev

    def dma(self, eng, fn, key, waits=()):
        ev = Ev('d', eng)
        ent = self.dma_sems.setdefault(key, [None, 0])
        ent[1] += 16
        ev.sem = key
        ev.count = ent[1]
        ev.pos = len(self.q[eng])
        self.q[eng].append((fn, self._reduce(waits, eng), ev))
        return ev

    def emit(self):
        nc = self.nc
        with contextlib.ExitStack() as st:
            for e in ENGS:
                self.eng_sem[e] = st.enter_context(nc.semaphore("s_" + e))
            for i, key in enumerate(self.dma_sems):
                self.dma_sems[key][0] = st.enter_context(nc.semaphore("d%d" % i))
            for e in ENGS:
                c = 0
                for (fn, ws, ev) in self.q[e]:
                    if ev.kind == 'c' and ev.needed:
                        c += 1
                        ev.count = c
                    if ev.kind == 'd' and str(ev.sem).startswith("const"):
                        ev.count = self.dma_sems[ev.sem][1]
            block = st.enter_context(nc.Block())
            sched = self

            def run(engname):
                def body(eh):
                    waited = {}
                    for (fn, ws, ev) in sched.q[engname]:
                        for w in ws:
                            if w.kind == 'c':
                                sem = sched.eng_sem[w.eng]
                                k = ('c', w.eng)
                            else:
                                sem = sched.dma_sems[w.sem][0]
                                k = ('d', w.sem)
                            if waited.get(k, 0) >= w.count:
                                continue
                            waited[k] = w.count
                            eh.wait_ge(sem, w.count)
                        ins = fn(eh)
                        if ins is None:
                            continue
                        if ev.kind == 'c':
                            if ev.needed:
                                ins.then_inc(sched.eng_sem[engname], 1)
                        else:
                            ins.then_inc(sched.dma_sems[ev.sem][0], 16)
                return body

            block.tensor(run("pe"))
            block.scalar(run("act"))
            block.vector(run("dve"))
            block.gpsimd(run("pool"))
            block.sync(run("sp"))


class RR:
    def __init__(self):
        self.segs = []

    def _cut(self, x):
        for i, s in enumerate(self.segs):
            if s[0] < x < s[1]:
                self.segs[i:i + 1] = [[s[0], x, list(s[2]), list(s[3])], [x, s[1], list(s[2]), list(s[3])]]
                return

    def cover(self, a, b):
        self._cut(a)
        self._cut(b)
        self.segs.sort(key=lambda s: s[0])
        pts = a
        new = []
        for s in self.segs:
            if s[1] <= a or s[0] >= b:
                continue
            if s[0] > pts:
                new.append([pts, s[0], [], []])
            pts = s[1]
        if pts < b:
            new.append([pts, b, [], []])
        self.segs += new
        self.segs.sort(key=lambda s: s[0])
        return [s for s in self.segs if s[0] >= a and s[1] <= b]

    def all_events(self):
        out = []
        for s in self.segs:
            out += s[2] + s[3]
        return out


class _Stop(Exception):
    pass


class K:
    pass


def build_nc():
    nc = bass.Bass("TRN2", target_bir_lowering=False)
    S = Sched(nc)

    def din(name, shape):
        return nc.dram_tensor(name, list(shape), F32, kind="ExternalInput").ap()

    def dout(name, shape):
        return nc.dram_tensor(name, list(shape), F32, kind="ExternalOutput").ap()

    xin = din("xin", [NT, D])
    sconv = din("sconv", [2 * NSEQ, D])
    ck = din("ck", [NSEQ, 128, 256])
    cv = din("cv", [NSEQ, 128, 256])
    ffn_w_in = din("ffn_w_in", [2, 2, D, 2 * DFF])
    ffn_w_out = din("ffn_w_out", [2, 2, DFF, D])
    conv_w_in = din("conv_w_in", [D, 3 * D])
    conv_w_out = din("conv_w_out", [D, D])
    w_kv = din("w_kv", [D, 512])
    w_q = din("w_q", [D, D])
    w_o = din("w_o", [D, D])
    gT_d = din("gT", [128, 64])
    ckT_d = din("ckT", [128, 24])
    sinkT_d = din("sinkT", [128, 40])
    cos_d = din("cosT", [128, NT])
    sin_d = din("sinT", [128, NT])
    ident_d = din("ident", [128, 128])
    swap_d = din("swapm", [128, 128])
    mstd_d = din("mstd", [128, 512])
    mfirst_d = din("mfirst", [128, 512])
    msamp_d = din("msamp", [128, 128])

    y_d = dout("y", [OWN + NS, D])
    up_d = dout("u_p", [2, D])
    us_d = dout("u_s", [2 * NSEQ, D])
    kp_d = dout("kp", [128, 256])
    vp_d = dout("vp", [128, 256])
    ks_d = dout("ks", [NSEQ, 128, 256])
    vs_d = dout("vs", [NSEQ, 128, 256])
    dbg_d = None
    if DEBUG.get("xT"):
        dbg_d = dout("dbg", [128, DC * NT])

    base = (nc._sbuf_addr_for_side('left') + 63) // 64 * 64
    cap = 229376
    cur = [base]

    def esize(dt):
        return 4 if dt == F32 else 2

    def alloc(name, shape, dt, at=None):
        n = 1
        for d_ in shape[1:]:
            n *= d_
        nbytes = (n * esize(dt) + 63) // 64 * 64
        if at is None:
            o = cur[0]
            cur[0] += nbytes
        else:
            o = at
        assert o + nbytes <= cap, (name, o, nbytes)
        return nc.alloc_sbuf_tensor_at(name, list(shape), dt, offset=o)

    xT = alloc("xT", [128, DC, NT], F32)
    XN_OFF = cur[0]
    xn = alloc("xn", [128, DC, NT], BF16)
    ident = alloc("ident", [128, 128], F32)
    swapm = alloc("swapm", [128, 128], BF16)
    identb = alloc("identb", [128, 128], BF16)
    ones = alloc("ones", [128, 128], BF16)
    gT = alloc("gT", [128, 64], F32)
    ckT = alloc("ckT", [128, 24], F32)
    esink = alloc("esink", [128, 40], F32)
    epst = alloc("epst", [128, 1], F32)
    mstd = alloc("mstd", [128, 512], BF16)
    mfirst = alloc("mfirst", [128, 512], BF16)
    msamp = alloc("msamp", [128, 128], BF16)
    uprevT = alloc("uprevT", [128, DC, 2 * NSEQ], F32)
    uoT = alloc("uoT", [128, DC, 34], F32)
    rstd = [alloc("rstd%d" % i, [128, 512], F32) for i in range(2)]
    rtmp = alloc("rtmp", [128, 512], F32)
    ptmp = alloc("ptmp", [128, 512], F32)
    sq = alloc("sq", [128, DC, 512], BF16)
    ARENA = cur[0]
    arena_size = cap - ARENA
    assert arena_size >= 82200, arena_size

    A0 = ARENA
    Wg = [alloc("Wg%d" % s, [128, DC, 256], BF16, at=A0 + s * 12288) for s in range(2)]
    Wu = [alloc("Wu%d" % s, [128, DC, 256], BF16, at=A0 + s * 12288 + 4096) for s in range(2)]
    Wo_ = [alloc("Wo%d" % s, [128, 2, 1024], BF16, at=A0 + s * 12288 + 8192) for s in range(2)]
    hbuf = [[alloc("h%d%d" % (p, f), [128, 512], BF16, at=A0 + 24576 + (2 * p + f) * 1024) for f in range(2)]
            for p in range(2)]
    sbuf_ = [alloc("s%d" % f, [128, 512], F32, at=A0 + 28672 + f * 2048) for f in range(2)]
    NXS = 8
    xs = [alloc("xs%d" % i, [128, D], F32, at=A0 + 32768 + i * 4096) for i in range(NXS)]
    Wbcz = [alloc("Wbcz%d" % s, [128, 3, DC, 128], BF16, at=A0 + s * 6144) for s in range(2)]
    c_sb = [alloc("c_sb%d" % i, [128, 512], F32, at=A0 + 12288 + i * 2048) for i in range(2)]
    tA = [alloc("tA%d" % i, [128, 512], F32, at=A0 + 16384 + i * 2048) for i in range(2)]
    tB = [alloc("tB%d" % i, [128, 512], F32, at=A0 + 20480 + i * 2048) for i in range(2)]
    usb = alloc("usb", [128, NSEQ, 6], F32, at=A0 + 24576)
    Wq = alloc("Wq", [128, DC, 1024], BF16, at=A0)
    Wo2 = alloc("Wo2", [128, DC, 1024], BF16, at=A0 + 16384)
    B0 = A0 + 32768
    vT = alloc("vT", [128, DC, NT], BF16, at=A0 + 29824)
    Wco = alloc("Wco", [128, DC, 1024], BF16, at=A0 + 29824 + 35968)
    ubuf = [alloc("ubuf%d" % i, [128, 516], F32, at=A0 + 25600 + i * 2112) for i in range(2)]
    assert A0 + 29824 + 35968 + 16384 <= cap
    KT = alloc("KT", [128, 2, NT], BF16, at=B0)
    Vb = alloc("Vb", [128, 19, 256], BF16, at=B0 + 8992)
    Vs_bf = alloc("Vs_bf", [NS, 256], BF16, at=B0 + 8992 + 9728)
    C0 = B0 + 8992 + 9728 + 8192
    cosT = alloc("cosT", [128, NT], F32, at=C0)
    sinT = alloc("sinT", [128, NT], F32, at=C0 + 8992)
    E0 = C0 + 2 * 8992
    wkv = alloc("wkv", [128, DC, 512], BF16, at=A0)
    kraw = [alloc("kraw%d" % i, [128, 512], BF16, at=A0 + 8192 + i * 1024) for i in range(2)]
    kt1 = [alloc("kt1_%d" % i, [128, 512], F32, at=A0 + 10240 + i * 2048) for i in range(2)]
    kcb = [alloc("kcb%d" % i, [128, 512], BF16, at=A0 + 10240 + i * 1024) for i in range(2)]
    kt2 = [alloc("kt2_%d" % i, [128, 512], F32, at=A0 + 14336 + i * 2048) for i in range(2)]
    kt3 = [alloc("kt3_%d" % i, [128, 512], F32, at=A0 + 18432 + i * 2048) for i in range(2)]
    KoutT = alloc("KoutT", [128, 2, 192], F32, at=A0 + 22528)
    kstage = alloc("kstage", [128, 256], F32, at=A0 + 24576)
    vstage = alloc("vstage", [128, 256], F32, at=A0 + 25600)
    ksstage = alloc("ksstage", [64, 256], F32, at=A0 + 26624)
    vsstage = alloc("vsstage", [64, 256], F32, at=A0 + 27648)
    ckst = [alloc("ckst%d" % i, [128, 256], F32, at=E0 + i * 1024) for i in range(2)]
    KcT = [alloc("KcT%d" % i, [128, 2, 128], BF16, at=E0 + 2048 + i * 512) for i in range(2)]
    Vc = [alloc("Vc%d" % i, [128, 256], BF16, at=E0 + 3072 + i * 512) for i in range(2)]
    ustage = alloc("ustage", [34, D], F32, at=A0 + 12288)
    assert E0 + 4096 <= cap, (E0, cap)
    xnA = alloc("xnA", [128, DC, 512], BF16, at=XN_OFF)
    QT = alloc("QT", [128, DC, 512], BF16, at=XN_OFF + 8192)
    attnT = alloc("attnT", [128, DC, 512], BF16, at=XN_OFF + 16384)
    PT = [alloc("PT%d" % i, [128, 512], BF16, at=XN_OFF + 24576 + i * 1024) for i in range(2)]
    qraw = [alloc("qraw%d" % i, [128, 512], BF16, at=XN_OFF + 26624 + i * 1024) for i in range(2)]
    qt1 = [alloc("qt1_%d" % i, [128, 512], F32, at=XN_OFF + 28672 + i * 2048) for i in range(2)]
    qt2 = alloc("qt2", [128, 512], F32, at=XN_OFF + 32768)
    qcb = [alloc("qcb%d" % i, [128, 512], BF16, at=XN_OFF + 28672 + i * 1024) for i in range(2)]
    PTC = alloc("PTC", [128, 1024], BF16, at=XN_OFF + 24576)
    PT.append(alloc("PT2", [128, 512], BF16, at=XN_OFF + 26624))
    lnDall = alloc("lnDall", [128, 512], F32, at=XN_OFF + 30720)
    lnD = [alloc("lnD%d" % i, [128, 128], F32, at=XN_OFF + 34816 + i * 512) for i in range(2)]
    assert XN_OFF + 34816 + 1024 <= XN_OFF + DC * NT * 2
    yblk = [alloc("yblk%d" % i, [128, DC, 128], F32, at=A0 + i * 4096) for i in range(2)]
    ystage = [alloc("ystage%d" % i, [128, D], F32, at=A0 + 8192 + i * 4096) for i in range(4)]

    PSA = nc.alloc_psum_tensor("psa", [128, 8, 512], F32)

    class _Bank:
        def __init__(self, i):
            self.i = i

        def __getitem__(self, idx):
            return PSA[idx[0], self.i, idx[1]]

    PS = [_Bank(i) for i in range(8)]

    class Res(RR):
        pass

    r_xT = RR()
    r_xn = RR()
    r_ps = [RR() for _ in range(8)]
    for r_ in r_ps:
        r_.excl = True
    r_const = RR()
    res_cache = {}

    def R(name):
        if name not in res_cache:
            res_cache[name] = RR()
        return res_cache[name]

    def W_(res, a=0, b=1):
        return (res, a, b)

    def deps_for(eng, reads, writes, extra):
        deps = []
        for (r, a, b) in reads:
            for s in r.cover(a, b):
                for ev in s[2]:
                    deps.append(ev)
                if getattr(r, "excl", False):
                    for ev in s[3]:
                        if not (ev.kind == 'c' and ev.eng == eng):
                            deps.append(ev)
        for (r, a, b) in writes:
            for s in r.cover(a, b):
                if not s[3]:
                    for ev in s[2]:
                        if not (ev.kind == 'c' and ev.eng == eng):
                            deps.append(ev)
                for ev in s[3]:
                    if not (ev.kind == 'c' and ev.eng == eng):
                        deps.append(ev)
        deps += [e for e in extra if e is not None]
        if eng == "pe":
            deps = [e for e in deps if not (e.kind == 'c' and e.eng == "pe")]
        return deps

    def commit(ev, reads, writes):
        for (r, a, b) in reads:
            for s in r.cover(a, b):
                s[3].append(ev)
        for (r, a, b) in writes:
            for s in r.cover(a, b):
                s[2] = [ev]
                s[3] = []

    def OP(eng, fn, reads=(), writes=(), extra=()):
        ev = S.op(eng, fn, deps_for(eng, reads, writes, extra))
        commit(ev, reads, writes)
        return ev

    def DMA(eng, fn, key, reads=(), writes=(), extra=()):
        ev = S.dma(eng, fn, key, deps_for("dma_" + eng, reads, writes, extra))
        commit(ev, reads, writes)
        return ev

    out_events = []

    def MM(out, lhsT, rhs, start=True, stop=True):
        return lambda e: e.matmul(out, lhsT, rhs, start=start, stop=stop)

    def MMX(out, lhsT, rhs, start=True, stop=True):
        return lambda e: e.matmul(out, lhsT, rhs, start=start, stop=stop, skip_group_check=True)

    def TR(out, in_, idn):
        return lambda e: e.transpose(out, in_, idn)

    def ACT(out, in_, func, bias=None, scale=None):
        kw = {}
        if bias is not None:
            kw["bias"] = bias
        if scale is not None:
            kw["scale"] = scale
        return lambda e: e.activation(out, in_, func, **kw)

    def TT(out, in0, in1, op):
        return lambda e: e.tensor_tensor(out, in0, in1, op)

    def TS(out, in0, s1, s2, op0, op1=None):
        if op1 is None:
            return lambda e: e.tensor_scalar(out, in0, s1, None, op0)
        return lambda e: e.tensor_scalar(out, in0, s1, s2, op0, op1)

    def STT(out, in0, scalar, in1, op0, op1):
        return lambda e: e.scalar_tensor_tensor(out, in0, scalar, in1, op0, op1)

    def CP(out, in_):
        return lambda e: e.tensor_copy(out, in_)

    def MS(ap, val):
        return lambda e: e.memset(ap, val)

    def RCP(out, in_):
        return lambda e: e.reciprocal(out, in_)

    def DM(out, in_):
        return lambda e: e.dma_start(out=out, in_=in_)

    def cres(i):
        return W_(r_const, i, i + 1)
    CONST = [W_(r_const, 0, 16)]
    DMA("sp", DM(ident[:], ident_d), "const", writes=[cres(0)])
    DMA("sp", DM(gT[:], gT_d), "const", writes=[cres(1)])
    DMA("sp", DM(ckT[:], ckT_d), "const", writes=[cres(2)])
    DMA("sp", DM(esink[:], sinkT_d), "const", writes=[W_(R("esink"))])
    DMA("pool", DM(swapm[:], swap_d), "constp", writes=[cres(4)])
    DMA("pool", DM(identb[:], ident_d), "constp", writes=[cres(8)])
    DMA("pool", DM(mstd[:], mstd_d), "constp", writes=[cres(5)])
    DMA("pool", DM(mfirst[:], mfirst_d), "constp", writes=[cres(6)])
    DMA("pool", DM(msamp[:], msamp_d), "constp", writes=[cres(7)])
    OP("dve", MS(ones[:], 1.0), writes=[W_(R("ones"))])
    OP("dve", MS(epst[:], EPS), writes=[W_(R("eps"))])

    ld_ctr = [0]

    def load_block(src_ap, r0, n, dst_fn, rdst):
        s_ = ld_ctr[0] % NXS
        ld_ctr[0] += 1
        xr = W_(R("xs%d" % s_))
        DMA("sp", DM(xs[s_][0:n, :], src_ap[r0:r0 + n, :]), "xs" + str(s_), writes=[xr])
        for hlf in range(2):
            bank = 6 + hlf
            for cc in range(4):
                c = hlf * 4 + cc
                OP("pe", TR(PS[bank][:, cc * 128:cc * 128 + n], xs[s_][0:n, c * 128:(c + 1) * 128], ident[0:n, 0:n]),
                   reads=[xr] + CONST, writes=[W_(r_ps[bank], cc * 128, cc * 128 + 128)])
            src = PS[bank][:, :].rearrange("p (c n) -> p c n", c=4)[:, :, 0:n]
            if hlf == 0:
                OP("dve", CP(dst_fn(hlf * 4, r0, n), src), reads=[W_(r_ps[bank], 0, 512)], writes=[rdst(r0, n)])
            else:
                OP("act", ACT(dst_fn(hlf * 4, r0, n), src, AF.Copy), reads=[W_(r_ps[bank], 0, 512)],
                   writes=[rdst(r0, n)])

    NBLK_X = (NT + 127) // 128
    x_loaded = [0]

    def load_x_upto(col_end):
        while x_loaded[0] < NBLK_X and x_loaded[0] * 128 < col_end:
            r0 = x_loaded[0] * 128
            n = min(128, NT - r0)
            load_block(xin, r0, n, lambda c0, r0_, n_: xT[:, c0:c0 + 4, r0_:r0_ + n_], lambda r0_, n_: W_(r_xT, r0_, r0_ + n_))
            x_loaded[0] += 1

    def split_tiles(a, b, n):
        w = b - a
        base_w = (w // n) // 2 * 2
        rem = w - base_w * n
        out = []
        x = a
        for i in range(n):
            ww = base_w + (2 if i < rem // 2 else 0)
            if i == n - 1:
                ww = b - x
            out.append((x, x + ww))
            x += ww
        return out

    TILES0 = split_tiles(0, NT, 5)
    TILES1 = [(max(a, HALO), b) for (a, b) in TILES0]
    norm_ctr = [0]

    def norm_tile(nidx, a, b, dst, dst_res, dst_off, extra=(), defer=False):
        w = b - a
        p6 = W_(r_ps[6], 0, 512)
        for hf in range(2):
            OP("act", ACT(sq[:, 4 * hf:4 * hf + 4, 0:w], xT[:, 4 * hf:4 * hf + 4, a:b], AF.Square),
               reads=[W_(r_xT, a, b)], writes=[W_(R("sq"), hf, hf + 1)])
        for c in range(DC):
            OP("pe", MM(PS[6][:, 0:w], ones[:, :], sq[:, c, 0:w], c == 0, c == DC - 1),
               reads=[W_(R("sq"), c // 4, c // 4 + 1), W_(R("ones"))], writes=[p6])
        OP("act", ACT(PS[6][:, 0:w], PS[6][:, 0:w], AF.Ln, bias=epst[:, 0:1], scale=1.0 / D),
           reads=[p6, W_(R("eps"))], writes=[p6])
        OP("act", ACT(PS[6][:, 0:w], PS[6][:, 0:w], AF.Exp, scale=-0.5), reads=[p6], writes=[p6])

        def part2():
            for c in range(DC):
                OP("dve", STT(dst[:, c, dst_off:dst_off + w], xT[:, c, a:b], gT[:, nidx * 8 + c:nidx * 8 + c + 1],
                              PS[6][:, 0:w], ALU.mult, ALU.mult),
                   reads=[W_(r_xT, a, b), p6] + CONST, writes=[dst_res(a, b)], extra=extra)
        if defer:
            return part2
        part2()

    def xn_res(a, b):
        return W_(r_xn, a, b)

    grp_ctr = [0]

    def ffn(l, i, nidx, tiles, phase_extra=(), pre_tile=None):
        w_in = ffn_w_in[l, i]
        w_out = ffn_w_out[l, i]
        pending = [None]
        it = [0]

        def flush():
            if pending[0] is not None:
                for st_ in range(4):
                    pending[0](st_)
                pending[0] = None

        ngrp = FCH // 2
        for gi in range(ngrp):
            s = grp_ctr[0] % 2
            grp_ctr[0] += 1
            f0 = gi * 2
            wr = R("W%d" % s)
            wres = W_(wr, 0, 3)
            ex = phase_extra if gi < 2 else ()
            DMA("pool", DM(Wg[s][:], w_in[:, f0 * 128:f0 * 128 + 256].rearrange("(c p) n -> p c n", p=128)),
                "W%d" % s, writes=[W_(wr, 0, 1)], extra=ex)
            DMA("pool", DM(Wu[s][:], w_in[:, DFF + f0 * 128:DFF + f0 * 128 + 256].rearrange("(c p) n -> p c n", p=128)),
                "W%d" % s, writes=[W_(wr, 1, 2)], extra=ex)
            DMA("pool", DM(Wo_[s][:], w_out[f0 * 128:f0 * 128 + 256, :].rearrange("(f p) n -> p f n", p=128)),
                "W%d" % s, writes=[W_(wr, 2, 3)], extra=ex)
            for ti, (a, b) in enumerate(tiles):
                if gi == 0:
                    if ti == 0:
                        if pre_tile is not None:
                            pre_tile(0)
                        norm_tile(nidx, a, b, xn, xn_res, a, extra=phase_extra)
                    if ti + 1 < len(tiles):
                        a2, b2 = tiles[ti + 1]
                        if pre_tile is not None:
                            pre_tile(ti + 1)
                        norm_tile(nidx, a2, b2, xn, xn_res, a2, extra=phase_extra)
                w = b - a
                par = it[0] % 2
                it[0] += 1
                prev = pending[0]
                pending[0] = None
                step = [0]

                def prev_pair():
                    if prev is not None:
                        prev(step[0])
                    step[0] += 1

                for fi in range(2):
                    gb, ub = 2 * fi, 2 * fi + 1
                    for c in range(DC):
                        OP("pe", MM(PS[gb][:, 0:w], Wg[s][:, c, fi * 128:(fi + 1) * 128], xn[:, c, a:b], c == 0, c == DC - 1),
                           reads=[wres, W_(r_xn, a, b)], writes=[W_(r_ps[gb], 0, 512)])
                    OP("act", ACT(sbuf_[fi][:, 0:w], PS[gb][:, 0:w], AF.Silu),
                       reads=[W_(r_ps[gb], 0, 512)], writes=[W_(R("s%d" % fi))])
                    prev_pair()
                    for c in range(DC):
                        OP("pe", MM(PS[ub][:, 0:w], Wu[s][:, c, fi * 128:(fi + 1) * 128], xn[:, c, a:b], c == 0, c == DC - 1),
                           reads=[wres, W_(r_xn, a, b)], writes=[W_(r_ps[ub], 0, 512)])
                    OP("dve", TT(hbuf[par][fi][:, 0:w], sbuf_[fi][:, 0:w], PS[ub][:, 0:w], ALU.mult),
                       reads=[W_(R("s%d" % fi)), W_(r_ps[ub], 0, 512)], writes=[W_(R("h%d%d" % (par, fi)))])
                    prev_pair()

                def wout(stepi, a=a, b=b, w=w, par=par, s=s, wres=wres):
                    for d_ in (2 * stepi, 2 * stepi + 1):
                        ob = (4, 5, 7)[d_ % 3]
                        for fi in range(2):
                            OP("pe", MM(PS[ob][:, 0:w], Wo_[s][:, fi, d_ * 128:(d_ + 1) * 128], hbuf[par][fi][:, 0:w],
                                        fi == 0, fi == 1),
                               reads=[wres, W_(R("h%d%d" % (par, fi)))], writes=[W_(r_ps[ob], 0, 512)])
                        OP("dve", STT(xT[:, d_, a:b], PS[ob][:, 0:w], 0.5, xT[:, d_, a:b], ALU.mult, ALU.add),
                           reads=[W_(r_ps[ob], 0, 512), W_(r_xT, a, b)], writes=[W_(r_xT, a, b)])
                pending[0] = wout
        flush()

    def gather(names):
        evs = []
        for n in names:
            if n in res_cache:
                evs += res_cache[n].all_events()
        return evs

    def dbg_dump(tag):
        if dbg_d is not None and DEBUG.get("xT") == tag:
            out_events.append(DMA("sp", DM(dbg_d, xT[:, :, :].rearrange("p c n -> p (c n)")),
                                  "dbg", reads=[W_(r_xT, 0, NT)]))
        if DEBUG.get("stop") == tag:
            raise _Stop()

    try:
        FFN_BUFS = ["W0", "W1", "h00", "h01", "h10", "h11", "s0", "s1"]
        dbg_dump("phase0")
        def pre0(ti):
            load_x_upto(TILES0[ti][1])
            if ti == len(TILES0) - 1:
                out_events.append(DMA("sp", DM(ks_d[:, 0:124, :], ck[:, 4:128, :]), "cachecp"))
                out_events.append(DMA("sp", DM(vs_d[:, 0:124, :], cv[:, 4:128, :]), "cachecp"))
                load_block(sconv, 0, 2 * NSEQ, lambda c0, r0_, n_: uprevT[:, c0:c0 + 4, r0_:r0_ + n_],
                           lambda r0_, n_: W_(R("uprevT")))
        ffn(0, 0, 0, TILES0, pre_tile=pre0)
        dbg_dump("ffn1")

        ffn_evs = gather(FFN_BUFS + ["xs%d" % i for i in range(8)])
        DMA("pool", DM(Wco[:], conv_w_out.rearrange("(c p) n -> p c n", p=128)), "Wco", writes=[W_(R("Wco"))])
        cw = conv_w_in.rearrange("(c p) (k m) -> p k c m", p=128, k=3)
        def conv_w_dma(fc_):
            s_ = fc_ % 2
            DMA("pool", DM(Wbcz[s_][:], cw[:, :, :, fc_ * 128:(fc_ + 1) * 128]), "Wbcz%d" % s_,
                writes=[W_(R("Wbcz%d" % s_))], extra=ffn_evs if fc_ < 2 else ())
        conv_w_dma(0)
        for fc in range(DC):
            s = fc % 2
            wres = W_(R("Wbcz%d" % s))
            if fc + 1 < DC:
                conv_w_dma(fc + 1)
            OP("pool", MS(ubuf[0][:, 0:2], 0.0), writes=[W_(R("ubuf0"), 0, 2)])
            for ti, (a, b) in enumerate(TILES0):
                if fc == 0:
                    if ti == 0:
                        norm_tile(1, a, b, xn, xn_res, a)
                    if ti + 1 < len(TILES0):
                        norm_tile(1, TILES0[ti + 1][0], TILES0[ti + 1][1], xn, xn_res, TILES0[ti + 1][0])
                w = b - a
                par = ti % 2
                bo = 3 * ((fc * len(TILES0) + ti) % 2)
                pb = min(b, NP_)
                wp = pb - a
                has_s = b > NP_
                for k in range(3):
                    for c in range(DC):
                        OP("pe", MM(PS[bo + k][:, 0:w], Wbcz[s][:, k, c, :], xn[:, c, a:b], c == 0, c == DC - 1),
                           reads=[wres, W_(r_xn, a, b)], writes=[W_(r_ps[bo + k], 0, 512)])
                ex = ffn_evs if (fc == 0 and ti < 2) else ()
                cr = W_(R("c_sb%d" % par))
                tAr = W_(R("tA%d" % par))
                tBr = W_(R("tB%d" % par))
                OP("act", ACT(c_sb[par][:, 0:w], PS[bo + 1][:, 0:w], AF.Copy), reads=[W_(r_ps[bo + 1], 0, 512)], writes=[cr], extra=ex)
                ub = ubuf[par]
                ur = R("ubuf%d" % par)
                OP("dve", TT(ub[:, 2:2 + wp], c_sb[par][:, 0:wp], PS[bo + 2][:, 0:wp], ALU.mult),
                   reads=[cr, W_(r_ps[bo + 2], 0, 512)], writes=[W_(ur, 2, 516)], extra=ex)
                if ti + 1 < len(TILES0):
                    OP("pool", CP(ubuf[1 - par][:, 0:2], ub[:, wp:wp + 2]),
                       reads=[W_(ur, 2, 516)], writes=[W_(R("ubuf%d" % (1 - par)), 0, 2)])
                OP("dve", TS(tA[par][:, 0:wp], ub[:, 0:wp], ckT[:, fc:fc + 1], None, ALU.mult),
                   reads=[W_(ur, 0, 516)] + CONST, writes=[tAr], extra=ex)
                OP("dve", STT(tB[par][:, 0:wp], ub[:, 1:1 + wp], ckT[:, 8 + fc:9 + fc], tA[par][:, 0:wp], ALU.mult, ALU.add),
                   reads=[W_(ur, 0, 516), tAr] + CONST, writes=[tBr], extra=ex)
                OP("dve", STT(tA[par][:, 0:wp], ub[:, 2:2 + wp], ckT[:, 16 + fc:17 + fc], tB[par][:, 0:wp], ALU.mult, ALU.add),
                   reads=[W_(ur, 0, 516), tBr] + CONST, writes=[tAr])
                OP("dve", TT(vT[:, fc, a:a + wp], tA[par][:, 0:wp], PS[bo + 0][:, 0:wp], ALU.mult),
                   reads=[tAr, W_(r_ps[bo + 0], 0, 512)], writes=[W_(R("vT"), fc * NT + a, fc * NT + a + wp)])
                if has_s:
                    OP("pool", CP(uoT[:, fc, 0:2], ub[:, wp:wp + 2]), reads=[W_(ur, 2, 516)], writes=[W_(R("uoT"), 0, 1)])
                    pv = uprevT[:, fc, :].rearrange("p (b j) -> p b j", j=2)
                    OP("pool", CP(usb[:, :, 0:2], pv), reads=[W_(R("uprevT"))], writes=[W_(R("usb"), 0, 1)], extra=ex)
                    c3 = c_sb[par][:, wp:wp + NS].rearrange("p (b t) -> p b t", t=4)
                    z3 = PS[bo + 2][:, wp:wp + NS].rearrange("p (b t) -> p b t", t=4)
                    b3 = PS[bo + 0][:, wp:wp + NS].rearrange("p (b t) -> p b t", t=4)
                    OP("dve", TT(usb[:, :, 2:6], c3, z3, ALU.mult),
                       reads=[cr, W_(r_ps[bo + 2], 0, 512)], writes=[W_(R("usb"), 1, 2)])
                    t3a = tA[par][:, 0:NS].rearrange("p (b t) -> p b t", t=4)
                    t3b = tB[par][:, 0:NS].rearrange("p (b t) -> p b t", t=4)
                    OP("dve", TS(t3a, usb[:, :, 0:4], ckT[:, fc:fc + 1], None, ALU.mult),
                       reads=[W_(R("usb"), 0, 2)] + CONST, writes=[tAr])
                    OP("dve", STT(t3b, usb[:, :, 1:5], ckT[:, 8 + fc:9 + fc], t3a, ALU.mult, ALU.add),
                       reads=[W_(R("usb"), 0, 2), tAr] + CONST, writes=[tBr])
                    OP("dve", STT(t3a, usb[:, :, 2:6], ckT[:, 16 + fc:17 + fc], t3b, ALU.mult, ALU.add),
                       reads=[W_(R("usb"), 0, 2), tBr] + CONST, writes=[tAr])
                    v3 = vT[:, fc, NP_:NT].rearrange("p (b t) -> p b t", t=4)
                    OP("dve", TT(v3, t3a, b3, ALU.mult),
                       reads=[tAr, W_(r_ps[bo + 0], 0, 512)], writes=[W_(R("vT"), fc * NT + NP_, fc * NT + NT)])
                    uo3 = uoT[:, fc, 2:34].rearrange("p (b j) -> p b j", j=2)
                    OP("pool", CP(uo3, usb[:, :, 4:6]), reads=[W_(R("usb"), 1, 2)], writes=[W_(R("uoT"), 1, 2)])
        for (a, b) in TILES0:
            w = b - a
            for d_ in range(DC):
                ob = 4 + d_ % 2
                for fc in range(DC):
                    OP("pe", MM(PS[ob][:, 0:w], Wco[:, fc, d_ * 128:(d_ + 1) * 128], vT[:, fc, a:b], fc == 0, fc == DC - 1),
                       reads=[W_(R("Wco")), W_(R("vT"), fc * NT + a, fc * NT + b)], writes=[W_(r_ps[ob], 0, 512)])
                OP("dve", TT(xT[:, d_, a:b], PS[ob][:, 0:w], xT[:, d_, a:b], ALU.add),
                   reads=[W_(r_ps[ob], 0, 512), W_(r_xT, a, b)], writes=[W_(r_xT, a, b)])
        for hlf in range(2):
            bank = 6 + hlf
            for cc in range(4):
                c = hlf * 4 + cc
                OP("pe", TR(PS[bank][0:34, cc * 128:(cc + 1) * 128], uoT[:, c, :], ident[:, :]),
                   reads=[W_(R("uoT"), 0, 2)] + CONST, writes=[W_(r_ps[bank], 0, 512)])
            OP("act", ACT(ustage[:, hlf * 512:(hlf + 1) * 512], PS[bank][0:34, :], AF.Copy),
               reads=[W_(r_ps[bank], 0, 512)], writes=[W_(R("ustage"), hlf, hlf + 1)], extra=gather(["c_sb0", "c_sb1"]))
        out_events.append(DMA("sp", DM(up_d, ustage[0:2, :]), "uout", reads=[W_(R("ustage"), 0, 2)]))
        out_events.append(DMA("sp", DM(us_d, ustage[2:34, :]), "uout", reads=[W_(R("ustage"), 0, 2)]))
        dbg_dump("conv")

        conv_evs = gather(["Wbcz0", "Wbcz1", "c_sb0", "c_sb1", "tA0", "tA1", "tB0", "tB1", "usb", "vT", "Wco",
                           "ubuf0", "ubuf1", "ustage"])
        ffn(0, 1, 2, TILES0, phase_extra=conv_evs)
        dbg_dump("ffn2")

        ffn_evs = gather(FFN_BUFS)
        conv2_evs = gather(["vT", "Wco", "ubuf0", "ubuf1"])
        DMA("pool", DM(wkv[:], w_kv.rearrange("(c p) n -> p c n", p=128)), "wkv", writes=[W_(R("wkv"))], extra=ffn_evs)
        DMA("sp", DM(cosT[:], cos_d), "tabs", writes=[W_(R("tabs"), 0, 1)], extra=conv2_evs)
        DMA("sp", DM(sinT[:], sin_d), "tabs", writes=[W_(R("tabs"), 1, 2)], extra=conv2_evs)
        TABS = W_(R("tabs"), 0, 2)

        def rope_dve(ps_raw, w, a, mbuf, mres, cbuf, cres_, extra=()):
            OP("dve", TT(mbuf[:, 0:w], PS[ps_raw][:, 0:w], sinT[:, a:a + w], ALU.mult),
               reads=[W_(r_ps[ps_raw], 0, 512), TABS], writes=[mres], extra=extra)
            OP("dve", TT(cbuf[:, 0:w], PS[ps_raw][:, 0:w], cosT[:, a:a + w], ALU.mult),
               reads=[W_(r_ps[ps_raw], 0, 512), TABS], writes=[cres_], extra=extra)

        def rope_pe(ps_rot, w, mbuf, mres, cbuf, cres_):
            OP("pe", MM(PS[ps_rot][:, 0:w], swapm[:, :], mbuf[:, 0:w], True, False),
               reads=[mres] + CONST, writes=[W_(r_ps[ps_rot], 0, 512)])
            OP("pe", MM(PS[ps_rot][:, 0:w], identb[:, :], cbuf[:, 0:w], False, True),
               reads=[cres_] + CONST, writes=[W_(r_ps[ps_rot], 0, 512)])

        def rope_chunk(ps_raw, ps_rot, w, a, mbuf, mres, cbuf, cres_, extra=()):
            rope_dve(ps_raw, w, a, mbuf, mres, cbuf, cres_, extra)
            rope_pe(ps_rot, w, mbuf, mres, cbuf, cres_)

        dbg_dump("kv_a")
        kv_ctr = 0
        norm_tile(3, TILES0[0][0], TILES0[0][1], xn, xn_res, TILES0[0][0])
        for ti, (a, b) in enumerate(TILES0):
            w = b - a
            pars = []
            for kc in range(2):
                par = kv_ctr % 2
                kv_ctr += 1
                pars.append(par)
                pr = 0 + par
                for c in range(DC):
                    OP("pe", MM(PS[pr][:, 0:w], wkv[:, c, kc * 128:(kc + 1) * 128], xn[:, c, a:b], c == 0, c == DC - 1),
                       reads=[W_(R("wkv")), W_(r_xn, a, b)], writes=[W_(r_ps[pr], 0, 512)])
            part2 = None
            if ti + 1 < len(TILES0):
                norm_tile(3, TILES0[ti + 1][0], TILES0[ti + 1][1], xn, xn_res, TILES0[ti + 1][0])
            for kc in range(2):
                par = pars[kc]
                pr, pt = 0 + par, 2 + par
                rope_chunk(pr, pt, w, a, kraw[par], W_(R("kraw%d" % par)), kcb[par], W_(R("kcb%d" % par)), extra=ffn_evs)
                OP("act", ACT(KT[:, kc, a:b], PS[pt][:, 0:w], AF.Copy),
                   reads=[W_(r_ps[pt], 0, 512)], writes=[W_(R("KT"), kc * NT + a, kc * NT + b)], extra=conv2_evs)
                lo, hi = max(a, NP_ - 128), min(b, NP_)
                if lo < hi:
                    OP("act", ACT(KoutT[:, kc, lo - (NP_ - 128):hi - (NP_ - 128)], PS[pt][:, lo - a:hi - a], AF.Copy),
                       reads=[W_(r_ps[pt], 0, 512)], writes=[W_(R("KoutT"), kc * 2, kc * 2 + 1)], extra=ffn_evs)
                if b > NP_:
                    OP("act", ACT(KoutT[:, kc, 128:192], PS[pt][:, NP_ - a:NT - a], AF.Copy),
                       reads=[W_(r_ps[pt], 0, 512)], writes=[W_(R("KoutT"), kc * 2 + 1, kc * 2 + 2)], extra=ffn_evs)
            if part2 is not None:
                part2()
        dbg_dump("kv_k")
        VBLK = [2] + [HALO + 128 * m for m in range(16)] + [HALO + 1796, HALO + 1924]
        for bi, c0 in enumerate(VBLK):
            bank = 4 + bi % 2
            for c in range(DC):
                OP("pe", MM(PS[bank][:, 0:256], xn[:, c, c0:c0 + 128], wkv[:, c, 256:512], c == 0, c == DC - 1),
                   reads=[W_(R("wkv")), W_(r_xn, c0, c0 + 128)], writes=[W_(r_ps[bank], 0, 512)])
            OP("act", ACT(Vb[:, bi, :], PS[bank][:, 0:256], AF.Copy),
               reads=[W_(r_ps[bank], 0, 512)], writes=[W_(R("Vb"), bi, bi + 1)], extra=conv2_evs)
            if bi == 18:
                OP("dve", CP(vstage[:, :], PS[bank][:, 0:256]),
                   reads=[W_(r_ps[bank], 0, 512)], writes=[W_(R("vstage"))], extra=ffn_evs)
                out_events.append(DMA("sp", DM(vp_d, vstage[:, :]), "vpo", reads=[W_(R("vstage"))]))
        dbg_dump("kv_v")
        for c in range(DC):
            OP("pe", MM(PS[4][0:NS, 0:256], xn[:, c, NP_:NT], wkv[:, c, 256:512], c == 0, c == DC - 1),
               reads=[W_(R("wkv")), W_(r_xn, NP_, NT)], writes=[W_(r_ps[4], 0, 512)])
        OP("dve", CP(vsstage[:, :], PS[4][0:NS, 0:256]),
           reads=[W_(r_ps[4], 0, 512)], writes=[W_(R("vsstage"))], extra=ffn_evs)
        for sb_ in range(NSEQ):
            out_events.append(DMA("sp", DM(vs_d[sb_, 124:128, :], vsstage[4 * sb_:4 * sb_ + 4, :]), "vso",
                                  reads=[W_(R("vsstage"))]))
        OP("act", ACT(Vs_bf[:, :], PS[4][0:NS, 0:256], AF.Copy),
           reads=[W_(r_ps[4], 0, 512)], writes=[W_(R("Vnew"))], extra=conv2_evs)
        dbg_dump("kv_vs")
        for kc in range(2):
            OP("pe", TR(PS[6][:, kc * 128:(kc + 1) * 128], KoutT[:, kc, 0:128], ident[:, :]),
               reads=[W_(R("KoutT"), 0, 4)] + CONST, writes=[W_(r_ps[6], 0, 512)])
        OP("dve", CP(kstage[:, :], PS[6][:, 0:256]), reads=[W_(r_ps[6], 0, 512)], writes=[W_(R("kstage"))], extra=ffn_evs)
        out_events.append(DMA("sp", DM(kp_d, kstage[:, :]), "kpo", reads=[W_(R("kstage"))]))
        for kc in range(2):
            OP("pe", TR(PS[7][0:NS, kc * 128:(kc + 1) * 128], KoutT[:, kc, 128:192], ident[:, :]),
               reads=[W_(R("KoutT"), 0, 4)] + CONST, writes=[W_(r_ps[7], 0, 512)])
        OP("dve", CP(ksstage[:, :], PS[7][0:NS, 0:256]), reads=[W_(r_ps[7], 0, 512)], writes=[W_(R("ksstage"))], extra=ffn_evs)
        for sb_ in range(NSEQ):
            out_events.append(DMA("sp", DM(ks_d[sb_, 124:128, :], ksstage[4 * sb_:4 * sb_ + 4, :]), "kso",
                                  reads=[W_(R("ksstage"))]))

        dbg_dump("kv")
        kv_evs = gather(["wkv", "kraw0", "kraw1", "kt1_0", "kt1_1", "kt2_0", "kt2_1", "kt3_0", "kt3_1", "kcb0", "kcb1", "KoutT",
                         "kstage", "vstage", "ksstage", "vsstage"])
        ffn(1, 0, 4, TILES1, phase_extra=kv_evs)
        dbg_dump("ffn3")

        ffn_evs = gather(FFN_BUFS)
        xn_evs = r_xn.all_events()
        for jp_ in range(DC):
            gc_, i4 = jp_ // 4, jp_ % 4
            for half in range(2):
                g = 2 * gc_ + half
                src = w_q[:, g * 256 + i4 * 64:g * 256 + (i4 + 1) * 64].rearrange("(c p) m -> p c m", p=128)
                c0_ = jp_ * 128 + half * 64
                DMA("pool", DM(Wq[:, :, c0_:c0_ + 64], src), "Wq%d" % jp_,
                    writes=[W_(R("Wq"), 2 * jp_ + half, 2 * jp_ + half + 1)], extra=ffn_evs)
        wi = 16
        for gc in range(2):
            for half in range(2):
                g = 2 * gc + half
                src2 = w_o[g * 256:(g + 1) * 256, :].rearrange("(i p) n -> p i n", p=64)
                dst2 = Wo2[half * 64:(half + 1) * 64, gc * 4:(gc + 1) * 4, :]
                DMA("pool", DM(dst2, src2), "Wo2", writes=[W_(R("Wq"), wi, wi + 1)], extra=ffn_evs)
                wi += 1
        WQR = W_(R("Wq"), 0, wi)
        WOR = W_(R("Wq"), 16, wi)
        ESK = W_(R("esink"))
        OP("act", ACT(esink[:, :], esink[:, :], AF.Exp), reads=[ESK], writes=[ESK])

        ATILES = [(HALO + 512 * i, HALO + 512 * (i + 1)) for i in range(4)] + [(NP_ - 128, NT)]
        QTR = W_(R("QT"))
        KTR = W_(R("KT"), 0, 2 * NT)
        VBR = W_(R("Vb"), 0, 19)
        u_ctr = [0]
        seq_ctr = [0]

        def unit_prompt(jp, qc0, qa, vprev_i, vcur_i, mask_ap):
            u = u_ctr[0]
            u_ctr[0] += 1
            par = u % 3
            zpar = u % 2
            gc = jp // 4
            xb = 2 * par
            zb = 6 + zpar
            ptr = W_(R("PT%d" % par)) if par < 2 else W_(R("qraw0"))
            lr = W_(R("lnD%d" % zpar))

            def A():
                for (bank, base) in ((xb, 0), (xb + 1, 64)):
                    OP("pe", MM(PS[bank][:, 0:128], KT[base:base + 64, gc, qa - 128:qa], QT[base:base + 64, jp, qc0:qc0 + 128]),
                       reads=[QTR, KTR], writes=[W_(r_ps[bank], 0, 512)])
                    OP("pe", MM(PS[bank][:, 128:256], KT[base:base + 64, gc, qa:qa + 128], QT[base:base + 64, jp, qc0:qc0 + 128]),
                       reads=[QTR, KTR], writes=[W_(r_ps[bank], 0, 512)])
                OP("act", ACT(PT[par][:, :].rearrange("p (b n) -> p b n", b=2), PSA[:, xb:xb + 2, 0:256], AF.Exp, scale=0.125),
                   reads=[W_(r_ps[xb], 0, 512), W_(r_ps[xb + 1], 0, 512)], writes=[ptr])
                OP("dve", TT(PT[par][:, :], PT[par][:, :], mask_ap, ALU.mult), reads=[ptr] + CONST, writes=[ptr])

            def B():
                for (base, off) in ((0, 0), (64, 256)):
                    g = 2 * gc + (base // 64)
                    OP("pe", MM(PS[zb][base:base + 64, 0:128], Vb[:, vprev_i, g * 64:(g + 1) * 64], PT[par][:, off:off + 128], True, False),
                       reads=[ptr, VBR], writes=[W_(r_ps[zb], 0, 512)])
                    OP("pe", MM(PS[zb][base:base + 64, 0:128], Vb[:, vcur_i, g * 64:(g + 1) * 64], PT[par][:, off + 128:off + 256], False, True),
                       reads=[ptr, VBR], writes=[W_(r_ps[zb], 0, 512)])
                    OP("pe", MM(PS[zb][base:base + 64, 128:256], ones[:, 0:64], PT[par][:, off:off + 128], True, False),
                       reads=[ptr, W_(R("ones"))], writes=[W_(r_ps[zb], 0, 512)])
                    OP("pe", MM(PS[zb][base:base + 64, 128:256], ones[:, 0:64], PT[par][:, off + 128:off + 256], False, True),
                       reads=[ptr, W_(R("ones"))], writes=[W_(r_ps[zb], 0, 512)])
                OP("act", ACT(lnD[zpar][:, 0:128], PS[zb][:, 128:256], AF.Ln, bias=esink[:, jp:jp + 1]),
                   reads=[W_(r_ps[zb], 0, 512), ESK], writes=[lr])
                OP("act", ACT(lnD[zpar][:, 0:128], lnD[zpar][:, 0:128], AF.Exp, scale=-1.0), reads=[lr], writes=[lr])
                OP("dve", TT(attnT[:, jp, qc0:qc0 + 128], PS[zb][:, 0:128], lnD[zpar][:, 0:128], ALU.mult),
                   reads=[W_(r_ps[zb], 0, 512), lr], writes=[W_(R("attnT"))])
            return A, B

        def sample_attention():
            P0, P1 = W_(R("PT0")), W_(R("PT1"))
            z4, z5 = W_(r_ps[4], 0, 512), W_(r_ps[5], 0, 512)
            for jp in range(DC):
                gc = jp // 4
                for (bank, base) in ((0, 0), (1, 64)):
                    OP("pe", MM(PS[bank][0:NS, jp * 64:(jp + 1) * 64], KT[base:base + 64, gc, NP_:NT],
                                QT[base:base + 64, jp, 128:192]),
                       reads=[QTR, KTR], writes=[W_(r_ps[bank], 0, 512)])
            OP("act", ACT(PTC[0:NS, :].rearrange("p (b n) -> p b n", b=2), PSA[0:NS, 0:2, :], AF.Exp, scale=0.125),
               reads=[W_(r_ps[0], 0, 512), W_(r_ps[1], 0, 512)], writes=[P0, P1])
            for k in range(16):
                OP("dve", TT(PTC[0:NS, k * 64:(k + 1) * 64], PTC[0:NS, k * 64:(k + 1) * 64], msamp[0:NS, 64:128], ALU.mult),
                   reads=[P0, P1] + CONST, writes=[P0, P1])
            for (base, hoff) in ((0, 0), (64, 512)):
                for jp in range(DC):
                    g = 2 * (jp // 4) + (base // 64)
                    c0 = hoff + jp * 64
                    OP("pe", MMX(PS[4][base:base + 64, jp * 64:(jp + 1) * 64], Vs_bf[:, g * 64:(g + 1) * 64],
                                 PTC[0:NS, c0:c0 + 64], jp == 0, False),
                       reads=[P0, P1, W_(R("Vnew"))], writes=[z4])
                for jp in range(DC):
                    c0 = hoff + jp * 64
                    OP("pe", MMX(PS[5][base:base + 64, jp * 64:(jp + 1) * 64], ones[0:NS, 0:64],
                                 PTC[0:NS, c0:c0 + 64], jp == 0, False),
                       reads=[P0, P1, W_(R("ones"))], writes=[z5])

            def seq_unit(sb_):
                par = sb_ % 2
                sp = sb_ % 2
                xb = 2 if par == 0 else 0
                ptr = W_(R("PT%d" % par))
                kcr = W_(R("KcT%d" % sp))
                vcr = W_(R("Vc%d" % sp))
                qc0 = 128 + 4 * sb_

                def A():
                    DMA("sp", DM(ckst[sp][:, :], ck[sb_]), "ckst%d" % sp, writes=[W_(R("ckst%d" % sp))])
                    DMA("pool", DM(Vc[sp][:, :], cv[sb_]), "Vc%d" % sp, writes=[vcr])
                    for kc in range(2):
                        OP("pe", TR(PS[6][:, kc * 128:(kc + 1) * 128], ckst[sp][:, kc * 128:(kc + 1) * 128], ident[:, :]),
                           reads=[W_(R("ckst%d" % sp))] + CONST, writes=[W_(r_ps[6], 0, 512)])
                    OP("act", ACT(KcT[sp][:, :, :], PS[6][:, 0:256].rearrange("p (c n) -> p c n", c=2), AF.Copy),
                       reads=[W_(r_ps[6], 0, 512)], writes=[kcr])
                    for (bank, base) in ((xb, 0), (xb + 1, 64)):
                        for jp in range(DC):
                            OP("pe", MM(PS[bank][:, jp * 4:jp * 4 + 4], KcT[sp][base:base + 64, jp // 4, :],
                                        QT[base:base + 64, jp, qc0:qc0 + 4]),
                               reads=[QTR, kcr], writes=[W_(r_ps[bank], 0, 512)])
                    OP("act", ACT(PT[par][:, 0:64].rearrange("p (b n) -> p b n", b=2), PSA[:, xb:xb + 2, 0:32], AF.Exp, scale=0.125),
                       reads=[W_(r_ps[xb], 0, 512), W_(r_ps[xb + 1], 0, 512)], writes=[ptr])
                    OP("dve", TT(PT[par][:, 0:64], PT[par][:, 0:64], msamp[:, 0:64], ALU.mult), reads=[ptr] + CONST, writes=[ptr])

                def B():
                    for (base, hoff) in ((0, 0), (64, 32)):
                        for jp in range(DC):
                            g = 2 * (jp // 4) + (base // 64)
                            col = jp * 64 + 4 * sb_
                            OP("pe", MMX(PS[4][base:base + 64, col:col + 4], Vc[sp][:, g * 64:(g + 1) * 64],
                                         PT[par][:, hoff + jp * 4:hoff + jp * 4 + 4], False, False),
                               reads=[ptr, vcr], writes=[z4])
                        for jp in range(DC):
                            col = jp * 64 + 4 * sb_
                            OP("pe", MMX(PS[5][base:base + 64, col:col + 4], ones[:, 0:64],
                                         PT[par][:, hoff + jp * 4:hoff + jp * 4 + 4], False, sb_ == NSEQ - 1),
                               reads=[ptr, W_(R("ones"))], writes=[z5])
                return A, B

            prevB_ = None
            for sb_ in range(NSEQ):
                A_, B_ = seq_unit(sb_)
                A_()
                if prevB_ is not None:
                    prevB_()
                prevB_ = B_
            prevB_()
            lr = W_(R("lnDall"))
            for jp in range(DC):
                OP("dve", TS(lnDall[:, jp * 64:(jp + 1) * 64], PS[5][:, jp * 64:(jp + 1) * 64], esink[:, jp:jp + 1], None, ALU.add),
                   reads=[z5, ESK], writes=[lr])
            OP("act", ACT(lnDall[:, :], lnDall[:, :], AF.Ln), reads=[lr], writes=[lr])
            OP("act", ACT(lnDall[:, :], lnDall[:, :], AF.Exp, scale=-1.0), reads=[lr], writes=[lr])
            OP("dve", TT(attnT[:, :, 128:192], PS[4][:, :].rearrange("p (j q) -> p j q", j=DC),
                         lnDall[:, :].rearrange("p (j q) -> p j q", j=DC), ALU.mult),
               reads=[z4, lr], writes=[W_(R("attnT"))])

        norm_tile(5, ATILES[0][0], ATILES[0][1], xnA, lambda a_, b_: W_(R("xnA")), 0, extra=xn_evs)
        for ti, (a, b) in enumerate(ATILES):
            w = b - a
            def q_tail(jp):
                par = jp % 2
                pt = 2 * (jp % 4) + 1
                rope_pe(pt, w, qraw[par], W_(R("qraw%d" % par)), qcb[par], W_(R("qcb%d" % par)))
                OP("act", ACT(QT[:, jp, 0:w], PS[pt][:, 0:w], AF.Copy), reads=[W_(r_ps[pt], 0, 512)], writes=[QTR])

            for jp in range(DC):
                par = jp % 2
                pr, pt = 2 * (jp % 4), 2 * (jp % 4) + 1
                for c in range(DC):
                    OP("pe", MM(PS[pr][:, 0:w], Wq[:, c, jp * 128:(jp + 1) * 128], xnA[:, c, 0:w], c == 0, c == DC - 1),
                       reads=[W_(R("Wq"), 2 * jp, 2 * jp + 2), W_(R("xnA"))], writes=[W_(r_ps[pr], 0, 512)])
                rope_dve(pr, w, a, qraw[par], W_(R("qraw%d" % par)), qcb[par], W_(R("qcb%d" % par)))
                if jp >= 1:
                    q_tail(jp - 1)
            q_tail(DC - 1)
            units = []
            if ti < 4:
                for bi in range(4):
                    m = ti * 4 + bi
                    for jp in range(DC):
                        units.append(unit_prompt(jp, 128 * bi, a + 128 * bi, m, m + 1, (mfirst if m == 0 else mstd)[:, :]))
            else:
                for jp in range(DC):
                    units.append(unit_prompt(jp, 0, a, 17, 18, mstd[:, :]))
            pend = []
            for ui, (A_, B_) in enumerate(units):
                A_()
                pend.append(B_)
                if len(pend) > 2:
                    pend.pop(0)()
                if ui == 4 and ti + 1 < len(ATILES):
                    norm_tile(5, ATILES[ti + 1][0], ATILES[ti + 1][1], xnA, lambda a_, b_: W_(R("xnA")), 0)
            for B_ in pend:
                B_()
            if ti == 4:
                sample_attention()
            lo = 0 if ti < 4 else 124
            for d_ in range(DC):
                ob = 4 + d_ % 4
                for jp in range(DC):
                    OP("pe", MM(PS[ob][:, 0:w], Wo2[:, jp, d_ * 128:(d_ + 1) * 128], attnT[:, jp, 0:w], jp == 0, jp == DC - 1),
                       reads=[WOR, W_(R("attnT"))], writes=[W_(r_ps[ob], 0, 512)])
                OP("dve", TT(xT[:, d_, a + lo:b], PS[ob][:, lo:w], xT[:, d_, a + lo:b], ALU.add),
                   reads=[W_(r_ps[ob], 0, 512), W_(r_xT, a + lo, b)], writes=[W_(r_xT, a + lo, b)])
        dbg_dump("attn")

        att_evs = gather(["Wq", "xnA", "QT", "attnT", "PT0", "PT1", "qraw0", "qraw1", "qcb0", "qcb1", "qt1_0", "qt1_1", "qt2",
                          "lnD0", "lnD1", "lnDall"])
        ffn(1, 1, 6, TILES1, phase_extra=att_evs)
        dbg_dump("ffn4")

        ffn_evs = gather(FFN_BUFS)
        nblk = (OWN + NS + 127) // 128
        blocks = []
        for bi in range(nblk):
            a = HALO + bi * 128
            b = min(a + 128, NT)
            blocks.append((bi, a, b))

        def fin_norm(bi, a, b):
            par = bi % 2
            yr = W_(R("yblk%d" % par))
            norm_tile(7, a, b, yblk[par], lambda a_, b_, yr=yr: yr, 0, extra=ffn_evs)

        def fin_out(bi, a, b):
            w = b - a
            par = bi % 2
            yr = W_(R("yblk%d" % par))
            for hlf in range(2):
                bank = 4 + hlf + 2 * (bi % 2) - (4 if False else 0)
                bank = [0, 1, 2, 3][2 * (bi % 2) + hlf]
                for cc in range(4):
                    c = hlf * 4 + cc
                    OP("pe", TR(PS[bank][0:w, cc * 128:(cc + 1) * 128], yblk[par][:, c, 0:w], ident[:, :]),
                       reads=[yr] + CONST, writes=[W_(r_ps[bank], 0, 512)])
                eng_copy = "act" if hlf == 0 else "pool_never"
                ys = bi % 4
                OP("act", ACT(ystage[ys][0:w, hlf * 512:(hlf + 1) * 512], PS[bank][0:w, :], AF.Copy),
                   reads=[W_(r_ps[bank], 0, 512)], writes=[W_(R("ystage%d" % ys), hlf, hlf + 1)], extra=ffn_evs)
            out_events.append(DMA("sp", DM(y_d[a - HALO:a - HALO + w, :], ystage[bi % 4][0:w, :]), "yo%d" % (bi % 4),
                                  reads=[W_(R("ystage%d" % (bi % 4)), 0, 2)]))

        fin_norm(*blocks[0])
        for i, blk_ in enumerate(blocks):
            if i + 1 < len(blocks):
                fin_norm(*blocks[i + 1])
            fin_out(*blk_)
    except _Stop:
        pass
    S.op("sp", lambda e: None, waits=out_events)
    S.emit()
    return nc


_NC_CACHE = {}


def _host_tables(core):
    c = core % 4
    p0 = c * OWN
    pos = np.concatenate([np.arange(p0 - HALO, p0 + OWN), 8192 + np.tile(np.arange(4), NSEQ)]).astype(np.int64)
    posf = np.maximum(pos, 0).astype(np.float32)
    inv = (1.0 / (10000.0 ** (np.arange(0, 64, 2, dtype=np.float32) / np.float32(64)))).astype(np.float32)
    ang = (posf[:, None] * inv[None, :]).astype(np.float32)
    cos = np.cos(ang).astype(np.float32)
    sin = np.sin(ang).astype(np.float32)
    p = np.arange(128)
    j = p % 32
    half = (p % 64) // 32
    cosT = np.ascontiguousarray(cos[:, j].T)
    sinT = np.ascontiguousarray((sin[:, j] * np.where(half == 0, 1.0, -1.0)[None, :]).T.astype(np.float32))
    return cosT, sinT


def kernel(x_prompt, x_sample, state_conv, cache_k, cache_v, meta_tokens, norm_g, ffn_w_in, ffn_w_out,
           conv_w_in, conv_kernel, conv_w_out, kv_norm_g, w_kv, w_q, w_o, sinks, final_norm_g):
    f32 = np.float32
    x_prompt = np.asarray(x_prompt, f32)
    x_sample = np.asarray(x_sample, f32)
    B = x_prompt.shape[0]
    if "nc" not in _NC_CACHE:
        _NC_CACHE["nc"] = build_nc()
    nc = _NC_CACHE["nc"]

    def fm(v):
        return np.ascontiguousarray(np.asarray(v, f32).reshape(DC, 128).T)

    ng = np.asarray(norm_g, f32)
    gT = np.concatenate([fm(ng[0, 0]), fm(ng[0, 1]), fm(ng[0, 2]), fm(kv_norm_g),
                         fm(ng[1, 0]), fm(ng[1, 1]), fm(ng[1, 2]), fm(final_norm_g)], axis=1)
    ckn = np.asarray(conv_kernel, f32)[0]
    ckT = np.concatenate([fm(ckn[0]), fm(ckn[1]), fm(ckn[2])], axis=1)
    sk = np.asarray(sinks, f32)[0]
    sinkT = np.zeros((128, 40), f32)
    for jp in range(8):
        gc, i = jp // 4, jp % 4
        sinkT[0:64, jp] = sk[4 * (2 * gc) + i]
        sinkT[64:128, jp] = sk[4 * (2 * gc + 1) + i]
        sinkT[:, 8 + 4 * jp:12 + 4 * jp] = sinkT[:, jp:jp + 1]
    ident = np.eye(128, dtype=f32)
    swapm = np.zeros((128, 128), f32)
    pp = np.arange(128)
    swapm[pp, pp ^ 32] = 1.0
    kk = np.arange(128)[:, None]
    qq = np.arange(128)[None, :]
    prev = (kk >= qq).astype(f32)
    curm = (kk <= qq).astype(f32)
    mstd = np.concatenate([prev, curm, prev, curm], axis=1)
    mfirst0 = np.concatenate([np.zeros_like(prev), curm, np.zeros_like(prev), curm], axis=1)
    ii = np.arange(128)[:, None]
    tt = np.arange(4)[None, :]
    sprev = (ii >= tt).astype(f32)
    scur = ((ii <= tt) & (ii < 4)).astype(f32)
    msamp = np.zeros((128, 128), f32)
    msamp[:, 0:64] = np.tile(sprev, (1, 16))
    kq = np.arange(64)
    msamp[0:64, 64:128] = ((kq[:, None] // 4 == kq[None, :] // 4) & (kq[:, None] % 4 <= kq[None, :] % 4)).astype(f32)

    shared = dict(
        ffn_w_in=np.asarray(ffn_w_in, f32), ffn_w_out=np.asarray(ffn_w_out, f32),
        conv_w_in=np.asarray(conv_w_in, f32)[0], conv_w_out=np.asarray(conv_w_out, f32)[0],
        w_kv=np.asarray(w_kv, f32), w_q=np.asarray(w_q, f32)[0], w_o=np.asarray(w_o, f32)[0],
        gT=gT, ckT=ckT, sinkT=sinkT, ident=ident, swapm=swapm, mstd=mstd, msamp=msamp)
    meta = np.asarray(meta_tokens, f32)
    sc = np.asarray(state_conv, f32)[0]
    ckc = np.asarray(cache_k, f32).reshape(128, 128, 256)
    cvc = np.asarray(cache_v, f32).reshape(128, 128, 256)
    in_maps = []
    for core in range(8):
        bseq, c = core // 4, core % 4
        xfull = np.concatenate([meta, x_prompt[bseq]], axis=0)
        p0 = c * OWN
        lo = p0 - HALO
        rows = np.zeros((NP_, D), f32)
        s0 = max(lo, 0)
        rows[s0 - lo:] = xfull[s0:p0 + OWN]
        xin = np.concatenate([rows, x_sample[16 * core:16 * core + 16].reshape(NS, D)], axis=0)
        cosT, sinT = _host_tables(core)
        m = dict(shared)
        m.update(xin=np.ascontiguousarray(xin),
                 sconv=np.ascontiguousarray(sc[16 * core:16 * core + 16].reshape(2 * NSEQ, D)),
                 ck=np.ascontiguousarray(ckc[16 * core:16 * core + 16]),
                 cv=np.ascontiguousarray(cvc[16 * core:16 * core + 16]),
                 cosT=cosT, sinT=sinT, mfirst=(mfirst0 if c == 0 else mstd))
        in_maps.append(m)
    res = run_bass_kernel_spmd(nc, in_maps, core_ids=list(range(8)))
    R_ = res.results
    _NC_CACHE["last"] = R_
    y_prompt = np.zeros((B, 8192, D), f32)
    y_sample = np.zeros((128, 4, D), f32)
    ncp = np.zeros((1, B, 2, D), f32)
    ncs = np.zeros((1, 128, 2, D), f32)
    nkp = np.zeros((B, 128, 4, 64), f32)
    nvp = np.zeros((B, 128, 4, 64), f32)
    nks = np.zeros((128, 128, 4, 64), f32)
    nvs = np.zeros((128, 128, 4, 64), f32)
    for core in range(8):
        bseq, c = core // 4, core % 4
        r = R_[core]
        y = np.asarray(r["y"])
        yp = y[0:OWN]
        p0 = c * OWN
        if c == 0:
            y_prompt[bseq, 0:OWN - 16] = yp[16:]
        else:
            y_prompt[bseq, p0 - 16:p0 - 16 + OWN] = yp
        y_sample[16 * core:16 * core + 16] = y[OWN:].reshape(16, 4, D)
        ncs[0, 16 * core:16 * core + 16] = np.asarray(r["u_s"]).reshape(16, 2, D)
        nks[16 * core:16 * core + 16] = np.asarray(r["ks"]).reshape(16, 128, 4, 64)
        nvs[16 * core:16 * core + 16] = np.asarray(r["vs"]).reshape(16, 128, 4, 64)
        if c == 3:
            ncp[0, bseq] = np.asarray(r["u_p"])
            nkp[bseq] = np.asarray(r["kp"]).reshape(128, 4, 64)
            nvp[bseq] = np.asarray(r["vp"]).reshape(128, 4, 64)
    return (y_prompt, y_sample, ncp, ncs, nkp, nvp, nks, nvs)
```

```python
import contextlib
import numpy as np
import concourse.bass as bass
import concourse.mybir as mybir
from concourse.alu_op_type import AluOpType as ALU
from concourse.bass_utils import run_bass_kernel_spmd

F32 = mybir.dt.float32
BF16 = mybir.dt.bfloat16
AF = mybir.ActivationFunctionType

D = 1024
DC = 8
DFF = 2816
FCH = 22
HALO = 130
OWN = 2052
NP_ = HALO + OWN
NS = 64
NT = NP_ + NS
NSEQ = 16
EPS = 1e-6
DEBUG = {}

ENGS = ("pe", "act", "dve", "pool", "sp")


class Ev:
    __slots__ = ("kind", "eng", "sem", "count", "needed", "pos")

    def __init__(self, kind, eng):
        self.kind = kind
        self.eng = eng
        self.sem = None
        self.count = None
        self.needed = False
        self.pos = None


class Sched:
    def __init__(self, nc):
        self.nc = nc
        self.q = {e: [] for e in ENGS}
        self.dma_sems = {}
        self.eng_sem = {}

    def _reduce(self, waits, eng):
        best = {}
        for w in waits:
            if w is None:
                continue
            if w.kind == 'c':
                k = ('c', w.eng)
                if k not in best or w.pos > best[k].pos:
                    best[k] = w
            else:
                k = ('d', w.sem)
                if k not in best or w.count > best[k].count:
                    best[k] = w
        ws = list(best.values())
        for w in ws:
            w.needed = True
        return ws

    def op(self, eng, fn, waits=()):
        ev = Ev('c', eng)
        ev.pos = len(self.q[eng])
        self.q[eng].append((fn, self._reduce(waits, eng), ev))
        return ev

    def dma(self, eng, fn, key, waits=()):
        ev = Ev('d', eng)
        ent = self.dma_sems.setdefault(key, [None, 0])
        ent[1] += 16
        ev.sem = key
        ev.count = ent[1]
        ev.pos = len(self.q[eng])
        self.q[eng].append((fn, self._reduce(waits, eng), ev))
        return ev

    def emit(self):
        nc = self.nc
        with contextlib.ExitStack() as st:
            for e in ENGS:
                self.eng_sem[e] = st.enter_context(nc.semaphore("s_" + e))
            for i, key in enumerate(self.dma_sems):
                self.dma_sems[key][0] = st.enter_context(nc.semaphore("d%d" % i))
            for e in ENGS:
                c = 0
                for (fn, ws, ev) in self.q[e]:
                    if ev.kind == 'c' and ev.needed:
                        c += 1
                        ev.count = c
                    if ev.kind == 'd' and str(ev.sem).startswith("const"):
                        ev.count = self.dma_sems[ev.sem][1]
            block = st.enter_context(nc.Block())
            sched = self

            def run(engname):
                def body(eh):
                    waited = {}
                    for (fn, ws, ev) in sched.q[engname]:
                        for w in ws:
                            if w.kind == 'c':
                                sem = sched.eng_sem[w.eng]
                                k = ('c', w.eng)
                            else:
                                sem = sched.dma_sems[w.sem][0]
                                k = ('d', w.sem)
                            if waited.get(k, 0) >= w.count:
                                continue
                            waited[k] = w.count
                            eh.wait_ge(sem, w.count)
                        ins = fn(eh)
                        if ins is None:
                            continue
                        if ev.kind == 'c':
                            if ev.needed:
                                ins.then_inc(sched.eng_sem[engname], 1)
                        else:
                            ins.then_inc(sched.dma_sems[ev.sem][0], 16)
                return body

            block.tensor(run("pe"))
            block.scalar(run("act"))
            block.vector(run("dve"))
            block.gpsimd(run("pool"))
            block.sync(run("sp"))


class RR:
    def __init__(self):
        self.segs = []

    def _cut(self, x):
        for i, s in enumerate(self.segs):
            if s[0] < x < s[1]:
                self.segs[i:i + 1] = [[s[0], x, list(s[2]), list(s[3])], [x, s[1], list(s[2]), list(s[3])]]
                return

    def cover(self, a, b):
        self._cut(a)
        self._cut(b)
        self.segs.sort(key=lambda s: s[0])
        pts = a
        new = []
        for s in self.segs:
            if s[1] <= a or s[0] >= b:
                continue
            if s[0] > pts:
                new.append([pts, s[0], [], []])
            pts = s[1]
        if pts < b:
            new.append([pts, b, [], []])
        self.segs += new
        self.segs.sort(key=lambda s: s[0])
        return [s for s in self.segs if s[0] >= a and s[1] <= b]

    def all_events(self):
        out = []
        for s in self.segs:
            out += s[2] + s[3]
        return out


class _Stop(Exception):
    pass


class K:
    pass


def build_nc():
    nc = bass.Bass("TRN2", target_bir_lowering=False)
    S = Sched(nc)

    def din(name, shape):
        return nc.dram_tensor(name, list(shape), F32, kind="ExternalInput").ap()

    def dout(name, shape):
        return nc.dram_tensor(name, list(shape), F32, kind="ExternalOutput").ap()

    xin = din("xin", [NT, D])
    sconv = din("sconv", [2 * NSEQ, D])
    ck = din("ck", [NSEQ, 128, 256])
    cv = din("cv", [NSEQ, 128, 256])
    ffn_w_in = din("ffn_w_in", [2, 2, D, 2 * DFF])
    ffn_w_out = din("ffn_w_out", [2, 2, DFF, D])
    conv_w_in = din("conv_w_in", [D, 3 * D])
    conv_w_out = din("conv_w_out", [D, D])
    w_kv = din("w_kv", [D, 512])
    w_q = din("w_q", [D, D])
    w_o = din("w_o", [D, D])
    gT_d = din("gT", [128, 64])
    ckT_d = din("ckT", [128, 24])
    sinkT_d = din("sinkT", [128, 40])
    cos_d = din("cosT", [128, NT])
    sin_d = din("sinT", [128, NT])
    ident_d = din("ident", [128, 128])
    swap_d = din("swapm", [128, 128])
    mstd_d = din("mstd", [128, 512])
    mfirst_d = din("mfirst", [128, 512])
    msamp_d = din("msamp", [128, 128])

    y_d = dout("y", [OWN + NS, D])
    up_d = dout("u_p", [2, D])
    us_d = dout("u_s", [2 * NSEQ, D])
    kp_d = dout("kp", [128, 256])
    vp_d = dout("vp", [128, 256])
    ks_d = dout("ks", [NSEQ, 128, 256])
    vs_d = dout("vs", [NSEQ, 128, 256])
    dbg_d = None
    if DEBUG.get("xT"):
        dbg_d = dout("dbg", [128, DC * NT])

    base = (nc._sbuf_addr_for_side('left') + 63) // 64 * 64
    cap = 229376
    cur = [base]

    def esize(dt):
        return 4 if dt == F32 else 2

    def alloc(name, shape, dt, at=None):
        n = 1
        for d_ in shape[1:]:
            n *= d_
        nbytes = (n * esize(dt) + 63) // 64 * 64
        if at is None:
            o = cur[0]
            cur[0] += nbytes
        else:
            o = at
        assert o + nbytes <= cap, (name, o, nbytes)
        return nc.alloc_sbuf_tensor_at(name, list(shape), dt, offset=o)

    xT = alloc("xT", [128, DC, NT], F32)
    XN_OFF = cur[0]
    xn = alloc("xn", [128, DC, NT], BF16)
    ident = alloc("ident", [128, 128], F32)
    swapm = alloc("swapm", [128, 128], BF16)
    identb = alloc("identb", [128, 128], BF16)
    ones = alloc("ones", [128, 128], BF16)
    gT = alloc("gT", [128, 64], F32)
    ckT = alloc("ckT", [128, 24], F32)
    esink = alloc("esink", [128, 40], F32)
    epst = alloc("epst", [128, 1], F32)
    mstd = alloc("mstd", [128, 512], BF16)
    mfirst = alloc("mfirst", [128, 512], BF16)
    msamp = alloc("msamp", [128, 128], BF16)
    uprevT = alloc("uprevT", [128, DC, 2 * NSEQ], F32)
    uoT = alloc("uoT", [128, DC, 34], F32)
    rstd = [alloc("rstd%d" % i, [128, 512], F32) for i in range(2)]
    rtmp = alloc("rtmp", [128, 512], F32)
    ptmp = alloc("ptmp", [128, 512], F32)
    sq = alloc("sq", [128, DC, 512], BF16)
    ARENA = cur[0]
    arena_size = cap - ARENA
    assert arena_size >= 82200, arena_size

    A0 = ARENA
    Wg = [alloc("Wg%d" % s, [128, DC, 256], BF16, at=A0 + s * 12288) for s in range(2)]
    Wu = [alloc("Wu%d" % s, [128, DC, 256], BF16, at=A0 + s * 12288 + 4096) for s in range(2)]
    Wo_ = [alloc("Wo%d" % s, [128, 2, 1024], BF16, at=A0 + s * 12288 + 8192) for s in range(2)]
    hbuf = [[alloc("h%d%d" % (p, f), [128, 512], BF16, at=A0 + 24576 + (2 * p + f) * 1024) for f in range(2)]
            for p in range(2)]
    sbuf_ = [alloc("s%d" % f, [128, 512], F32, at=A0 + 28672 + f * 2048) for f in range(2)]
    NXS = 8
    xs = [alloc("xs%d" % i, [128, D], F32, at=A0 + 32768 + i * 4096) for i in range(NXS)]
    Wbcz = [alloc("Wbcz%d" % s, [128, 3, DC, 128], BF16, at=A0 + s * 6144) for s in range(2)]
    c_sb = [alloc("c_sb%d" % i, [128, 512], F32, at=A0 + 12288 + i * 2048) for i in range(2)]
    tA = [alloc("tA%d" % i, [128, 512], F32, at=A0 + 16384 + i * 2048) for i in range(2)]
    tB = [alloc("tB%d" % i, [128, 512], F32, at=A0 + 20480 + i * 2048) for i in range(2)]
    usb = alloc("usb", [128, NSEQ, 6], F32, at=A0 + 24576)
    Wq = alloc("Wq", [128, DC, 1024], BF16, at=A0)
    Wo2 = alloc("Wo2", [128, DC, 1024], BF16, at=A0 + 16384)
    B0 = A0 + 32768
    vT = alloc("vT", [128, DC, NT], BF16, at=A0 + 29824)
    Wco = alloc("Wco", [128, DC, 1024], BF16, at=A0 + 29824 + 35968)
    ubuf = [alloc("ubuf%d" % i, [128, 516], F32, at=A0 + 25600 + i * 2112) for i in range(2)]
    assert A0 + 29824 + 35968 + 16384 <= cap
    KT = alloc("KT", [128, 2, NT], BF16, at=B0)
    Vb = alloc("Vb", [128, 19, 256], BF16, at=B0 + 8992)
    Vs_bf = alloc("Vs_bf", [NS, 256], BF16, at=B0 + 8992 + 9728)
    C0 = B0 + 8992 + 9728 + 8192
    cosT = alloc("cosT", [128, NT], F32, at=C0)
    sinT = alloc("sinT", [128, NT], F32, at=C0 + 8992)
    E0 = C0 + 2 * 8992
    wkv = alloc("wkv", [128, DC, 512], BF16, at=A0)
    kraw = [alloc("kraw%d" % i, [128, 512], BF16, at=A0 + 8192 + i * 1024) for i in range(2)]
    kt1 = [alloc("kt1_%d" % i, [128, 512], F32, at=A0 + 10240 + i * 2048) for i in range(2)]
    kcb = [alloc("kcb%d" % i, [128, 512], BF16, at=A0 + 10240 + i * 1024) for i in range(2)]
    kt2 = [alloc("kt2_%d" % i, [128, 512], F32, at=A0 + 14336 + i * 2048) for i in range(2)]
    kt3 = [alloc("kt3_%d" % i, [128, 512], F32, at=A0 + 18432 + i * 2048) for i in range(2)]
    KoutT = alloc("KoutT", [128, 2, 192], F32, at=A0 + 22528)
    kstage = alloc("kstage", [128, 256], F32, at=A0 + 24576)
    vstage = alloc("vstage", [128, 256], F32, at=A0 + 25600)
    ksstage = alloc("ksstage", [64, 256], F32, at=A0 + 26624)
    vsstage = alloc("vsstage", [64, 256], F32, at=A0 + 27648)
    ckst = [alloc("ckst%d" % i, [128, 256], F32, at=E0 + i * 1024) for i in range(2)]
    KcT = [alloc("KcT%d" % i, [128, 2, 128], BF16, at=E0 + 2048 + i * 512) for i in range(2)]
    Vc = [alloc("Vc%d" % i, [128, 256], BF16, at=E0 + 3072 + i * 512) for i in range(2)]
    ustage = alloc("ustage", [34, D], F32, at=A0 + 12288)
    assert E0 + 4096 <= cap, (E0, cap)
    xnA = alloc("xnA", [128, DC, 512], BF16, at=XN_OFF)
    QT = alloc("QT", [128, DC, 512], BF16, at=XN_OFF + 8192)
    attnT = alloc("attnT", [128, DC, 512], BF16, at=XN_OFF + 16384)
    PT = [alloc("PT%d" % i, [128, 512], BF16, at=XN_OFF + 24576 + i * 1024) for i in range(2)]
    qraw = [alloc("qraw%d" % i, [128, 512], BF16, at=XN_OFF + 26624 + i * 1024) for i in range(2)]
    qt1 = [alloc("qt1_%d" % i, [128, 512], F32, at=XN_OFF + 28672 + i * 2048) for i in range(2)]
    qt2 = alloc("qt2", [128, 512], F32, at=XN_OFF + 32768)
    qcb = [alloc("qcb%d" % i, [128, 512], BF16, at=XN_OFF + 28672 + i * 1024) for i in range(2)]
    PTC = alloc("PTC", [128, 1024], BF16, at=XN_OFF + 24576)
    PT.append(alloc("PT2", [128, 512], BF16, at=XN_OFF + 26624))
    lnDall = alloc("lnDall", [128, 512], F32, at=XN_OFF + 30720)
    lnD = [alloc("lnD%d" % i, [128, 128], F32, at=XN_OFF + 34816 + i * 512) for i in range(2)]
    assert XN_OFF + 34816 + 1024 <= XN_OFF + DC * NT * 2
    yblk = [alloc("yblk%d" % i, [128, DC, 256], F32, at=A0 + i * 8192) for i in range(2)]
    ystage = [alloc("ystage%d" % i, [128, D], F32, at=A0 + 16384 + i * 4096) for i in range(4)]

    PSA = nc.alloc_psum_tensor("psa", [128, 8, 512], F32)

    class _Bank:
        def __init__(self, i):
            self.i = i

        def __getitem__(self, idx):
            return PSA[idx[0], self.i, idx[1]]

    PS = [_Bank(i) for i in range(8)]

    class Res(RR):
        pass

    r_xT = RR()
    r_xn = RR()
    r_ps = [RR() for _ in range(8)]
    for r_ in r_ps:
        r_.excl = True
    r_const = RR()
    res_cache = {}

    def R(name):
        if name not in res_cache:
            res_cache[name] = RR()
        return res_cache[name]

    def W_(res, a=0, b=1):
        return (res, a, b)

    def deps_for(eng, reads, writes, extra):
        deps = []
        for (r, a, b) in reads:
            for s in r.cover(a, b):
                for ev in s[2]:
                    deps.append(ev)
                if getattr(r, "excl", False):
                    for ev in s[3]:
                        if not (ev.kind == 'c' and ev.eng == eng):
                            deps.append(ev)
        for (r, a, b) in writes:
            for s in r.cover(a, b):
                if not s[3]:
                    for ev in s[2]:
                        if not (ev.kind == 'c' and ev.eng == eng):
                            deps.append(ev)
                for ev in s[3]:
                    if not (ev.kind == 'c' and ev.eng == eng):
                        deps.append(ev)
        deps += [e for e in extra if e is not None]
        if eng == "pe":
            deps = [e for e in deps if not (e.kind == 'c' and e.eng == "pe")]
        return deps

    def commit(ev, reads, writes):
        for (r, a, b) in reads:
            for s in r.cover(a, b):
                s[3].append(ev)
        for (r, a, b) in writes:
            for s in r.cover(a, b):
                s[2] = [ev]
                s[3] = []

    def OP(eng, fn, reads=(), writes=(), extra=()):
        ev = S.op(eng, fn, deps_for(eng, reads, writes, extra))
        commit(ev, reads, writes)
        return ev

    def DMA(eng, fn, key, reads=(), writes=(), extra=()):
        ev = S.dma(eng, fn, key, deps_for("dma_" + eng, reads, writes, extra))
        commit(ev, reads, writes)
        return ev

    out_events = []

    def MM(out, lhsT, rhs, start=True, stop=True):
        return lambda e: e.matmul(out, lhsT, rhs, start=start, stop=stop)

    def MMX(out, lhsT, rhs, start=True, stop=True):
        return lambda e: e.matmul(out, lhsT, rhs, start=start, stop=stop, skip_group_check=True)

    def TR(out, in_, idn):
        return lambda e: e.transpose(out, in_, idn)

    def ACT(out, in_, func, bias=None, scale=None):
        kw = {}
        if bias is not None:
            kw["bias"] = bias
        if scale is not None:
            kw["scale"] = scale
        return lambda e: e.activation(out, in_, func, **kw)

    def TT(out, in0, in1, op):
        return lambda e: e.tensor_tensor(out, in0, in1, op)

    def TS(out, in0, s1, s2, op0, op1=None):
        if op1 is None:
            return lambda e: e.tensor_scalar(out, in0, s1, None, op0)
        return lambda e: e.tensor_scalar(out, in0, s1, s2, op0, op1)

    def STT(out, in0, scalar, in1, op0, op1):
        return lambda e: e.scalar_tensor_tensor(out, in0, scalar, in1, op0, op1)

    def CP(out, in_):
        return lambda e: e.tensor_copy(out, in_)

    def MS(ap, val):
        return lambda e: e.memset(ap, val)

    def RCP(out, in_):
        return lambda e: e.reciprocal(out, in_)

    def DM(out, in_):
        return lambda e: e.dma_start(out=out, in_=in_)

    def cres(i):
        return W_(r_const, i, i + 1)
    CONST = [W_(r_const, 0, 16)]
    DMA("sp", DM(ident[:], ident_d), "const", writes=[cres(0)])
    DMA("sp", DM(gT[:], gT_d), "const", writes=[cres(1)])
    DMA("sp", DM(ckT[:], ckT_d), "const", writes=[cres(2)])
    DMA("sp", DM(esink[:], sinkT_d), "const", writes=[W_(R("esink"))])
    DMA("pool", DM(swapm[:], swap_d), "constp", writes=[cres(4)])
    DMA("pool", DM(identb[:], ident_d), "constp", writes=[cres(8)])
    DMA("pool", DM(mstd[:], mstd_d), "constp", writes=[cres(5)])
    DMA("pool", DM(mfirst[:], mfirst_d), "constp", writes=[cres(6)])
    DMA("pool", DM(msamp[:], msamp_d), "constp", writes=[cres(7)])
    OP("dve", MS(ones[:], 1.0), writes=[W_(R("ones"))])
    OP("dve", MS(epst[:], EPS), writes=[W_(R("eps"))])

    ld_ctr = [0]

    def load_block(src_ap, r0, n, dst_fn, rdst):
        s_ = ld_ctr[0] % NXS
        ld_ctr[0] += 1
        xr = W_(R("xs%d" % s_))
        DMA("sp", DM(xs[s_][0:n, :], src_ap[r0:r0 + n, :]), "xs" + str(s_), writes=[xr])
        for hlf in range(2):
            bank = 6 + hlf
            for cc in range(4):
                c = hlf * 4 + cc
                OP("pe", TR(PS[bank][:, cc * 128:cc * 128 + n], xs[s_][0:n, c * 128:(c + 1) * 128], ident[0:n, 0:n]),
                   reads=[xr] + CONST, writes=[W_(r_ps[bank], cc * 128, cc * 128 + 128)])
            src = PS[bank][:, :].rearrange("p (c n) -> p c n", c=4)[:, :, 0:n]
            if hlf == 0:
                OP("dve", CP(dst_fn(hlf * 4, r0, n), src), reads=[W_(r_ps[bank], 0, 512)], writes=[rdst(r0, n)])
            else:
                OP("act", ACT(dst_fn(hlf * 4, r0, n), src, AF.Copy), reads=[W_(r_ps[bank], 0, 512)],
                   writes=[rdst(r0, n)])

    NBLK_X = (NT + 127) // 128
    x_loaded = [0]

    def load_x_upto(col_end):
        while x_loaded[0] < NBLK_X and x_loaded[0] * 128 < col_end:
            r0 = x_loaded[0] * 128
            n = min(128, NT - r0)
            load_block(xin, r0, n, lambda c0, r0_, n_: xT[:, c0:c0 + 4, r0_:r0_ + n_], lambda r0_, n_: W_(r_xT, r0_, r0_ + n_))
            x_loaded[0] += 1

    def split_tiles(a, b, n):
        w = b - a
        base_w = (w // n) // 2 * 2
        rem = w - base_w * n
        out = []
        x = a
        for i in range(n):
            ww = base_w + (2 if i < rem // 2 else 0)
            if i == n - 1:
                ww = b - x
            out.append((x, x + ww))
            x += ww
        return out

    TILES0 = split_tiles(0, NT, 5)
    TILES1 = [(max(a, HALO), b) for (a, b) in TILES0]
    norm_ctr = [0]

    def norm_tile(nidx, a, b, dst, dst_res, dst_off, extra=(), defer=False):
        w = b - a
        p6 = W_(r_ps[6], 0, 512)
        for hf in range(2):
            OP("act", ACT(sq[:, 4 * hf:4 * hf + 4, 0:w], xT[:, 4 * hf:4 * hf + 4, a:b], AF.Square),
               reads=[W_(r_xT, a, b)], writes=[W_(R("sq"), hf, hf + 1)])
        for c in range(DC):
            OP("pe", MM(PS[6][:, 0:w], ones[:, :], sq[:, c, 0:w], c == 0, c == DC - 1),
               reads=[W_(R("sq"), c // 4, c // 4 + 1), W_(R("ones"))], writes=[p6])
        OP("act", ACT(PS[6][:, 0:w], PS[6][:, 0:w], AF.Ln, bias=epst[:, 0:1], scale=1.0 / D),
           reads=[p6, W_(R("eps"))], writes=[p6])
        OP("act", ACT(PS[6][:, 0:w], PS[6][:, 0:w], AF.Exp, scale=-0.5), reads=[p6], writes=[p6])

        def part2():
            for c in range(DC):
                OP("dve", STT(dst[:, c, dst_off:dst_off + w], xT[:, c, a:b], gT[:, nidx * 8 + c:nidx * 8 + c + 1],
                              PS[6][:, 0:w], ALU.mult, ALU.mult),
                   reads=[W_(r_xT, a, b), p6] + CONST, writes=[dst_res(a, b)], extra=extra)
        if defer:
            return part2
        part2()

    def xn_res(a, b):
        return W_(r_xn, a, b)

    grp_ctr = [0]

    def ffn(l, i, nidx, tiles, phase_extra=(), pre_tile=None):
        w_in = ffn_w_in[l, i]
        w_out = ffn_w_out[l, i]
        pending = [None]
        it = [0]

        def flush():
            if pending[0] is not None:
                for st_ in range(4):
                    pending[0](st_)
                pending[0] = None

        ngrp = FCH // 2
        for gi in range(ngrp):
            s = grp_ctr[0] % 2
            grp_ctr[0] += 1
            f0 = gi * 2
            wr = R("W%d" % s)
            wres = W_(wr, 0, 3)
            ex = phase_extra if gi < 2 else ()
            DMA("pool", DM(Wg[s][:], w_in[:, f0 * 128:f0 * 128 + 256].rearrange("(c p) n -> p c n", p=128)),
                "W%d" % s, writes=[W_(wr, 0, 1)], extra=ex)
            DMA("pool", DM(Wu[s][:], w_in[:, DFF + f0 * 128:DFF + f0 * 128 + 256].rearrange("(c p) n -> p c n", p=128)),
                "W%d" % s, writes=[W_(wr, 1, 2)], extra=ex)
            DMA("pool", DM(Wo_[s][:], w_out[f0 * 128:f0 * 128 + 256, :].rearrange("(f p) n -> p f n", p=128)),
                "W%d" % s, writes=[W_(wr, 2, 3)], extra=ex)
            for ti, (a, b) in enumerate(tiles):
                if gi == 0:
                    if ti == 0:
                        if pre_tile is not None:
                            pre_tile(0)
                        norm_tile(nidx, a, b, xn, xn_res, a, extra=phase_extra)
                    if ti + 1 < len(tiles):
                        a2, b2 = tiles[ti + 1]
                        if pre_tile is not None:
                            pre_tile(ti + 1)
                        norm_tile(nidx, a2, b2, xn, xn_res, a2, extra=phase_extra)
                w = b - a
                par = it[0] % 2
                it[0] += 1
                prev = pending[0]
                pending[0] = None
                step = [0]

                def prev_pair():
                    if prev is not None:
                        prev(step[0])
                    step[0] += 1

                for fi in range(2):
                    gb, ub = 2 * fi, 2 * fi + 1
                    for c in range(DC):
                        OP("pe", MM(PS[gb][:, 0:w], Wg[s][:, c, fi * 128:(fi + 1) * 128], xn[:, c, a:b], c == 0, c == DC - 1),
                           reads=[wres, W_(r_xn, a, b)], writes=[W_(r_ps[gb], 0, 512)])
                    OP("act", ACT(sbuf_[fi][:, 0:w], PS[gb][:, 0:w], AF.Silu),
                       reads=[W_(r_ps[gb], 0, 512)], writes=[W_(R("s%d" % fi))])
                    prev_pair()
                    for c in range(DC):
                        OP("pe", MM(PS[ub][:, 0:w], Wu[s][:, c, fi * 128:(fi + 1) * 128], xn[:, c, a:b], c == 0, c == DC - 1),
                           reads=[wres, W_(r_xn, a, b)], writes=[W_(r_ps[ub], 0, 512)])
                    OP("dve", TT(hbuf[par][fi][:, 0:w], sbuf_[fi][:, 0:w], PS[ub][:, 0:w], ALU.mult),
                       reads=[W_(R("s%d" % fi)), W_(r_ps[ub], 0, 512)], writes=[W_(R("h%d%d" % (par, fi)))])
                    prev_pair()

                def wout(stepi, a=a, b=b, w=w, par=par, s=s, wres=wres):
                    for d_ in (2 * stepi, 2 * stepi + 1):
                        ob = (4, 5, 7)[d_ % 3]
                        for fi in range(2):
                            OP("pe", MM(PS[ob][:, 0:w], Wo_[s][:, fi, d_ * 128:(d_ + 1) * 128], hbuf[par][fi][:, 0:w],
                                        fi == 0, fi == 1),
                               reads=[wres, W_(R("h%d%d" % (par, fi)))], writes=[W_(r_ps[ob], 0, 512)])
                        OP("dve", STT(xT[:, d_, a:b], PS[ob][:, 0:w], 0.5, xT[:, d_, a:b], ALU.mult, ALU.add),
                           reads=[W_(r_ps[ob], 0, 512), W_(r_xT, a, b)], writes=[W_(r_xT, a, b)])
                pending[0] = wout
        flush()

    def gather(names):
        evs = []
        for n in names:
            if n in res_cache:
                evs += res_cache[n].all_events()
        return evs

    def dbg_dump(tag):
        if dbg_d is not None and DEBUG.get("xT") == tag:
            out_events.append(DMA("sp", DM(dbg_d, xT[:, :, :].rearrange("p c n -> p (c n)")),
                                  "dbg", reads=[W_(r_xT, 0, NT)]))
        if DEBUG.get("stop") == tag:
            raise _Stop()

    try:
        FFN_BUFS = ["W0", "W1", "h00", "h01", "h10", "h11", "s0", "s1"]
        dbg_dump("phase0")
        def pre0(ti):
            load_x_upto(TILES0[ti][1])
            if ti == len(TILES0) - 1:
                out_events.append(DMA("sp", DM(ks_d[:, 0:124, :], ck[:, 4:128, :]), "cachecp"))
                out_events.append(DMA("sp", DM(vs_d[:, 0:124, :], cv[:, 4:128, :]), "cachecp"))
                load_block(sconv, 0, 2 * NSEQ, lambda c0, r0_, n_: uprevT[:, c0:c0 + 4, r0_:r0_ + n_],
                           lambda r0_, n_: W_(R("uprevT")))
        ffn(0, 0, 0, TILES0, pre_tile=pre0)
        dbg_dump("ffn1")

        ffn_evs = gather(FFN_BUFS + ["xs%d" % i for i in range(8)])
        DMA("pool", DM(Wco[:], conv_w_out.rearrange("(c p) n -> p c n", p=128)), "Wco", writes=[W_(R("Wco"))])
        cw = conv_w_in.rearrange("(c p) (k m) -> p k c m", p=128, k=3)
        def conv_w_dma(fc_):
            s_ = fc_ % 2
            DMA("pool", DM(Wbcz[s_][:], cw[:, :, :, fc_ * 128:(fc_ + 1) * 128]), "Wbcz%d" % s_,
                writes=[W_(R("Wbcz%d" % s_))], extra=ffn_evs if fc_ < 2 else ())
        conv_w_dma(0)
        for fc in range(DC):
            s = fc % 2
            wres = W_(R("Wbcz%d" % s))
            if fc + 1 < DC:
                conv_w_dma(fc + 1)
            OP("pool", MS(ubuf[0][:, 0:2], 0.0), writes=[W_(R("ubuf0"), 0, 2)])
            for ti, (a, b) in enumerate(TILES0):
                if fc == 0:
                    if ti == 0:
                        norm_tile(1, a, b, xn, xn_res, a)
                    if ti + 1 < len(TILES0):
                        norm_tile(1, TILES0[ti + 1][0], TILES0[ti + 1][1], xn, xn_res, TILES0[ti + 1][0])
                w = b - a
                par = ti % 2
                bo = 3 * ((fc * len(TILES0) + ti) % 2)
                pb = min(b, NP_)
                wp = pb - a
                has_s = b > NP_
                for k in range(3):
                    for c in range(DC):
                        OP("pe", MM(PS[bo + k][:, 0:w], Wbcz[s][:, k, c, :], xn[:, c, a:b], c == 0, c == DC - 1),
                           reads=[wres, W_(r_xn, a, b)], writes=[W_(r_ps[bo + k], 0, 512)])
                ex = ffn_evs if (fc == 0 and ti < 2) else ()
                cr = W_(R("c_sb%d" % par))
                tAr = W_(R("tA%d" % par))
                tBr = W_(R("tB%d" % par))
                OP("act", ACT(c_sb[par][:, 0:w], PS[bo + 1][:, 0:w], AF.Copy), reads=[W_(r_ps[bo + 1], 0, 512)], writes=[cr], extra=ex)
                ub = ubuf[par]
                ur = R("ubuf%d" % par)
                OP("dve", TT(ub[:, 2:2 + wp], c_sb[par][:, 0:wp], PS[bo + 2][:, 0:wp], ALU.mult),
                   reads=[cr, W_(r_ps[bo + 2], 0, 512)], writes=[W_(ur, 2, 516)], extra=ex)
                if ti + 1 < len(TILES0):
                    OP("pool", CP(ubuf[1 - par][:, 0:2], ub[:, wp:wp + 2]),
                       reads=[W_(ur, 2, 516)], writes=[W_(R("ubuf%d" % (1 - par)), 0, 2)])
                OP("dve", TS(tA[par][:, 0:wp], ub[:, 0:wp], ckT[:, fc:fc + 1], None, ALU.mult),
                   reads=[W_(ur, 0, 516)] + CONST, writes=[tAr], extra=ex)
                OP("dve", STT(tB[par][:, 0:wp], ub[:, 1:1 + wp], ckT[:, 8 + fc:9 + fc], tA[par][:, 0:wp], ALU.mult, ALU.add),
                   reads=[W_(ur, 0, 516), tAr] + CONST, writes=[tBr], extra=ex)
                OP("dve", STT(tA[par][:, 0:wp], ub[:, 2:2 + wp], ckT[:, 16 + fc:17 + fc], tB[par][:, 0:wp], ALU.mult, ALU.add),
                   reads=[W_(ur, 0, 516), tBr] + CONST, writes=[tAr])
                OP("dve", TT(vT[:, fc, a:a + wp], tA[par][:, 0:wp], PS[bo + 0][:, 0:wp], ALU.mult),
                   reads=[tAr, W_(r_ps[bo + 0], 0, 512)], writes=[W_(R("vT"), fc * NT + a, fc * NT + a + wp)])
                if has_s:
                    OP("pool", CP(uoT[:, fc, 0:2], ub[:, wp:wp + 2]), reads=[W_(ur, 2, 516)], writes=[W_(R("uoT"), 0, 1)])
                    pv = uprevT[:, fc, :].rearrange("p (b j) -> p b j", j=2)
                    OP("pool", CP(usb[:, :, 0:2], pv), reads=[W_(R("uprevT"))], writes=[W_(R("usb"), 0, 1)], extra=ex)
                    c3 = c_sb[par][:, wp:wp + NS].rearrange("p (b t) -> p b t", t=4)
                    z3 = PS[bo + 2][:, wp:wp + NS].rearrange("p (b t) -> p b t", t=4)
                    b3 = PS[bo + 0][:, wp:wp + NS].rearrange("p (b t) -> p b t", t=4)
                    OP("dve", TT(usb[:, :, 2:6], c3, z3, ALU.mult),
                       reads=[cr, W_(r_ps[bo + 2], 0, 512)], writes=[W_(R("usb"), 1, 2)])
                    t3a = tA[par][:, 0:NS].rearrange("p (b t) -> p b t", t=4)
                    t3b = tB[par][:, 0:NS].rearrange("p (b t) -> p b t", t=4)
                    OP("dve", TS(t3a, usb[:, :, 0:4], ckT[:, fc:fc + 1], None, ALU.mult),
                       reads=[W_(R("usb"), 0, 2)] + CONST, writes=[tAr])
                    OP("dve", STT(t3b, usb[:, :, 1:5], ckT[:, 8 + fc:9 + fc], t3a, ALU.mult, ALU.add),
                       reads=[W_(R("usb"), 0, 2), tAr] + CONST, writes=[tBr])
                    OP("dve", STT(t3a, usb[:, :, 2:6], ckT[:, 16 + fc:17 + fc], t3b, ALU.mult, ALU.add),
                       reads=[W_(R("usb"), 0, 2), tBr] + CONST, writes=[tAr])
                    v3 = vT[:, fc, NP_:NT].rearrange("p (b t) -> p b t", t=4)
                    OP("dve", TT(v3, t3a, b3, ALU.mult),
                       reads=[tAr, W_(r_ps[bo + 0], 0, 512)], writes=[W_(R("vT"), fc * NT + NP_, fc * NT + NT)])
                    uo3 = uoT[:, fc, 2:34].rearrange("p (b j) -> p b j", j=2)
                    OP("pool", CP(uo3, usb[:, :, 4:6]), reads=[W_(R("usb"), 1, 2)], writes=[W_(R("uoT"), 1, 2)])
        for (a, b) in TILES0:
            w = b - a
            for d_ in range(DC):
                ob = 4 + d_ % 2
                for fc in range(DC):
                    OP("pe", MM(PS[ob][:, 0:w], Wco[:, fc, d_ * 128:(d_ + 1) * 128], vT[:, fc, a:b], fc == 0, fc == DC - 1),
                       reads=[W_(R("Wco")), W_(R("vT"), fc * NT + a, fc * NT + b)], writes=[W_(r_ps[ob], 0, 512)])
                OP("dve", TT(xT[:, d_, a:b], PS[ob][:, 0:w], xT[:, d_, a:b], ALU.add),
                   reads=[W_(r_ps[ob], 0, 512), W_(r_xT, a, b)], writes=[W_(r_xT, a, b)])
        for hlf in range(2):
            bank = 6 + hlf
            for cc in range(4):
                c = hlf * 4 + cc
                OP("pe", TR(PS[bank][0:34, cc * 128:(cc + 1) * 128], uoT[:, c, :], ident[:, :]),
                   reads=[W_(R("uoT"), 0, 2)] + CONST, writes=[W_(r_ps[bank], 0, 512)])
            OP("act", ACT(ustage[:, hlf * 512:(hlf + 1) * 512], PS[bank][0:34, :], AF.Copy),
               reads=[W_(r_ps[bank], 0, 512)], writes=[W_(R("ustage"), hlf, hlf + 1)], extra=gather(["c_sb0", "c_sb1"]))
        out_events.append(DMA("sp", DM(up_d, ustage[0:2, :]), "uout", reads=[W_(R("ustage"), 0, 2)]))
        out_events.append(DMA("sp", DM(us_d, ustage[2:34, :]), "uout", reads=[W_(R("ustage"), 0, 2)]))
        dbg_dump("conv")

        conv_evs = gather(["Wbcz0", "Wbcz1", "c_sb0", "c_sb1", "tA0", "tA1", "tB0", "tB1", "usb", "vT", "Wco",
                           "ubuf0", "ubuf1", "ustage"])
        ffn(0, 1, 2, TILES0, phase_extra=conv_evs)
        dbg_dump("ffn2")

        ffn_evs = gather(FFN_BUFS)
        conv2_evs = gather(["vT", "Wco", "ubuf0", "ubuf1"])
        DMA("pool", DM(wkv[:], w_kv.rearrange("(c p) n -> p c n", p=128)), "wkv", writes=[W_(R("wkv"))], extra=ffn_evs)
        DMA("sp", DM(cosT[:], cos_d), "tabs", writes=[W_(R("tabs"), 0, 1)], extra=conv2_evs)
        DMA("sp", DM(sinT[:], sin_d), "tabs", writes=[W_(R("tabs"), 1, 2)], extra=conv2_evs)
        TABS = W_(R("tabs"), 0, 2)

        def rope_dve(ps_raw, w, a, mbuf, mres, cbuf, cres_, extra=()):
            OP("dve", TT(mbuf[:, 0:w], PS[ps_raw][:, 0:w], sinT[:, a:a + w], ALU.mult),
               reads=[W_(r_ps[ps_raw], 0, 512), TABS], writes=[mres], extra=extra)
            OP("dve", TT(cbuf[:, 0:w], PS[ps_raw][:, 0:w], cosT[:, a:a + w], ALU.mult),
               reads=[W_(r_ps[ps_raw], 0, 512), TABS], writes=[cres_], extra=extra)

        def rope_pe(ps_rot, w, mbuf, mres, cbuf, cres_):
            OP("pe", MM(PS[ps_rot][:, 0:w], swapm[:, :], mbuf[:, 0:w], True, False),
               reads=[mres] + CONST, writes=[W_(r_ps[ps_rot], 0, 512)])
            OP("pe", MM(PS[ps_rot][:, 0:w], identb[:, :], cbuf[:, 0:w], False, True),
               reads=[cres_] + CONST, writes=[W_(r_ps[ps_rot], 0, 512)])

        def rope_chunk(ps_raw, ps_rot, w, a, mbuf, mres, cbuf, cres_, extra=()):
            rope_dve(ps_raw, w, a, mbuf, mres, cbuf, cres_, extra)
            rope_pe(ps_rot, w, mbuf, mres, cbuf, cres_)

        dbg_dump("kv_a")
        kv_ctr = 0
        norm_tile(3, TILES0[0][0], TILES0[0][1], xn, xn_res, TILES0[0][0])
        for ti, (a, b) in enumerate(TILES0):
            w = b - a
            pars = []
            for kc in range(2):
                par = kv_ctr % 2
                kv_ctr += 1
                pars.append(par)
                pr = 0 + par
                for c in range(DC):
                    OP("pe", MM(PS[pr][:, 0:w], wkv[:, c, kc * 128:(kc + 1) * 128], xn[:, c, a:b], c == 0, c == DC - 1),
                       reads=[W_(R("wkv")), W_(r_xn, a, b)], writes=[W_(r_ps[pr], 0, 512)])
            part2 = None
            if ti + 1 < len(TILES0):
                norm_tile(3, TILES0[ti + 1][0], TILES0[ti + 1][1], xn, xn_res, TILES0[ti + 1][0])
            for kc in range(2):
                par = pars[kc]
                pr, pt = 0 + par, 2 + par
                rope_chunk(pr, pt, w, a, kraw[par], W_(R("kraw%d" % par)), kcb[par], W_(R("kcb%d" % par)), extra=ffn_evs)
                OP("act", ACT(KT[:, kc, a:b], PS[pt][:, 0:w], AF.Copy),
                   reads=[W_(r_ps[pt], 0, 512)], writes=[W_(R("KT"), kc * NT + a, kc * NT + b)], extra=conv2_evs)
                lo, hi = max(a, NP_ - 128), min(b, NP_)
                if lo < hi:
                    OP("act", ACT(KoutT[:, kc, lo - (NP_ - 128):hi - (NP_ - 128)], PS[pt][:, lo - a:hi - a], AF.Copy),
                       reads=[W_(r_ps[pt], 0, 512)], writes=[W_(R("KoutT"), kc * 2, kc * 2 + 1)], extra=ffn_evs)
                if b > NP_:
                    OP("act", ACT(KoutT[:, kc, 128:192], PS[pt][:, NP_ - a:NT - a], AF.Copy),
                       reads=[W_(r_ps[pt], 0, 512)], writes=[W_(R("KoutT"), kc * 2 + 1, kc * 2 + 2)], extra=ffn_evs)
            if part2 is not None:
                part2()
        dbg_dump("kv_k")
        VBLK = [2] + [HALO + 128 * m for m in range(16)] + [HALO + 1796, HALO + 1924]
        for bi, c0 in enumerate(VBLK):
            bank = 4 + bi % 2
            for c in range(DC):
                OP("pe", MM(PS[bank][:, 0:256], xn[:, c, c0:c0 + 128], wkv[:, c, 256:512], c == 0, c == DC - 1),
                   reads=[W_(R("wkv")), W_(r_xn, c0, c0 + 128)], writes=[W_(r_ps[bank], 0, 512)])
            OP("act", ACT(Vb[:, bi, :], PS[bank][:, 0:256], AF.Copy),
               reads=[W_(r_ps[bank], 0, 512)], writes=[W_(R("Vb"), bi, bi + 1)], extra=conv2_evs)
            if bi == 18:
                OP("dve", CP(vstage[:, :], PS[bank][:, 0:256]),
                   reads=[W_(r_ps[bank], 0, 512)], writes=[W_(R("vstage"))], extra=ffn_evs)
                out_events.append(DMA("sp", DM(vp_d, vstage[:, :]), "vpo", reads=[W_(R("vstage"))]))
        dbg_dump("kv_v")
        for c in range(DC):
            OP("pe", MM(PS[4][0:NS, 0:256], xn[:, c, NP_:NT], wkv[:, c, 256:512], c == 0, c == DC - 1),
               reads=[W_(R("wkv")), W_(r_xn, NP_, NT)], writes=[W_(r_ps[4], 0, 512)])
        OP("dve", CP(vsstage[:, :], PS[4][0:NS, 0:256]),
           reads=[W_(r_ps[4], 0, 512)], writes=[W_(R("vsstage"))], extra=ffn_evs)
        for sb_ in range(NSEQ):
            out_events.append(DMA("sp", DM(vs_d[sb_, 124:128, :], vsstage[4 * sb_:4 * sb_ + 4, :]), "vso",
                                  reads=[W_(R("vsstage"))]))
        OP("act", ACT(Vs_bf[:, :], PS[4][0:NS, 0:256], AF.Copy),
           reads=[W_(r_ps[4], 0, 512)], writes=[W_(R("Vnew"))], extra=conv2_evs)
        dbg_dump("kv_vs")
        for kc in range(2):
            OP("pe", TR(PS[6][:, kc * 128:(kc + 1) * 128], KoutT[:, kc, 0:128], ident[:, :]),
               reads=[W_(R("KoutT"), 0, 4)] + CONST, writes=[W_(r_ps[6], 0, 512)])
        OP("dve", CP(kstage[:, :], PS[6][:, 0:256]), reads=[W_(r_ps[6], 0, 512)], writes=[W_(R("kstage"))], extra=ffn_evs)
        out_events.append(DMA("sp", DM(kp_d, kstage[:, :]), "kpo", reads=[W_(R("kstage"))]))
        for kc in range(2):
            OP("pe", TR(PS[7][0:NS, kc * 128:(kc + 1) * 128], KoutT[:, kc, 128:192], ident[:, :]),
               reads=[W_(R("KoutT"), 0, 4)] + CONST, writes=[W_(r_ps[7], 0, 512)])
        OP("dve", CP(ksstage[:, :], PS[7][0:NS, 0:256]), reads=[W_(r_ps[7], 0, 512)], writes=[W_(R("ksstage"))], extra=ffn_evs)
        for sb_ in range(NSEQ):
            out_events.append(DMA("sp", DM(ks_d[sb_, 124:128, :], ksstage[4 * sb_:4 * sb_ + 4, :]), "kso",
                                  reads=[W_(R("ksstage"))]))

        dbg_dump("kv")
        kv_evs = gather(["wkv", "kraw0", "kraw1", "kt1_0", "kt1_1", "kt2_0", "kt2_1", "kt3_0", "kt3_1", "kcb0", "kcb1", "KoutT",
                         "kstage", "vstage", "ksstage", "vsstage"])
        ffn(1, 0, 4, TILES1, phase_extra=kv_evs)
        dbg_dump("ffn3")

        ffn_evs = gather(FFN_BUFS)
        xn_evs = r_xn.all_events()
        for jp_ in range(DC):
            gc_, i4 = jp_ // 4, jp_ % 4
            for half in range(2):
                g = 2 * gc_ + half
                src = w_q[:, g * 256 + i4 * 64:g * 256 + (i4 + 1) * 64].rearrange("(c p) m -> p c m", p=128)
                c0_ = jp_ * 128 + half * 64
                DMA("pool", DM(Wq[:, :, c0_:c0_ + 64], src), "Wq%d" % jp_,
                    writes=[W_(R("Wq"), 2 * jp_ + half, 2 * jp_ + half + 1)], extra=ffn_evs)
        wi = 16
        for gc in range(2):
            for half in range(2):
                g = 2 * gc + half
                src2 = w_o[g * 256:(g + 1) * 256, :].rearrange("(i p) n -> p i n", p=64)
                dst2 = Wo2[half * 64:(half + 1) * 64, gc * 4:(gc + 1) * 4, :]
                DMA("pool", DM(dst2, src2), "Wo2", writes=[W_(R("Wq"), wi, wi + 1)], extra=ffn_evs)
                wi += 1
        WQR = W_(R("Wq"), 0, wi)
        WOR = W_(R("Wq"), 16, wi)
        ESK = W_(R("esink"))
        OP("act", ACT(esink[:, :], esink[:, :], AF.Exp), reads=[ESK], writes=[ESK])

        ATILES = [(HALO + 512 * i, HALO + 512 * (i + 1)) for i in range(4)] + [(NP_ - 128, NT)]
        QTR = W_(R("QT"))
        KTR = W_(R("KT"), 0, 2 * NT)
        VBR = W_(R("Vb"), 0, 19)
        u_ctr = [0]
        seq_ctr = [0]

        def unit_prompt(jp, qc0, qa, vprev_i, vcur_i, mask_ap):
            u = u_ctr[0]
            u_ctr[0] += 1
            par = u % 3
            zpar = u % 2
            gc = jp // 4
            xb = 2 * par
            zb = 6 + zpar
            ptr = W_(R("PT%d" % par)) if par < 2 else W_(R("qraw0"))
            lr = W_(R("lnD%d" % zpar))

            def A():
                for (bank, base) in ((xb, 0), (xb + 1, 64)):
                    OP("pe", MM(PS[bank][:, 0:128], KT[base:base + 64, gc, qa - 128:qa], QT[base:base + 64, jp, qc0:qc0 + 128]),
                       reads=[QTR, KTR], writes=[W_(r_ps[bank], 0, 512)])
                    OP("pe", MM(PS[bank][:, 128:256], KT[base:base + 64, gc, qa:qa + 128], QT[base:base + 64, jp, qc0:qc0 + 128]),
                       reads=[QTR, KTR], writes=[W_(r_ps[bank], 0, 512)])
                OP("act", ACT(PT[par][:, :].rearrange("p (b n) -> p b n", b=2), PSA[:, xb:xb + 2, 0:256], AF.Exp, scale=0.125),
                   reads=[W_(r_ps[xb], 0, 512), W_(r_ps[xb + 1], 0, 512)], writes=[ptr])
                OP("dve", TT(PT[par][:, :], PT[par][:, :], mask_ap, ALU.mult), reads=[ptr] + CONST, writes=[ptr])

            def B():
                for (base, off) in ((0, 0), (64, 256)):
                    g = 2 * gc + (base // 64)
                    OP("pe", MM(PS[zb][base:base + 64, 0:128], Vb[:, vprev_i, g * 64:(g + 1) * 64], PT[par][:, off:off + 128], True, False),
                       reads=[ptr, VBR], writes=[W_(r_ps[zb], 0, 512)])
                    OP("pe", MM(PS[zb][base:base + 64, 0:128], Vb[:, vcur_i, g * 64:(g + 1) * 64], PT[par][:, off + 128:off + 256], False, True),
                       reads=[ptr, VBR], writes=[W_(r_ps[zb], 0, 512)])
                    OP("pe", MM(PS[zb][base:base + 64, 128:256], ones[:, 0:64], PT[par][:, off:off + 128], True, False),
                       reads=[ptr, W_(R("ones"))], writes=[W_(r_ps[zb], 0, 512)])
                    OP("pe", MM(PS[zb][base:base + 64, 128:256], ones[:, 0:64], PT[par][:, off + 128:off + 256], False, True),
                       reads=[ptr, W_(R("ones"))], writes=[W_(r_ps[zb], 0, 512)])
                OP("act", ACT(lnD[zpar][:, 0:128], PS[zb][:, 128:256], AF.Ln, bias=esink[:, jp:jp + 1]),
                   reads=[W_(r_ps[zb], 0, 512), ESK], writes=[lr])
                OP("act", ACT(lnD[zpar][:, 0:128], lnD[zpar][:, 0:128], AF.Exp, scale=-1.0), reads=[lr], writes=[lr])
                OP("dve", TT(attnT[:, jp, qc0:qc0 + 128], PS[zb][:, 0:128], lnD[zpar][:, 0:128], ALU.mult),
                   reads=[W_(r_ps[zb], 0, 512), lr], writes=[W_(R("attnT"))])
            return A, B

        def sample_attention():
            P0, P1 = W_(R("PT0")), W_(R("PT1"))
            z4, z5 = W_(r_ps[4], 0, 512), W_(r_ps[5], 0, 512)
            for jp in range(DC):
                gc = jp // 4
                for (bank, base) in ((0, 0), (1, 64)):
                    OP("pe", MM(PS[bank][0:NS, jp * 64:(jp + 1) * 64], KT[base:base + 64, gc, NP_:NT],
                                QT[base:base + 64, jp, 128:192]),
                       reads=[QTR, KTR], writes=[W_(r_ps[bank], 0, 512)])
            OP("act", ACT(PTC[0:NS, :].rearrange("p (b n) -> p b n", b=2), PSA[0:NS, 0:2, :], AF.Exp, scale=0.125),
               reads=[W_(r_ps[0], 0, 512), W_(r_ps[1], 0, 512)], writes=[P0, P1])
            for k in range(16):
                OP("dve", TT(PTC[0:NS, k * 64:(k + 1) * 64], PTC[0:NS, k * 64:(k + 1) * 64], msamp[0:NS, 64:128], ALU.mult),
                   reads=[P0, P1] + CONST, writes=[P0, P1])
            for (base, hoff) in ((0, 0), (64, 512)):
                for jp in range(DC):
                    g = 2 * (jp // 4) + (base // 64)
                    c0 = hoff + jp * 64
                    OP("pe", MMX(PS[4][base:base + 64, jp * 64:(jp + 1) * 64], Vs_bf[:, g * 64:(g + 1) * 64],
                                 PTC[0:NS, c0:c0 + 64], jp == 0, False),
                       reads=[P0, P1, W_(R("Vnew"))], writes=[z4])
                for jp in range(DC):
                    c0 = hoff + jp * 64
                    OP("pe", MMX(PS[5][base:base + 64, jp * 64:(jp + 1) * 64], ones[0:NS, 0:64],
                                 PTC[0:NS, c0:c0 + 64], jp == 0, False),
                       reads=[P0, P1, W_(R("ones"))], writes=[z5])

            def seq_unit(sb_):
                par = sb_ % 2
                sp = sb_ % 2
                xb = 2 if par == 0 else 0
                ptr = W_(R("PT%d" % par))
                kcr = W_(R("KcT%d" % sp))
                vcr = W_(R("Vc%d" % sp))
                qc0 = 128 + 4 * sb_

                def A():
                    DMA("sp", DM(ckst[sp][:, :], ck[sb_]), "ckst%d" % sp, writes=[W_(R("ckst%d" % sp))])
                    DMA("pool", DM(Vc[sp][:, :], cv[sb_]), "Vc%d" % sp, writes=[vcr])
                    for kc in range(2):
                        OP("pe", TR(PS[6][:, kc * 128:(kc + 1) * 128], ckst[sp][:, kc * 128:(kc + 1) * 128], ident[:, :]),
                           reads=[W_(R("ckst%d" % sp))] + CONST, writes=[W_(r_ps[6], 0, 512)])
                    OP("act", ACT(KcT[sp][:, :, :], PS[6][:, 0:256].rearrange("p (c n) -> p c n", c=2), AF.Copy),
                       reads=[W_(r_ps[6], 0, 512)], writes=[kcr])
                    for (bank, base) in ((xb, 0), (xb + 1, 64)):
                        for jp in range(DC):
                            OP("pe", MM(PS[bank][:, jp * 4:jp * 4 + 4], KcT[sp][base:base + 64, jp // 4, :],
                                        QT[base:base + 64, jp, qc0:qc0 + 4]),
                               reads=[QTR, kcr], writes=[W_(r_ps[bank], 0, 512)])
                    OP("act", ACT(PT[par][:, 0:64].rearrange("p (b n) -> p b n", b=2), PSA[:, xb:xb + 2, 0:32], AF.Exp, scale=0.125),
                       reads=[W_(r_ps[xb], 0, 512), W_(r_ps[xb + 1], 0, 512)], writes=[ptr])
                    OP("dve", TT(PT[par][:, 0:64], PT[par][:, 0:64], msamp[:, 0:64], ALU.mult), reads=[ptr] + CONST, writes=[ptr])

                def B():
                    for (base, hoff) in ((0, 0), (64, 32)):
                        for jp in range(DC):
                            g = 2 * (jp // 4) + (base // 64)
                            col = jp * 64 + 4 * sb_
                            OP("pe", MMX(PS[4][base:base + 64, col:col + 4], Vc[sp][:, g * 64:(g + 1) * 64],
                                         PT[par][:, hoff + jp * 4:hoff + jp * 4 + 4], False, False),
                               reads=[ptr, vcr], writes=[z4])
                        for jp in range(DC):
                            col = jp * 64 + 4 * sb_
                            OP("pe", MMX(PS[5][base:base + 64, col:col + 4], ones[:, 0:64],
                                         PT[par][:, hoff + jp * 4:hoff + jp * 4 + 4], False, sb_ == NSEQ - 1),
                               reads=[ptr, W_(R("ones"))], writes=[z5])
                return A, B

            prevB_ = None
            for sb_ in range(NSEQ):
                A_, B_ = seq_unit(sb_)
                A_()
                if prevB_ is not None:
                    prevB_()
                prevB_ = B_
            prevB_()
            lr = W_(R("lnDall"))
            for jp in range(DC):
                OP("dve", TS(lnDall[:, jp * 64:(jp + 1) * 64], PS[5][:, jp * 64:(jp + 1) * 64], esink[:, jp:jp + 1], None, ALU.add),
                   reads=[z5, ESK], writes=[lr])
            OP("act", ACT(lnDall[:, :], lnDall[:, :], AF.Ln), reads=[lr], writes=[lr])
            OP("act", ACT(lnDall[:, :], lnDall[:, :], AF.Exp, scale=-1.0), reads=[lr], writes=[lr])
            OP("dve", TT(attnT[:, :, 128:192], PS[4][:, :].rearrange("p (j q) -> p j q", j=DC),
                         lnDall[:, :].rearrange("p (j q) -> p j q", j=DC), ALU.mult),
               reads=[z4, lr], writes=[W_(R("attnT"))])

        norm_tile(5, ATILES[0][0], ATILES[0][1], xnA, lambda a_, b_: W_(R("xnA")), 0, extra=xn_evs)
        for ti, (a, b) in enumerate(ATILES):
            w = b - a
            def q_tail(jp):
                par = jp % 2
                pt = 2 * (jp % 4) + 1
                rope_pe(pt, w, qraw[par], W_(R("qraw%d" % par)), qcb[par], W_(R("qcb%d" % par)))
                OP("act", ACT(QT[:, jp, 0:w], PS[pt][:, 0:w], AF.Copy), reads=[W_(r_ps[pt], 0, 512)], writes=[QTR])

            for jp in range(DC):
                par = jp % 2
                pr, pt = 2 * (jp % 4), 2 * (jp % 4) + 1
                for c in range(DC):
                    OP("pe", MM(PS[pr][:, 0:w], Wq[:, c, jp * 128:(jp + 1) * 128], xnA[:, c, 0:w], c == 0, c == DC - 1),
                       reads=[W_(R("Wq"), 2 * jp, 2 * jp + 2), W_(R("xnA"))], writes=[W_(r_ps[pr], 0, 512)])
                rope_dve(pr, w, a, qraw[par], W_(R("qraw%d" % par)), qcb[par], W_(R("qcb%d" % par)))
                if jp >= 1:
                    q_tail(jp - 1)
            q_tail(DC - 1)
            units = []
            if ti < 4:
                for bi in range(4):
                    m = ti * 4 + bi
                    for jp in range(DC):
                        units.append(unit_prompt(jp, 128 * bi, a + 128 * bi, m, m + 1, (mfirst if m == 0 else mstd)[:, :]))
            else:
                for jp in range(DC):
                    units.append(unit_prompt(jp, 0, a, 17, 18, mstd[:, :]))
            pend = []
            for ui, (A_, B_) in enumerate(units):
                A_()
                pend.append(B_)
                if len(pend) > 2:
                    pend.pop(0)()
                if ui == 4 and ti + 1 < len(ATILES):
                    norm_tile(5, ATILES[ti + 1][0], ATILES[ti + 1][1], xnA, lambda a_, b_: W_(R("xnA")), 0)
            for B_ in pend:
                B_()
            if ti == 4:
                sample_attention()
            lo = 0 if ti < 4 else 124
            for d_ in range(DC):
                ob = 4 + d_ % 4
                for jp in range(DC):
                    OP("pe", MM(PS[ob][:, 0:w], Wo2[:, jp, d_ * 128:(d_ + 1) * 128], attnT[:, jp, 0:w], jp == 0, jp == DC - 1),
                       reads=[WOR, W_(R("attnT"))], writes=[W_(r_ps[ob], 0, 512)])
                OP("dve", TT(xT[:, d_, a + lo:b], PS[ob][:, lo:w], xT[:, d_, a + lo:b], ALU.add),
                   reads=[W_(r_ps[ob], 0, 512), W_(r_xT, a + lo, b)], writes=[W_(r_xT, a + lo, b)])
        dbg_dump("attn")

        att_evs = gather(["Wq", "xnA", "QT", "attnT", "PT0", "PT1", "qraw0", "qraw1", "qcb0", "qcb1", "qt1_0", "qt1_1", "qt2",
                          "lnD0", "lnD1", "lnDall"])
        ffn(1, 1, 6, TILES1, phase_extra=att_evs)
        dbg_dump("ffn4")

        ffn_evs = gather(FFN_BUFS)
        FB = 256
        nblk = (OWN + NS + FB - 1) // FB
        blocks = []
        for bi in range(nblk):
            a = HALO + bi * FB
            b = min(a + FB, NT)
            blocks.append((bi, a, b))

        def fin_norm(bi, a, b):
            par = bi % 2
            yr = W_(R("yblk%d" % par))
            norm_tile(7, a, b, yblk[par], lambda a_, b_, yr=yr: yr, 0, extra=ffn_evs)

        sub_ctr = [0]

        def fin_out(bi, a, b):
            par = bi % 2
            yr = W_(R("yblk%d" % par))
            nsub = (b - a + 127) // 128
            for sub in range(nsub):
                a_s = a + sub * 128
                ws = min(128, b - a_s)
                sc = sub_ctr[0]
                sub_ctr[0] += 1
                ys = sc % 4
                for hlf in range(2):
                    bank = 2 * (sc % 2) + hlf
                    for cc in range(4):
                        c = hlf * 4 + cc
                        OP("pe", TR(PS[bank][0:ws, cc * 128:(cc + 1) * 128], yblk[par][:, c, sub * 128:sub * 128 + ws], ident[:, :]),
                           reads=[yr] + CONST, writes=[W_(r_ps[bank], 0, 512)])
                    OP("act", ACT(ystage[ys][0:ws, hlf * 512:(hlf + 1) * 512], PS[bank][0:ws, :], AF.Copy),
                       reads=[W_(r_ps[bank], 0, 512)], writes=[W_(R("ystage%d" % ys), hlf, hlf + 1)], extra=ffn_evs)
                out_events.append(DMA("sp", DM(y_d[a_s - HALO:a_s - HALO + ws, :], ystage[ys][0:ws, :]), "yo%d" % ys,
                                      reads=[W_(R("ystage%d" % ys), 0, 2)]))

        fin_norm(*blocks[0])
        for i, blk_ in enumerate(blocks):
            if i + 1 < len(blocks):
                fin_norm(*blocks[i + 1])
            fin_out(*blk_)
    except _Stop:
        pass
    S.op("sp", lambda e: None, waits=out_events)
    S.emit()
    return nc


_NC_CACHE = {}


def _host_tables(core):
    c = core % 4
    p0 = c * OWN
    pos = np.concatenate([np.arange(p0 - HALO, p0 + OWN), 8192 + np.tile(np.arange(4), NSEQ)]).astype(np.int64)
    posf = np.maximum(pos, 0).astype(np.float32)
    inv = (1.0 / (10000.0 ** (np.arange(0, 64, 2, dtype=np.float32) / np.float32(64)))).astype(np.float32)
    ang = (posf[:, None] * inv[None, :]).astype(np.float32)
    cos = np.cos(ang).astype(np.float32)
    sin = np.sin(ang).astype(np.float32)
    p = np.arange(128)
    j = p % 32
    half = (p % 64) // 32
    cosT = np.ascontiguousarray(cos[:, j].T)
    sinT = np.ascontiguousarray((sin[:, j] * np.where(half == 0, 1.0, -1.0)[None, :]).T.astype(np.float32))
    return cosT, sinT


def kernel(x_prompt, x_sample, state_conv, cache_k, cache_v, meta_tokens, norm_g, ffn_w_in, ffn_w_out,
           conv_w_in, conv_kernel, conv_w_out, kv_norm_g, w_kv, w_q, w_o, sinks, final_norm_g):
    f32 = np.float32
    x_prompt = np.asarray(x_prompt, f32)
    x_sample = np.asarray(x_sample, f32)
    B = x_prompt.shape[0]
    if "nc" not in _NC_CACHE:
        _NC_CACHE["nc"] = build_nc()
    nc = _NC_CACHE["nc"]

    def fm(v):
        return np.ascontiguousarray(np.asarray(v, f32).reshape(DC, 128).T)

    ng = np.asarray(norm_g, f32)
    gT = np.concatenate([fm(ng[0, 0]), fm(ng[0, 1]), fm(ng[0, 2]), fm(kv_norm_g),
                         fm(ng[1, 0]), fm(ng[1, 1]), fm(ng[1, 2]), fm(final_norm_g)], axis=1)
    ckn = np.asarray(conv_kernel, f32)[0]
    ckT = np.concatenate([fm(ckn[0]), fm(ckn[1]), fm(ckn[2])], axis=1)
    sk = np.asarray(sinks, f32)[0]
    sinkT = np.zeros((128, 40), f32)
    for jp in range(8):
        gc, i = jp // 4, jp % 4
        sinkT[0:64, jp] = sk[4 * (2 * gc) + i]
        sinkT[64:128, jp] = sk[4 * (2 * gc + 1) + i]
        sinkT[:, 8 + 4 * jp:12 + 4 * jp] = sinkT[:, jp:jp + 1]
    ident = np.eye(128, dtype=f32)
    swapm = np.zeros((128, 128), f32)
    pp = np.arange(128)
    swapm[pp, pp ^ 32] = 1.0
    kk = np.arange(128)[:, None]
    qq = np.arange(128)[None, :]
    prev = (kk >= qq).astype(f32)
    curm = (kk <= qq).astype(f32)
    mstd = np.concatenate([prev, curm, prev, curm], axis=1)
    mfirst0 = np.concatenate([np.zeros_like(prev), curm, np.zeros_like(prev), curm], axis=1)
    ii = np.arange(128)[:, None]
    tt = np.arange(4)[None, :]
    sprev = (ii >= tt).astype(f32)
    scur = ((ii <= tt) & (ii < 4)).astype(f32)
    msamp = np.zeros((128, 128), f32)
    msamp[:, 0:64] = np.tile(sprev, (1, 16))
    kq = np.arange(64)
    msamp[0:64, 64:128] = ((kq[:, None] // 4 == kq[None, :] // 4) & (kq[:, None] % 4 <= kq[None, :] % 4)).astype(f32)

    shared = dict(
        ffn_w_in=np.asarray(ffn_w_in, f32), ffn_w_out=np.asarray(ffn_w_out, f32),
        conv_w_in=np.asarray(conv_w_in, f32)[0], conv_w_out=np.asarray(conv_w_out, f32)[0],
        w_kv=np.asarray(w_kv, f32), w_q=np.asarray(w_q, f32)[0], w_o=np.asarray(w_o, f32)[0],
        gT=gT, ckT=ckT, sinkT=sinkT, ident=ident, swapm=swapm, mstd=mstd, msamp=msamp)
    meta = np.asarray(meta_tokens, f32)
    sc = np.asarray(state_conv, f32)[0]
    ckc = np.asarray(cache_k, f32).reshape(128, 128, 256)
    cvc = np.asarray(cache_v, f32).reshape(128, 128, 256)
    in_maps = []
    for core in range(8):
        bseq, c = core // 4, core % 4
        xfull = np.concatenate([meta, x_prompt[bseq]], axis=0)
        p0 = c * OWN
        lo = p0 - HALO
        rows = np.zeros((NP_, D), f32)
        s0 = max(lo, 0)
        rows[s0 - lo:] = xfull[s0:p0 + OWN]
        xin = np.concatenate([rows, x_sample[16 * core:16 * core + 16].reshape(NS, D)], axis=0)
        cosT, sinT = _host_tables(core)
        m = dict(shared)
        m.update(xin=np.ascontiguousarray(xin),
                 sconv=np.ascontiguousarray(sc[16 * core:16 * core + 16].reshape(2 * NSEQ, D)),
                 ck=np.ascontiguousarray(ckc[16 * core:16 * core + 16]),
                 cv=np.ascontiguousarray(cvc[16 * core:16 * core + 16]),
                 cosT=cosT, sinT=sinT, mfirst=(mfirst0 if c == 0 else mstd))
        in_maps.append(m)
    res = run_bass_kernel_spmd(nc, in_maps, core_ids=list(range(8)))
    R_ = res.results
    _NC_CACHE["last"] = R_
    y_prompt = np.zeros((B, 8192, D), f32)
    y_sample = np.zeros((128, 4, D), f32)
    ncp = np.zeros((1, B, 2, D), f32)
    ncs = np.zeros((1, 128, 2, D), f32)
    nkp = np.zeros((B, 128, 4, 64), f32)
    nvp = np.zeros((B, 128, 4, 64), f32)
    nks = np.zeros((128, 128, 4, 64), f32)
    nvs = np.zeros((128, 128, 4, 64), f32)
    for core in range(8):
        bseq, c = core // 4, core % 4
        r = R_[core]
        y = np.asarray(r["y"])
        yp = y[0:OWN]
        p0 = c * OWN
        if c == 0:
            y_prompt[bseq, 0:OWN - 16] = yp[16:]
        else:
            y_prompt[bseq, p0 - 16:p0 - 16 + OWN] = yp
        y_sample[16 * core:16 * core + 16] = y[OWN:].reshape(16, 4, D)
        ncs[0, 16 * core:16 * core + 16] = np.asarray(r["u_s"]).reshape(16, 2, D)
        nks[16 * core:16 * core + 16] = np.asarray(r["ks"]).reshape(16, 128, 4, 64)
        nvs[16 * core:16 * core + 16] = np.asarray(r["vs"]).reshape(16, 128, 4, 64)
        if c == 3:
            ncp[0, bseq] = np.asarray(r["u_p"])
            nkp[bseq] = np.asarray(r["kp"]).reshape(128, 4, 64)
            nvp[bseq] = np.asarray(r["vp"]).reshape(128, 4, 64)
    return (y_prompt, y_sample, ncp, ncs, nkp, nvp, nks, nvs)
```

```python
import contextlib
import numpy as np
import concourse.bass as bass
import concourse.mybir as mybir
from concourse.alu_op_type import AluOpType as ALU
from concourse.bass_utils import run_bass_kernel_spmd

F32 = mybir.dt.float32
BF16 = mybir.dt.bfloat16
AF = mybir.ActivationFunctionType

D = 1024
DC = 8
DFF = 2816
FCH = 22
HALO = 130
OWN = 2052
NP_ = HALO + OWN
NS = 64
NT = NP_ + NS
NSEQ = 16
EPS = 1e-6
DEBUG = {}

ENGS = ("pe", "act", "dve", "pool", "sp")


class Ev:
    __slots__ = ("kind", "eng", "sem", "count", "needed", "pos")

    def __init__(self, kind, eng):
        self.kind = kind
        self.eng = eng
        self.sem = None
        self.count = None
        self.needed = False
        self.pos = None


class Sched:
    def __init__(self, nc):
        self.nc = nc
        self.q = {e: [] for e in ENGS}
        self.dma_sems = {}
        self.eng_sem = {}

    def _reduce(self, waits, eng):
        best = {}
        for w in waits:
            if w is None:
                continue
            if w.kind == 'c':
                k = ('c', w.eng)
                if k not in best or w.pos > best[k].pos:
                    best[k] = w
            else:
                k = ('d', w.sem)
                if k not in best or w.count > best[k].count:
                    best[k] = w
        ws = list(best.values())
        for w in ws:
            w.needed = True
        return ws

    def op(self, eng, fn, waits=()):
        ev = Ev('c', eng)
        ev.pos = len(self.q[eng])
        self.q[eng].append((fn, self._reduce(waits, eng), ev))
        return ev

    def dma(self, eng, fn, key, waits=()):
        ev = Ev('d', eng)
        ent = self.dma_sems.setdefault(key, [None, 0])
        ent[1] += 16
        ev.sem = key
        ev.count = ent[1]
        ev.pos = len(self.q[eng])
        self.q[eng].append((fn, self._reduce(waits, eng), ev))
        return ev

    def emit(self):
        nc = self.nc
        with contextlib.ExitStack() as st:
            for e in ENGS:
                self.eng_sem[e] = st.enter_context(nc.semaphore("s_" + e))
            for i, key in enumerate(self.dma_sems):
                self.dma_sems[key][0] = st.enter_context(nc.semaphore("d%d" % i))
            for e in ENGS:
                c = 0
                for (fn, ws, ev) in self.q[e]:
                    if ev.kind == 'c' and ev.needed:
                        c += 1
                        ev.count = c
                    if ev.kind == 'd' and str(ev.sem).startswith("const"):
                        ev.count = self.dma_sems[ev.sem][1]
            block = st.enter_context(nc.Block())
            sched = self

            def run(engname):
                def body(eh):
                    waited = {}
                    for (fn, ws, ev) in sched.q[engname]:
                        for w in ws:
                            if w.kind == 'c':
                                sem = sched.eng_sem[w.eng]
                                k = ('c', w.eng)
                            else:
                                sem = sched.dma_sems[w.sem][0]
                                k = ('d', w.sem)
                            if waited.get(k, 0) >= w.count:
                                continue
                            waited[k] = w.count
                            eh.wait_ge(sem, w.count)
                        ins = fn(eh)
                        if ins is None:
                            continue
                        if ev.kind == 'c':
                            if ev.needed:
                                ins.then_inc(sched.eng_sem[engname], 1)
                        else:
                            ins.then_inc(sched.dma_sems[ev.sem][0], 16)
                return body

            block.tensor(run("pe"))
            block.scalar(run("act"))
            block.vector(run("dve"))
            block.gpsimd(run("pool"))
            block.sync(run("sp"))


class RR:
    def __init__(self):
        self.segs = []

    def _cut(self, x):
        for i, s in enumerate(self.segs):
            if s[0] < x < s[1]:
                self.segs[i:i + 1] = [[s[0], x, list(s[2]), list(s[3])], [x, s[1], list(s[2]), list(s[3])]]
                return

    def cover(self, a, b):
        self._cut(a)
        self._cut(b)
        self.segs.sort(key=lambda s: s[0])
        pts = a
        new = []
        for s in self.segs:
            if s[1] <= a or s[0] >= b:
                continue
            if s[0] > pts:
                new.append([pts, s[0], [], []])
            pts = s[1]
        if pts < b:
            new.append([pts, b, [], []])
        self.segs += new
        self.segs.sort(key=lambda s: s[0])
        return [s for s in self.segs if s[0] >= a and s[1] <= b]

    def all_events(self):
        out = []
        for s in self.segs:
            out += s[2] + s[3]
        return out


class _Stop(Exception):
    pass


class K:
    pass


def build_nc():
    nc = bass.Bass("TRN2", target_bir_lowering=False)
    S = Sched(nc)

    def din(name, shape):
        return nc.dram_tensor(name, list(shape), F32, kind="ExternalInput").ap()

    def dout(name, shape):
        return nc.dram_tensor(name, list(shape), F32, kind="ExternalOutput").ap()

    xin = din("xin", [NT, D])
    sconv = din("sconv", [2 * NSEQ, D])
    ck = din("ck", [NSEQ, 128, 256])
    cv = din("cv", [NSEQ, 128, 256])
    ffn_w_in = din("ffn_w_in", [2, 2, D, 2 * DFF])
    ffn_w_out = din("ffn_w_out", [2, 2, DFF, D])
    conv_w_in = din("conv_w_in", [D, 3 * D])
    conv_w_out = din("conv_w_out", [D, D])
    w_kv = din("w_kv", [D, 512])
    w_q = din("w_q", [D, D])
    w_o = din("w_o", [D, D])
    gT_d = din("gT", [128, 64])
    ckT_d = din("ckT", [128, 24])
    sinkT_d = din("sinkT", [128, 40])
    cos_d = din("cosT", [128, NT])
    sin_d = din("sinT", [128, NT])
    ident_d = din("ident", [128, 128])
    swap_d = din("swapm", [128, 128])
    mstd_d = din("mstd", [128, 512])
    mfirst_d = din("mfirst", [128, 512])
    msamp_d = din("msamp", [128, 128])

    y_d = dout("y", [OWN + NS, D])
    up_d = dout("u_p", [2, D])
    us_d = dout("u_s", [2 * NSEQ, D])
    kp_d = dout("kp", [128, 256])
    vp_d = dout("vp", [128, 256])
    ks_d = dout("ks", [NSEQ, 128, 256])
    vs_d = dout("vs", [NSEQ, 128, 256])
    dbg_d = None
    if DEBUG.get("xT"):
        dbg_d = dout("dbg", [128, DC * NT])

    base = (nc._sbuf_addr_for_side('left') + 63) // 64 * 64
    cap = 229376
    cur = [base]

    def esize(dt):
        return 4 if dt == F32 else 2

    def alloc(name, shape, dt, at=None):
        n = 1
        for d_ in shape[1:]:
            n *= d_
        nbytes = (n * esize(dt) + 63) // 64 * 64
        if at is None:
            o = cur[0]
            cur[0] += nbytes
        else:
            o = at
        assert o + nbytes <= cap, (name, o, nbytes)
        return nc.alloc_sbuf_tensor_at(name, list(shape), dt, offset=o)

    xT = alloc("xT", [128, DC, NT], F32)
    XN_OFF = cur[0]
    xn = alloc("xn", [128, DC, NT], BF16)
    ident = alloc("ident", [128, 128], F32)
    swapm = alloc("swapm", [128, 128], BF16)
    identb = alloc("identb", [128, 128], BF16)
    ones = alloc("ones", [128, 128], BF16)
    gT = alloc("gT", [128, 64], F32)
    ckT = alloc("ckT", [128, 24], F32)
    esink = alloc("esink", [128, 40], F32)
    epst = alloc("epst", [128, 1], F32)
    mstd = alloc("mstd", [128, 512], BF16)
    mfirst = alloc("mfirst", [128, 512], BF16)
    msamp = alloc("msamp", [128, 128], BF16)
    uprevT = alloc("uprevT", [128, DC, 2 * NSEQ], F32)
    uoT = alloc("uoT", [128, DC, 34], F32)
    rstd = [alloc("rstd%d" % i, [128, 512], F32) for i in range(2)]
    rtmp = alloc("rtmp", [128, 512], F32)
    ptmp = alloc("ptmp", [128, 512], F32)
    sq = alloc("sq", [128, DC, 512], BF16)
    ARENA = cur[0]
    arena_size = cap - ARENA
    assert arena_size >= 82200, arena_size

    A0 = ARENA
    Wg = [alloc("Wg%d" % s, [128, DC, 256], BF16, at=A0 + s * 12288) for s in range(2)]
    Wu = [alloc("Wu%d" % s, [128, DC, 256], BF16, at=A0 + s * 12288 + 4096) for s in range(2)]
    Wo_ = [alloc("Wo%d" % s, [128, 2, 1024], BF16, at=A0 + s * 12288 + 8192) for s in range(2)]
    hbuf = [[alloc("h%d%d" % (p, f), [128, 512], BF16, at=A0 + 24576 + (2 * p + f) * 1024) for f in range(2)]
            for p in range(2)]
    sbuf_ = [alloc("s%d" % f, [128, 512], F32, at=A0 + 28672 + f * 2048) for f in range(2)]
    NXS = 8
    xs = [alloc("xs%d" % i, [128, D], F32, at=A0 + 32768 + i * 4096) for i in range(NXS)]
    Wbcz = [alloc("Wbcz%d" % s, [128, 3, DC, 128], BF16, at=A0 + 12288 + s * 6144) for s in range(2)]
    c_sb = [alloc("c_sb%d" % i, [128, 512], F32, at=A0 + 0 + i * 2048) for i in range(2)]
    tA = [alloc("tA%d" % i, [128, 512], F32, at=A0 + 4096 + i * 2048) for i in range(2)]
    tB = [alloc("tB%d" % i, [128, 512], F32, at=A0 + 8192 + i * 2048) for i in range(2)]
    usb = alloc("usb", [128, NSEQ, 6], F32, at=A0 + 24576)
    Wq = alloc("Wq", [128, DC, 1024], BF16, at=A0)
    Wo2 = alloc("Wo2", [128, DC, 1024], BF16, at=A0 + 16384)
    B0 = A0 + 32768
    vT = alloc("vT", [128, DC, NT], BF16, at=A0 + 29824)
    Wco = alloc("Wco", [128, DC, 1024], BF16, at=A0 + 29824 + 35968)
    ubuf = [alloc("ubuf%d" % i, [128, 516], F32, at=A0 + 25600 + i * 2112) for i in range(2)]
    assert A0 + 29824 + 35968 + 16384 <= cap
    KT = alloc("KT", [128, 2, NT], BF16, at=B0)
    Vb = alloc("Vb", [128, 19, 256], BF16, at=B0 + 8992)
    Vs_bf = alloc("Vs_bf", [NS, 256], BF16, at=B0 + 8992 + 9728)
    C0 = B0 + 8992 + 9728 + 8192
    cosT = alloc("cosT", [128, NT], F32, at=C0)
    sinT = alloc("sinT", [128, NT], F32, at=C0 + 8992)
    E0 = C0 + 2 * 8992
    wkv = alloc("wkv", [128, DC, 512], BF16, at=A0)
    kraw = [alloc("kraw%d" % i, [128, 512], BF16, at=A0 + 8192 + i * 1024) for i in range(2)]
    kt1 = [alloc("kt1_%d" % i, [128, 512], F32, at=A0 + 10240 + i * 2048) for i in range(2)]
    kcb = [alloc("kcb%d" % i, [128, 512], BF16, at=A0 + 10240 + i * 1024) for i in range(2)]
    kt2 = [alloc("kt2_%d" % i, [128, 512], F32, at=A0 + 14336 + i * 2048) for i in range(2)]
    kt3 = [alloc("kt3_%d" % i, [128, 512], F32, at=A0 + 18432 + i * 2048) for i in range(2)]
    KoutT = alloc("KoutT", [128, 2, 192], F32, at=A0 + 22528)
    kstage = alloc("kstage", [128, 256], F32, at=A0 + 24576)
    vstage = alloc("vstage", [128, 256], F32, at=A0 + 25600)
    ksstage = alloc("ksstage", [64, 256], F32, at=A0 + 26624)
    vsstage = alloc("vsstage", [64, 256], F32, at=A0 + 27648)
    ckst = [alloc("ckst%d" % i, [128, 256], F32, at=E0 + i * 1024) for i in range(2)]
    KcT = [alloc("KcT%d" % i, [128, 2, 128], BF16, at=E0 + 2048 + i * 512) for i in range(2)]
    Vc = [alloc("Vc%d" % i, [128, 256], BF16, at=E0 + 3072 + i * 512) for i in range(2)]
    ustage = alloc("ustage", [34, D], F32, at=A0 + 0)
    assert E0 + 4096 <= cap, (E0, cap)
    xnA = alloc("xnA", [128, DC, 512], BF16, at=XN_OFF)
    QT = alloc("QT", [128, DC, 512], BF16, at=XN_OFF + 8192)
    attnT = alloc("attnT", [128, DC, 512], BF16, at=XN_OFF + 16384)
    PT = [alloc("PT%d" % i, [128, 512], BF16, at=XN_OFF + 24576 + i * 1024) for i in range(2)]
    qraw = [alloc("qraw%d" % i, [128, 512], BF16, at=XN_OFF + 26624 + i * 1024) for i in range(2)]
    qt1 = [alloc("qt1_%d" % i, [128, 512], F32, at=XN_OFF + 28672 + i * 2048) for i in range(2)]
    qt2 = alloc("qt2", [128, 512], F32, at=XN_OFF + 32768)
    qcb = [alloc("qcb%d" % i, [128, 512], BF16, at=XN_OFF + 28672 + i * 1024) for i in range(2)]
    PTC = alloc("PTC", [128, 1024], BF16, at=XN_OFF + 24576)
    PT.append(alloc("PT2", [128, 512], BF16, at=XN_OFF + 26624))
    lnDall = alloc("lnDall", [128, 512], F32, at=XN_OFF + 30720)
    lnD = [alloc("lnD%d" % i, [128, 128], F32, at=XN_OFF + 34816 + i * 512) for i in range(2)]
    assert XN_OFF + 34816 + 1024 <= XN_OFF + DC * NT * 2
    yblk = [alloc("yblk%d" % i, [128, DC, 256], F32, at=A0 + i * 8192) for i in range(2)]
    ystage = [alloc("ystage%d" % i, [128, D], F32, at=A0 + 16384 + i * 4096) for i in range(4)]

    PSA = nc.alloc_psum_tensor("psa", [128, 8, 512], F32)

    class _Bank:
        def __init__(self, i):
            self.i = i

        def __getitem__(self, idx):
            return PSA[idx[0], self.i, idx[1]]

    PS = [_Bank(i) for i in range(8)]

    class Res(RR):
        pass

    r_xT = RR()
    r_xn = RR()
    r_ps = [RR() for _ in range(8)]
    for r_ in r_ps:
        r_.excl = True
    r_const = RR()
    res_cache = {}

    def R(name):
        if name not in res_cache:
            res_cache[name] = RR()
        return res_cache[name]

    def W_(res, a=0, b=1):
        return (res, a, b)

    def deps_for(eng, reads, writes, extra):
        deps = []
        for (r, a, b) in reads:
            for s in r.cover(a, b):
                for ev in s[2]:
                    deps.append(ev)
                if getattr(r, "excl", False):
                    for ev in s[3]:
                        if not (ev.kind == 'c' and ev.eng == eng):
                            deps.append(ev)
        for (r, a, b) in writes:
            for s in r.cover(a, b):
                if not s[3]:
                    for ev in s[2]:
                        if not (ev.kind == 'c' and ev.eng == eng):
                            deps.append(ev)
                for ev in s[3]:
                    if not (ev.kind == 'c' and ev.eng == eng):
                        deps.append(ev)
        deps += [e for e in extra if e is not None]
        if eng == "pe":
            deps = [e for e in deps if not (e.kind == 'c' and e.eng == "pe")]
        return deps

    def commit(ev, reads, writes):
        for (r, a, b) in reads:
            for s in r.cover(a, b):
                s[3].append(ev)
        for (r, a, b) in writes:
            for s in r.cover(a, b):
                s[2] = [ev]
                s[3] = []

    def OP(eng, fn, reads=(), writes=(), extra=()):
        ev = S.op(eng, fn, deps_for(eng, reads, writes, extra))
        commit(ev, reads, writes)
        return ev

    def DMA(eng, fn, key, reads=(), writes=(), extra=()):
        ev = S.dma(eng, fn, key, deps_for("dma_" + eng, reads, writes, extra))
        commit(ev, reads, writes)
        return ev

    out_events = []

    def MM(out, lhsT, rhs, start=True, stop=True):
        return lambda e: e.matmul(out, lhsT, rhs, start=start, stop=stop)

    def MMX(out, lhsT, rhs, start=True, stop=True):
        return lambda e: e.matmul(out, lhsT, rhs, start=start, stop=stop, skip_group_check=True)

    def TR(out, in_, idn):
        return lambda e: e.transpose(out, in_, idn)

    def ACT(out, in_, func, bias=None, scale=None):
        kw = {}
        if bias is not None:
            kw["bias"] = bias
        if scale is not None:
            kw["scale"] = scale
        return lambda e: e.activation(out, in_, func, **kw)

    def TT(out, in0, in1, op):
        return lambda e: e.tensor_tensor(out, in0, in1, op)

    def TS(out, in0, s1, s2, op0, op1=None):
        if op1 is None:
            return lambda e: e.tensor_scalar(out, in0, s1, None, op0)
        return lambda e: e.tensor_scalar(out, in0, s1, s2, op0, op1)

    def STT(out, in0, scalar, in1, op0, op1):
        return lambda e: e.scalar_tensor_tensor(out, in0, scalar, in1, op0, op1)

    def CP(out, in_):
        return lambda e: e.tensor_copy(out, in_)

    def MS(ap, val):
        return lambda e: e.memset(ap, val)

    def RCP(out, in_):
        return lambda e: e.reciprocal(out, in_)

    def DM(out, in_):
        return lambda e: e.dma_start(out=out, in_=in_)

    def cres(i):
        return W_(r_const, i, i + 1)
    CONST = [W_(r_const, 0, 16)]
    DMA("sp", DM(ident[:], ident_d), "const", writes=[cres(0)])
    DMA("sp", DM(gT[:], gT_d), "const", writes=[cres(1)])
    DMA("sp", DM(ckT[:], ckT_d), "const", writes=[cres(2)])
    DMA("sp", DM(esink[:], sinkT_d), "const", writes=[W_(R("esink"))])
    DMA("pool", DM(swapm[:], swap_d), "constp", writes=[cres(4)])
    DMA("pool", DM(identb[:], ident_d), "constp", writes=[cres(8)])
    DMA("pool", DM(mstd[:], mstd_d), "constp", writes=[cres(5)])
    DMA("pool", DM(mfirst[:], mfirst_d), "constp", writes=[cres(6)])
    DMA("pool", DM(msamp[:], msamp_d), "constp", writes=[cres(7)])
    OP("dve", MS(ones[:], 1.0), writes=[W_(R("ones"))])
    OP("dve", MS(epst[:], EPS), writes=[W_(R("eps"))])

    ld_ctr = [0]

    def load_block(src_ap, r0, n, dst_fn, rdst):
        s_ = ld_ctr[0] % NXS
        ld_ctr[0] += 1
        xr = W_(R("xs%d" % s_))
        DMA("sp", DM(xs[s_][0:n, :], src_ap[r0:r0 + n, :]), "xs" + str(s_), writes=[xr])
        for hlf in range(2):
            bank = 6 + hlf
            for cc in range(4):
                c = hlf * 4 + cc
                OP("pe", TR(PS[bank][:, cc * 128:cc * 128 + n], xs[s_][0:n, c * 128:(c + 1) * 128], ident[0:n, 0:n]),
                   reads=[xr] + CONST, writes=[W_(r_ps[bank], cc * 128, cc * 128 + 128)])
            src = PS[bank][:, :].rearrange("p (c n) -> p c n", c=4)[:, :, 0:n]
            if hlf == 0:
                OP("dve", CP(dst_fn(hlf * 4, r0, n), src), reads=[W_(r_ps[bank], 0, 512)], writes=[rdst(r0, n)])
            else:
                OP("act", ACT(dst_fn(hlf * 4, r0, n), src, AF.Copy), reads=[W_(r_ps[bank], 0, 512)],
                   writes=[rdst(r0, n)])

    NBLK_X = (NT + 127) // 128
    x_loaded = [0]

    def load_x_upto(col_end):
        while x_loaded[0] < NBLK_X and x_loaded[0] * 128 < col_end:
            r0 = x_loaded[0] * 128
            n = min(128, NT - r0)
            load_block(xin, r0, n, lambda c0, r0_, n_: xT[:, c0:c0 + 4, r0_:r0_ + n_], lambda r0_, n_: W_(r_xT, r0_, r0_ + n_))
            x_loaded[0] += 1

    def split_tiles(a, b, n):
        w = b - a
        base_w = (w // n) // 2 * 2
        rem = w - base_w * n
        out = []
        x = a
        for i in range(n):
            ww = base_w + (2 if i < rem // 2 else 0)
            if i == n - 1:
                ww = b - x
            out.append((x, x + ww))
            x += ww
        return out

    TILES0 = split_tiles(0, NT, 5)
    TILES1 = [(max(a, HALO), b) for (a, b) in TILES0]
    norm_ctr = [0]

    def norm_tile(nidx, a, b, dst, dst_res, dst_off, extra=(), defer=False):
        w = b - a
        p6 = W_(r_ps[6], 0, 512)
        for hf in range(2):
            OP("act", ACT(sq[:, 4 * hf:4 * hf + 4, 0:w], xT[:, 4 * hf:4 * hf + 4, a:b], AF.Square),
               reads=[W_(r_xT, a, b)], writes=[W_(R("sq"), hf, hf + 1)])
        for c in range(DC):
            OP("pe", MM(PS[6][:, 0:w], ones[:, :], sq[:, c, 0:w], c == 0, c == DC - 1),
               reads=[W_(R("sq"), c // 4, c // 4 + 1), W_(R("ones"))], writes=[p6])
        OP("act", ACT(PS[6][:, 0:w], PS[6][:, 0:w], AF.Ln, bias=epst[:, 0:1], scale=1.0 / D),
           reads=[p6, W_(R("eps"))], writes=[p6])
        OP("act", ACT(PS[6][:, 0:w], PS[6][:, 0:w], AF.Exp, scale=-0.5), reads=[p6], writes=[p6])

        def part2():
            for c in range(DC):
                OP("dve", STT(dst[:, c, dst_off:dst_off + w], xT[:, c, a:b], gT[:, nidx * 8 + c:nidx * 8 + c + 1],
                              PS[6][:, 0:w], ALU.mult, ALU.mult),
                   reads=[W_(r_xT, a, b), p6] + CONST, writes=[dst_res(a, b)], extra=extra)
        if defer:
            return part2
        part2()

    def xn_res(a, b):
        return W_(r_xn, a, b)

    grp_ctr = [0]

    def ffn(l, i, nidx, tiles, phase_extra=(), pre_tile=None, w0_extra=None):
        w_in = ffn_w_in[l, i]
        w_out = ffn_w_out[l, i]
        pending = [None]
        it = [0]

        def flush():
            if pending[0] is not None:
                for st_ in range(4):
                    pending[0](st_)
                pending[0] = None

        ngrp = FCH // 2
        for gi in range(ngrp):
            s = grp_ctr[0] % 2
            grp_ctr[0] += 1
            f0 = gi * 2
            wr = R("W%d" % s)
            wres = W_(wr, 0, 3)
            ex = phase_extra if gi < 2 else ()
            if gi == 0 and w0_extra is not None:
                ex = w0_extra
            DMA("pool", DM(Wg[s][:], w_in[:, f0 * 128:f0 * 128 + 256].rearrange("(c p) n -> p c n", p=128)),
                "W%d" % s, writes=[W_(wr, 0, 1)], extra=ex)
            DMA("pool", DM(Wu[s][:], w_in[:, DFF + f0 * 128:DFF + f0 * 128 + 256].rearrange("(c p) n -> p c n", p=128)),
                "W%d" % s, writes=[W_(wr, 1, 2)], extra=ex)
            DMA("pool", DM(Wo_[s][:], w_out[f0 * 128:f0 * 128 + 256, :].rearrange("(f p) n -> p f n", p=128)),
                "W%d" % s, writes=[W_(wr, 2, 3)], extra=ex)
            for ti, (a, b) in enumerate(tiles):
                if gi == 0:
                    if ti == 0:
                        if pre_tile is not None:
                            pre_tile(0)
                        norm_tile(nidx, a, b, xn, xn_res, a, extra=phase_extra)
                    if ti + 1 < len(tiles):
                        a2, b2 = tiles[ti + 1]
                        if pre_tile is not None:
                            pre_tile(ti + 1)
                        norm_tile(nidx, a2, b2, xn, xn_res, a2, extra=phase_extra)
                w = b - a
                par = it[0] % 2
                it[0] += 1
                prev = pending[0]
                pending[0] = None
                step = [0]

                def prev_pair():
                    if prev is not None:
                        prev(step[0])
                    step[0] += 1

                for fi in range(2):
                    gb, ub = 2 * fi, 2 * fi + 1
                    for c in range(DC):
                        OP("pe", MM(PS[gb][:, 0:w], Wg[s][:, c, fi * 128:(fi + 1) * 128], xn[:, c, a:b], c == 0, c == DC - 1),
                           reads=[wres, W_(r_xn, a, b)], writes=[W_(r_ps[gb], 0, 512)])
                    OP("act", ACT(sbuf_[fi][:, 0:w], PS[gb][:, 0:w], AF.Silu),
                       reads=[W_(r_ps[gb], 0, 512)], writes=[W_(R("s%d" % fi))])
                    prev_pair()
                    for c in range(DC):
                        OP("pe", MM(PS[ub][:, 0:w], Wu[s][:, c, fi * 128:(fi + 1) * 128], xn[:, c, a:b], c == 0, c == DC - 1),
                           reads=[wres, W_(r_xn, a, b)], writes=[W_(r_ps[ub], 0, 512)])
                    OP("dve", TT(hbuf[par][fi][:, 0:w], sbuf_[fi][:, 0:w], PS[ub][:, 0:w], ALU.mult),
                       reads=[W_(R("s%d" % fi)), W_(r_ps[ub], 0, 512)], writes=[W_(R("h%d%d" % (par, fi)))])
                    prev_pair()

                def wout(stepi, a=a, b=b, w=w, par=par, s=s, wres=wres):
                    for d_ in (2 * stepi, 2 * stepi + 1):
                        ob = (4, 5, 7)[d_ % 3]
                        for fi in range(2):
                            OP("pe", MM(PS[ob][:, 0:w], Wo_[s][:, fi, d_ * 128:(d_ + 1) * 128], hbuf[par][fi][:, 0:w],
                                        fi == 0, fi == 1),
                               reads=[wres, W_(R("h%d%d" % (par, fi)))], writes=[W_(r_ps[ob], 0, 512)])
                        OP("dve", STT(xT[:, d_, a:b], PS[ob][:, 0:w], 0.5, xT[:, d_, a:b], ALU.mult, ALU.add),
                           reads=[W_(r_ps[ob], 0, 512), W_(r_xT, a, b)], writes=[W_(r_xT, a, b)])
                pending[0] = wout
        flush()

    def gather(names):
        evs = []
        for n in names:
            if n in res_cache:
                evs += res_cache[n].all_events()
        return evs

    def dbg_dump(tag):
        if dbg_d is not None and DEBUG.get("xT") == tag:
            out_events.append(DMA("sp", DM(dbg_d, xT[:, :, :].rearrange("p c n -> p (c n)")),
                                  "dbg", reads=[W_(r_xT, 0, NT)]))
        if DEBUG.get("stop") == tag:
            raise _Stop()

    try:
        FFN_BUFS = ["W0", "W1", "h00", "h01", "h10", "h11", "s0", "s1"]
        dbg_dump("phase0")
        def pre0(ti):
            load_x_upto(TILES0[ti][1])
            if ti == len(TILES0) - 1:
                out_events.append(DMA("sp", DM(ks_d[:, 0:124, :], ck[:, 4:128, :]), "cachecp"))
                out_events.append(DMA("sp", DM(vs_d[:, 0:124, :], cv[:, 4:128, :]), "cachecp"))
                load_block(sconv, 0, 2 * NSEQ, lambda c0, r0_, n_: uprevT[:, c0:c0 + 4, r0_:r0_ + n_],
                           lambda r0_, n_: W_(R("uprevT")))
        ffn(0, 0, 0, TILES0, pre_tile=pre0)
        dbg_dump("ffn1")

        ffn_evs = gather(FFN_BUFS + ["xs%d" % i for i in range(8)])
        DMA("pool", DM(Wco[:], conv_w_out.rearrange("(c p) n -> p c n", p=128)), "Wco", writes=[W_(R("Wco"))])
        cw = conv_w_in.rearrange("(c p) (k m) -> p k c m", p=128, k=3)
        def conv_w_dma(fc_):
            s_ = fc_ % 2
            DMA("pool", DM(Wbcz[s_][:], cw[:, :, :, fc_ * 128:(fc_ + 1) * 128]), "Wbcz%d" % s_,
                writes=[W_(R("Wbcz%d" % s_))], extra=gather(["W1"]) if fc_ < 2 else ())
        conv_w_dma(0)
        for fc in range(DC):
            s = fc % 2
            wres = W_(R("Wbcz%d" % s))
            if fc + 1 < DC:
                conv_w_dma(fc + 1)
            OP("pool", MS(ubuf[0][:, 0:2], 0.0), writes=[W_(R("ubuf0"), 0, 2)])
            for ti, (a, b) in enumerate(TILES0):
                if fc == 0:
                    if ti == 0:
                        norm_tile(1, a, b, xn, xn_res, a)
                    if ti + 1 < len(TILES0):
                        norm_tile(1, TILES0[ti + 1][0], TILES0[ti + 1][1], xn, xn_res, TILES0[ti + 1][0])
                w = b - a
                par = ti % 2
                bo = 3 * ((fc * len(TILES0) + ti) % 2)
                pb = min(b, NP_)
                wp = pb - a
                has_s = b > NP_
                for k in range(3):
                    for c in range(DC):
                        OP("pe", MM(PS[bo + k][:, 0:w], Wbcz[s][:, k, c, :], xn[:, c, a:b], c == 0, c == DC - 1),
                           reads=[wres, W_(r_xn, a, b)], writes=[W_(r_ps[bo + k], 0, 512)])
                ex = ffn_evs if (fc == 0 and ti < 2) else ()
                cr = W_(R("c_sb%d" % par))
                tAr = W_(R("tA%d" % par))
                tBr = W_(R("tB%d" % par))
                OP("act", ACT(c_sb[par][:, 0:w], PS[bo + 1][:, 0:w], AF.Copy), reads=[W_(r_ps[bo + 1], 0, 512)], writes=[cr], extra=ex)
                ub = ubuf[par]
                ur = R("ubuf%d" % par)
                OP("dve", TT(ub[:, 2:2 + wp], c_sb[par][:, 0:wp], PS[bo + 2][:, 0:wp], ALU.mult),
                   reads=[cr, W_(r_ps[bo + 2], 0, 512)], writes=[W_(ur, 2, 516)], extra=ex)
                if ti + 1 < len(TILES0):
                    OP("pool", CP(ubuf[1 - par][:, 0:2], ub[:, wp:wp + 2]),
                       reads=[W_(ur, 2, 516)], writes=[W_(R("ubuf%d" % (1 - par)), 0, 2)])
                OP("dve", TS(tA[par][:, 0:wp], ub[:, 0:wp], ckT[:, fc:fc + 1], None, ALU.mult),
                   reads=[W_(ur, 0, 516)] + CONST, writes=[tAr], extra=ex)
                OP("dve", STT(tB[par][:, 0:wp], ub[:, 1:1 + wp], ckT[:, 8 + fc:9 + fc], tA[par][:, 0:wp], ALU.mult, ALU.add),
                   reads=[W_(ur, 0, 516), tAr] + CONST, writes=[tBr], extra=ex)
                OP("dve", STT(tA[par][:, 0:wp], ub[:, 2:2 + wp], ckT[:, 16 + fc:17 + fc], tB[par][:, 0:wp], ALU.mult, ALU.add),
                   reads=[W_(ur, 0, 516), tBr] + CONST, writes=[tAr])
                OP("dve", TT(vT[:, fc, a:a + wp], tA[par][:, 0:wp], PS[bo + 0][:, 0:wp], ALU.mult),
                   reads=[tAr, W_(r_ps[bo + 0], 0, 512)], writes=[W_(R("vT"), fc * NT + a, fc * NT + a + wp)])
                if has_s:
                    OP("pool", CP(uoT[:, fc, 0:2], ub[:, wp:wp + 2]), reads=[W_(ur, 2, 516)], writes=[W_(R("uoT"), 0, 1)])
                    pv = uprevT[:, fc, :].rearrange("p (b j) -> p b j", j=2)
                    OP("pool", CP(usb[:, :, 0:2], pv), reads=[W_(R("uprevT"))], writes=[W_(R("usb"), 0, 1)], extra=ex)
                    c3 = c_sb[par][:, wp:wp + NS].rearrange("p (b t) -> p b t", t=4)
                    z3 = PS[bo + 2][:, wp:wp + NS].rearrange("p (b t) -> p b t", t=4)
                    b3 = PS[bo + 0][:, wp:wp + NS].rearrange("p (b t) -> p b t", t=4)
                    OP("dve", TT(usb[:, :, 2:6], c3, z3, ALU.mult),
                       reads=[cr, W_(r_ps[bo + 2], 0, 512)], writes=[W_(R("usb"), 1, 2)])
                    t3a = tA[par][:, 0:NS].rearrange("p (b t) -> p b t", t=4)
                    t3b = tB[par][:, 0:NS].rearrange("p (b t) -> p b t", t=4)
                    OP("dve", TS(t3a, usb[:, :, 0:4], ckT[:, fc:fc + 1], None, ALU.mult),
                       reads=[W_(R("usb"), 0, 2)] + CONST, writes=[tAr])
                    OP("dve", STT(t3b, usb[:, :, 1:5], ckT[:, 8 + fc:9 + fc], t3a, ALU.mult, ALU.add),
                       reads=[W_(R("usb"), 0, 2), tAr] + CONST, writes=[tBr])
                    OP("dve", STT(t3a, usb[:, :, 2:6], ckT[:, 16 + fc:17 + fc], t3b, ALU.mult, ALU.add),
                       reads=[W_(R("usb"), 0, 2), tBr] + CONST, writes=[tAr])
                    v3 = vT[:, fc, NP_:NT].rearrange("p (b t) -> p b t", t=4)
                    OP("dve", TT(v3, t3a, b3, ALU.mult),
                       reads=[tAr, W_(r_ps[bo + 0], 0, 512)], writes=[W_(R("vT"), fc * NT + NP_, fc * NT + NT)])
                    uo3 = uoT[:, fc, 2:34].rearrange("p (b j) -> p b j", j=2)
                    OP("pool", CP(uo3, usb[:, :, 4:6]), reads=[W_(R("usb"), 1, 2)], writes=[W_(R("uoT"), 1, 2)])
        for (a, b) in TILES0:
            w = b - a
            for d_ in range(DC):
                ob = 4 + d_ % 2
                for fc in range(DC):
                    OP("pe", MM(PS[ob][:, 0:w], Wco[:, fc, d_ * 128:(d_ + 1) * 128], vT[:, fc, a:b], fc == 0, fc == DC - 1),
                       reads=[W_(R("Wco")), W_(R("vT"), fc * NT + a, fc * NT + b)], writes=[W_(r_ps[ob], 0, 512)])
                OP("dve", TT(xT[:, d_, a:b], PS[ob][:, 0:w], xT[:, d_, a:b], ALU.add),
                   reads=[W_(r_ps[ob], 0, 512), W_(r_xT, a, b)], writes=[W_(r_xT, a, b)])
        for hlf in range(2):
            bank = 6 + hlf
            for cc in range(4):
                c = hlf * 4 + cc
                OP("pe", TR(PS[bank][0:34, cc * 128:(cc + 1) * 128], uoT[:, c, :], ident[:, :]),
                   reads=[W_(R("uoT"), 0, 2)] + CONST, writes=[W_(r_ps[bank], 0, 512)])
            OP("act", ACT(ustage[:, hlf * 512:(hlf + 1) * 512], PS[bank][0:34, :], AF.Copy),
               reads=[W_(r_ps[bank], 0, 512)], writes=[W_(R("ustage"), hlf, hlf + 1)], extra=gather(["c_sb0", "c_sb1"]))
        out_events.append(DMA("sp", DM(up_d, ustage[0:2, :]), "uout", reads=[W_(R("ustage"), 0, 2)]))
        out_events.append(DMA("sp", DM(us_d, ustage[2:34, :]), "uout", reads=[W_(R("ustage"), 0, 2)]))
        dbg_dump("conv")

        conv_evs = gather(["Wbcz0", "Wbcz1", "c_sb0", "c_sb1", "tA0", "tA1", "tB0", "tB1", "usb", "vT", "Wco",
                           "ubuf0", "ubuf1", "ustage"])
        assert grp_ctr[0] % 2 == 1
        ffn(0, 1, 2, TILES0, phase_extra=conv_evs, w0_extra=gather(["Wbcz0", "Wbcz1", "W1"]))
        dbg_dump("ffn2")

        ffn_evs = gather(FFN_BUFS)
        conv2_evs = gather(["vT", "Wco", "ubuf0", "ubuf1"])
        DMA("pool", DM(wkv[:], w_kv.rearrange("(c p) n -> p c n", p=128)), "wkv", writes=[W_(R("wkv"))], extra=ffn_evs)
        DMA("sp", DM(cosT[:], cos_d), "tabs", writes=[W_(R("tabs"), 0, 1)], extra=conv2_evs)
        DMA("sp", DM(sinT[:], sin_d), "tabs", writes=[W_(R("tabs"), 1, 2)], extra=conv2_evs)
        TABS = W_(R("tabs"), 0, 2)

        def rope_dve(ps_raw, w, a, mbuf, mres, cbuf, cres_, extra=()):
            OP("dve", TT(mbuf[:, 0:w], PS[ps_raw][:, 0:w], sinT[:, a:a + w], ALU.mult),
               reads=[W_(r_ps[ps_raw], 0, 512), TABS], writes=[mres], extra=extra)
            OP("dve", TT(cbuf[:, 0:w], PS[ps_raw][:, 0:w], cosT[:, a:a + w], ALU.mult),
               reads=[W_(r_ps[ps_raw], 0, 512), TABS], writes=[cres_], extra=extra)

        def rope_pe(ps_rot, w, mbuf, mres, cbuf, cres_):
            OP("pe", MM(PS[ps_rot][:, 0:w], swapm[:, :], mbuf[:, 0:w], True, False),
               reads=[mres] + CONST, writes=[W_(r_ps[ps_rot], 0, 512)])
            OP("pe", MM(PS[ps_rot][:, 0:w], identb[:, :], cbuf[:, 0:w], False, True),
               reads=[cres_] + CONST, writes=[W_(r_ps[ps_rot], 0, 512)])

        def rope_chunk(ps_raw, ps_rot, w, a, mbuf, mres, cbuf, cres_, extra=()):
            rope_dve(ps_raw, w, a, mbuf, mres, cbuf, cres_, extra)
            rope_pe(ps_rot, w, mbuf, mres, cbuf, cres_)

        dbg_dump("kv_a")
        kv_ctr = 0
        norm_tile(3, TILES0[0][0], TILES0[0][1], xn, xn_res, TILES0[0][0])
        for ti, (a, b) in enumerate(TILES0):
            w = b - a
            pars = []
            for kc in range(2):
                par = kv_ctr % 2
                kv_ctr += 1
                pars.append(par)
                pr = 0 + par
                for c in range(DC):
                    OP("pe", MM(PS[pr][:, 0:w], wkv[:, c, kc * 128:(kc + 1) * 128], xn[:, c, a:b], c == 0, c == DC - 1),
                       reads=[W_(R("wkv")), W_(r_xn, a, b)], writes=[W_(r_ps[pr], 0, 512)])
            part2 = None
            if ti + 1 < len(TILES0):
                norm_tile(3, TILES0[ti + 1][0], TILES0[ti + 1][1], xn, xn_res, TILES0[ti + 1][0])
            for kc in range(2):
                par = pars[kc]
                pr, pt = 0 + par, 2 + par
                rope_chunk(pr, pt, w, a, kraw[par], W_(R("kraw%d" % par)), kcb[par], W_(R("kcb%d" % par)), extra=ffn_evs)
                OP("act", ACT(KT[:, kc, a:b], PS[pt][:, 0:w], AF.Copy),
                   reads=[W_(r_ps[pt], 0, 512)], writes=[W_(R("KT"), kc * NT + a, kc * NT + b)], extra=conv2_evs)
                lo, hi = max(a, NP_ - 128), min(b, NP_)
                if lo < hi:
                    OP("act", ACT(KoutT[:, kc, lo - (NP_ - 128):hi - (NP_ - 128)], PS[pt][:, lo - a:hi - a], AF.Copy),
                       reads=[W_(r_ps[pt], 0, 512)], writes=[W_(R("KoutT"), kc * 2, kc * 2 + 1)], extra=ffn_evs)
                if b > NP_:
                    OP("act", ACT(KoutT[:, kc, 128:192], PS[pt][:, NP_ - a:NT - a], AF.Copy),
                       reads=[W_(r_ps[pt], 0, 512)], writes=[W_(R("KoutT"), kc * 2 + 1, kc * 2 + 2)], extra=ffn_evs)
            if part2 is not None:
                part2()
        dbg_dump("kv_k")
        VBLK = [2] + [HALO + 128 * m for m in range(16)] + [HALO + 1796, HALO + 1924]
        for bi, c0 in enumerate(VBLK):
            bank = 4 + bi % 2
            for c in range(DC):
                OP("pe", MM(PS[bank][:, 0:256], xn[:, c, c0:c0 + 128], wkv[:, c, 256:512], c == 0, c == DC - 1),
                   reads=[W_(R("wkv")), W_(r_xn, c0, c0 + 128)], writes=[W_(r_ps[bank], 0, 512)])
            OP("act", ACT(Vb[:, bi, :], PS[bank][:, 0:256], AF.Copy),
               reads=[W_(r_ps[bank], 0, 512)], writes=[W_(R("Vb"), bi, bi + 1)], extra=conv2_evs)
            if bi == 18:
                OP("dve", CP(vstage[:, :], PS[bank][:, 0:256]),
                   reads=[W_(r_ps[bank], 0, 512)], writes=[W_(R("vstage"))], extra=ffn_evs)
                out_events.append(DMA("sp", DM(vp_d, vstage[:, :]), "vpo", reads=[W_(R("vstage"))]))
        dbg_dump("kv_v")
        for c in range(DC):
            OP("pe", MM(PS[4][0:NS, 0:256], xn[:, c, NP_:NT], wkv[:, c, 256:512], c == 0, c == DC - 1),
               reads=[W_(R("wkv")), W_(r_xn, NP_, NT)], writes=[W_(r_ps[4], 0, 512)])
        OP("dve", CP(vsstage[:, :], PS[4][0:NS, 0:256]),
           reads=[W_(r_ps[4], 0, 512)], writes=[W_(R("vsstage"))], extra=ffn_evs)
        for sb_ in range(NSEQ):
            out_events.append(DMA("sp", DM(vs_d[sb_, 124:128, :], vsstage[4 * sb_:4 * sb_ + 4, :]), "vso",
                                  reads=[W_(R("vsstage"))]))
        OP("act", ACT(Vs_bf[:, :], PS[4][0:NS, 0:256], AF.Copy),
           reads=[W_(r_ps[4], 0, 512)], writes=[W_(R("Vnew"))], extra=conv2_evs)
        dbg_dump("kv_vs")
        for kc in range(2):
            OP("pe", TR(PS[6][:, kc * 128:(kc + 1) * 128], KoutT[:, kc, 0:128], ident[:, :]),
               reads=[W_(R("KoutT"), 0, 4)] + CONST, writes=[W_(r_ps[6], 0, 512)])
        OP("dve", CP(kstage[:, :], PS[6][:, 0:256]), reads=[W_(r_ps[6], 0, 512)], writes=[W_(R("kstage"))], extra=ffn_evs)
        out_events.append(DMA("sp", DM(kp_d, kstage[:, :]), "kpo", reads=[W_(R("kstage"))]))
        for kc in range(2):
            OP("pe", TR(PS[7][0:NS, kc * 128:(kc + 1) * 128], KoutT[:, kc, 128:192], ident[:, :]),
               reads=[W_(R("KoutT"), 0, 4)] + CONST, writes=[W_(r_ps[7], 0, 512)])
        OP("dve", CP(ksstage[:, :], PS[7][0:NS, 0:256]), reads=[W_(r_ps[7], 0, 512)], writes=[W_(R("ksstage"))], extra=ffn_evs)
        for sb_ in range(NSEQ):
            out_events.append(DMA("sp", DM(ks_d[sb_, 124:128, :], ksstage[4 * sb_:4 * sb_ + 4, :]), "kso",
                                  reads=[W_(R("ksstage"))]))

        dbg_dump("kv")
        kv_evs = gather(["wkv", "kraw0", "kraw1", "kt1_0", "kt1_1", "kt2_0", "kt2_1", "kt3_0", "kt3_1", "kcb0", "kcb1", "KoutT",
                         "kstage", "vstage", "ksstage", "vsstage"])
        ffn(1, 0, 4, TILES1, phase_extra=kv_evs)
        dbg_dump("ffn3")

        ffn_evs = gather(FFN_BUFS)
        xn_evs = r_xn.all_events()
        for jp_ in range(DC):
            gc_, i4 = jp_ // 4, jp_ % 4
            for half in range(2):
                g = 2 * gc_ + half
                src = w_q[:, g * 256 + i4 * 64:g * 256 + (i4 + 1) * 64].rearrange("(c p) m -> p c m", p=128)
                c0_ = jp_ * 128 + half * 64
                DMA("pool", DM(Wq[:, :, c0_:c0_ + 64], src), "Wq%d" % jp_,
                    writes=[W_(R("Wq"), 2 * jp_ + half, 2 * jp_ + half + 1)], extra=ffn_evs)
        wi = 16
        for gc in range(2):
            for half in range(2):
                g = 2 * gc + half
                src2 = w_o[g * 256:(g + 1) * 256, :].rearrange("(i p) n -> p i n", p=64)
                dst2 = Wo2[half * 64:(half + 1) * 64, gc * 4:(gc + 1) * 4, :]
                DMA("pool", DM(dst2, src2), "Wo2", writes=[W_(R("Wq"), wi, wi + 1)], extra=ffn_evs)
                wi += 1
        WQR = W_(R("Wq"), 0, wi)
        WOR = W_(R("Wq"), 16, wi)
        ESK = W_(R("esink"))
        OP("act", ACT(esink[:, :], esink[:, :], AF.Exp), reads=[ESK], writes=[ESK])

        ATILES = [(HALO + 512 * i, HALO + 512 * (i + 1)) for i in range(4)] + [(NP_ - 128, NT)]
        QTR = W_(R("QT"))
        KTR = W_(R("KT"), 0, 2 * NT)
        VBR = W_(R("Vb"), 0, 19)
        u_ctr = [0]
        seq_ctr = [0]

        def unit_prompt(jp, qc0, qa, vprev_i, vcur_i, mask_ap):
            u = u_ctr[0]
            u_ctr[0] += 1
            par = u % 3
            zpar = u % 2
            gc = jp // 4
            xb = 2 * par
            zb = 6 + zpar
            ptr = W_(R("PT%d" % par)) if par < 2 else W_(R("qraw0"))
            lr = W_(R("lnD%d" % zpar))

            def A():
                for (bank, base) in ((xb, 0), (xb + 1, 64)):
                    OP("pe", MM(PS[bank][:, 0:128], KT[base:base + 64, gc, qa - 128:qa], QT[base:base + 64, jp, qc0:qc0 + 128]),
                       reads=[QTR, KTR], writes=[W_(r_ps[bank], 0, 512)])
                    OP("pe", MM(PS[bank][:, 128:256], KT[base:base + 64, gc, qa:qa + 128], QT[base:base + 64, jp, qc0:qc0 + 128]),
                       reads=[QTR, KTR], writes=[W_(r_ps[bank], 0, 512)])
                OP("act", ACT(PT[par][:, :].rearrange("p (b n) -> p b n", b=2), PSA[:, xb:xb + 2, 0:256], AF.Exp, scale=0.125),
                   reads=[W_(r_ps[xb], 0, 512), W_(r_ps[xb + 1], 0, 512)], writes=[ptr])
                OP("dve", TT(PT[par][:, :], PT[par][:, :], mask_ap, ALU.mult), reads=[ptr] + CONST, writes=[ptr])

            def B():
                for (base, off) in ((0, 0), (64, 256)):
                    g = 2 * gc + (base // 64)
                    OP("pe", MM(PS[zb][base:base + 64, 0:128], Vb[:, vprev_i, g * 64:(g + 1) * 64], PT[par][:, off:off + 128], True, False),
                       reads=[ptr, VBR], writes=[W_(r_ps[zb], 0, 512)])
                    OP("pe", MM(PS[zb][base:base + 64, 0:128], Vb[:, vcur_i, g * 64:(g + 1) * 64], PT[par][:, off + 128:off + 256], False, True),
                       reads=[ptr, VBR], writes=[W_(r_ps[zb], 0, 512)])
                    OP("pe", MM(PS[zb][base:base + 64, 128:256], ones[:, 0:64], PT[par][:, off:off + 128], True, False),
                       reads=[ptr, W_(R("ones"))], writes=[W_(r_ps[zb], 0, 512)])
                    OP("pe", MM(PS[zb][base:base + 64, 128:256], ones[:, 0:64], PT[par][:, off + 128:off + 256], False, True),
                       reads=[ptr, W_(R("ones"))], writes=[W_(r_ps[zb], 0, 512)])
                OP("act", ACT(lnD[zpar][:, 0:128], PS[zb][:, 128:256], AF.Ln, bias=esink[:, jp:jp + 1]),
                   reads=[W_(r_ps[zb], 0, 512), ESK], writes=[lr])
                OP("act", ACT(lnD[zpar][:, 0:128], lnD[zpar][:, 0:128], AF.Exp, scale=-1.0), reads=[lr], writes=[lr])
                OP("dve", TT(attnT[:, jp, qc0:qc0 + 128], PS[zb][:, 0:128], lnD[zpar][:, 0:128], ALU.mult),
                   reads=[W_(r_ps[zb], 0, 512), lr], writes=[W_(R("attnT"))])
            return A, B

        def sample_attention():
            P0, P1 = W_(R("PT0")), W_(R("PT1"))
            z4, z5 = W_(r_ps[4], 0, 512), W_(r_ps[5], 0, 512)
            for jp in range(DC):
                gc = jp // 4
                for (bank, base) in ((0, 0), (1, 64)):
                    OP("pe", MM(PS[bank][0:NS, jp * 64:(jp + 1) * 64], KT[base:base + 64, gc, NP_:NT],
                                QT[base:base + 64, jp, 128:192]),
                       reads=[QTR, KTR], writes=[W_(r_ps[bank], 0, 512)])
            OP("act", ACT(PTC[0:NS, :].rearrange("p (b n) -> p b n", b=2), PSA[0:NS, 0:2, :], AF.Exp, scale=0.125),
               reads=[W_(r_ps[0], 0, 512), W_(r_ps[1], 0, 512)], writes=[P0, P1])
            for k in range(16):
                OP("dve", TT(PTC[0:NS, k * 64:(k + 1) * 64], PTC[0:NS, k * 64:(k + 1) * 64], msamp[0:NS, 64:128], ALU.mult),
                   reads=[P0, P1] + CONST, writes=[P0, P1])
            for (base, hoff) in ((0, 0), (64, 512)):
                for jp in range(DC):
                    g = 2 * (jp // 4) + (base // 64)
                    c0 = hoff + jp * 64
                    OP("pe", MMX(PS[4][base:base + 64, jp * 64:(jp + 1) * 64], Vs_bf[:, g * 64:(g + 1) * 64],
                                 PTC[0:NS, c0:c0 + 64], jp == 0, False),
                       reads=[P0, P1, W_(R("Vnew"))], writes=[z4])
                for jp in range(DC):
                    c0 = hoff + jp * 64
                    OP("pe", MMX(PS[5][base:base + 64, jp * 64:(jp + 1) * 64], ones[0:NS, 0:64],
                                 PTC[0:NS, c0:c0 + 64], jp == 0, False),
                       reads=[P0, P1, W_(R("ones"))], writes=[z5])

            def seq_unit(sb_):
                par = sb_ % 2
                sp = sb_ % 2
                xb = 2 if par == 0 else 0
                ptr = W_(R("PT%d" % par))
                kcr = W_(R("KcT%d" % sp))
                vcr = W_(R("Vc%d" % sp))
                qc0 = 128 + 4 * sb_

                def A():
                    DMA("sp", DM(ckst[sp][:, :], ck[sb_]), "ckst%d" % sp, writes=[W_(R("ckst%d" % sp))])
                    DMA("pool", DM(Vc[sp][:, :], cv[sb_]), "Vc%d" % sp, writes=[vcr])
                    for kc in range(2):
                        OP("pe", TR(PS[6][:, kc * 128:(kc + 1) * 128], ckst[sp][:, kc * 128:(kc + 1) * 128], ident[:, :]),
                           reads=[W_(R("ckst%d" % sp))] + CONST, writes=[W_(r_ps[6], 0, 512)])
                    OP("act", ACT(KcT[sp][:, :, :], PS[6][:, 0:256].rearrange("p (c n) -> p c n", c=2), AF.Copy),
                       reads=[W_(r_ps[6], 0, 512)], writes=[kcr])
                    for (bank, base) in ((xb, 0), (xb + 1, 64)):
                        for jp in range(DC):
                            OP("pe", MM(PS[bank][:, jp * 4:jp * 4 + 4], KcT[sp][base:base + 64, jp // 4, :],
                                        QT[base:base + 64, jp, qc0:qc0 + 4]),
                               reads=[QTR, kcr], writes=[W_(r_ps[bank], 0, 512)])
                    OP("act", ACT(PT[par][:, 0:64].rearrange("p (b n) -> p b n", b=2), PSA[:, xb:xb + 2, 0:32], AF.Exp, scale=0.125),
                       reads=[W_(r_ps[xb], 0, 512), W_(r_ps[xb + 1], 0, 512)], writes=[ptr])
                    OP("dve", TT(PT[par][:, 0:64], PT[par][:, 0:64], msamp[:, 0:64], ALU.mult), reads=[ptr] + CONST, writes=[ptr])

                def B():
                    for (base, hoff) in ((0, 0), (64, 32)):
                        for jp in range(DC):
                            g = 2 * (jp // 4) + (base // 64)
                            col = jp * 64 + 4 * sb_
                            OP("pe", MMX(PS[4][base:base + 64, col:col + 4], Vc[sp][:, g * 64:(g + 1) * 64],
                                         PT[par][:, hoff + jp * 4:hoff + jp * 4 + 4], False, False),
                               reads=[ptr, vcr], writes=[z4])
                        for jp in range(DC):
                            col = jp * 64 + 4 * sb_
                            OP("pe", MMX(PS[5][base:base + 64, col:col + 4], ones[:, 0:64],
                                         PT[par][:, hoff + jp * 4:hoff + jp * 4 + 4], False, sb_ == NSEQ - 1),
                               reads=[ptr, W_(R("ones"))], writes=[z5])
                return A, B

            prevB_ = None
            for sb_ in range(NSEQ):
                A_, B_ = seq_unit(sb_)
                A_()
                if prevB_ is not None:
                    prevB_()
                prevB_ = B_
            prevB_()
            lr = W_(R("lnDall"))
            for jp in range(DC):
                OP("dve", TS(lnDall[:, jp * 64:(jp + 1) * 64], PS[5][:, jp * 64:(jp + 1) * 64], esink[:, jp:jp + 1], None, ALU.add),
                   reads=[z5, ESK], writes=[lr])
            OP("act", ACT(lnDall[:, :], lnDall[:, :], AF.Ln), reads=[lr], writes=[lr])
            OP("act", ACT(lnDall[:, :], lnDall[:, :], AF.Exp, scale=-1.0), reads=[lr], writes=[lr])
            OP("dve", TT(attnT[:, :, 128:192], PS[4][:, :].rearrange("p (j q) -> p j q", j=DC),
                         lnDall[:, :].rearrange("p (j q) -> p j q", j=DC), ALU.mult),
               reads=[z4, lr], writes=[W_(R("attnT"))])

        norm_tile(5, ATILES[0][0], ATILES[0][1], xnA, lambda a_, b_: W_(R("xnA")), 0, extra=xn_evs)
        for ti, (a, b) in enumerate(ATILES):
            w = b - a
            def q_tail(jp):
                par = jp % 2
                pt = 2 * (jp % 4) + 1
                rope_pe(pt, w, qraw[par], W_(R("qraw%d" % par)), qcb[par], W_(R("qcb%d" % par)))
                OP("act", ACT(QT[:, jp, 0:w], PS[pt][:, 0:w], AF.Copy), reads=[W_(r_ps[pt], 0, 512)], writes=[QTR])

            for jp in range(DC):
                par = jp % 2
                pr, pt = 2 * (jp % 4), 2 * (jp % 4) + 1
                for c in range(DC):
                    OP("pe", MM(PS[pr][:, 0:w], Wq[:, c, jp * 128:(jp + 1) * 128], xnA[:, c, 0:w], c == 0, c == DC - 1),
                       reads=[W_(R("Wq"), 2 * jp, 2 * jp + 2), W_(R("xnA"))], writes=[W_(r_ps[pr], 0, 512)])
                rope_dve(pr, w, a, qraw[par], W_(R("qraw%d" % par)), qcb[par], W_(R("qcb%d" % par)))
                if jp >= 1:
                    q_tail(jp - 1)
            q_tail(DC - 1)
            units = []
            if ti < 4:
                for bi in range(4):
                    m = ti * 4 + bi
                    for jp in range(DC):
                        units.append(unit_prompt(jp, 128 * bi, a + 128 * bi, m, m + 1, (mfirst if m == 0 else mstd)[:, :]))
            else:
                for jp in range(DC):
                    units.append(unit_prompt(jp, 0, a, 17, 18, mstd[:, :]))
            pend = []
            for ui, (A_, B_) in enumerate(units):
                A_()
                pend.append(B_)
                if len(pend) > 2:
                    pend.pop(0)()
                if ui == 4 and ti + 1 < len(ATILES):
                    norm_tile(5, ATILES[ti + 1][0], ATILES[ti + 1][1], xnA, lambda a_, b_: W_(R("xnA")), 0)
            for B_ in pend:
                B_()
            if ti == 4:
                sample_attention()
            lo = 0 if ti < 4 else 124
            for d_ in range(DC):
                ob = 4 + d_ % 4
                for jp in range(DC):
                    OP("pe", MM(PS[ob][:, 0:w], Wo2[:, jp, d_ * 128:(d_ + 1) * 128], attnT[:, jp, 0:w], jp == 0, jp == DC - 1),
                       reads=[WOR, W_(R("attnT"))], writes=[W_(r_ps[ob], 0, 512)])
                OP("dve", TT(xT[:, d_, a + lo:b], PS[ob][:, lo:w], xT[:, d_, a + lo:b], ALU.add),
                   reads=[W_(r_ps[ob], 0, 512), W_(r_xT, a + lo, b)], writes=[W_(r_xT, a + lo, b)])
        dbg_dump("attn")

        att_evs = gather(["Wq", "xnA", "QT", "attnT", "PT0", "PT1", "qraw0", "qraw1", "qcb0", "qcb1", "qt1_0", "qt1_1", "qt2",
                          "lnD0", "lnD1", "lnDall"])
        ffn(1, 1, 6, TILES1, phase_extra=att_evs)
        dbg_dump("ffn4")

        ffn_evs = gather(FFN_BUFS)
        FB = 256
        nblk = (OWN + NS + FB - 1) // FB
        blocks = []
        for bi in range(nblk):
            a = HALO + bi * FB
            b = min(a + FB, NT)
            blocks.append((bi, a, b))

        def fin_norm(bi, a, b):
            par = bi % 2
            yr = W_(R("yblk%d" % par))
            norm_tile(7, a, b, yblk[par], lambda a_, b_, yr=yr: yr, 0, extra=ffn_evs)

        sub_ctr = [0]

        def fin_out(bi, a, b):
            par = bi % 2
            yr = W_(R("yblk%d" % par))
            nsub = (b - a + 127) // 128
            for sub in range(nsub):
                a_s = a + sub * 128
                ws = min(128, b - a_s)
                sc = sub_ctr[0]
                sub_ctr[0] += 1
                ys = sc % 4
                for hlf in range(2):
                    bank = 2 * (sc % 2) + hlf
                    for cc in range(4):
                        c = hlf * 4 + cc
                        OP("pe", TR(PS[bank][0:ws, cc * 128:(cc + 1) * 128], yblk[par][:, c, sub * 128:sub * 128 + ws], ident[:, :]),
                           reads=[yr] + CONST, writes=[W_(r_ps[bank], 0, 512)])
                    OP("act", ACT(ystage[ys][0:ws, hlf * 512:(hlf + 1) * 512], PS[bank][0:ws, :], AF.Copy),
                       reads=[W_(r_ps[bank], 0, 512)], writes=[W_(R("ystage%d" % ys), hlf, hlf + 1)], extra=ffn_evs)
                out_events.append(DMA("sp", DM(y_d[a_s - HALO:a_s - HALO + ws, :], ystage[ys][0:ws, :]), "yo%d" % ys,
                                      reads=[W_(R("ystage%d" % ys), 0, 2)]))

        fin_norm(*blocks[0])
        for i, blk_ in enumerate(blocks):
            if i + 1 < len(blocks):
                fin_norm(*blocks[i + 1])
            fin_out(*blk_)
    except _Stop:
        pass
    S.op("sp", lambda e: None, waits=out_events)
    S.emit()
    return nc


_NC_CACHE = {}


def _host_tables(core):
    c = core % 4
    p0 = c * OWN
    pos = np.concatenate([np.arange(p0 - HALO, p0 + OWN), 8192 + np.tile(np.arange(4), NSEQ)]).astype(np.int64)
    posf = np.maximum(pos, 0).astype(np.float32)
    inv = (1.0 / (10000.0 ** (np.arange(0, 64, 2, dtype=np.float32) / np.float32(64)))).astype(np.float32)
    ang = (posf[:, None] * inv[None, :]).astype(np.float32)
    cos = np.cos(ang).astype(np.float32)
    sin = np.sin(ang).astype(np.float32)
    p = np.arange(128)
    j = p % 32
    half = (p % 64) // 32
    cosT = np.ascontiguousarray(cos[:, j].T)
    sinT = np.ascontiguousarray((sin[:, j] * np.where(half == 0, 1.0, -1.0)[None, :]).T.astype(np.float32))
    return cosT, sinT


def kernel(x_prompt, x_sample, state_conv, cache_k, cache_v, meta_tokens, norm_g, ffn_w_in, ffn_w_out,
           conv_w_in, conv_kernel, conv_w_out, kv_norm_g, w_kv, w_q, w_o, sinks, final_norm_g):
    f32 = np.float32
    x_prompt = np.asarray(x_prompt, f32)
    x_sample = np.asarray(x_sample, f32)
    B = x_prompt.shape[0]
    if "nc" not in _NC_CACHE:
        _NC_CACHE["nc"] = build_nc()
    nc = _NC_CACHE["nc"]

    def fm(v):
        return np.ascontiguousarray(np.asarray(v, f32).reshape(DC, 128).T)

    ng = np.asarray(norm_g, f32)
    gT = np.concatenate([fm(ng[0, 0]), fm(ng[0, 1]), fm(ng[0, 2]), fm(kv_norm_g),
                         fm(ng[1, 0]), fm(ng[1, 1]), fm(ng[1, 2]), fm(final_norm_g)], axis=1)
    ckn = np.asarray(conv_kernel, f32)[0]
    ckT = np.concatenate([fm(ckn[0]), fm(ckn[1]), fm(ckn[2])], axis=1)
    sk = np.asarray(sinks, f32)[0]
    sinkT = np.zeros((128, 40), f32)
    for jp in range(8):
        gc, i = jp // 4, jp % 4
        sinkT[0:64, jp] = sk[4 * (2 * gc) + i]
        sinkT[64:128, jp] = sk[4 * (2 * gc + 1) + i]
        sinkT[:, 8 + 4 * jp:12 + 4 * jp] = sinkT[:, jp:jp + 1]
    ident = np.eye(128, dtype=f32)
    swapm = np.zeros((128, 128), f32)
    pp = np.arange(128)
    swapm[pp, pp ^ 32] = 1.0
    kk = np.arange(128)[:, None]
    qq = np.arange(128)[None, :]
    prev = (kk >= qq).astype(f32)
    curm = (kk <= qq).astype(f32)
    mstd = np.concatenate([prev, curm, prev, curm], axis=1)
    mfirst0 = np.concatenate([np.zeros_like(prev), curm, np.zeros_like(prev), curm], axis=1)
    ii = np.arange(128)[:, None]
    tt = np.arange(4)[None, :]
    sprev = (ii >= tt).astype(f32)
    scur = ((ii <= tt) & (ii < 4)).astype(f32)
    msamp = np.zeros((128, 128), f32)
    msamp[:, 0:64] = np.tile(sprev, (1, 16))
    kq = np.arange(64)
    msamp[0:64, 64:128] = ((kq[:, None] // 4 == kq[None, :] // 4) & (kq[:, None] % 4 <= kq[None, :] % 4)).astype(f32)

    shared = dict(
        ffn_w_in=np.asarray(ffn_w_in, f32), ffn_w_out=np.asarray(ffn_w_out, f32),
        conv_w_in=np.asarray(conv_w_in, f32)[0], conv_w_out=np.asarray(conv_w_out, f32)[0],
        w_kv=np.asarray(w_kv, f32), w_q=np.asarray(w_q, f32)[0], w_o=np.asarray(w_o, f32)[0],
        gT=gT, ckT=ckT, sinkT=sinkT, ident=ident, swapm=swapm, mstd=mstd, msamp=msamp)
    meta = np.asarray(meta_tokens, f32)
    sc = np.asarray(state_conv, f32)[0]
    ckc = np.asarray(cache_k, f32).reshape(128, 128, 256)
    cvc = np.asarray(cache_v, f32).reshape(128, 128, 256)
    in_maps = []
    for core in range(8):
        bseq, c = core // 4, core % 4
        xfull = np.concatenate([meta, x_prompt[bseq]], axis=0)
        p0 = c * OWN
        lo = p0 - HALO
        rows = np.zeros((NP_, D), f32)
        s0 = max(lo, 0)
        rows[s0 - lo:] = xfull[s0:p0 + OWN]
        xin = np.concatenate([rows, x_sample[16 * core:16 * core + 16].reshape(NS, D)], axis=0)
        cosT, sinT = _host_tables(core)
        m = dict(shared)
        m.update(xin=np.ascontiguousarray(xin),
                 sconv=np.ascontiguousarray(sc[16 * core:16 * core + 16].reshape(2 * NSEQ, D)),
                 ck=np.ascontiguousarray(ckc[16 * core:16 * core + 16]),
                 cv=np.ascontiguousarray(cvc[16 * core:16 * core + 16]),
                 cosT=cosT, sinT=sinT, mfirst=(mfirst0 if c == 0 else mstd))
        in_maps.append(m)
    res = run_bass_kernel_spmd(nc, in_maps, core_ids=list(range(8)))
    R_ = res.results
    _NC_CACHE["last"] = R_
    y_prompt = np.zeros((B, 8192, D), f32)
    y_sample = np.zeros((128, 4, D), f32)
    ncp = np.zeros((1, B, 2, D), f32)
    ncs = np.zeros((1, 128, 2, D), f32)
    nkp = np.zeros((B, 128, 4, 64), f32)
    nvp = np.zeros((B, 128, 4, 64), f32)
    nks = np.zeros((128, 128, 4, 64), f32)
    nvs = np.zeros((128, 128, 4, 64), f32)
    for core in range(8):
        bseq, c = core // 4, core % 4
        r = R_[core]
        y = np.asarray(r["y"])
        yp = y[0:OWN]
        p0 = c * OWN
        if c == 0:
            y_prompt[bseq, 0:OWN - 16] = yp[16:]
        else:
            y_prompt[bseq, p0 - 16:p0 - 16 + OWN] = yp
        y_sample[16 * core:16 * core + 16] = y[OWN:].reshape(16, 4, D)
        ncs[0, 16 * core:16 * core + 16] = np.asarray(r["u_s"]).reshape(16, 2, D)
        nks[16 * core:16 * core + 16] = np.asarray(r["ks"]).reshape(16, 128, 4, 64)
        nvs[16 * core:16 * core + 16] = np.asarray(r["vs"]).reshape(16, 128, 4, 64)
        if c == 3:
            ncp[0, bseq] = np.asarray(r["u_p"])
            nkp[bseq] = np.asarray(r["kp"]).reshape(128, 4, 64)
            nvp[bseq] = np.asarray(r["vp"]).reshape(128, 4, 64)
    return (y_prompt, y_sample, ncp, ncs, nkp, nvp, nks, nvs)
```

```python
import contextlib
import numpy as np
import concourse.bass as bass
import concourse.mybir as mybir
from concourse.alu_op_type import AluOpType as ALU
from concourse.bass_utils import run_bass_kernel_spmd

F32 = mybir.dt.float32
BF16 = mybir.dt.bfloat16
AF = mybir.ActivationFunctionType

D = 1024
DC = 8
DFF = 2816
FCH = 22
HALO = 130
OWN = 2052
NP_ = HALO + OWN
NS = 64
NT = NP_ + NS
NSEQ = 16
EPS = 1e-6
DEBUG = {}

ENGS = ("pe", "act", "dve", "pool", "sp")


class Ev:
    __slots__ = ("kind", "eng", "sem", "count", "needed", "pos")

    def __init__(self, kind, eng):
        self.kind = kind
        self.eng = eng
        self.sem = None
        self.count = None
        self.needed = False
        self.pos = None


class Sched:
    def __init__(self, nc):
        self.nc = nc
        self.q = {e: [] for e in ENGS}
        self.dma_sems = {}
        self.eng_sem = {}

    def _reduce(self, waits, eng):
        best = {}
        for w in waits:
            if w is None:
                continue
            if w.kind == 'c':
                k = ('c', w.eng)
                if k not in best or w.pos > best[k].pos:
                    best[k] = w
            else:
                k = ('d', w.sem)
                if k not in best or w.count > best[k].count:
                    best[k] = w
        ws = list(best.values())
        for w in ws:
            w.needed = True
        return ws

    def op(self, eng, fn, waits=()):
        ev = Ev('c', eng)
        ev.pos = len(self.q[eng])
        self.q[eng].append((fn, self._reduce(waits, eng), ev))
        return ev

    def dma(self, eng, fn, key, waits=()):
        ev = Ev('d', eng)
        ent = self.dma_sems.setdefault(key, [None, 0])
        ent[1] += 16
        ev.sem = key
        ev.count = ent[1]
        ev.pos = len(self.q[eng])
        self.q[eng].append((fn, self._reduce(waits, eng), ev))
        return ev

    def emit(self):
        nc = self.nc
        with contextlib.ExitStack() as st:
            for e in ENGS:
                self.eng_sem[e] = st.enter_context(nc.semaphore("s_" + e))
            for i, key in enumerate(self.dma_sems):
                self.dma_sems[key][0] = st.enter_context(nc.semaphore("d%d" % i))
            for e in ENGS:
                c = 0
                for (fn, ws, ev) in self.q[e]:
                    if ev.kind == 'c' and ev.needed:
                        c += 1
                        ev.count = c
                    if ev.kind == 'd' and str(ev.sem).startswith("const"):
                        ev.count = self.dma_sems[ev.sem][1]
            block = st.enter_context(nc.Block())
            sched = self

            def run(engname):
                def body(eh):
                    waited = {}
                    for (fn, ws, ev) in sched.q[engname]:
                        for w in ws:
                            if w.kind == 'c':
                                sem = sched.eng_sem[w.eng]
                                k = ('c', w.eng)
                            else:
                                sem = sched.dma_sems[w.sem][0]
                                k = ('d', w.sem)
                            if waited.get(k, 0) >= w.count:
                                continue
                            waited[k] = w.count
                            eh.wait_ge(sem, w.count)
                        ins = fn(eh)
                        if ins is None:
                            continue
                        if ev.kind == 'c':
                            if ev.needed:
                                ins.then_inc(sched.eng_sem[engname], 1)
                        else:
                            ins.then_inc(sched.dma_sems[ev.sem][0], 16)
                return body

            block.tensor(run("pe"))
            block.scalar(run("act"))
            block.vector(run("dve"))
            block.gpsimd(run("pool"))
            block.sync(run("sp"))


class RR:
    def __init__(self):
        self.segs = []

    def _cut(self, x):
        for i, s in enumerate(self.segs):
            if s[0] < x < s[1]:
                self.segs[i:i + 1] = [[s[0], x, list(s[2]), list(s[3])], [x, s[1], list(s[2]), list(s[3])]]
                return

    def cover(self, a, b):
        self._cut(a)
        self._cut(b)
        self.segs.sort(key=lambda s: s[0])
        pts = a
        new = []
        for s in self.segs:
            if s[1] <= a or s[0] >= b:
                continue
            if s[0] > pts:
                new.append([pts, s[0], [], []])
            pts = s[1]
        if pts < b:
            new.append([pts, b, [], []])
        self.segs += new
        self.segs.sort(key=lambda s: s[0])
        return [s for s in self.segs if s[0] >= a and s[1] <= b]

    def all_events(self):
        out = []
        for s in self.segs:
            out += s[2] + s[3]
        return out


class _Stop(Exception):
    pass


class K:
    pass


def build_nc():
    nc = bass.Bass("TRN2", target_bir_lowering=False)
    S = Sched(nc)

    def din(name, shape):
        return nc.dram_tensor(name, list(shape), F32, kind="ExternalInput").ap()

    def dout(name, shape):
        return nc.dram_tensor(name, list(shape), F32, kind="ExternalOutput").ap()

    xin = din("xin", [NT, D])
    sconv = din("sconv", [2 * NSEQ, D])
    ck = din("ck", [NSEQ, 128, 256])
    cv = din("cv", [NSEQ, 128, 256])
    ffn_w_in = din("ffn_w_in", [2, 2, D, 2 * DFF])
    ffn_w_out = din("ffn_w_out", [2, 2, DFF, D])
    conv_w_in = din("conv_w_in", [D, 3 * D])
    conv_w_out = din("conv_w_out", [D, D])
    w_kv = din("w_kv", [D, 512])
    w_q = din("w_q", [D, D])
    w_o = din("w_o", [D, D])
    gT_d = din("gT", [128, 64])
    ckT_d = din("ckT", [128, 24])
    sinkT_d = din("sinkT", [128, 40])
    cos_d = din("cosT", [128, NT])
    sin_d = din("sinT", [128, NT])
    ident_d = din("ident", [128, 128])
    swap_d = din("swapm", [128, 128])
    mstd_d = din("mstd", [128, 512])
    mfirst_d = din("mfirst", [128, 512])
    msamp_d = din("msamp", [128, 128])

    y_d = dout("y", [OWN + NS, D])
    up_d = dout("u_p", [2, D])
    us_d = dout("u_s", [2 * NSEQ, D])
    kp_d = dout("kp", [128, 256])
    vp_d = dout("vp", [128, 256])
    ks_d = dout("ks", [NSEQ, 128, 256])
    vs_d = dout("vs", [NSEQ, 128, 256])
    dbg_d = None
    if DEBUG.get("xT"):
        dbg_d = dout("dbg", [128, DC * NT])

    base = (nc._sbuf_addr_for_side('left') + 63) // 64 * 64
    cap = 229376
    cur = [base]

    def esize(dt):
        return 4 if dt == F32 else 2

    def alloc(name, shape, dt, at=None):
        n = 1
        for d_ in shape[1:]:
            n *= d_
        nbytes = (n * esize(dt) + 63) // 64 * 64
        if at is None:
            o = cur[0]
            cur[0] += nbytes
        else:
            o = at
        assert o + nbytes <= cap, (name, o, nbytes)
        return nc.alloc_sbuf_tensor_at(name, list(shape), dt, offset=o)

    xT = alloc("xT", [128, DC, NT], F32)
    XN_OFF = cur[0]
    xn = alloc("xn", [128, DC, NT], BF16)
    ident = alloc("ident", [128, 128], F32)
    swapm = alloc("swapm", [128, 128], BF16)
    identb = alloc("identb", [128, 128], BF16)
    ones = alloc("ones", [128, 128], BF16)
    gT = alloc("gT", [128, 64], F32)
    ckT = alloc("ckT", [128, 24], F32)
    esink = alloc("esink", [128, 40], F32)
    epst = alloc("epst", [128, 1], F32)
    mstd = alloc("mstd", [128, 512], BF16)
    mfirst = alloc("mfirst", [128, 512], BF16)
    msamp = alloc("msamp", [128, 128], BF16)
    uprevT = alloc("uprevT", [128, DC, 2 * NSEQ], F32)
    uoT = alloc("uoT", [128, DC, 34], F32)
    rstd = [alloc("rstd%d" % i, [128, 512], F32) for i in range(2)]
    rtmp = alloc("rtmp", [128, 512], F32)
    ptmp = alloc("ptmp", [128, 512], F32)
    sq = alloc("sq", [128, DC, 512], BF16)
    ARENA = cur[0]
    arena_size = cap - ARENA
    assert arena_size >= 82200, arena_size

    A0 = ARENA
    Wg = [alloc("Wg%d" % s, [128, DC, 256], BF16, at=A0 + s * 12288) for s in range(2)]
    Wu = [alloc("Wu%d" % s, [128, DC, 256], BF16, at=A0 + s * 12288 + 4096) for s in range(2)]
    Wo_ = [alloc("Wo%d" % s, [128, 2, 1024], BF16, at=A0 + s * 12288 + 8192) for s in range(2)]
    hbuf = [[alloc("h%d%d" % (p, f), [128, 512], BF16, at=A0 + 24576 + (2 * p + f) * 1024) for f in range(2)]
            for p in range(2)]
    sbuf_ = [alloc("s%d" % f, [128, 512], F32, at=A0 + 28672 + f * 2048) for f in range(2)]
    NXS = 8
    xs = [alloc("xs%d" % i, [128, D], F32, at=A0 + 32768 + i * 4096) for i in range(NXS)]
    Wbcz = [alloc("Wbcz%d" % s, [128, 3, DC, 128], BF16, at=A0 + 12288 + s * 6144) for s in range(2)]
    c_sb = [alloc("c_sb%d" % i, [128, 512], F32, at=A0 + 0 + i * 2048) for i in range(2)]
    tA = [alloc("tA%d" % i, [128, 512], F32, at=A0 + 4096 + i * 2048) for i in range(2)]
    tB = [alloc("tB%d" % i, [128, 512], F32, at=A0 + 8192 + i * 2048) for i in range(2)]
    usb = alloc("usb", [128, NSEQ, 6], F32, at=A0 + 24576)
    Wq = alloc("Wq", [128, DC, 1024], BF16, at=A0)
    Wo2 = alloc("Wo2", [128, DC, 1024], BF16, at=A0 + 16384)
    B0 = A0 + 32768
    vT = alloc("vT", [128, DC, NT], BF16, at=A0 + 29824)
    Wco = alloc("Wco", [128, DC, 1024], BF16, at=A0 + 29824 + 35968)
    ubuf = [alloc("ubuf%d" % i, [128, 516], F32, at=A0 + 25600 + i * 2112) for i in range(2)]
    assert A0 + 29824 + 35968 + 16384 <= cap
    KT = alloc("KT", [128, 2, NT], BF16, at=B0)
    Vb = alloc("Vb", [128, 19, 256], BF16, at=B0 + 8992)
    Vs_bf = alloc("Vs_bf", [NS, 256], BF16, at=B0 + 8992 + 9728)
    C0 = B0 + 8992 + 9728 + 8192
    cosT = alloc("cosT", [128, NT], F32, at=C0)
    sinT = alloc("sinT", [128, NT], F32, at=C0 + 8992)
    E0 = C0 + 2 * 8992
    wkv = alloc("wkv", [128, DC, 512], BF16, at=A0)
    kraw = [alloc("kraw%d" % i, [128, 512], BF16, at=A0 + 8192 + i * 1024) for i in range(2)]
    kt1 = [alloc("kt1_%d" % i, [128, 512], F32, at=A0 + 10240 + i * 2048) for i in range(2)]
    kcb = [alloc("kcb%d" % i, [128, 512], BF16, at=A0 + 10240 + i * 1024) for i in range(2)]
    kt2 = [alloc("kt2_%d" % i, [128, 512], F32, at=A0 + 14336 + i * 2048) for i in range(2)]
    kt3 = [alloc("kt3_%d" % i, [128, 512], F32, at=A0 + 18432 + i * 2048) for i in range(2)]
    KoutT = alloc("KoutT", [128, 2, 192], F32, at=A0 + 22528)
    kstage = alloc("kstage", [128, 256], F32, at=A0 + 24576)
    vstage = alloc("vstage", [128, 256], F32, at=A0 + 25600)
    ksstage = alloc("ksstage", [64, 256], F32, at=A0 + 26624)
    vsstage = alloc("vsstage", [64, 256], F32, at=A0 + 27648)
    ckst = [alloc("ckst%d" % i, [128, 256], F32, at=E0 + i * 1024) for i in range(2)]
    KcT = [alloc("KcT%d" % i, [128, 2, 128], BF16, at=E0 + 2048 + i * 512) for i in range(2)]
    Vc = [alloc("Vc%d" % i, [128, 256], BF16, at=E0 + 3072 + i * 512) for i in range(2)]
    ustage = alloc("ustage", [34, D], F32, at=A0 + 0)
    assert E0 + 4096 <= cap, (E0, cap)
    xnA = alloc("xnA", [128, DC, 512], BF16, at=XN_OFF)
    QT = alloc("QT", [128, DC, 512], BF16, at=XN_OFF + 8192)
    attnT = alloc("attnT", [128, DC, 512], BF16, at=XN_OFF + 16384)
    PT = [alloc("PT%d" % i, [128, 512], BF16, at=XN_OFF + 24576 + i * 1024) for i in range(2)]
    qraw = [alloc("qraw%d" % i, [128, 512], BF16, at=XN_OFF + 26624 + i * 1024) for i in range(2)]
    qt1 = [alloc("qt1_%d" % i, [128, 512], F32, at=XN_OFF + 28672 + i * 2048) for i in range(2)]
    qt2 = alloc("qt2", [128, 512], F32, at=XN_OFF + 32768)
    qcb = [alloc("qcb%d" % i, [128, 512], BF16, at=XN_OFF + 28672 + i * 1024) for i in range(2)]
    PTC = alloc("PTC", [128, 1024], BF16, at=XN_OFF + 24576)
    PT.append(alloc("PT2", [128, 512], BF16, at=XN_OFF + 26624))
    lnDall = alloc("lnDall", [128, 512], F32, at=XN_OFF + 30720)
    lnD = [alloc("lnD%d" % i, [128, 128], F32, at=XN_OFF + 34816 + i * 512) for i in range(2)]
    assert XN_OFF + 34816 + 1024 <= XN_OFF + DC * NT * 2
    yblk = [alloc("yblk%d" % i, [128, DC, 512], F32, at=A0 + i * 16384) for i in range(2)]
    ystage = [alloc("ystage%d" % i, [128, D], F32, at=A0 + 32768 + i * 4096) for i in range(4)]

    PSA = nc.alloc_psum_tensor("psa", [128, 8, 512], F32)

    class _Bank:
        def __init__(self, i):
            self.i = i

        def __getitem__(self, idx):
            return PSA[idx[0], self.i, idx[1]]

    PS = [_Bank(i) for i in range(8)]

    class Res(RR):
        pass

    r_xT = RR()
    r_xn = RR()
    r_ps = [RR() for _ in range(8)]
    for r_ in r_ps:
        r_.excl = True
    r_const = RR()
    res_cache = {}

    def R(name):
        if name not in res_cache:
            res_cache[name] = RR()
        return res_cache[name]

    def W_(res, a=0, b=1):
        return (res, a, b)

    def deps_for(eng, reads, writes, extra):
        deps = []
        for (r, a, b) in reads:
            for s in r.cover(a, b):
                for ev in s[2]:
                    deps.append(ev)
                if getattr(r, "excl", False):
                    for ev in s[3]:
                        if not (ev.kind == 'c' and ev.eng == eng):
                            deps.append(ev)
        for (r, a, b) in writes:
            for s in r.cover(a, b):
                if not s[3]:
                    for ev in s[2]:
                        if not (ev.kind == 'c' and ev.eng == eng):
                            deps.append(ev)
                for ev in s[3]:
                    if not (ev.kind == 'c' and ev.eng == eng):
                        deps.append(ev)
        deps += [e for e in extra if e is not None]
        if eng == "pe":
            deps = [e for e in deps if not (e.kind == 'c' and e.eng == "pe")]
        return deps

    def commit(ev, reads, writes):
        for (r, a, b) in reads:
            for s in r.cover(a, b):
                s[3].append(ev)
        for (r, a, b) in writes:
            for s in r.cover(a, b):
                s[2] = [ev]
                s[3] = []

    def OP(eng, fn, reads=(), writes=(), extra=()):
        ev = S.op(eng, fn, deps_for(eng, reads, writes, extra))
        commit(ev, reads, writes)
        return ev

    def DMA(eng, fn, key, reads=(), writes=(), extra=()):
        ev = S.dma(eng, fn, key, deps_for("dma_" + eng, reads, writes, extra))
        commit(ev, reads, writes)
        return ev

    out_events = []

    def MM(out, lhsT, rhs, start=True, stop=True):
        return lambda e: e.matmul(out, lhsT, rhs, start=start, stop=stop)

    def MMX(out, lhsT, rhs, start=True, stop=True):
        return lambda e: e.matmul(out, lhsT, rhs, start=start, stop=stop, skip_group_check=True)

    def TR(out, in_, idn):
        return lambda e: e.transpose(out, in_, idn)

    def ACT(out, in_, func, bias=None, scale=None):
        kw = {}
        if bias is not None:
            kw["bias"] = bias
        if scale is not None:
            kw["scale"] = scale
        return lambda e: e.activation(out, in_, func, **kw)

    def TT(out, in0, in1, op):
        return lambda e: e.tensor_tensor(out, in0, in1, op)

    def TS(out, in0, s1, s2, op0, op1=None):
        if op1 is None:
            return lambda e: e.tensor_scalar(out, in0, s1, None, op0)
        return lambda e: e.tensor_scalar(out, in0, s1, s2, op0, op1)

    def STT(out, in0, scalar, in1, op0, op1):
        return lambda e: e.scalar_tensor_tensor(out, in0, scalar, in1, op0, op1)

    def CP(out, in_):
        return lambda e: e.tensor_copy(out, in_)

    def MS(ap, val):
        return lambda e: e.memset(ap, val)

    def RCP(out, in_):
        return lambda e: e.reciprocal(out, in_)

    def DM(out, in_):
        return lambda e: e.dma_start(out=out, in_=in_)

    def cres(i):
        return W_(r_const, i, i + 1)
    CONST = [W_(r_const, 0, 16)]
    DMA("sp", DM(ident[:], ident_d), "const", writes=[cres(0)])
    DMA("sp", DM(gT[:], gT_d), "const", writes=[cres(1)])
    DMA("sp", DM(ckT[:], ckT_d), "const", writes=[cres(2)])
    DMA("sp", DM(esink[:], sinkT_d), "const", writes=[W_(R("esink"))])
    DMA("pool", DM(swapm[:], swap_d), "constp", writes=[cres(4)])
    DMA("pool", DM(identb[:], ident_d), "constp", writes=[cres(8)])
    DMA("pool", DM(mstd[:], mstd_d), "constp", writes=[cres(5)])
    DMA("pool", DM(mfirst[:], mfirst_d), "constp", writes=[cres(6)])
    DMA("pool", DM(msamp[:], msamp_d), "constp", writes=[cres(7)])
    OP("dve", MS(ones[:], 1.0), writes=[W_(R("ones"))])
    OP("dve", MS(epst[:], EPS), writes=[W_(R("eps"))])

    ld_ctr = [0]

    def load_block(src_ap, r0, n, dst_fn, rdst):
        s_ = ld_ctr[0] % NXS
        ld_ctr[0] += 1
        xr = W_(R("xs%d" % s_))
        DMA("sp", DM(xs[s_][0:n, :], src_ap[r0:r0 + n, :]), "xs" + str(s_), writes=[xr])
        for hlf in range(2):
            bank = 6 + hlf
            for cc in range(4):
                c = hlf * 4 + cc
                OP("pe", TR(PS[bank][:, cc * 128:cc * 128 + n], xs[s_][0:n, c * 128:(c + 1) * 128], ident[0:n, 0:n]),
                   reads=[xr] + CONST, writes=[W_(r_ps[bank], cc * 128, cc * 128 + 128)])
            src = PS[bank][:, :].rearrange("p (c n) -> p c n", c=4)[:, :, 0:n]
            if hlf == 0:
                OP("dve", CP(dst_fn(hlf * 4, r0, n), src), reads=[W_(r_ps[bank], 0, 512)], writes=[rdst(r0, n)])
            else:
                OP("act", ACT(dst_fn(hlf * 4, r0, n), src, AF.Copy), reads=[W_(r_ps[bank], 0, 512)],
                   writes=[rdst(r0, n)])

    NBLK_X = (NT + 127) // 128
    x_loaded = [0]

    def load_x_upto(col_end):
        while x_loaded[0] < NBLK_X and x_loaded[0] * 128 < col_end:
            r0 = x_loaded[0] * 128
            n = min(128, NT - r0)
            load_block(xin, r0, n, lambda c0, r0_, n_: xT[:, c0:c0 + 4, r0_:r0_ + n_], lambda r0_, n_: W_(r_xT, r0_, r0_ + n_))
            x_loaded[0] += 1

    def split_tiles(a, b, n):
        w = b - a
        base_w = (w // n) // 2 * 2
        rem = w - base_w * n
        out = []
        x = a
        for i in range(n):
            ww = base_w + (2 if i < rem // 2 else 0)
            if i == n - 1:
                ww = b - x
            out.append((x, x + ww))
            x += ww
        return out

    TILES0 = split_tiles(0, NT, 5)
    TILES1 = [(max(a, HALO), b) for (a, b) in TILES0]
    norm_ctr = [0]

    def norm_tile(nidx, a, b, dst, dst_res, dst_off, extra=(), defer=False):
        w = b - a
        p6 = W_(r_ps[6], 0, 512)
        for hf in range(2):
            OP("act", ACT(sq[:, 4 * hf:4 * hf + 4, 0:w], xT[:, 4 * hf:4 * hf + 4, a:b], AF.Square),
               reads=[W_(r_xT, a, b)], writes=[W_(R("sq"), hf, hf + 1)])
        for c in range(DC):
            OP("pe", MM(PS[6][:, 0:w], ones[:, :], sq[:, c, 0:w], c == 0, c == DC - 1),
               reads=[W_(R("sq"), c // 4, c // 4 + 1), W_(R("ones"))], writes=[p6])
        OP("act", ACT(PS[6][:, 0:w], PS[6][:, 0:w], AF.Ln, bias=epst[:, 0:1], scale=1.0 / D),
           reads=[p6, W_(R("eps"))], writes=[p6])
        OP("act", ACT(PS[6][:, 0:w], PS[6][:, 0:w], AF.Exp, scale=-0.5), reads=[p6], writes=[p6])

        def part2():
            for c in range(DC):
                OP("dve", STT(dst[:, c, dst_off:dst_off + w], xT[:, c, a:b], gT[:, nidx * 8 + c:nidx * 8 + c + 1],
                              PS[6][:, 0:w], ALU.mult, ALU.mult),
                   reads=[W_(r_xT, a, b), p6] + CONST, writes=[dst_res(a, b)], extra=extra)
        if defer:
            return part2
        part2()

    def xn_res(a, b):
        return W_(r_xn, a, b)

    grp_ctr = [0]

    def ffn(l, i, nidx, tiles, phase_extra=(), pre_tile=None, w0_extra=None):
        w_in = ffn_w_in[l, i]
        w_out = ffn_w_out[l, i]
        pending = [None]
        it = [0]

        def flush():
            if pending[0] is not None:
                for st_ in range(4):
                    pending[0](st_)
                pending[0] = None

        ngrp = FCH // 2
        for gi in range(ngrp):
            s = grp_ctr[0] % 2
            grp_ctr[0] += 1
            f0 = gi * 2
            wr = R("W%d" % s)
            wres = W_(wr, 0, 3)
            ex = phase_extra if gi < 2 else ()
            if gi == 0 and w0_extra is not None:
                ex = w0_extra
            DMA("pool", DM(Wg[s][:], w_in[:, f0 * 128:f0 * 128 + 256].rearrange("(c p) n -> p c n", p=128)),
                "W%d" % s, writes=[W_(wr, 0, 1)], extra=ex)
            DMA("pool", DM(Wu[s][:], w_in[:, DFF + f0 * 128:DFF + f0 * 128 + 256].rearrange("(c p) n -> p c n", p=128)),
                "W%d" % s, writes=[W_(wr, 1, 2)], extra=ex)
            DMA("pool", DM(Wo_[s][:], w_out[f0 * 128:f0 * 128 + 256, :].rearrange("(f p) n -> p f n", p=128)),
                "W%d" % s, writes=[W_(wr, 2, 3)], extra=ex)
            for ti, (a, b) in enumerate(tiles):
                if gi == 0:
                    if ti == 0:
                        if pre_tile is not None:
                            pre_tile(0)
                        norm_tile(nidx, a, b, xn, xn_res, a, extra=phase_extra)
                    if ti + 1 < len(tiles):
                        a2, b2 = tiles[ti + 1]
                        if pre_tile is not None:
                            pre_tile(ti + 1)
                        norm_tile(nidx, a2, b2, xn, xn_res, a2, extra=phase_extra)
                w = b - a
                par = it[0] % 2
                it[0] += 1
                prev = pending[0]
                pending[0] = None
                step = [0]

                def prev_pair():
                    if prev is not None:
                        prev(step[0])
                    step[0] += 1

                for fi in range(2):
                    gb, ub = 2 * fi, 2 * fi + 1
                    for c in range(DC):
                        OP("pe", MM(PS[gb][:, 0:w], Wg[s][:, c, fi * 128:(fi + 1) * 128], xn[:, c, a:b], c == 0, c == DC - 1),
                           reads=[wres, W_(r_xn, a, b)], writes=[W_(r_ps[gb], 0, 512)])
                    OP("act", ACT(sbuf_[fi][:, 0:w], PS[gb][:, 0:w], AF.Silu),
                       reads=[W_(r_ps[gb], 0, 512)], writes=[W_(R("s%d" % fi))])
                    prev_pair()
                    for c in range(DC):
                        OP("pe", MM(PS[ub][:, 0:w], Wu[s][:, c, fi * 128:(fi + 1) * 128], xn[:, c, a:b], c == 0, c == DC - 1),
                           reads=[wres, W_(r_xn, a, b)], writes=[W_(r_ps[ub], 0, 512)])
                    OP("dve", TT(hbuf[par][fi][:, 0:w], sbuf_[fi][:, 0:w], PS[ub][:, 0:w], ALU.mult),
                       reads=[W_(R("s%d" % fi)), W_(r_ps[ub], 0, 512)], writes=[W_(R("h%d%d" % (par, fi)))])
                    prev_pair()

                def wout(stepi, a=a, b=b, w=w, par=par, s=s, wres=wres):
                    for d_ in (2 * stepi, 2 * stepi + 1):
                        ob = (4, 5, 7)[d_ % 3]
                        for fi in range(2):
                            OP("pe", MM(PS[ob][:, 0:w], Wo_[s][:, fi, d_ * 128:(d_ + 1) * 128], hbuf[par][fi][:, 0:w],
                                        fi == 0, fi == 1),
                               reads=[wres, W_(R("h%d%d" % (par, fi)))], writes=[W_(r_ps[ob], 0, 512)])
                        OP("dve", STT(xT[:, d_, a:b], PS[ob][:, 0:w], 0.5, xT[:, d_, a:b], ALU.mult, ALU.add),
                           reads=[W_(r_ps[ob], 0, 512), W_(r_xT, a, b)], writes=[W_(r_xT, a, b)])
                pending[0] = wout
        flush()

    def gather(names):
        evs = []
        for n in names:
            if n in res_cache:
                evs += res_cache[n].all_events()
        return evs

    def dbg_dump(tag):
        if dbg_d is not None and DEBUG.get("xT") == tag:
            out_events.append(DMA("sp", DM(dbg_d, xT[:, :, :].rearrange("p c n -> p (c n)")),
                                  "dbg", reads=[W_(r_xT, 0, NT)]))
        if DEBUG.get("stop") == tag:
            raise _Stop()

    try:
        FFN_BUFS = ["W0", "W1", "h00", "h01", "h10", "h11", "s0", "s1"]
        dbg_dump("phase0")
        def pre0(ti):
            load_x_upto(TILES0[ti][1])
            if ti == len(TILES0) - 1:
                out_events.append(DMA("sp", DM(ks_d[:, 0:124, :], ck[:, 4:128, :]), "cachecp"))
                out_events.append(DMA("sp", DM(vs_d[:, 0:124, :], cv[:, 4:128, :]), "cachecp"))
                load_block(sconv, 0, 2 * NSEQ, lambda c0, r0_, n_: uprevT[:, c0:c0 + 4, r0_:r0_ + n_],
                           lambda r0_, n_: W_(R("uprevT")))
        ffn(0, 0, 0, TILES0, pre_tile=pre0)
        dbg_dump("ffn1")

        ffn_evs = gather(FFN_BUFS + ["xs%d" % i for i in range(8)])
        DMA("pool", DM(Wco[:], conv_w_out.rearrange("(c p) n -> p c n", p=128)), "Wco", writes=[W_(R("Wco"))])
        cw = conv_w_in.rearrange("(c p) (k m) -> p k c m", p=128, k=3)
        def conv_w_dma(fc_):
            s_ = fc_ % 2
            DMA("pool", DM(Wbcz[s_][:], cw[:, :, :, fc_ * 128:(fc_ + 1) * 128]), "Wbcz%d" % s_,
                writes=[W_(R("Wbcz%d" % s_))], extra=gather(["W1"]) if fc_ < 2 else ())
        conv_w_dma(0)
        for fc in range(DC):
            s = fc % 2
            wres = W_(R("Wbcz%d" % s))
            if fc + 1 < DC:
                conv_w_dma(fc + 1)
            OP("pool", MS(ubuf[0][:, 0:2], 0.0), writes=[W_(R("ubuf0"), 0, 2)])
            for ti, (a, b) in enumerate(TILES0):
                if fc == 0:
                    if ti == 0:
                        norm_tile(1, a, b, xn, xn_res, a)
                    if ti + 1 < len(TILES0):
                        norm_tile(1, TILES0[ti + 1][0], TILES0[ti + 1][1], xn, xn_res, TILES0[ti + 1][0])
                w = b - a
                par = ti % 2
                bo = 3 * ((fc * len(TILES0) + ti) % 2)
                pb = min(b, NP_)
                wp = pb - a
                has_s = b > NP_
                for k in range(3):
                    for c in range(DC):
                        OP("pe", MM(PS[bo + k][:, 0:w], Wbcz[s][:, k, c, :], xn[:, c, a:b], c == 0, c == DC - 1),
                           reads=[wres, W_(r_xn, a, b)], writes=[W_(r_ps[bo + k], 0, 512)])
                ex = ffn_evs if (fc == 0 and ti < 2) else ()
                cr = W_(R("c_sb%d" % par))
                tAr = W_(R("tA%d" % par))
                tBr = W_(R("tB%d" % par))
                OP("act", ACT(c_sb[par][:, 0:w], PS[bo + 1][:, 0:w], AF.Copy), reads=[W_(r_ps[bo + 1], 0, 512)], writes=[cr], extra=ex)
                ub = ubuf[par]
                ur = R("ubuf%d" % par)
                OP("dve", TT(ub[:, 2:2 + wp], c_sb[par][:, 0:wp], PS[bo + 2][:, 0:wp], ALU.mult),
                   reads=[cr, W_(r_ps[bo + 2], 0, 512)], writes=[W_(ur, 2, 516)], extra=ex)
                if ti + 1 < len(TILES0):
                    OP("pool", CP(ubuf[1 - par][:, 0:2], ub[:, wp:wp + 2]),
                       reads=[W_(ur, 2, 516)], writes=[W_(R("ubuf%d" % (1 - par)), 0, 2)])
                OP("dve", TS(tA[par][:, 0:wp], ub[:, 0:wp], ckT[:, fc:fc + 1], None, ALU.mult),
                   reads=[W_(ur, 0, 516)] + CONST, writes=[tAr], extra=ex)
                OP("dve", STT(tB[par][:, 0:wp], ub[:, 1:1 + wp], ckT[:, 8 + fc:9 + fc], tA[par][:, 0:wp], ALU.mult, ALU.add),
                   reads=[W_(ur, 0, 516), tAr] + CONST, writes=[tBr], extra=ex)
                OP("dve", STT(tA[par][:, 0:wp], ub[:, 2:2 + wp], ckT[:, 16 + fc:17 + fc], tB[par][:, 0:wp], ALU.mult, ALU.add),
                   reads=[W_(ur, 0, 516), tBr] + CONST, writes=[tAr])
                OP("dve", TT(vT[:, fc, a:a + wp], tA[par][:, 0:wp], PS[bo + 0][:, 0:wp], ALU.mult),
                   reads=[tAr, W_(r_ps[bo + 0], 0, 512)], writes=[W_(R("vT"), fc * NT + a, fc * NT + a + wp)])
                if has_s:
                    OP("pool", CP(uoT[:, fc, 0:2], ub[:, wp:wp + 2]), reads=[W_(ur, 2, 516)], writes=[W_(R("uoT"), 0, 1)])
                    pv = uprevT[:, fc, :].rearrange("p (b j) -> p b j", j=2)
                    OP("pool", CP(usb[:, :, 0:2], pv), reads=[W_(R("uprevT"))], writes=[W_(R("usb"), 0, 1)], extra=ex)
                    c3 = c_sb[par][:, wp:wp + NS].rearrange("p (b t) -> p b t", t=4)
                    z3 = PS[bo + 2][:, wp:wp + NS].rearrange("p (b t) -> p b t", t=4)
                    b3 = PS[bo + 0][:, wp:wp + NS].rearrange("p (b t) -> p b t", t=4)
                    OP("dve", TT(usb[:, :, 2:6], c3, z3, ALU.mult),
                       reads=[cr, W_(r_ps[bo + 2], 0, 512)], writes=[W_(R("usb"), 1, 2)])
                    t3a = tA[par][:, 0:NS].rearrange("p (b t) -> p b t", t=4)
                    t3b = tB[par][:, 0:NS].rearrange("p (b t) -> p b t", t=4)
                    OP("dve", TS(t3a, usb[:, :, 0:4], ckT[:, fc:fc + 1], None, ALU.mult),
                       reads=[W_(R("usb"), 0, 2)] + CONST, writes=[tAr])
                    OP("dve", STT(t3b, usb[:, :, 1:5], ckT[:, 8 + fc:9 + fc], t3a, ALU.mult, ALU.add),
                       reads=[W_(R("usb"), 0, 2), tAr] + CONST, writes=[tBr])
                    OP("dve", STT(t3a, usb[:, :, 2:6], ckT[:, 16 + fc:17 + fc], t3b, ALU.mult, ALU.add),
                       reads=[W_(R("usb"), 0, 2), tBr] + CONST, writes=[tAr])
                    v3 = vT[:, fc, NP_:NT].rearrange("p (b t) -> p b t", t=4)
                    OP("dve", TT(v3, t3a, b3, ALU.mult),
                       reads=[tAr, W_(r_ps[bo + 0], 0, 512)], writes=[W_(R("vT"), fc * NT + NP_, fc * NT + NT)])
                    uo3 = uoT[:, fc, 2:34].rearrange("p (b j) -> p b j", j=2)
                    OP("pool", CP(uo3, usb[:, :, 4:6]), reads=[W_(R("usb"), 1, 2)], writes=[W_(R("uoT"), 1, 2)])
        for (a, b) in TILES0:
            w = b - a
            for d_ in range(DC):
                ob = 4 + d_ % 2
                for fc in range(DC):
                    OP("pe", MM(PS[ob][:, 0:w], Wco[:, fc, d_ * 128:(d_ + 1) * 128], vT[:, fc, a:b], fc == 0, fc == DC - 1),
                       reads=[W_(R("Wco")), W_(R("vT"), fc * NT + a, fc * NT + b)], writes=[W_(r_ps[ob], 0, 512)])
                OP("dve", TT(xT[:, d_, a:b], PS[ob][:, 0:w], xT[:, d_, a:b], ALU.add),
                   reads=[W_(r_ps[ob], 0, 512), W_(r_xT, a, b)], writes=[W_(r_xT, a, b)])
        for hlf in range(2):
            bank = 6 + hlf
            for cc in range(4):
                c = hlf * 4 + cc
                OP("pe", TR(PS[bank][0:34, cc * 128:(cc + 1) * 128], uoT[:, c, :], ident[:, :]),
                   reads=[W_(R("uoT"), 0, 2)] + CONST, writes=[W_(r_ps[bank], 0, 512)])
            OP("act", ACT(ustage[:, hlf * 512:(hlf + 1) * 512], PS[bank][0:34, :], AF.Copy),
               reads=[W_(r_ps[bank], 0, 512)], writes=[W_(R("ustage"), hlf, hlf + 1)], extra=gather(["c_sb0", "c_sb1"]))
        out_events.append(DMA("sp", DM(up_d, ustage[0:2, :]), "uout", reads=[W_(R("ustage"), 0, 2)]))
        out_events.append(DMA("sp", DM(us_d, ustage[2:34, :]), "uout", reads=[W_(R("ustage"), 0, 2)]))
        dbg_dump("conv")

        conv_evs = gather(["Wbcz0", "Wbcz1", "c_sb0", "c_sb1", "tA0", "tA1", "tB0", "tB1", "usb", "vT", "Wco",
                           "ubuf0", "ubuf1", "ustage"])
        assert grp_ctr[0] % 2 == 1
        ffn(0, 1, 2, TILES0, phase_extra=conv_evs, w0_extra=gather(["Wbcz0", "Wbcz1", "W1"]))
        dbg_dump("ffn2")

        ffn_evs = gather(FFN_BUFS)
        conv2_evs = gather(["vT", "Wco", "ubuf0", "ubuf1"])
        DMA("pool", DM(wkv[:], w_kv.rearrange("(c p) n -> p c n", p=128)), "wkv", writes=[W_(R("wkv"))], extra=ffn_evs)
        DMA("sp", DM(cosT[:], cos_d), "tabs", writes=[W_(R("tabs"), 0, 1)], extra=conv2_evs)
        DMA("sp", DM(sinT[:], sin_d), "tabs", writes=[W_(R("tabs"), 1, 2)], extra=conv2_evs)
        TABS = W_(R("tabs"), 0, 2)

        def rope_dve(ps_raw, w, a, mbuf, mres, cbuf, cres_, extra=()):
            OP("dve", TT(mbuf[:, 0:w], PS[ps_raw][:, 0:w], sinT[:, a:a + w], ALU.mult),
               reads=[W_(r_ps[ps_raw], 0, 512), TABS], writes=[mres], extra=extra)
            OP("dve", TT(cbuf[:, 0:w], PS[ps_raw][:, 0:w], cosT[:, a:a + w], ALU.mult),
               reads=[W_(r_ps[ps_raw], 0, 512), TABS], writes=[cres_], extra=extra)

        def rope_pe(ps_rot, w, mbuf, mres, cbuf, cres_):
            OP("pe", MM(PS[ps_rot][:, 0:w], swapm[:, :], mbuf[:, 0:w], True, False),
               reads=[mres] + CONST, writes=[W_(r_ps[ps_rot], 0, 512)])
            OP("pe", MM(PS[ps_rot][:, 0:w], identb[:, :], cbuf[:, 0:w], False, True),
               reads=[cres_] + CONST, writes=[W_(r_ps[ps_rot], 0, 512)])

        def rope_chunk(ps_raw, ps_rot, w, a, mbuf, mres, cbuf, cres_, extra=()):
            rope_dve(ps_raw, w, a, mbuf, mres, cbuf, cres_, extra)
            rope_pe(ps_rot, w, mbuf, mres, cbuf, cres_)

        dbg_dump("kv_a")
        kv_ctr = 0
        norm_tile(3, TILES0[0][0], TILES0[0][1], xn, xn_res, TILES0[0][0])
        for ti, (a, b) in enumerate(TILES0):
            w = b - a
            pars = []
            for kc in range(2):
                par = kv_ctr % 2
                kv_ctr += 1
                pars.append(par)
                pr = 0 + par
                for c in range(DC):
                    OP("pe", MM(PS[pr][:, 0:w], wkv[:, c, kc * 128:(kc + 1) * 128], xn[:, c, a:b], c == 0, c == DC - 1),
                       reads=[W_(R("wkv")), W_(r_xn, a, b)], writes=[W_(r_ps[pr], 0, 512)])
            part2 = None
            if ti + 1 < len(TILES0):
                norm_tile(3, TILES0[ti + 1][0], TILES0[ti + 1][1], xn, xn_res, TILES0[ti + 1][0])
            for kc in range(2):
                par = pars[kc]
                pr, pt = 0 + par, 2 + par
                rope_chunk(pr, pt, w, a, kraw[par], W_(R("kraw%d" % par)), kcb[par], W_(R("kcb%d" % par)), extra=ffn_evs)
                OP("act", ACT(KT[:, kc, a:b], PS[pt][:, 0:w], AF.Copy),
                   reads=[W_(r_ps[pt], 0, 512)], writes=[W_(R("KT"), kc * NT + a, kc * NT + b)], extra=conv2_evs)
                lo, hi = max(a, NP_ - 128), min(b, NP_)
                if lo < hi:
                    OP("act", ACT(KoutT[:, kc, lo - (NP_ - 128):hi - (NP_ - 128)], PS[pt][:, lo - a:hi - a], AF.Copy),
                       reads=[W_(r_ps[pt], 0, 512)], writes=[W_(R("KoutT"), kc * 2, kc * 2 + 1)], extra=ffn_evs)
                if b > NP_:
                    OP("act", ACT(KoutT[:, kc, 128:192], PS[pt][:, NP_ - a:NT - a], AF.Copy),
                       reads=[W_(r_ps[pt], 0, 512)], writes=[W_(R("KoutT"), kc * 2 + 1, kc * 2 + 2)], extra=ffn_evs)
            if part2 is not None:
                part2()
        dbg_dump("kv_k")
        VBLK = [2] + [HALO + 128 * m for m in range(16)] + [HALO + 1796, HALO + 1924]
        for bi, c0 in enumerate(VBLK):
            bank = 4 + bi % 2
            for c in range(DC):
                OP("pe", MM(PS[bank][:, 0:256], xn[:, c, c0:c0 + 128], wkv[:, c, 256:512], c == 0, c == DC - 1),
                   reads=[W_(R("wkv")), W_(r_xn, c0, c0 + 128)], writes=[W_(r_ps[bank], 0, 512)])
            OP("act", ACT(Vb[:, bi, :], PS[bank][:, 0:256], AF.Copy),
               reads=[W_(r_ps[bank], 0, 512)], writes=[W_(R("Vb"), bi, bi + 1)], extra=conv2_evs)
            if bi == 18:
                OP("dve", CP(vstage[:, :], PS[bank][:, 0:256]),
                   reads=[W_(r_ps[bank], 0, 512)], writes=[W_(R("vstage"))], extra=ffn_evs)
                out_events.append(DMA("sp", DM(vp_d, vstage[:, :]), "vpo", reads=[W_(R("vstage"))]))
        dbg_dump("kv_v")
        for c in range(DC):
            OP("pe", MM(PS[4][0:NS, 0:256], xn[:, c, NP_:NT], wkv[:, c, 256:512], c == 0, c == DC - 1),
               reads=[W_(R("wkv")), W_(r_xn, NP_, NT)], writes=[W_(r_ps[4], 0, 512)])
        OP("dve", CP(vsstage[:, :], PS[4][0:NS, 0:256]),
           reads=[W_(r_ps[4], 0, 512)], writes=[W_(R("vsstage"))], extra=ffn_evs)
        for sb_ in range(NSEQ):
            out_events.append(DMA("sp", DM(vs_d[sb_, 124:128, :], vsstage[4 * sb_:4 * sb_ + 4, :]), "vso",
                                  reads=[W_(R("vsstage"))]))
        OP("act", ACT(Vs_bf[:, :], PS[4][0:NS, 0:256], AF.Copy),
           reads=[W_(r_ps[4], 0, 512)], writes=[W_(R("Vnew"))], extra=conv2_evs)
        dbg_dump("kv_vs")
        for kc in range(2):
            OP("pe", TR(PS[6][:, kc * 128:(kc + 1) * 128], KoutT[:, kc, 0:128], ident[:, :]),
               reads=[W_(R("KoutT"), 0, 4)] + CONST, writes=[W_(r_ps[6], 0, 512)])
        OP("dve", CP(kstage[:, :], PS[6][:, 0:256]), reads=[W_(r_ps[6], 0, 512)], writes=[W_(R("kstage"))], extra=ffn_evs)
        out_events.append(DMA("sp", DM(kp_d, kstage[:, :]), "kpo", reads=[W_(R("kstage"))]))
        for kc in range(2):
            OP("pe", TR(PS[7][0:NS, kc * 128:(kc + 1) * 128], KoutT[:, kc, 128:192], ident[:, :]),
               reads=[W_(R("KoutT"), 0, 4)] + CONST, writes=[W_(r_ps[7], 0, 512)])
        OP("dve", CP(ksstage[:, :], PS[7][0:NS, 0:256]), reads=[W_(r_ps[7], 0, 512)], writes=[W_(R("ksstage"))], extra=ffn_evs)
        for sb_ in range(NSEQ):
            out_events.append(DMA("sp", DM(ks_d[sb_, 124:128, :], ksstage[4 * sb_:4 * sb_ + 4, :]), "kso",
                                  reads=[W_(R("ksstage"))]))

        dbg_dump("kv")
        kv_evs = gather(["wkv", "kraw0", "kraw1", "kt1_0", "kt1_1", "kt2_0", "kt2_1", "kt3_0", "kt3_1", "kcb0", "kcb1", "KoutT",
                         "kstage", "vstage", "ksstage", "vsstage"])
        ffn(1, 0, 4, TILES1, phase_extra=kv_evs)
        dbg_dump("ffn3")

        ffn_evs = gather(FFN_BUFS)
        xn_evs = r_xn.all_events()
        for jp_ in range(DC):
            gc_, i4 = jp_ // 4, jp_ % 4
            for half in range(2):
                g = 2 * gc_ + half
                src = w_q[:, g * 256 + i4 * 64:g * 256 + (i4 + 1) * 64].rearrange("(c p) m -> p c m", p=128)
                c0_ = jp_ * 128 + half * 64
                DMA("pool", DM(Wq[:, :, c0_:c0_ + 64], src), "Wq%d" % jp_,
                    writes=[W_(R("Wq"), 2 * jp_ + half, 2 * jp_ + half + 1)], extra=ffn_evs)
        wi = 16
        for gc in range(2):
            for half in range(2):
                g = 2 * gc + half
                src2 = w_o[g * 256:(g + 1) * 256, :].rearrange("(i p) n -> p i n", p=64)
                dst2 = Wo2[half * 64:(half + 1) * 64, gc * 4:(gc + 1) * 4, :]
                DMA("pool", DM(dst2, src2), "Wo2", writes=[W_(R("Wq"), wi, wi + 1)], extra=ffn_evs)
                wi += 1
        WQR = W_(R("Wq"), 0, wi)
        WOR = W_(R("Wq"), 16, wi)
        ESK = W_(R("esink"))
        OP("act", ACT(esink[:, :], esink[:, :], AF.Exp), reads=[ESK], writes=[ESK])

        ATILES = [(HALO + 512 * i, HALO + 512 * (i + 1)) for i in range(4)] + [(NP_ - 128, NT)]
        QTR = W_(R("QT"))
        KTR = W_(R("KT"), 0, 2 * NT)
        VBR = W_(R("Vb"), 0, 19)
        u_ctr = [0]
        seq_ctr = [0]

        def unit_prompt(jp, qc0, qa, vprev_i, vcur_i, mask_ap):
            u = u_ctr[0]
            u_ctr[0] += 1
            par = u % 3
            zpar = u % 2
            gc = jp // 4
            xb = 2 * par
            zb = 6 + zpar
            ptr = W_(R("PT%d" % par)) if par < 2 else W_(R("qraw0"))
            lr = W_(R("lnD%d" % zpar))

            def A():
                for (bank, base) in ((xb, 0), (xb + 1, 64)):
                    OP("pe", MM(PS[bank][:, 0:128], KT[base:base + 64, gc, qa - 128:qa], QT[base:base + 64, jp, qc0:qc0 + 128]),
                       reads=[QTR, KTR], writes=[W_(r_ps[bank], 0, 512)])
                    OP("pe", MM(PS[bank][:, 128:256], KT[base:base + 64, gc, qa:qa + 128], QT[base:base + 64, jp, qc0:qc0 + 128]),
                       reads=[QTR, KTR], writes=[W_(r_ps[bank], 0, 512)])
                OP("act", ACT(PT[par][:, :].rearrange("p (b n) -> p b n", b=2), PSA[:, xb:xb + 2, 0:256], AF.Exp, scale=0.125),
                   reads=[W_(r_ps[xb], 0, 512), W_(r_ps[xb + 1], 0, 512)], writes=[ptr])
                OP("dve", TT(PT[par][:, :], PT[par][:, :], mask_ap, ALU.mult), reads=[ptr] + CONST, writes=[ptr])

            def B():
                for (base, off) in ((0, 0), (64, 256)):
                    g = 2 * gc + (base // 64)
                    OP("pe", MM(PS[zb][base:base + 64, 0:128], Vb[:, vprev_i, g * 64:(g + 1) * 64], PT[par][:, off:off + 128], True, False),
                       reads=[ptr, VBR], writes=[W_(r_ps[zb], 0, 512)])
                    OP("pe", MM(PS[zb][base:base + 64, 0:128], Vb[:, vcur_i, g * 64:(g + 1) * 64], PT[par][:, off + 128:off + 256], False, True),
                       reads=[ptr, VBR], writes=[W_(r_ps[zb], 0, 512)])
                    OP("pe", MM(PS[zb][base:base + 64, 128:256], ones[:, 0:64], PT[par][:, off:off + 128], True, False),
                       reads=[ptr, W_(R("ones"))], writes=[W_(r_ps[zb], 0, 512)])
                    OP("pe", MM(PS[zb][base:base + 64, 128:256], ones[:, 0:64], PT[par][:, off + 128:off + 256], False, True),
                       reads=[ptr, W_(R("ones"))], writes=[W_(r_ps[zb], 0, 512)])
                OP("act", ACT(lnD[zpar][:, 0:128], PS[zb][:, 128:256], AF.Ln, bias=esink[:, jp:jp + 1]),
                   reads=[W_(r_ps[zb], 0, 512), ESK], writes=[lr])
                OP("act", ACT(lnD[zpar][:, 0:128], lnD[zpar][:, 0:128], AF.Exp, scale=-1.0), reads=[lr], writes=[lr])
                OP("dve", TT(attnT[:, jp, qc0:qc0 + 128], PS[zb][:, 0:128], lnD[zpar][:, 0:128], ALU.mult),
                   reads=[W_(r_ps[zb], 0, 512), lr], writes=[W_(R("attnT"))])
            return A, B

        def sample_attention():
            P0, P1 = W_(R("PT0")), W_(R("PT1"))
            z4, z5 = W_(r_ps[4], 0, 512), W_(r_ps[5], 0, 512)
            for jp in range(DC):
                gc = jp // 4
                for (bank, base) in ((0, 0), (1, 64)):
                    OP("pe", MM(PS[bank][0:NS, jp * 64:(jp + 1) * 64], KT[base:base + 64, gc, NP_:NT],
                                QT[base:base + 64, jp, 128:192]),
                       reads=[QTR, KTR], writes=[W_(r_ps[bank], 0, 512)])
            OP("act", ACT(PTC[0:NS, :].rearrange("p (b n) -> p b n", b=2), PSA[0:NS, 0:2, :], AF.Exp, scale=0.125),
               reads=[W_(r_ps[0], 0, 512), W_(r_ps[1], 0, 512)], writes=[P0, P1])
            for k in range(16):
                OP("dve", TT(PTC[0:NS, k * 64:(k + 1) * 64], PTC[0:NS, k * 64:(k + 1) * 64], msamp[0:NS, 64:128], ALU.mult),
                   reads=[P0, P1] + CONST, writes=[P0, P1])
            for (base, hoff) in ((0, 0), (64, 512)):
                for jp in range(DC):
                    g = 2 * (jp // 4) + (base // 64)
                    c0 = hoff + jp * 64
                    OP("pe", MMX(PS[4][base:base + 64, jp * 64:(jp + 1) * 64], Vs_bf[:, g * 64:(g + 1) * 64],
                                 PTC[0:NS, c0:c0 + 64], jp == 0, False),
                       reads=[P0, P1, W_(R("Vnew"))], writes=[z4])
                for jp in range(DC):
                    c0 = hoff + jp * 64
                    OP("pe", MMX(PS[5][base:base + 64, jp * 64:(jp + 1) * 64], ones[0:NS, 0:64],
                                 PTC[0:NS, c0:c0 + 64], jp == 0, False),
                       reads=[P0, P1, W_(R("ones"))], writes=[z5])

            def seq_unit(sb_):
                par = sb_ % 2
                sp = sb_ % 2
                xb = 2 if par == 0 else 0
                ptr = W_(R("PT%d" % par))
                kcr = W_(R("KcT%d" % sp))
                vcr = W_(R("Vc%d" % sp))
                qc0 = 128 + 4 * sb_

                def A():
                    DMA("sp", DM(ckst[sp][:, :], ck[sb_]), "ckst%d" % sp, writes=[W_(R("ckst%d" % sp))])
                    DMA("pool", DM(Vc[sp][:, :], cv[sb_]), "Vc%d" % sp, writes=[vcr])
                    for kc in range(2):
                        OP("pe", TR(PS[6][:, kc * 128:(kc + 1) * 128], ckst[sp][:, kc * 128:(kc + 1) * 128], ident[:, :]),
                           reads=[W_(R("ckst%d" % sp))] + CONST, writes=[W_(r_ps[6], 0, 512)])
                    OP("act", ACT(KcT[sp][:, :, :], PS[6][:, 0:256].rearrange("p (c n) -> p c n", c=2), AF.Copy),
                       reads=[W_(r_ps[6], 0, 512)], writes=[kcr])
                    for (bank, base) in ((xb, 0), (xb + 1, 64)):
                        for jp in range(DC):
                            OP("pe", MM(PS[bank][:, jp * 4:jp * 4 + 4], KcT[sp][base:base + 64, jp // 4, :],
                                        QT[base:base + 64, jp, qc0:qc0 + 4]),
                               reads=[QTR, kcr], writes=[W_(r_ps[bank], 0, 512)])
                    OP("act", ACT(PT[par][:, 0:64].rearrange("p (b n) -> p b n", b=2), PSA[:, xb:xb + 2, 0:32], AF.Exp, scale=0.125),
                       reads=[W_(r_ps[xb], 0, 512), W_(r_ps[xb + 1], 0, 512)], writes=[ptr])
                    OP("dve", TT(PT[par][:, 0:64], PT[par][:, 0:64], msamp[:, 0:64], ALU.mult), reads=[ptr] + CONST, writes=[ptr])

                def B():
                    for (base, hoff) in ((0, 0), (64, 32)):
                        for jp in range(DC):
                            g = 2 * (jp // 4) + (base // 64)
                            col = jp * 64 + 4 * sb_
                            OP("pe", MMX(PS[4][base:base + 64, col:col + 4], Vc[sp][:, g * 64:(g + 1) * 64],
                                         PT[par][:, hoff + jp * 4:hoff + jp * 4 + 4], False, False),
                               reads=[ptr, vcr], writes=[z4])
                        for jp in range(DC):
                            col = jp * 64 + 4 * sb_
                            OP("pe", MMX(PS[5][base:base + 64, col:col + 4], ones[:, 0:64],
                                         PT[par][:, hoff + jp * 4:hoff + jp * 4 + 4], False, sb_ == NSEQ - 1),
                               reads=[ptr, W_(R("ones"))], writes=[z5])
                return A, B

            prevB_ = None
            for sb_ in range(NSEQ):
                A_, B_ = seq_unit(sb_)
                A_()
                if prevB_ is not None:
                    prevB_()
                prevB_ = B_
            prevB_()
            lr = W_(R("lnDall"))
            for jp in range(DC):
                OP("dve", TS(lnDall[:, jp * 64:(jp + 1) * 64], PS[5][:, jp * 64:(jp + 1) * 64], esink[:, jp:jp + 1], None, ALU.add),
                   reads=[z5, ESK], writes=[lr])
            OP("act", ACT(lnDall[:, :], lnDall[:, :], AF.Ln), reads=[lr], writes=[lr])
            OP("act", ACT(lnDall[:, :], lnDall[:, :], AF.Exp, scale=-1.0), reads=[lr], writes=[lr])
            OP("dve", TT(attnT[:, :, 128:192], PS[4][:, :].rearrange("p (j q) -> p j q", j=DC),
                         lnDall[:, :].rearrange("p (j q) -> p j q", j=DC), ALU.mult),
               reads=[z4, lr], writes=[W_(R("attnT"))])

        norm_tile(5, ATILES[0][0], ATILES[0][1], xnA, lambda a_, b_: W_(R("xnA")), 0, extra=xn_evs)
        for ti, (a, b) in enumerate(ATILES):
            w = b - a
            def q_tail(jp):
                par = jp % 2
                pt = 2 * (jp % 4) + 1
                rope_pe(pt, w, qraw[par], W_(R("qraw%d" % par)), qcb[par], W_(R("qcb%d" % par)))
                OP("act", ACT(QT[:, jp, 0:w], PS[pt][:, 0:w], AF.Copy), reads=[W_(r_ps[pt], 0, 512)], writes=[QTR])

            for jp in range(DC):
                par = jp % 2
                pr, pt = 2 * (jp % 4), 2 * (jp % 4) + 1
                for c in range(DC):
                    OP("pe", MM(PS[pr][:, 0:w], Wq[:, c, jp * 128:(jp + 1) * 128], xnA[:, c, 0:w], c == 0, c == DC - 1),
                       reads=[W_(R("Wq"), 2 * jp, 2 * jp + 2), W_(R("xnA"))], writes=[W_(r_ps[pr], 0, 512)])
                rope_dve(pr, w, a, qraw[par], W_(R("qraw%d" % par)), qcb[par], W_(R("qcb%d" % par)))
                if jp >= 1:
                    q_tail(jp - 1)
            q_tail(DC - 1)
            units = []
            if ti < 4:
                for bi in range(4):
                    m = ti * 4 + bi
                    for jp in range(DC):
                        units.append(unit_prompt(jp, 128 * bi, a + 128 * bi, m, m + 1, (mfirst if m == 0 else mstd)[:, :]))
            else:
                for jp in range(DC):
                    units.append(unit_prompt(jp, 0, a, 17, 18, mstd[:, :]))
            pend = []
            for ui, (A_, B_) in enumerate(units):
                A_()
                pend.append(B_)
                if len(pend) > 2:
                    pend.pop(0)()
                if ui == 4 and ti + 1 < len(ATILES):
                    norm_tile(5, ATILES[ti + 1][0], ATILES[ti + 1][1], xnA, lambda a_, b_: W_(R("xnA")), 0)
            for B_ in pend:
                B_()
            if ti == 4:
                sample_attention()
            lo = 0 if ti < 4 else 124
            for d_ in range(DC):
                ob = 4 + d_ % 4
                for jp in range(DC):
                    OP("pe", MM(PS[ob][:, 0:w], Wo2[:, jp, d_ * 128:(d_ + 1) * 128], attnT[:, jp, 0:w], jp == 0, jp == DC - 1),
                       reads=[WOR, W_(R("attnT"))], writes=[W_(r_ps[ob], 0, 512)])
                OP("dve", TT(xT[:, d_, a + lo:b], PS[ob][:, lo:w], xT[:, d_, a + lo:b], ALU.add),
                   reads=[W_(r_ps[ob], 0, 512), W_(r_xT, a + lo, b)], writes=[W_(r_xT, a + lo, b)])
        dbg_dump("attn")

        att_evs = gather(["Wq", "xnA", "QT", "attnT", "PT0", "PT1", "qraw0", "qraw1", "qcb0", "qcb1", "qt1_0", "qt1_1", "qt2",
                          "lnD0", "lnD1", "lnDall"])
        ffn(1, 1, 6, TILES1, phase_extra=att_evs)
        dbg_dump("ffn4")

        ffn_evs = gather(FFN_BUFS)
        FB = 512
        nblk = (OWN + NS + FB - 1) // FB
        blocks = []
        for bi in range(nblk):
            a = HALO + bi * FB
            b = min(a + FB, NT)
            blocks.append((bi, a, b))

        def fin_norm(bi, a, b):
            par = bi % 2
            yr = W_(R("yblk%d" % par))
            norm_tile(7, a, b, yblk[par], lambda a_, b_, yr=yr: yr, 0, extra=ffn_evs)

        sub_ctr = [0]

        def fin_out(bi, a, b):
            par = bi % 2
            yr = W_(R("yblk%d" % par))
            nsub = (b - a + 127) // 128
            for sub in range(nsub):
                a_s = a + sub * 128
                ws = min(128, b - a_s)
                sc = sub_ctr[0]
                sub_ctr[0] += 1
                ys = sc % 4
                for hlf in range(2):
                    bank = 2 * (sc % 2) + hlf
                    for cc in range(4):
                        c = hlf * 4 + cc
                        OP("pe", TR(PS[bank][0:ws, cc * 128:(cc + 1) * 128], yblk[par][:, c, sub * 128:sub * 128 + ws], ident[:, :]),
                           reads=[yr] + CONST, writes=[W_(r_ps[bank], 0, 512)])
                    OP("act", ACT(ystage[ys][0:ws, hlf * 512:(hlf + 1) * 512], PS[bank][0:ws, :], AF.Copy),
                       reads=[W_(r_ps[bank], 0, 512)], writes=[W_(R("ystage%d" % ys), hlf, hlf + 1)], extra=ffn_evs)
                out_events.append(DMA("sp", DM(y_d[a_s - HALO:a_s - HALO + ws, :], ystage[ys][0:ws, :]), "yo%d" % ys,
                                      reads=[W_(R("ystage%d" % ys), 0, 2)]))

        fin_norm(*blocks[0])
        for i, blk_ in enumerate(blocks):
            if i + 1 < len(blocks):
                fin_norm(*blocks[i + 1])
            fin_out(*blk_)
    except _Stop:
        pass
    S.op("sp", lambda e: None, waits=out_events)
    S.emit()
    return nc


_NC_CACHE = {}


def _host_tables(core):
    c = core % 4
    p0 = c * OWN
    pos = np.concatenate([np.arange(p0 - HALO, p0 + OWN), 8192 + np.tile(np.arange(4), NSEQ)]).astype(np.int64)
    posf = np.maximum(pos, 0).astype(np.float32)
    inv = (1.0 / (10000.0 ** (np.arange(0, 64, 2, dtype=np.float32) / np.float32(64)))).astype(np.float32)
    ang = (posf[:, None] * inv[None, :]).astype(np.float32)
    cos = np.cos(ang).astype(np.float32)
    sin = np.sin(ang).astype(np.float32)
    p = np.arange(128)
    j = p % 32
    half = (p % 64) // 32
    cosT = np.ascontiguousarray(cos[:, j].T)
    sinT = np.ascontiguousarray((sin[:, j] * np.where(half == 0, 1.0, -1.0)[None, :]).T.astype(np.float32))
    return cosT, sinT


def kernel(x_prompt, x_sample, state_conv, cache_k, cache_v, meta_tokens, norm_g, ffn_w_in, ffn_w_out,
           conv_w_in, conv_kernel, conv_w_out, kv_norm_g, w_kv, w_q, w_o, sinks, final_norm_g):
    f32 = np.float32
    x_prompt = np.asarray(x_prompt, f32)
    x_sample = np.asarray(x_sample, f32)
    B = x_prompt.shape[0]
    if "nc" not in _NC_CACHE:
        _NC_CACHE["nc"] = build_nc()
    nc = _NC_CACHE["nc"]

    def fm(v):
        return np.ascontiguousarray(np.asarray(v, f32).reshape(DC, 128).T)

    ng = np.asarray(norm_g, f32)
    gT = np.concatenate([fm(ng[0, 0]), fm(ng[0, 1]), fm(ng[0, 2]), fm(kv_norm_g),
                         fm(ng[1, 0]), fm(ng[1, 1]), fm(ng[1, 2]), fm(final_norm_g)], axis=1)
    ckn = np.asarray(conv_kernel, f32)[0]
    ckT = np.concatenate([fm(ckn[0]), fm(ckn[1]), fm(ckn[2])], axis=1)
    sk = np.asarray(sinks, f32)[0]
    sinkT = np.zeros((128, 40), f32)
    for jp in range(8):
        gc, i = jp // 4, jp % 4
        sinkT[0:64, jp] = sk[4 * (2 * gc) + i]
        sinkT[64:128, jp] = sk[4 * (2 * gc + 1) + i]
        sinkT[:, 8 + 4 * jp:12 + 4 * jp] = sinkT[:, jp:jp + 1]
    ident = np.eye(128, dtype=f32)
    swapm = np.zeros((128, 128), f32)
    pp = np.arange(128)
    swapm[pp, pp ^ 32] = 1.0
    kk = np.arange(128)[:, None]
    qq = np.arange(128)[None, :]
    prev = (kk >= qq).astype(f32)
    curm = (kk <= qq).astype(f32)
    mstd = np.concatenate([prev, curm, prev, curm], axis=1)
    mfirst0 = np.concatenate([np.zeros_like(prev), curm, np.zeros_like(prev), curm], axis=1)
    ii = np.arange(128)[:, None]
    tt = np.arange(4)[None, :]
    sprev = (ii >= tt).astype(f32)
    scur = ((ii <= tt) & (ii < 4)).astype(f32)
    msamp = np.zeros((128, 128), f32)
    msamp[:, 0:64] = np.tile(sprev, (1, 16))
    kq = np.arange(64)
    msamp[0:64, 64:128] = ((kq[:, None] // 4 == kq[None, :] // 4) & (kq[:, None] % 4 <= kq[None, :] % 4)).astype(f32)

    shared = dict(
        ffn_w_in=np.asarray(ffn_w_in, f32), ffn_w_out=np.asarray(ffn_w_out, f32),
        conv_w_in=np.asarray(conv_w_in, f32)[0], conv_w_out=np.asarray(conv_w_out, f32)[0],
        w_kv=np.asarray(w_kv, f32), w_q=np.asarray(w_q, f32)[0], w_o=np.asarray(w_o, f32)[0],
        gT=gT, ckT=ckT, sinkT=sinkT, ident=ident, swapm=swapm, mstd=mstd, msamp=msamp)
    meta = np.asarray(meta_tokens, f32)
    sc = np.asarray(state_conv, f32)[0]
    ckc = np.asarray(cache_k, f32).reshape(128, 128, 256)
    cvc = np.asarray(cache_v, f32).reshape(128, 128, 256)
    in_maps = []
    for core in range(8):
        bseq, c = core // 4, core % 4
        xfull = np.concatenate([meta, x_prompt[bseq]], axis=0)
        p0 = c * OWN
        lo = p0 - HALO
        rows = np.zeros((NP_, D), f32)
        s0 = max(lo, 0)
        rows[s0 - lo:] = xfull[s0:p0 + OWN]
        xin = np.concatenate([rows, x_sample[16 * core:16 * core + 16].reshape(NS, D)], axis=0)
        cosT, sinT = _host_tables(core)
        m = dict(shared)
        m.update(xin=np.ascontiguousarray(xin),
                 sconv=np.ascontiguousarray(sc[16 * core:16 * core + 16].reshape(2 * NSEQ, D)),
                 ck=np.ascontiguousarray(ckc[16 * core:16 * core + 16]),
                 cv=np.ascontiguousarray(cvc[16 * core:16 * core + 16]),
                 cosT=cosT, sinT=sinT, mfirst=(mfirst0 if c == 0 else mstd))
        in_maps.append(m)
    res = run_bass_kernel_spmd(nc, in_maps, core_ids=list(range(8)))
    R_ = res.results
    _NC_CACHE["last"] = R_
    y_prompt = np.zeros((B, 8192, D), f32)
    y_sample = np.zeros((128, 4, D), f32)
    ncp = np.zeros((1, B, 2, D), f32)
    ncs = np.zeros((1, 128, 2, D), f32)
    nkp = np.zeros((B, 128, 4, 64), f32)
    nvp = np.zeros((B, 128, 4, 64), f32)
    nks = np.zeros((128, 128, 4, 64), f32)
    nvs = np.zeros((128, 128, 4, 64), f32)
    for core in range(8):
        bseq, c = core // 4, core % 4
        r = R_[core]
        y = np.asarray(r["y"])
        yp = y[0:OWN]
        p0 = c * OWN
        if c == 0:
            y_prompt[bseq, 0:OWN - 16] = yp[16:]
        else:
            y_prompt[bseq, p0 - 16:p0 - 16 + OWN] = yp
        y_sample[16 * core:16 * core + 16] = y[OWN:].reshape(16, 4, D)
        ncs[0, 16 * core:16 * core + 16] = np.asarray(r["u_s"]).reshape(16, 2, D)
        nks[16 * core:16 * core + 16] = np.asarray(r["ks"]).reshape(16, 128, 4, 64)
        nvs[16 * core:16 * core + 16] = np.asarray(r["vs"]).reshape(16, 128, 4, 64)
        if c == 3:
            ncp[0, bseq] = np.asarray(r["u_p"])
            nkp[bseq] = np.asarray(r["kp"]).reshape(128, 4, 64)
            nvp[bseq] = np.asarray(r["vp"]).reshape(128, 4, 64)
    return (y_prompt, y_sample, ncp, ncs, nkp, nvp, nks, nvs)
```

```python
import contextlib
import numpy as np
import concourse.bass as bass
import concourse.mybir as mybir
from concourse.alu_op_type import AluOpType as ALU
from concourse.bass_utils import run_bass_kernel_spmd

F32 = mybir.dt.float32
BF16 = mybir.dt.bfloat16
AF = mybir.ActivationFunctionType

D = 1024
DC = 8
DFF = 2816
FCH = 22
HALO = 130
OWN = 2052
NP_ = HALO + OWN
NS = 64
NT = NP_ + NS
NSEQ = 16
EPS = 1e-6
DEBUG = {}

ENGS = ("pe", "act", "dve", "pool", "sp")


class Ev:
    __slots__ = ("kind", "eng", "sem", "count", "needed", "pos")

    def __init__(self, kind, eng):
        self.kind = kind
        self.eng = eng
        self.sem = None
        self.count = None
        self.needed = False
        self.pos = None


class Sched:
    def __init__(self, nc):
        self.nc = nc
        self.q = {e: [] for e in ENGS}
        self.dma_sems = {}
        self.eng_sem = {}

    def _reduce(self, waits, eng):
        best = {}
        for w in waits:
            if w is None:
                continue
            if w.kind == 'c':
                k = ('c', w.eng)
                if k not in best or w.pos > best[k].pos:
                    best[k] = w
            else:
                k = ('d', w.sem)
                if k not in best or w.count > best[k].count:
                    best[k] = w
        ws = list(best.values())
        for w in ws:
            w.needed = True
        return ws

    def op(self, eng, fn, waits=()):
        ev = Ev('c', eng)
        ev.pos = len(self.q[eng])
        self.q[eng].append((fn, self._reduce(waits, eng), ev))
        return ev

    def dma(self, eng, fn, key, waits=()):
        ev = Ev('d', eng)
        ent = self.dma_sems.setdefault(key, [None, 0])
        ent[1] += 16
        ev.sem = key
        ev.count = ent[1]
        ev.pos = len(self.q[eng])
        self.q[eng].append((fn, self._reduce(waits, eng), ev))
        return ev

    def emit(self):
        nc = self.nc
        with contextlib.ExitStack() as st:
            for e in ENGS:
                self.eng_sem[e] = st.enter_context(nc.semaphore("s_" + e))
            for i, key in enumerate(self.dma_sems):
                self.dma_sems[key][0] = st.enter_context(nc.semaphore("d%d" % i))
            for e in ENGS:
                c = 0
                for (fn, ws, ev) in self.q[e]:
                    if ev.kind == 'c' and ev.needed:
                        c += 1
                        ev.count = c
                    if ev.kind == 'd' and str(ev.sem).startswith("const"):
                        ev.count = self.dma_sems[ev.sem][1]
            block = st.enter_context(nc.Block())
            sched = self

            def run(engname):
                def body(eh):
                    waited = {}
                    for (fn, ws, ev) in sched.q[engname]:
                        for w in ws:
                            if w.kind == 'c':
                                sem = sched.eng_sem[w.eng]
                                k = ('c', w.eng)
                            else:
                                sem = sched.dma_sems[w.sem][0]
                                k = ('d', w.sem)
                            if waited.get(k, 0) >= w.count:
                                continue
                            waited[k] = w.count
                            eh.wait_ge(sem, w.count)
                        ins = fn(eh)
                        if ins is None:
                            continue
                        if ev.kind == 'c':
                            if ev.needed:
                                ins.then_inc(sched.eng_sem[engname], 1)
                        else:
                            ins.then_inc(sched.dma_sems[ev.sem][0], 16)
                return body

            block.tensor(run("pe"))
            block.scalar(run("act"))
            block.vector(run("dve"))
            block.gpsimd(run("pool"))
            block.sync(run("sp"))


class RR:
    def __init__(self):
        self.segs = []

    def _cut(self, x):
        for i, s in enumerate(self.segs):
            if s[0] < x < s[1]:
                self.segs[i:i + 1] = [[s[0], x, list(s[2]), list(s[3])], [x, s[1], list(s[2]), list(s[3])]]
                return

    def cover(self, a, b):
        self._cut(a)
        self._cut(b)
        self.segs.sort(key=lambda s: s[0])
        pts = a
        new = []
        for s in self.segs:
            if s[1] <= a or s[0] >= b:
                continue
            if s[0] > pts:
                new.append([pts, s[0], [], []])
            pts = s[1]
        if pts < b:
            new.append([pts, b, [], []])
        self.segs += new
        self.segs.sort(key=lambda s: s[0])
        return [s for s in self.segs if s[0] >= a and s[1] <= b]

    def all_events(self):
        out = []
        for s in self.segs:
            out += s[2] + s[3]
        return out


class _Stop(Exception):
    pass


class K:
    pass


def build_nc():
    nc = bass.Bass("TRN2", target_bir_lowering=False)
    S = Sched(nc)

    def din(name, shape):
        return nc.dram_tensor(name, list(shape), F32, kind="ExternalInput").ap()

    def dout(name, shape):
        return nc.dram_tensor(name, list(shape), F32, kind="ExternalOutput").ap()

    xin = din("xin", [NT, D])
    sconv = din("sconv", [2 * NSEQ, D])
    ck = din("ck", [NSEQ, 128, 256])
    cv = din("cv", [NSEQ, 128, 256])
    ffn_w_in = din("ffn_w_in", [2, 2, D, 2 * DFF])
    ffn_w_out = din("ffn_w_out", [2, 2, DFF, D])
    conv_w_in = din("conv_w_in", [D, 3 * D])
    conv_w_out = din("conv_w_out", [D, D])
    w_kv = din("w_kv", [D, 512])
    w_q = din("w_q", [D, D])
    w_o = din("w_o", [D, D])
    gT_d = din("gT", [128, 64])
    ckT_d = din("ckT", [128, 24])
    sinkT_d = din("sinkT", [128, 40])
    cos_d = din("cosT", [128, NT])
    sin_d = din("sinT", [128, NT])
    ident_d = din("ident", [128, 128])
    swap_d = din("swapm", [128, 128])
    mstd_d = din("mstd", [128, 512])
    mfirst_d = din("mfirst", [128, 512])
    msamp_d = din("msamp", [128, 128])

    y_d = dout("y", [OWN + NS, D])
    up_d = dout("u_p", [2, D])
    us_d = dout("u_s", [2 * NSEQ, D])
    kp_d = dout("kp", [128, 256])
    vp_d = dout("vp", [128, 256])
    ks_d = dout("ks", [NSEQ, 128, 256])
    vs_d = dout("vs", [NSEQ, 128, 256])
    dbg_d = None
    if DEBUG.get("xT"):
        dbg_d = dout("dbg", [128, DC * NT])

    base = (nc._sbuf_addr_for_side('left') + 63) // 64 * 64
    cap = 229376
    cur = [base]

    def esize(dt):
        return 4 if dt == F32 else 2

    def alloc(name, shape, dt, at=None):
        n = 1
        for d_ in shape[1:]:
            n *= d_
        nbytes = (n * esize(dt) + 63) // 64 * 64
        if at is None:
            o = cur[0]
            cur[0] += nbytes
        else:
            o = at
        assert o + nbytes <= cap, (name, o, nbytes)
        return nc.alloc_sbuf_tensor_at(name, list(shape), dt, offset=o)

    xT = alloc("xT", [128, DC, NT], F32)
    XN_OFF = cur[0]
    xn = alloc("xn", [128, DC, NT], BF16)
    ident = alloc("ident", [128, 128], F32)
    swapm = alloc("swapm", [128, 128], BF16)
    identb = alloc("identb", [128, 128], BF16)
    ones = alloc("ones", [128, 128], BF16)
    gT = alloc("gT", [128, 64], F32)
    ckT = alloc("ckT", [128, 24], F32)
    esink = alloc("esink", [128, 40], F32)
    epst = alloc("epst", [128, 1], F32)
    mstd = alloc("mstd", [128, 512], BF16)
    mfirst = alloc("mfirst", [128, 512], BF16)
    msamp = alloc("msamp", [128, 128], BF16)
    uprevT = alloc("uprevT", [128, DC, 2 * NSEQ], F32)
    uoT = alloc("uoT", [128, DC, 34], F32)
    rstd = [alloc("rstd%d" % i, [128, 512], F32) for i in range(2)]
    rtmp = alloc("rtmp", [128, 512], F32)
    ptmp = alloc("ptmp", [128, 512], F32)
    sq = alloc("sq", [128, DC, 512], BF16)
    ARENA = cur[0]
    arena_size = cap - ARENA
    assert arena_size >= 82200, arena_size

    A0 = ARENA
    Wg = [alloc("Wg%d" % s, [128, DC, 256], BF16, at=A0 + s * 12288) for s in range(2)]
    Wu = [alloc("Wu%d" % s, [128, DC, 256], BF16, at=A0 + s * 12288 + 4096) for s in range(2)]
    Wo_ = [alloc("Wo%d" % s, [128, 2, 1024], BF16, at=A0 + s * 12288 + 8192) for s in range(2)]
    hbuf = [[alloc("h%d%d" % (p, f), [128, 512], BF16, at=A0 + 24576 + (2 * p + f) * 1024) for f in range(2)]
            for p in range(2)]
    sbuf_ = [alloc("s%d" % f, [128, 512], F32, at=A0 + 28672 + f * 2048) for f in range(2)]
    NXS = 8
    xs = [alloc("xs%d" % i, [128, D], F32, at=A0 + 32768 + i * 4096) for i in range(NXS)]
    Wbcz = [alloc("Wbcz%d" % s, [128, 3, DC, 128], BF16, at=A0 + 12288 + s * 6144) for s in range(2)]
    c_sb = [alloc("c_sb%d" % i, [128, 512], F32, at=A0 + 0 + i * 2048) for i in range(2)]
    tA = [alloc("tA%d" % i, [128, 512], F32, at=A0 + 4096 + i * 2048) for i in range(2)]
    tB = [alloc("tB%d" % i, [128, 512], F32, at=A0 + 8192 + i * 2048) for i in range(2)]
    usb = alloc("usb", [128, NSEQ, 6], F32, at=A0 + 24576)
    Wq = alloc("Wq", [128, DC, 1024], BF16, at=A0)
    Wo2 = alloc("Wo2", [128, DC, 1024], BF16, at=A0 + 16384)
    B0 = A0 + 32768
    vT = alloc("vT", [128, DC, NT], BF16, at=A0 + 29824)
    Wco = alloc("Wco", [128, DC, 1024], BF16, at=A0 + 29824 + 35968)
    ubuf = [alloc("ubuf%d" % i, [128, 516], F32, at=A0 + 25600 + i * 2112) for i in range(2)]
    assert A0 + 29824 + 35968 + 16384 <= cap
    KT = alloc("KT", [128, 2, NT], BF16, at=B0)
    Vb = alloc("Vb", [128, 19, 256], BF16, at=B0 + 8992)
    Vs_bf = alloc("Vs_bf", [NS, 256], BF16, at=B0 + 8992 + 9728)
    C0 = B0 + 8992 + 9728 + 8192
    cosT = alloc("cosT", [128, NT], F32, at=C0)
    sinT = alloc("sinT", [128, NT], F32, at=C0 + 8992)
    E0 = C0 + 2 * 8992
    wkv = alloc("wkv", [128, DC, 512], BF16, at=A0)
    kraw = [alloc("kraw%d" % i, [128, 512], BF16, at=A0 + 8192 + i * 1024) for i in range(2)]
    kt1 = [alloc("kt1_%d" % i, [128, 512], F32, at=A0 + 10240 + i * 2048) for i in range(2)]
    kcb = [alloc("kcb%d" % i, [128, 512], BF16, at=A0 + 10240 + i * 1024) for i in range(2)]
    kt2 = [alloc("kt2_%d" % i, [128, 512], F32, at=A0 + 14336 + i * 2048) for i in range(2)]
    kt3 = [alloc("kt3_%d" % i, [128, 512], F32, at=A0 + 18432 + i * 2048) for i in range(2)]
    KoutT = alloc("KoutT", [128, 2, 192], F32, at=A0 + 22528)
    kstage = alloc("kstage", [128, 256], F32, at=A0 + 24576)
    vstage = alloc("vstage", [128, 256], F32, at=A0 + 25600)
    ksstage = alloc("ksstage", [64, 256], F32, at=A0 + 26624)
    vsstage = alloc("vsstage", [64, 256], F32, at=A0 + 27648)
    ckst = [alloc("ckst%d" % i, [128, 256], F32, at=E0 + i * 1024) for i in range(2)]
    KcT = [alloc("KcT%d" % i, [128, 2, 128], BF16, at=E0 + 2048 + i * 512) for i in range(2)]
    Vc = [alloc("Vc%d" % i, [128, 256], BF16, at=E0 + 3072 + i * 512) for i in range(2)]
    ustage = alloc("ustage", [34, D], F32, at=A0 + 0)
    assert E0 + 4096 <= cap, (E0, cap)
    xnA = alloc("xnA", [128, DC, 512], BF16, at=XN_OFF)
    QT = alloc("QT", [128, DC, 512], BF16, at=XN_OFF + 8192)
    attnT = alloc("attnT", [128, DC, 512], BF16, at=XN_OFF + 16384)
    PT = [alloc("PT%d" % i, [128, 512], BF16, at=XN_OFF + 24576 + i * 1024) for i in range(2)]
    qraw = [alloc("qraw%d" % i, [128, 512], BF16, at=XN_OFF + 26624 + i * 1024) for i in range(2)]
    qt1 = [alloc("qt1_%d" % i, [128, 512], F32, at=XN_OFF + 28672 + i * 2048) for i in range(2)]
    qt2 = alloc("qt2", [128, 512], F32, at=XN_OFF + 32768)
    qcb = [alloc("qcb%d" % i, [128, 512], BF16, at=XN_OFF + 28672 + i * 1024) for i in range(2)]
    PTC = alloc("PTC", [128, 1024], BF16, at=XN_OFF + 24576)
    PT.append(alloc("PT2", [128, 512], BF16, at=XN_OFF + 26624))
    lnDall = alloc("lnDall", [128, 512], F32, at=XN_OFF + 30720)
    lnD = [alloc("lnD%d" % i, [128, 128], F32, at=XN_OFF + 34816 + i * 512) for i in range(2)]
    assert XN_OFF + 34816 + 1024 <= XN_OFF + DC * NT * 2
    yblk = [alloc("yblk%d" % i, [128, DC, 512], F32, at=A0 + i * 16384) for i in range(2)]
    ystage = [alloc("ystage%d" % i, [128, D], F32, at=A0 + 32768 + i * 4096) for i in range(4)]

    PSA = nc.alloc_psum_tensor("psa", [128, 8, 512], F32)

    class _Bank:
        def __init__(self, i):
            self.i = i

        def __getitem__(self, idx):
            return PSA[idx[0], self.i, idx[1]]

    PS = [_Bank(i) for i in range(8)]

    class Res(RR):
        pass

    r_xT = RR()
    r_xn = RR()
    r_ps = [RR() for _ in range(8)]
    for r_ in r_ps:
        r_.excl = True
    r_const = RR()
    res_cache = {}

    def R(name):
        if name not in res_cache:
            res_cache[name] = RR()
        return res_cache[name]

    def W_(res, a=0, b=1):
        return (res, a, b)

    def deps_for(eng, reads, writes, extra):
        deps = []
        for (r, a, b) in reads:
            for s in r.cover(a, b):
                for ev in s[2]:
                    deps.append(ev)
                if getattr(r, "excl", False):
                    for ev in s[3]:
                        if not (ev.kind == 'c' and ev.eng == eng):
                            deps.append(ev)
        for (r, a, b) in writes:
            for s in r.cover(a, b):
                if not s[3]:
                    for ev in s[2]:
                        if not (ev.kind == 'c' and ev.eng == eng):
                            deps.append(ev)
                for ev in s[3]:
                    if not (ev.kind == 'c' and ev.eng == eng):
                        deps.append(ev)
        deps += [e for e in extra if e is not None]
        if eng == "pe":
            deps = [e for e in deps if not (e.kind == 'c' and e.eng == "pe")]
        return deps

    def commit(ev, reads, writes):
        for (r, a, b) in reads:
            for s in r.cover(a, b):
                s[3].append(ev)
        for (r, a, b) in writes:
            for s in r.cover(a, b):
                s[2] = [ev]
                s[3] = []

    def OP(eng, fn, reads=(), writes=(), extra=()):
        ev = S.op(eng, fn, deps_for(eng, reads, writes, extra))
        commit(ev, reads, writes)
        return ev

    def DMA(eng, fn, key, reads=(), writes=(), extra=()):
        ev = S.dma(eng, fn, key, deps_for("dma_" + eng, reads, writes, extra))
        commit(ev, reads, writes)
        return ev

    out_events = []

    def MM(out, lhsT, rhs, start=True, stop=True):
        return lambda e: e.matmul(out, lhsT, rhs, start=start, stop=stop)

    def MMX(out, lhsT, rhs, start=True, stop=True):
        return lambda e: e.matmul(out, lhsT, rhs, start=start, stop=stop, skip_group_check=True)

    def TR(out, in_, idn):
        return lambda e: e.transpose(out, in_, idn)

    def ACT(out, in_, func, bias=None, scale=None):
        kw = {}
        if bias is not None:
            kw["bias"] = bias
        if scale is not None:
            kw["scale"] = scale
        return lambda e: e.activation(out, in_, func, **kw)

    def TT(out, in0, in1, op):
        return lambda e: e.tensor_tensor(out, in0, in1, op)

    def TS(out, in0, s1, s2, op0, op1=None):
        if op1 is None:
            return lambda e: e.tensor_scalar(out, in0, s1, None, op0)
        return lambda e: e.tensor_scalar(out, in0, s1, s2, op0, op1)

    def STT(out, in0, scalar, in1, op0, op1):
        return lambda e: e.scalar_tensor_tensor(out, in0, scalar, in1, op0, op1)

    def CP(out, in_):
        return lambda e: e.tensor_copy(out, in_)

    def MS(ap, val):
        return lambda e: e.memset(ap, val)

    def RCP(out, in_):
        return lambda e: e.reciprocal(out, in_)

    def DM(out, in_):
        return lambda e: e.dma_start(out=out, in_=in_)

    def cres(i):
        return W_(r_const, i, i + 1)
    CONST = [W_(r_const, 0, 16)]
    DMA("sp", DM(ident[:], ident_d), "const", writes=[cres(0)])
    DMA("sp", DM(gT[:], gT_d), "const", writes=[cres(1)])
    DMA("sp", DM(ckT[:], ckT_d), "const", writes=[cres(2)])
    DMA("sp", DM(esink[:], sinkT_d), "const", writes=[W_(R("esink"))])
    DMA("pool", DM(swapm[:], swap_d), "constp", writes=[cres(4)])
    DMA("pool", DM(identb[:], ident_d), "constp", writes=[cres(8)])
    DMA("pool", DM(mstd[:], mstd_d), "constp", writes=[cres(5)])
    DMA("pool", DM(mfirst[:], mfirst_d), "constp", writes=[cres(6)])
    DMA("pool", DM(msamp[:], msamp_d), "constp", writes=[cres(7)])
    OP("dve", MS(ones[:], 1.0), writes=[W_(R("ones"))])
    OP("dve", MS(epst[:], EPS), writes=[W_(R("eps"))])

    ld_ctr = [0]

    def load_block(src_ap, r0, n, dst_fn, rdst):
        s_ = ld_ctr[0] % NXS
        ld_ctr[0] += 1
        xr = W_(R("xs%d" % s_))
        DMA("sp", DM(xs[s_][0:n, :], src_ap[r0:r0 + n, :]), "xs" + str(s_), writes=[xr])
        for hlf in range(2):
            bank = 6 + hlf
            for cc in range(4):
                c = hlf * 4 + cc
                OP("pe", TR(PS[bank][:, cc * 128:cc * 128 + n], xs[s_][0:n, c * 128:(c + 1) * 128], ident[0:n, 0:n]),
                   reads=[xr] + CONST, writes=[W_(r_ps[bank], cc * 128, cc * 128 + 128)])
            src = PS[bank][:, :].rearrange("p (c n) -> p c n", c=4)[:, :, 0:n]
            if hlf == 0:
                OP("dve", CP(dst_fn(hlf * 4, r0, n), src), reads=[W_(r_ps[bank], 0, 512)], writes=[rdst(r0, n)])
            else:
                OP("act", ACT(dst_fn(hlf * 4, r0, n), src, AF.Copy), reads=[W_(r_ps[bank], 0, 512)],
                   writes=[rdst(r0, n)])

    NBLK_X = (NT + 127) // 128
    x_loaded = [0]

    def load_x_upto(col_end):
        while x_loaded[0] < NBLK_X and x_loaded[0] * 128 < col_end:
            r0 = x_loaded[0] * 128
            n = min(128, NT - r0)
            load_block(xin, r0, n, lambda c0, r0_, n_: xT[:, c0:c0 + 4, r0_:r0_ + n_], lambda r0_, n_: W_(r_xT, r0_, r0_ + n_))
            x_loaded[0] += 1

    def split_tiles(a, b, n):
        w = b - a
        base_w = (w // n) // 2 * 2
        rem = w - base_w * n
        out = []
        x = a
        for i in range(n):
            ww = base_w + (2 if i < rem // 2 else 0)
            if i == n - 1:
                ww = b - x
            out.append((x, x + ww))
            x += ww
        return out

    TILES0 = split_tiles(0, NT, 5)
    TILES1 = [(max(a, HALO), b) for (a, b) in TILES0]
    norm_ctr = [0]

    def norm_tile(nidx, a, b, dst, dst_res, dst_off, extra=(), defer=False):
        w = b - a
        p6 = W_(r_ps[6], 0, 512)
        for hf in range(2):
            OP("act", ACT(sq[:, 4 * hf:4 * hf + 4, 0:w], xT[:, 4 * hf:4 * hf + 4, a:b], AF.Square),
               reads=[W_(r_xT, a, b)], writes=[W_(R("sq"), hf, hf + 1)])
        for c in range(DC):
            OP("pe", MM(PS[6][:, 0:w], ones[:, :], sq[:, c, 0:w], c == 0, c == DC - 1),
               reads=[W_(R("sq"), c // 4, c // 4 + 1), W_(R("ones"))], writes=[p6])
        OP("act", ACT(PS[6][:, 0:w], PS[6][:, 0:w], AF.Ln, bias=epst[:, 0:1], scale=1.0 / D),
           reads=[p6, W_(R("eps"))], writes=[p6])
        OP("act", ACT(PS[6][:, 0:w], PS[6][:, 0:w], AF.Exp, scale=-0.5), reads=[p6], writes=[p6])

        def part2():
            for c in range(DC):
                OP("dve", STT(dst[:, c, dst_off:dst_off + w], xT[:, c, a:b], gT[:, nidx * 8 + c:nidx * 8 + c + 1],
                              PS[6][:, 0:w], ALU.mult, ALU.mult),
                   reads=[W_(r_xT, a, b), p6] + CONST, writes=[dst_res(a, b)], extra=extra)
        if defer:
            return part2
        part2()

    def xn_res(a, b):
        return W_(r_xn, a, b)

    grp_ctr = [0]

    def ffn(l, i, nidx, tiles, phase_extra=(), pre_tile=None, w0_extra=None):
        w_in = ffn_w_in[l, i]
        w_out = ffn_w_out[l, i]
        pending = [None]
        it = [0]

        def flush():
            if pending[0] is not None:
                for st_ in range(4):
                    pending[0](st_)
                pending[0] = None

        ngrp = FCH // 2
        for gi in range(ngrp):
            s = grp_ctr[0] % 2
            grp_ctr[0] += 1
            f0 = gi * 2
            wr = R("W%d" % s)
            wres = W_(wr, 0, 3)
            ex = phase_extra if gi < 2 else ()
            if gi == 0 and w0_extra is not None:
                ex = w0_extra
            DMA("pool", DM(Wg[s][:], w_in[:, f0 * 128:f0 * 128 + 256].rearrange("(c p) n -> p c n", p=128)),
                "W%d" % s, writes=[W_(wr, 0, 1)], extra=ex)
            DMA("pool", DM(Wu[s][:], w_in[:, DFF + f0 * 128:DFF + f0 * 128 + 256].rearrange("(c p) n -> p c n", p=128)),
                "W%d" % s, writes=[W_(wr, 1, 2)], extra=ex)
            DMA("pool", DM(Wo_[s][:], w_out[f0 * 128:f0 * 128 + 256, :].rearrange("(f p) n -> p f n", p=128)),
                "W%d" % s, writes=[W_(wr, 2, 3)], extra=ex)
            for ti, (a, b) in enumerate(tiles):
                if gi == 0:
                    if ti == 0:
                        if pre_tile is not None:
                            pre_tile(0)
                        norm_tile(nidx, a, b, xn, xn_res, a, extra=phase_extra)
                    if ti + 1 < len(tiles):
                        a2, b2 = tiles[ti + 1]
                        if pre_tile is not None:
                            pre_tile(ti + 1)
                        norm_tile(nidx, a2, b2, xn, xn_res, a2, extra=phase_extra)
                w = b - a
                par = it[0] % 2
                it[0] += 1
                prev = pending[0]
                pending[0] = None
                step = [0]

                def prev_pair():
                    if prev is not None:
                        prev(step[0])
                    step[0] += 1

                for fi in range(2):
                    gb, ub = 2 * fi, 2 * fi + 1
                    for c in range(DC):
                        OP("pe", MM(PS[gb][:, 0:w], Wg[s][:, c, fi * 128:(fi + 1) * 128], xn[:, c, a:b], c == 0, c == DC - 1),
                           reads=[wres, W_(r_xn, a, b)], writes=[W_(r_ps[gb], 0, 512)])
                    OP("act", ACT(sbuf_[fi][:, 0:w], PS[gb][:, 0:w], AF.Silu),
                       reads=[W_(r_ps[gb], 0, 512)], writes=[W_(R("s%d" % fi))])
                    prev_pair()
                    for c in range(DC):
                        OP("pe", MM(PS[ub][:, 0:w], Wu[s][:, c, fi * 128:(fi + 1) * 128], xn[:, c, a:b], c == 0, c == DC - 1),
                           reads=[wres, W_(r_xn, a, b)], writes=[W_(r_ps[ub], 0, 512)])
                    OP("dve", TT(hbuf[par][fi][:, 0:w], sbuf_[fi][:, 0:w], PS[ub][:, 0:w], ALU.mult),
                       reads=[W_(R("s%d" % fi)), W_(r_ps[ub], 0, 512)], writes=[W_(R("h%d%d" % (par, fi)))])
                    prev_pair()

                def wout(stepi, a=a, b=b, w=w, par=par, s=s, wres=wres):
                    for d_ in (2 * stepi, 2 * stepi + 1):
                        ob = (4, 5, 7)[d_ % 3]
                        for fi in range(2):
                            OP("pe", MM(PS[ob][:, 0:w], Wo_[s][:, fi, d_ * 128:(d_ + 1) * 128], hbuf[par][fi][:, 0:w],
                                        fi == 0, fi == 1),
                               reads=[wres, W_(R("h%d%d" % (par, fi)))], writes=[W_(r_ps[ob], 0, 512)])
                        OP("dve", STT(xT[:, d_, a:b], PS[ob][:, 0:w], 0.5, xT[:, d_, a:b], ALU.mult, ALU.add),
                           reads=[W_(r_ps[ob], 0, 512), W_(r_xT, a, b)], writes=[W_(r_xT, a, b)])
                pending[0] = wout
        flush()

    def gather(names):
        evs = []
        for n in names:
            if n in res_cache:
                evs += res_cache[n].all_events()
        return evs

    def dbg_dump(tag):
        if dbg_d is not None and DEBUG.get("xT") == tag:
            out_events.append(DMA("sp", DM(dbg_d, xT[:, :, :].rearrange("p c n -> p (c n)")),
                                  "dbg", reads=[W_(r_xT, 0, NT)]))
        if DEBUG.get("stop") == tag:
            raise _Stop()

    try:
        FFN_BUFS = ["W0", "W1", "h00", "h01", "h10", "h11", "s0", "s1"]
        dbg_dump("phase0")
        def pre0(ti):
            load_x_upto(TILES0[ti][1])
            if ti == len(TILES0) - 1:
                out_events.append(DMA("sp", DM(ks_d[:, 0:124, :], ck[:, 4:128, :]), "cachecp"))
                out_events.append(DMA("sp", DM(vs_d[:, 0:124, :], cv[:, 4:128, :]), "cachecp"))
                load_block(sconv, 0, 2 * NSEQ, lambda c0, r0_, n_: uprevT[:, c0:c0 + 4, r0_:r0_ + n_],
                           lambda r0_, n_: W_(R("uprevT")))
        ffn(0, 0, 0, TILES0, pre_tile=pre0)
        dbg_dump("ffn1")

        ffn_evs = gather(FFN_BUFS + ["xs%d" % i for i in range(8)])
        DMA("pool", DM(Wco[:], conv_w_out.rearrange("(c p) n -> p c n", p=128)), "Wco", writes=[W_(R("Wco"))])
        cw = conv_w_in.rearrange("(c p) (k m) -> p k c m", p=128, k=3)
        def conv_w_dma(fc_):
            s_ = fc_ % 2
            DMA("pool", DM(Wbcz[s_][:], cw[:, :, :, fc_ * 128:(fc_ + 1) * 128]), "Wbcz%d" % s_,
                writes=[W_(R("Wbcz%d" % s_))], extra=gather(["W1"]) if fc_ < 2 else ())
        conv_w_dma(0)
        for fc in range(DC):
            s = fc % 2
            wres = W_(R("Wbcz%d" % s))
            if fc + 1 < DC:
                conv_w_dma(fc + 1)
            OP("pool", MS(ubuf[0][:, 0:2], 0.0), writes=[W_(R("ubuf0"), 0, 2)])
            for ti, (a, b) in enumerate(TILES0):
                if fc == 0:
                    if ti == 0:
                        norm_tile(1, a, b, xn, xn_res, a)
                    if ti + 1 < len(TILES0):
                        norm_tile(1, TILES0[ti + 1][0], TILES0[ti + 1][1], xn, xn_res, TILES0[ti + 1][0])
                w = b - a
                par = ti % 2
                bo = 3 * ((fc * len(TILES0) + ti) % 2)
                pb = min(b, NP_)
                wp = pb - a
                has_s = b > NP_
                for k in range(3):
                    for c in range(DC):
                        OP("pe", MM(PS[bo + k][:, 0:w], Wbcz[s][:, k, c, :], xn[:, c, a:b], c == 0, c == DC - 1),
                           reads=[wres, W_(r_xn, a, b)], writes=[W_(r_ps[bo + k], 0, 512)])
                ex = ffn_evs if (fc == 0 and ti < 2) else ()
                cr = W_(R("c_sb%d" % par))
                tAr = W_(R("tA%d" % par))
                tBr = W_(R("tB%d" % par))
                OP("act", ACT(c_sb[par][:, 0:w], PS[bo + 1][:, 0:w], AF.Copy), reads=[W_(r_ps[bo + 1], 0, 512)], writes=[cr], extra=ex)
                ub = ubuf[par]
                ur = R("ubuf%d" % par)
                OP("dve", TT(ub[:, 2:2 + wp], c_sb[par][:, 0:wp], PS[bo + 2][:, 0:wp], ALU.mult),
                   reads=[cr, W_(r_ps[bo + 2], 0, 512)], writes=[W_(ur, 2, 516)], extra=ex)
                if ti + 1 < len(TILES0):
                    OP("pool", CP(ubuf[1 - par][:, 0:2], ub[:, wp:wp + 2]),
                       reads=[W_(ur, 2, 516)], writes=[W_(R("ubuf%d" % (1 - par)), 0, 2)])
                OP("dve", TS(tA[par][:, 0:wp], ub[:, 0:wp], ckT[:, fc:fc + 1], None, ALU.mult),
                   reads=[W_(ur, 0, 516)] + CONST, writes=[tAr], extra=ex)
                OP("dve", STT(tB[par][:, 0:wp], ub[:, 1:1 + wp], ckT[:, 8 + fc:9 + fc], tA[par][:, 0:wp], ALU.mult, ALU.add),
                   reads=[W_(ur, 0, 516), tAr] + CONST, writes=[tBr], extra=ex)
                OP("dve", STT(tA[par][:, 0:wp], ub[:, 2:2 + wp], ckT[:, 16 + fc:17 + fc], tB[par][:, 0:wp], ALU.mult, ALU.add),
                   reads=[W_(ur, 0, 516), tBr] + CONST, writes=[tAr])
                OP("dve", TT(vT[:, fc, a:a + wp], tA[par][:, 0:wp], PS[bo + 0][:, 0:wp], ALU.mult),
                   reads=[tAr, W_(r_ps[bo + 0], 0, 512)], writes=[W_(R("vT"), fc * NT + a, fc * NT + a + wp)])
                if has_s:
                    OP("pool", CP(uoT[:, fc, 0:2], ub[:, wp:wp + 2]), reads=[W_(ur, 2, 516)], writes=[W_(R("uoT"), 0, 1)])
                    pv = uprevT[:, fc, :].rearrange("p (b j) -> p b j", j=2)
                    OP("pool", CP(usb[:, :, 0:2], pv), reads=[W_(R("uprevT"))], writes=[W_(R("usb"), 0, 1)], extra=ex)
                    c3 = c_sb[par][:, wp:wp + NS].rearrange("p (b t) -> p b t", t=4)
                    z3 = PS[bo + 2][:, wp:wp + NS].rearrange("p (b t) -> p b t", t=4)
                    b3 = PS[bo + 0][:, wp:wp + NS].rearrange("p (b t) -> p b t", t=4)
                    OP("dve", TT(usb[:, :, 2:6], c3, z3, ALU.mult),
                       reads=[cr, W_(r_ps[bo + 2], 0, 512)], writes=[W_(R("usb"), 1, 2)])
                    t3a = tA[par][:, 0:NS].rearrange("p (b t) -> p b t", t=4)
                    t3b = tB[par][:, 0:NS].rearrange("p (b t) -> p b t", t=4)
                    OP("dve", TS(t3a, usb[:, :, 0:4], ckT[:, fc:fc + 1], None, ALU.mult),
                       reads=[W_(R("usb"), 0, 2)] + CONST, writes=[tAr])
                    OP("dve", STT(t3b, usb[:, :, 1:5], ckT[:, 8 + fc:9 + fc], t3a, ALU.mult, ALU.add),
                       reads=[W_(R("usb"), 0, 2), tAr] + CONST, writes=[tBr])
                    OP("dve", STT(t3a, usb[:, :, 2:6], ckT[:, 16 + fc:17 + fc], t3b, ALU.mult, ALU.add),
                       reads=[W_(R("usb"), 0, 2), tBr] + CONST, writes=[tAr])
                    v3 = vT[:, fc, NP_:NT].rearrange("p (b t) -> p b t", t=4)
                    OP("dve", TT(v3, t3a, b3, ALU.mult),
                       reads=[tAr, W_(r_ps[bo + 0], 0, 512)], writes=[W_(R("vT"), fc * NT + NP_, fc * NT + NT)])
                    uo3 = uoT[:, fc, 2:34].rearrange("p (b j) -> p b j", j=2)
                    OP("pool", CP(uo3, usb[:, :, 4:6]), reads=[W_(R("usb"), 1, 2)], writes=[W_(R("uoT"), 1, 2)])
        for (a, b) in TILES0:
            w = b - a
            for d_ in range(DC):
                ob = 4 + d_ % 2
                for fc in range(DC):
                    OP("pe", MM(PS[ob][:, 0:w], Wco[:, fc, d_ * 128:(d_ + 1) * 128], vT[:, fc, a:b], fc == 0, fc == DC - 1),
                       reads=[W_(R("Wco")), W_(R("vT"), fc * NT + a, fc * NT + b)], writes=[W_(r_ps[ob], 0, 512)])
                OP("dve", TT(xT[:, d_, a:b], PS[ob][:, 0:w], xT[:, d_, a:b], ALU.add),
                   reads=[W_(r_ps[ob], 0, 512), W_(r_xT, a, b)], writes=[W_(r_xT, a, b)])
        for hlf in range(2):
            bank = 6 + hlf
            for cc in range(4):
                c = hlf * 4 + cc
                OP("pe", TR(PS[bank][0:34, cc * 128:(cc + 1) * 128], uoT[:, c, :], ident[:, :]),
                   reads=[W_(R("uoT"), 0, 2)] + CONST, writes=[W_(r_ps[bank], 0, 512)])
            OP("act", ACT(ustage[:, hlf * 512:(hlf + 1) * 512], PS[bank][0:34, :], AF.Copy),
               reads=[W_(r_ps[bank], 0, 512)], writes=[W_(R("ustage"), hlf, hlf + 1)], extra=gather(["c_sb0", "c_sb1"]))
        out_events.append(DMA("sp", DM(up_d, ustage[0:2, :]), "uout", reads=[W_(R("ustage"), 0, 2)]))
        out_events.append(DMA("sp", DM(us_d, ustage[2:34, :]), "uout", reads=[W_(R("ustage"), 0, 2)]))
        dbg_dump("conv")

        conv_evs = gather(["Wbcz0", "Wbcz1", "c_sb0", "c_sb1", "tA0", "tA1", "tB0", "tB1", "usb", "vT", "Wco",
                           "ubuf0", "ubuf1", "ustage"])
        assert grp_ctr[0] % 2 == 1
        ffn(0, 1, 2, TILES0, phase_extra=conv_evs, w0_extra=gather(["Wbcz0", "Wbcz1", "W1"]))
        dbg_dump("ffn2")

        ffn_evs = gather(FFN_BUFS)
        conv2_evs = gather(["vT", "Wco", "ubuf0", "ubuf1"])
        DMA("pool", DM(wkv[:], w_kv.rearrange("(c p) n -> p c n", p=128)), "wkv", writes=[W_(R("wkv"))], extra=ffn_evs)
        DMA("sp", DM(cosT[:], cos_d), "tabs", writes=[W_(R("tabs"), 0, 1)], extra=conv2_evs)
        DMA("sp", DM(sinT[:], sin_d), "tabs", writes=[W_(R("tabs"), 1, 2)], extra=conv2_evs)
        TABS = W_(R("tabs"), 0, 2)

        def rope_dve(ps_raw, w, a, mbuf, mres, cbuf, cres_, extra=()):
            OP("dve", TT(mbuf[:, 0:w], PS[ps_raw][:, 0:w], sinT[:, a:a + w], ALU.mult),
               reads=[W_(r_ps[ps_raw], 0, 512), TABS], writes=[mres], extra=extra)
            OP("dve", TT(cbuf[:, 0:w], PS[ps_raw][:, 0:w], cosT[:, a:a + w], ALU.mult),
               reads=[W_(r_ps[ps_raw], 0, 512), TABS], writes=[cres_], extra=extra)

        def rope_pe(ps_rot, w, mbuf, mres, cbuf, cres_):
            OP("pe", MM(PS[ps_rot][:, 0:w], swapm[:, :], mbuf[:, 0:w], True, False),
               reads=[mres] + CONST, writes=[W_(r_ps[ps_rot], 0, 512)])
            OP("pe", MM(PS[ps_rot][:, 0:w], identb[:, :], cbuf[:, 0:w], False, True),
               reads=[cres_] + CONST, writes=[W_(r_ps[ps_rot], 0, 512)])

        def rope_chunk(ps_raw, ps_rot, w, a, mbuf, mres, cbuf, cres_, extra=()):
            rope_dve(ps_raw, w, a, mbuf, mres, cbuf, cres_, extra)
            rope_pe(ps_rot, w, mbuf, mres, cbuf, cres_)

        dbg_dump("kv_a")
        kv_ctr = 0
        norm_tile(3, TILES0[0][0], TILES0[0][1], xn, xn_res, TILES0[0][0])
        for ti, (a, b) in enumerate(TILES0):
            w = b - a
            pars = []
            for kc in range(2):
                par = kv_ctr % 2
                kv_ctr += 1
                pars.append(par)
                pr = 0 + par
                for c in range(DC):
                    OP("pe", MM(PS[pr][:, 0:w], wkv[:, c, kc * 128:(kc + 1) * 128], xn[:, c, a:b], c == 0, c == DC - 1),
                       reads=[W_(R("wkv")), W_(r_xn, a, b)], writes=[W_(r_ps[pr], 0, 512)])
            part2 = None
            if ti + 1 < len(TILES0):
                norm_tile(3, TILES0[ti + 1][0], TILES0[ti + 1][1], xn, xn_res, TILES0[ti + 1][0])
            for kc in range(2):
                par = pars[kc]
                pr, pt = 0 + par, 2 + par
                rope_chunk(pr, pt, w, a, kraw[par], W_(R("kraw%d" % par)), kcb[par], W_(R("kcb%d" % par)), extra=ffn_evs)
                OP("act", ACT(KT[:, kc, a:b], PS[pt][:, 0:w], AF.Copy),
                   reads=[W_(r_ps[pt], 0, 512)], writes=[W_(R("KT"), kc * NT + a, kc * NT + b)], extra=conv2_evs)
                lo, hi = max(a, NP_ - 128), min(b, NP_)
                if lo < hi:
                    OP("act", ACT(KoutT[:, kc, lo - (NP_ - 128):hi - (NP_ - 128)], PS[pt][:, lo - a:hi - a], AF.Copy),
                       reads=[W_(r_ps[pt], 0, 512)], writes=[W_(R("KoutT"), kc * 2, kc * 2 + 1)], extra=ffn_evs)
                if b > NP_:
                    OP("act", ACT(KoutT[:, kc, 128:192], PS[pt][:, NP_ - a:NT - a], AF.Copy),
                       reads=[W_(r_ps[pt], 0, 512)], writes=[W_(R("KoutT"), kc * 2 + 1, kc * 2 + 2)], extra=ffn_evs)
            if part2 is not None:
                part2()
        dbg_dump("kv_k")
        VBLK = [2] + [HALO + 128 * m for m in range(16)] + [HALO + 1796, HALO + 1924]
        for bi, c0 in enumerate(VBLK):
            bank = 4 + bi % 2
            for c in range(DC):
                OP("pe", MM(PS[bank][:, 0:256], xn[:, c, c0:c0 + 128], wkv[:, c, 256:512], c == 0, c == DC - 1),
                   reads=[W_(R("wkv")), W_(r_xn, c0, c0 + 128)], writes=[W_(r_ps[bank], 0, 512)])
            OP("act", ACT(Vb[:, bi, :], PS[bank][:, 0:256], AF.Copy),
               reads=[W_(r_ps[bank], 0, 512)], writes=[W_(R("Vb"), bi, bi + 1)], extra=conv2_evs)
            if bi == 18:
                OP("dve", CP(vstage[:, :], PS[bank][:, 0:256]),
                   reads=[W_(r_ps[bank], 0, 512)], writes=[W_(R("vstage"))], extra=ffn_evs)
                out_events.append(DMA("sp", DM(vp_d, vstage[:, :]), "vpo", reads=[W_(R("vstage"))]))
        dbg_dump("kv_v")
        for c in range(DC):
            OP("pe", MM(PS[4][0:NS, 0:256], xn[:, c, NP_:NT], wkv[:, c, 256:512], c == 0, c == DC - 1),
               reads=[W_(R("wkv")), W_(r_xn, NP_, NT)], writes=[W_(r_ps[4], 0, 512)])
        OP("dve", CP(vsstage[:, :], PS[4][0:NS, 0:256]),
           reads=[W_(r_ps[4], 0, 512)], writes=[W_(R("vsstage"))], extra=ffn_evs)
        for sb_ in range(NSEQ):
            out_events.append(DMA("sp", DM(vs_d[sb_, 124:128, :], vsstage[4 * sb_:4 * sb_ + 4, :]), "vso",
                                  reads=[W_(R("vsstage"))]))
        OP("act", ACT(Vs_bf[:, :], PS[4][0:NS, 0:256], AF.Copy),
           reads=[W_(r_ps[4], 0, 512)], writes=[W_(R("Vnew"))], extra=conv2_evs)
        dbg_dump("kv_vs")
        for kc in range(2):
            OP("pe", TR(PS[6][:, kc * 128:(kc + 1) * 128], KoutT[:, kc, 0:128], ident[:, :]),
               reads=[W_(R("KoutT"), 0, 4)] + CONST, writes=[W_(r_ps[6], 0, 512)])
        OP("dve", CP(kstage[:, :], PS[6][:, 0:256]), reads=[W_(r_ps[6], 0, 512)], writes=[W_(R("kstage"))], extra=ffn_evs)
        out_events.append(DMA("sp", DM(kp_d, kstage[:, :]), "kpo", reads=[W_(R("kstage"))]))
        for kc in range(2):
            OP("pe", TR(PS[7][0:NS, kc * 128:(kc + 1) * 128], KoutT[:, kc, 128:192], ident[:, :]),
               reads=[W_(R("KoutT"), 0, 4)] + CONST, writes=[W_(r_ps[7], 0, 512)])
        OP("dve", CP(ksstage[:, :], PS[7][0:NS, 0:256]), reads=[W_(r_ps[7], 0, 512)], writes=[W_(R("ksstage"))], extra=ffn_evs)
        for sb_ in range(NSEQ):
            out_events.append(DMA("sp", DM(ks_d[sb_, 124:128, :], ksstage[4 * sb_:4 * sb_ + 4, :]), "kso",
                                  reads=[W_(R("ksstage"))]))

        dbg_dump("kv")
        kv_evs = gather(["wkv", "kraw0", "kraw1", "kt1_0", "kt1_1", "kt2_0", "kt2_1", "kt3_0", "kt3_1", "kcb0", "kcb1", "KoutT",
                         "kstage", "vstage", "ksstage", "vsstage"])
        ffn(1, 0, 4, TILES1, phase_extra=kv_evs)
        dbg_dump("ffn3")

        ffn_evs = gather(FFN_BUFS)
        xn_evs = r_xn.all_events()
        for jp_ in range(DC):
            gc_, i4 = jp_ // 4, jp_ % 4
            for half in range(2):
                g = 2 * gc_ + half
                src = w_q[:, g * 256 + i4 * 64:g * 256 + (i4 + 1) * 64].rearrange("(c p) m -> p c m", p=128)
                c0_ = jp_ * 128 + half * 64
                DMA("pool", DM(Wq[:, :, c0_:c0_ + 64], src), "Wq%d" % jp_,
                    writes=[W_(R("Wq"), 2 * jp_ + half, 2 * jp_ + half + 1)], extra=ffn_evs)
        wi = 16
        for gc in range(2):
            for half in range(2):
                g = 2 * gc + half
                src2 = w_o[g * 256:(g + 1) * 256, :].rearrange("(i p) n -> p i n", p=64)
                dst2 = Wo2[half * 64:(half + 1) * 64, gc * 4:(gc + 1) * 4, :]
                DMA("pool", DM(dst2, src2), "Wo2", writes=[W_(R("Wq"), wi, wi + 1)], extra=ffn_evs)
                wi += 1
        WQR = W_(R("Wq"), 0, wi)
        WOR = W_(R("Wq"), 16, wi)
        ESK = W_(R("esink"))
        OP("act", ACT(esink[:, :], esink[:, :], AF.Exp), reads=[ESK], writes=[ESK])

        ATILES = [(HALO + 512 * i, HALO + 512 * (i + 1)) for i in range(4)] + [(NP_ - 128, NT)]
        QTR = W_(R("QT"))
        KTR = W_(R("KT"), 0, 2 * NT)
        VBR = W_(R("Vb"), 0, 19)
        u_ctr = [0]
        seq_ctr = [0]

        def unit_prompt(jp, qc0, qa, vprev_i, vcur_i, mask_ap):
            u = u_ctr[0]
            u_ctr[0] += 1
            par = u % 3
            zpar = u % 2
            gc = jp // 4
            xb = 2 * par
            zb = 6 + zpar
            ptr = W_(R("PT%d" % par)) if par < 2 else W_(R("qraw0"))
            lr = W_(R("lnD%d" % zpar))

            def A():
                for (bank, base) in ((xb, 0), (xb + 1, 64)):
                    OP("pe", MM(PS[bank][:, 0:128], KT[base:base + 64, gc, qa - 128:qa], QT[base:base + 64, jp, qc0:qc0 + 128]),
                       reads=[QTR, KTR], writes=[W_(r_ps[bank], 0, 512)])
                    OP("pe", MM(PS[bank][:, 128:256], KT[base:base + 64, gc, qa:qa + 128], QT[base:base + 64, jp, qc0:qc0 + 128]),
                       reads=[QTR, KTR], writes=[W_(r_ps[bank], 0, 512)])
                OP("act", ACT(PT[par][:, :].rearrange("p (b n) -> p b n", b=2), PSA[:, xb:xb + 2, 0:256], AF.Exp, scale=0.125),
                   reads=[W_(r_ps[xb], 0, 512), W_(r_ps[xb + 1], 0, 512)], writes=[ptr])
                OP("dve", TT(PT[par][:, :], PT[par][:, :], mask_ap, ALU.mult), reads=[ptr] + CONST, writes=[ptr])

            def B():
                for (base, off) in ((0, 0), (64, 256)):
                    g = 2 * gc + (base // 64)
                    OP("pe", MM(PS[zb][base:base + 64, 0:128], Vb[:, vprev_i, g * 64:(g + 1) * 64], PT[par][:, off:off + 128], True, False),
                       reads=[ptr, VBR], writes=[W_(r_ps[zb], 0, 512)])
                    OP("pe", MM(PS[zb][base:base + 64, 0:128], Vb[:, vcur_i, g * 64:(g + 1) * 64], PT[par][:, off + 128:off + 256], False, True),
                       reads=[ptr, VBR], writes=[W_(r_ps[zb], 0, 512)])
                    OP("pe", MM(PS[zb][base:base + 64, 128:256], ones[:, 0:64], PT[par][:, off:off + 128], True, False),
                       reads=[ptr, W_(R("ones"))], writes=[W_(r_ps[zb], 0, 512)])
                    OP("pe", MM(PS[zb][base:base + 64, 128:256], ones[:, 0:64], PT[par][:, off + 128:off + 256], False, True),
                       reads=[ptr, W_(R("ones"))], writes=[W_(r_ps[zb], 0, 512)])
                OP("act", ACT(lnD[zpar][:, 0:128], PS[zb][:, 128:256], AF.Ln, bias=esink[:, jp:jp + 1]),
                   reads=[W_(r_ps[zb], 0, 512), ESK], writes=[lr])
                OP("act", ACT(lnD[zpar][:, 0:128], lnD[zpar][:, 0:128], AF.Exp, scale=-1.0), reads=[lr], writes=[lr])
                OP("dve", TT(attnT[:, jp, qc0:qc0 + 128], PS[zb][:, 0:128], lnD[zpar][:, 0:128], ALU.mult),
                   reads=[W_(r_ps[zb], 0, 512), lr], writes=[W_(R("attnT"))])
            return A, B

        def sample_attention():
            P0, P1 = W_(R("PT0")), W_(R("PT1"))
            z4, z5 = W_(r_ps[4], 0, 512), W_(r_ps[5], 0, 512)
            for jp in range(DC):
                gc = jp // 4
                for (bank, base) in ((0, 0), (1, 64)):
                    OP("pe", MM(PS[bank][0:NS, jp * 64:(jp + 1) * 64], KT[base:base + 64, gc, NP_:NT],
                                QT[base:base + 64, jp, 128:192]),
                       reads=[QTR, KTR], writes=[W_(r_ps[bank], 0, 512)])
            OP("act", ACT(PTC[0:NS, :].rearrange("p (b n) -> p b n", b=2), PSA[0:NS, 0:2, :], AF.Exp, scale=0.125),
               reads=[W_(r_ps[0], 0, 512), W_(r_ps[1], 0, 512)], writes=[P0, P1])
            for k in range(16):
                OP("dve", TT(PTC[0:NS, k * 64:(k + 1) * 64], PTC[0:NS, k * 64:(k + 1) * 64], msamp[0:NS, 64:128], ALU.mult),
                   reads=[P0, P1] + CONST, writes=[P0, P1])
            for (base, hoff) in ((0, 0), (64, 512)):
                for jp in range(DC):
                    g = 2 * (jp // 4) + (base // 64)
                    c0 = hoff + jp * 64
                    OP("pe", MMX(PS[4][base:base + 64, jp * 64:(jp + 1) * 64], Vs_bf[:, g * 64:(g + 1) * 64],
                                 PTC[0:NS, c0:c0 + 64], jp == 0, False),
                       reads=[P0, P1, W_(R("Vnew"))], writes=[z4])
                for jp in range(DC):
                    c0 = hoff + jp * 64
                    OP("pe", MMX(PS[5][base:base + 64, jp * 64:(jp + 1) * 64], ones[0:NS, 0:64],
                                 PTC[0:NS, c0:c0 + 64], jp == 0, False),
                       reads=[P0, P1, W_(R("ones"))], writes=[z5])

            def seq_unit(sb_):
                par = sb_ % 2
                sp = sb_ % 2
                xb = 2 if par == 0 else 0
                ptr = W_(R("PT%d" % par))
                kcr = W_(R("KcT%d" % sp))
                vcr = W_(R("Vc%d" % sp))
                qc0 = 128 + 4 * sb_

                def A():
                    DMA("sp", DM(ckst[sp][:, :], ck[sb_]), "ckst%d" % sp, writes=[W_(R("ckst%d" % sp))])
                    DMA("pool", DM(Vc[sp][:, :], cv[sb_]), "Vc%d" % sp, writes=[vcr])
                    for kc in range(2):
                        OP("pe", TR(PS[6][:, kc * 128:(kc + 1) * 128], ckst[sp][:, kc * 128:(kc + 1) * 128], ident[:, :]),
                           reads=[W_(R("ckst%d" % sp))] + CONST, writes=[W_(r_ps[6], 0, 512)])
                    OP("act", ACT(KcT[sp][:, :, :], PS[6][:, 0:256].rearrange("p (c n) -> p c n", c=2), AF.Copy),
                       reads=[W_(r_ps[6], 0, 512)], writes=[kcr])
                    for (bank, base) in ((xb, 0), (xb + 1, 64)):
                        for jp in range(DC):
                            OP("pe", MM(PS[bank][:, jp * 4:jp * 4 + 4], KcT[sp][base:base + 64, jp // 4, :],
                                        QT[base:base + 64, jp, qc0:qc0 + 4]),
                               reads=[QTR, kcr], writes=[W_(r_ps[bank], 0, 512)])
                    OP("act", ACT(PT[par][:, 0:64].rearrange("p (b n) -> p b n", b=2), PSA[:, xb:xb + 2, 0:32], AF.Exp, scale=0.125),
                       reads=[W_(r_ps[xb], 0, 512), W_(r_ps[xb + 1], 0, 512)], writes=[ptr])
                    OP("dve", TT(PT[par][:, 0:64], PT[par][:, 0:64], msamp[:, 0:64], ALU.mult), reads=[ptr] + CONST, writes=[ptr])

                def B():
                    for (base, hoff) in ((0, 0), (64, 32)):
                        for jp in range(DC):
                            g = 2 * (jp // 4) + (base // 64)
                            col = jp * 64 + 4 * sb_
                            OP("pe", MMX(PS[4][base:base + 64, col:col + 4], Vc[sp][:, g * 64:(g + 1) * 64],
                                         PT[par][:, hoff + jp * 4:hoff + jp * 4 + 4], False, False),
                               reads=[ptr, vcr], writes=[z4])
                        for jp in range(DC):
                            col = jp * 64 + 4 * sb_
                            OP("pe", MMX(PS[5][base:base + 64, col:col + 4], ones[:, 0:64],
                                         PT[par][:, hoff + jp * 4:hoff + jp * 4 + 4], False, sb_ == NSEQ - 1),
                               reads=[ptr, W_(R("ones"))], writes=[z5])
                return A, B

            prevB_ = None
            for sb_ in range(NSEQ):
                A_, B_ = seq_unit(sb_)
                A_()
                if prevB_ is not None:
                    prevB_()
                prevB_ = B_
            prevB_()
            lr = W_(R("lnDall"))
            for jp in range(DC):
                OP("dve", TS(lnDall[:, jp * 64:(jp + 1) * 64], PS[5][:, jp * 64:(jp + 1) * 64], esink[:, jp:jp + 1], None, ALU.add),
                   reads=[z5, ESK], writes=[lr])
            OP("act", ACT(lnDall[:, :], lnDall[:, :], AF.Ln), reads=[lr], writes=[lr])
            OP("act", ACT(lnDall[:, :], lnDall[:, :], AF.Exp, scale=-1.0), reads=[lr], writes=[lr])
            OP("dve", TT(attnT[:, :, 128:192], PS[4][:, :].rearrange("p (j q) -> p j q", j=DC),
                         lnDall[:, :].rearrange("p (j q) -> p j q", j=DC), ALU.mult),
               reads=[z4, lr], writes=[W_(R("attnT"))])

        norm_tile(5, ATILES[0][0], ATILES[0][1], xnA, lambda a_, b_: W_(R("xnA")), 0, extra=xn_evs)
        for ti, (a, b) in enumerate(ATILES):
            w = b - a
            def q_tail(jp):
                par = jp % 2
                pt = 2 * (jp % 4) + 1
                rope_pe(pt, w, qraw[par], W_(R("qraw%d" % par)), qcb[par], W_(R("qcb%d" % par)))
                OP("act", ACT(QT[:, jp, 0:w], PS[pt][:, 0:w], AF.Copy), reads=[W_(r_ps[pt], 0, 512)], writes=[QTR])

            for jp in range(DC):
                par = jp % 2
                pr, pt = 2 * (jp % 4), 2 * (jp % 4) + 1
                for c in range(DC):
                    OP("pe", MM(PS[pr][:, 0:w], Wq[:, c, jp * 128:(jp + 1) * 128], xnA[:, c, 0:w], c == 0, c == DC - 1),
                       reads=[W_(R("Wq"), 2 * jp, 2 * jp + 2), W_(R("xnA"))], writes=[W_(r_ps[pr], 0, 512)])
                rope_dve(pr, w, a, qraw[par], W_(R("qraw%d" % par)), qcb[par], W_(R("qcb%d" % par)))
                if jp >= 1:
                    q_tail(jp - 1)
            q_tail(DC - 1)
            units = []
            if ti < 4:
                for bi in range(4):
                    m = ti * 4 + bi
                    for jp in range(DC):
                        units.append(unit_prompt(jp, 128 * bi, a + 128 * bi, m, m + 1, (mfirst if m == 0 else mstd)[:, :]))
            else:
                for jp in range(DC):
                    units.append(unit_prompt(jp, 0, a, 17, 18, mstd[:, :]))
            pend = []
            for ui, (A_, B_) in enumerate(units):
                A_()
                pend.append(B_)
                if len(pend) > 2:
                    pend.pop(0)()
                if ui == 4 and ti + 1 < len(ATILES):
                    norm_tile(5, ATILES[ti + 1][0], ATILES[ti + 1][1], xnA, lambda a_, b_: W_(R("xnA")), 0)
            for B_ in pend:
                B_()
            if ti == 4:
                sample_attention()
            lo = 0 if ti < 4 else 124
            for d_ in range(DC):
                ob = 4 + d_ % 4
                for jp in range(DC):
                    OP("pe", MM(PS[ob][:, 0:w], Wo2[:, jp, d_ * 128:(d_ + 1) * 128], attnT[:, jp, 0:w], jp == 0, jp == DC - 1),
                       reads=[WOR, W_(R("attnT"))], writes=[W_(r_ps[ob], 0, 512)])
                OP("dve", TT(xT[:, d_, a + lo:b], PS[ob][:, lo:w], xT[:, d_, a + lo:b], ALU.add),
                   reads=[W_(r_ps[ob], 0, 512), W_(r_xT, a + lo, b)], writes=[W_(r_xT, a + lo, b)])
        dbg_dump("attn")

        att_evs = gather(["Wq", "xnA", "QT", "attnT", "PT0", "PT1", "qraw0", "qraw1", "qcb0", "qcb1", "qt1_0", "qt1_1", "qt2",
                          "lnD0", "lnD1", "lnDall"])
        if grp_ctr[0] % 2 == 1:
            grp_ctr[0] += 1
        wq_evs = []
        for seg_ in R("Wq").cover(0, 16):
            wq_evs += seg_[2] + seg_[3]
        ffn(1, 1, 6, TILES1, phase_extra=att_evs, w0_extra=wq_evs)
        dbg_dump("ffn4")

        ffn_evs = gather(FFN_BUFS)
        FB = 512
        nblk = (OWN + NS + FB - 1) // FB
        blocks = []
        for bi in range(nblk):
            a = HALO + bi * FB
            b = min(a + FB, NT)
            blocks.append((bi, a, b))

        def fin_norm(bi, a, b):
            par = bi % 2
            yr = W_(R("yblk%d" % par))
            norm_tile(7, a, b, yblk[par], lambda a_, b_, yr=yr: yr, 0, extra=ffn_evs)

        sub_ctr = [0]

        def fin_out(bi, a, b):
            par = bi % 2
            yr = W_(R("yblk%d" % par))
            nsub = (b - a + 127) // 128
            for sub in range(nsub):
                a_s = a + sub * 128
                ws = min(128, b - a_s)
                sc = sub_ctr[0]
                sub_ctr[0] += 1
                ys = sc % 4
                for hlf in range(2):
                    bank = 2 * (sc % 2) + hlf
                    for cc in range(4):
                        c = hlf * 4 + cc
                        OP("pe", TR(PS[bank][0:ws, cc * 128:(cc + 1) * 128], yblk[par][:, c, sub * 128:sub * 128 + ws], ident[:, :]),
                           reads=[yr] + CONST, writes=[W_(r_ps[bank], 0, 512)])
                    OP("act", ACT(ystage[ys][0:ws, hlf * 512:(hlf + 1) * 512], PS[bank][0:ws, :], AF.Copy),
                       reads=[W_(r_ps[bank], 0, 512)], writes=[W_(R("ystage%d" % ys), hlf, hlf + 1)], extra=ffn_evs)
                out_events.append(DMA("sp", DM(y_d[a_s - HALO:a_s - HALO + ws, :], ystage[ys][0:ws, :]), "yo%d" % ys,
                                      reads=[W_(R("ystage%d" % ys), 0, 2)]))

        fin_norm(*blocks[0])
        for i, blk_ in enumerate(blocks):
            if i + 1 < len(blocks):
                fin_norm(*blocks[i + 1])
            fin_out(*blk_)
    except _Stop:
        pass
    S.op("sp", lambda e: None, waits=out_events)
    S.emit()
    return nc


_NC_CACHE = {}


def _host_tables(core):
    c = core % 4
    p0 = c * OWN
    pos = np.concatenate([np.arange(p0 - HALO, p0 + OWN), 8192 + np.tile(np.arange(4), NSEQ)]).astype(np.int64)
    posf = np.maximum(pos, 0).astype(np.float32)
    inv = (1.0 / (10000.0 ** (np.arange(0, 64, 2, dtype=np.float32) / np.float32(64)))).astype(np.float32)
    ang = (posf[:, None] * inv[None, :]).astype(np.float32)
    cos = np.cos(ang).astype(np.float32)
    sin = np.sin(ang).astype(np.float32)
    p = np.arange(128)
    j = p % 32
    half = (p % 64) // 32
    cosT = np.ascontiguousarray(cos[:, j].T)
    sinT = np.ascontiguousarray((sin[:, j] * np.where(half == 0, 1.0, -1.0)[None, :]).T.astype(np.float32))
    return cosT, sinT


def kernel(x_prompt, x_sample, state_conv, cache_k, cache_v, meta_tokens, norm_g, ffn_w_in, ffn_w_out,
           conv_w_in, conv_kernel, conv_w_out, kv_norm_g, w_kv, w_q, w_o, sinks, final_norm_g):
    f32 = np.float32
    x_prompt = np.asarray(x_prompt, f32)
    x_sample = np.asarray(x_sample, f32)
    B = x_prompt.shape[0]
    if "nc" not in _NC_CACHE:
        _NC_CACHE["nc"] = build_nc()
    nc = _NC_CACHE["nc"]

    def fm(v):
        return np.ascontiguousarray(np.asarray(v, f32).reshape(DC, 128).T)

    ng = np.asarray(norm_g, f32)
    gT = np.concatenate([fm(ng[0, 0]), fm(ng[0, 1]), fm(ng[0, 2]), fm(kv_norm_g),
                         fm(ng[1, 0]), fm(ng[1, 1]), fm(ng[1, 2]), fm(final_norm_g)], axis=1)
    ckn = np.asarray(conv_kernel, f32)[0]
    ckT = np.concatenate([fm(ckn[0]), fm(ckn[1]), fm(ckn[2])], axis=1)
    sk = np.asarray(sinks, f32)[0]
    sinkT = np.zeros((128, 40), f32)
    for jp in range(8):
        gc, i = jp // 4, jp % 4
        sinkT[0:64, jp] = sk[4 * (2 * gc) + i]
        sinkT[64:128, jp] = sk[4 * (2 * gc + 1) + i]
        sinkT[:, 8 + 4 * jp:12 + 4 * jp] = sinkT[:, jp:jp + 1]
    ident = np.eye(128, dtype=f32)
    swapm = np.zeros((128, 128), f32)
    pp = np.arange(128)
    swapm[pp, pp ^ 32] = 1.0
    kk = np.arange(128)[:, None]
    qq = np.arange(128)[None, :]
    prev = (kk >= qq).astype(f32)
    curm = (kk <= qq).astype(f32)
    mstd = np.concatenate([prev, curm, prev, curm], axis=1)
    mfirst0 = np.concatenate([np.zeros_like(prev), curm, np.zeros_like(prev), curm], axis=1)
    ii = np.arange(128)[:, None]
    tt = np.arange(4)[None, :]
    sprev = (ii >= tt).astype(f32)
    scur = ((ii <= tt) & (ii < 4)).astype(f32)
    msamp = np.zeros((128, 128), f32)
    msamp[:, 0:64] = np.tile(sprev, (1, 16))
    kq = np.arange(64)
    msamp[0:64, 64:128] = ((kq[:, None] // 4 == kq[None, :] // 4) & (kq[:, None] % 4 <= kq[None, :] % 4)).astype(f32)

    shared = dict(
        ffn_w_in=np.asarray(ffn_w_in, f32), ffn_w_out=np.asarray(ffn_w_out, f32),
        conv_w_in=np.asarray(conv_w_in, f32)[0], conv_w_out=np.asarray(conv_w_out, f32)[0],
        w_kv=np.asarray(w_kv, f32), w_q=np.asarray(w_q, f32)[0], w_o=np.asarray(w_o, f32)[0],
        gT=gT, ckT=ckT, sinkT=sinkT, ident=ident, swapm=swapm, mstd=mstd, msamp=msamp)
    meta = np.asarray(meta_tokens, f32)
    sc = np.asarray(state_conv, f32)[0]
    ckc = np.asarray(cache_k, f32).reshape(128, 128, 256)
    cvc = np.asarray(cache_v, f32).reshape(128, 128, 256)
    in_maps = []
    for core in range(8):
        bseq, c = core // 4, core % 4
        xfull = np.concatenate([meta, x_prompt[bseq]], axis=0)
        p0 = c * OWN
        lo = p0 - HALO
        rows = np.zeros((NP_, D), f32)
        s0 = max(lo, 0)
        rows[s0 - lo:] = xfull[s0:p0 + OWN]
        xin = np.concatenate([rows, x_sample[16 * core:16 * core + 16].reshape(NS, D)], axis=0)
        cosT, sinT = _host_tables(core)
        m = dict(shared)
        m.update(xin=np.ascontiguousarray(xin),
                 sconv=np.ascontiguousarray(sc[16 * core:16 * core + 16].reshape(2 * NSEQ, D)),
                 ck=np.ascontiguousarray(ckc[16 * core:16 * core + 16]),
                 cv=np.ascontiguousarray(cvc[16 * core:16 * core + 16]),
                 cosT=cosT, sinT=sinT, mfirst=(mfirst0 if c == 0 else mstd))
        in_maps.append(m)
    res = run_bass_kernel_spmd(nc, in_maps, core_ids=list(range(8)))
    R_ = res.results
    _NC_CACHE["last"] = R_
    y_prompt = np.zeros((B, 8192, D), f32)
    y_sample = np.zeros((128, 4, D), f32)
    ncp = np.zeros((1, B, 2, D), f32)
    ncs = np.zeros((1, 128, 2, D), f32)
    nkp = np.zeros((B, 128, 4, 64), f32)
    nvp = np.zeros((B, 128, 4, 64), f32)
    nks = np.zeros((128, 128, 4, 64), f32)
    nvs = np.zeros((128, 128, 4, 64), f32)
    for core in range(8):
        bseq, c = core // 4, core % 4
        r = R_[core]
        y = np.asarray(r["y"])
        yp = y[0:OWN]
        p0 = c * OWN
        if c == 0:
            y_prompt[bseq, 0:OWN - 16] = yp[16:]
        else:
            y_prompt[bseq, p0 - 16:p0 - 16 + OWN] = yp
        y_sample[16 * core:16 * core + 16] = y[OWN:].reshape(16, 4, D)
        ncs[0, 16 * core:16 * core + 16] = np.asarray(r["u_s"]).reshape(16, 2, D)
        nks[16 * core:16 * core + 16] = np.asarray(r["ks"]).reshape(16, 128, 4, 64)
        nvs[16 * core:16 * core + 16] = np.asarray(r["vs"]).reshape(16, 128, 4, 64)
        if c == 3:
            ncp[0, bseq] = np.asarray(r["u_p"])
            nkp[bseq] = np.asarray(r["kp"]).reshape(128, 4, 64)
            nvp[bseq] = np.asarray(r["vp"]).reshape(128, 4, 64)
    return (y_prompt, y_sample, ncp, ncs, nkp, nvp, nks, nvs)
```

```python
import contextlib
import numpy as np
import concourse.bass as bass
import concourse.mybir as mybir
from concourse.alu_op_type import AluOpType as ALU
from concourse.bass_utils import run_bass_kernel_spmd

F32 = mybir.dt.float32
BF16 = mybir.dt.bfloat16
AF = mybir.ActivationFunctionType

D = 1024
DC = 8
DFF = 2816
FCH = 22
HALO = 130
OWN = 2052
NP_ = HALO + OWN
NS = 64
NT = NP_ + NS
NSEQ = 16
EPS = 1e-6
DEBUG = {}

ENGS = ("pe", "act", "dve", "pool", "sp")


class Ev:
    __slots__ = ("kind", "eng", "sem", "count", "needed", "pos")

    def __init__(self, kind, eng):
        self.kind = kind
        self.eng = eng
        self.sem = None
        self.count = None
        self.needed = False
        self.pos = None


class Sched:
    def __init__(self, nc):
        self.nc = nc
        self.q = {e: [] for e in ENGS}
        self.dma_sems = {}
        self.eng_sem = {}

    def _reduce(self, waits, eng):
        best = {}
        for w in waits:
            if w is None:
                continue
            if w.kind == 'c':
                k = ('c', w.eng)
                if k not in best or w.pos > best[k].pos:
                    best[k] = w
            else:
                k = ('d', w.sem)
                if k not in best or w.count > best[k].count:
                    best[k] = w
        ws = list(best.values())
        for w in ws:
            w.needed = True
        return ws

    def op(self, eng, fn, waits=()):
        ev = Ev('c', eng)
        ev.pos = len(self.q[eng])
        self.q[eng].append((fn, self._reduce(waits, eng), ev))
        return ev

    def dma(self, eng, fn, key, waits=()):
        ev = Ev('d', eng)
        ent = self.dma_sems.setdefault(key, [None, 0])
        ent[1] += 16
        ev.sem = key
        ev.count = ent[1]
        ev.pos = len(self.q[eng])
        self.q[eng].append((fn, self._reduce(waits, eng), ev))
        return ev

    def emit(self):
        nc = self.nc
        with contextlib.ExitStack() as st:
            for e in ENGS:
                self.eng_sem[e] = st.enter_context(nc.semaphore("s_" + e))
            for i, key in enumerate(self.dma_sems):
                self.dma_sems[key][0] = st.enter_context(nc.semaphore("d%d" % i))
            for e in ENGS:
                c = 0
                for (fn, ws, ev) in self.q[e]:
                    if ev.kind == 'c' and ev.needed:
                        c += 1
                        ev.count = c
                    if ev.kind == 'd' and str(ev.sem).startswith("const"):
                        ev.count = self.dma_sems[ev.sem][1]
            block = st.enter_context(nc.Block())
            sched = self

            def run(engname):
                def body(eh):
                    waited = {}
                    for (fn, ws, ev) in sched.q[engname]:
                        for w in ws:
                            if w.kind == 'c':
                                sem = sched.eng_sem[w.eng]
                                k = ('c', w.eng)
                            else:
                                sem = sched.dma_sems[w.sem][0]
                                k = ('d', w.sem)
                            if waited.get(k, 0) >= w.count:
                                continue
                            waited[k] = w.count
                            eh.wait_ge(sem, w.count)
                        ins = fn(eh)
                        if ins is None:
                            continue
                        if ev.kind == 'c':
                            if ev.needed:
                                ins.then_inc(sched.eng_sem[engname], 1)
                        else:
                            ins.then_inc(sched.dma_sems[ev.sem][0], 16)
                return body

            block.tensor(run("pe"))
            block.scalar(run("act"))
            block.vector(run("dve"))
            block.gpsimd(run("pool"))
            block.sync(run("sp"))


class RR:
    def __init__(self):
        self.segs = []

    def _cut(self, x):
        for i, s in enumerate(self.segs):
            if s[0] < x < s[1]:
                self.segs[i:i + 1] = [[s[0], x, list(s[2]), list(s[3])], [x, s[1], list(s[2]), list(s[3])]]
                return

    def cover(self, a, b):
        self._cut(a)
        self._cut(b)
        self.segs.sort(key=lambda s: s[0])
        pts = a
        new = []
        for s in self.segs:
            if s[1] <= a or s[0] >= b:
                continue
            if s[0] > pts:
                new.append([pts, s[0], [], []])
            pts = s[1]
        if pts < b:
            new.append([pts, b, [], []])
        self.segs += new
        self.segs.sort(key=lambda s: s[0])
        return [s for s in self.segs if s[0] >= a and s[1] <= b]

    def all_events(self):
        out = []
        for s in self.segs:
            out += s[2] + s[3]
        return out


class _Stop(Exception):
    pass


class K:
    pass


def build_nc():
    nc = bass.Bass("TRN2", target_bir_lowering=False)
    S = Sched(nc)

    def din(name, shape):
        return nc.dram_tensor(name, list(shape), F32, kind="ExternalInput").ap()

    def dout(name, shape):
        return nc.dram_tensor(name, list(shape), F32, kind="ExternalOutput").ap()

    xin = din("xin", [NT, D])
    sconv = din("sconv", [2 * NSEQ, D])
    ck = din("ck", [NSEQ, 128, 256])
    cv = din("cv", [NSEQ, 128, 256])
    ffn_w_in = din("ffn_w_in", [2, 2, D, 2 * DFF])
    ffn_w_out = din("ffn_w_out", [2, 2, DFF, D])
    conv_w_in = din("conv_w_in", [D, 3 * D])
    conv_w_out = din("conv_w_out", [D, D])
    w_kv = din("w_kv", [D, 512])
    w_q = din("w_q", [D, D])
    w_o = din("w_o", [D, D])
    gT_d = din("gT", [128, 64])
    ckT_d = din("ckT", [128, 24])
    sinkT_d = din("sinkT", [128, 40])
    cos_d = din("cosT", [128, NT])
    sin_d = din("sinT", [128, NT])
    ident_d = din("ident", [128, 128])
    swap_d = din("swapm", [128, 128])
    mstd_d = din("mstd", [128, 512])
    mfirst_d = din("mfirst", [128, 512])
    msamp_d = din("msamp", [128, 128])

    y_d = dout("y", [OWN + NS, D])
    up_d = dout("u_p", [2, D])
    us_d = dout("u_s", [2 * NSEQ, D])
    kp_d = dout("kp", [128, 256])
    vp_d = dout("vp", [128, 256])
    ks_d = dout("ks", [NSEQ, 128, 256])
    vs_d = dout("vs", [NSEQ, 128, 256])
    dbg_d = None
    if DEBUG.get("xT"):
        dbg_d = dout("dbg", [128, DC * NT])

    base = (nc._sbuf_addr_for_side('left') + 63) // 64 * 64
    cap = 229376
    cur = [base]

    def esize(dt):
        return 4 if dt == F32 else 2

    def alloc(name, shape, dt, at=None):
        n = 1
        for d_ in shape[1:]:
            n *= d_
        nbytes = (n * esize(dt) + 63) // 64 * 64
        if at is None:
            o = cur[0]
            cur[0] += nbytes
        else:
            o = at
        assert o + nbytes <= cap, (name, o, nbytes)
        return nc.alloc_sbuf_tensor_at(name, list(shape), dt, offset=o)

    xT = alloc("xT", [128, DC, NT], F32)
    XN_OFF = cur[0]
    xn = alloc("xn", [128, DC, NT], BF16)
    ident = alloc("ident", [128, 128], F32)
    swapm = alloc("swapm", [128, 128], BF16)
    identb = alloc("identb", [128, 128], BF16)
    ones = alloc("ones", [128, 128], BF16)
    gT = alloc("gT", [128, 64], F32)
    ckT = alloc("ckT", [128, 24], F32)
    esink = alloc("esink", [128, 40], F32)
    epst = alloc("epst", [128, 1], F32)
    mstd = alloc("mstd", [128, 512], BF16)
    mfirst = alloc("mfirst", [128, 512], BF16)
    msamp = alloc("msamp", [128, 128], BF16)
    uprevT = alloc("uprevT", [128, DC, 2 * NSEQ], F32)
    uoT = alloc("uoT", [128, DC, 34], F32)
    rstd = [alloc("rstd%d" % i, [128, 512], F32) for i in range(2)]
    rtmp = alloc("rtmp", [128, 512], F32)
    ptmp = alloc("ptmp", [128, 512], F32)
    sq = alloc("sq", [128, DC, 512], BF16)
    ARENA = cur[0]
    arena_size = cap - ARENA
    assert arena_size >= 82200, arena_size

    A0 = ARENA
    Wg = [alloc("Wg%d" % s, [128, DC, 256], BF16, at=A0 + s * 12288) for s in range(2)]
    Wu = [alloc("Wu%d" % s, [128, DC, 256], BF16, at=A0 + s * 12288 + 4096) for s in range(2)]
    Wo_ = [alloc("Wo%d" % s, [128, 2, 1024], BF16, at=A0 + s * 12288 + 8192) for s in range(2)]
    hbuf = [[alloc("h%d%d" % (p, f), [128, 512], BF16, at=A0 + 24576 + (2 * p + f) * 1024) for f in range(2)]
            for p in range(2)]
    sbuf_ = [alloc("s%d" % f, [128, 512], F32, at=A0 + 28672 + f * 2048) for f in range(2)]
    NXS = 8
    xs = [alloc("xs%d" % i, [128, D], F32, at=A0 + 32768 + i * 4096) for i in range(NXS)]
    Wbcz = [alloc("Wbcz%d" % s, [128, 3, DC, 128], BF16, at=A0 + 12288 + s * 6144) for s in range(2)]
    c_sb = [alloc("c_sb%d" % i, [128, 512], F32, at=A0 + 0 + i * 2048) for i in range(2)]
    tA = [alloc("tA%d" % i, [128, 512], F32, at=A0 + 4096 + i * 2048) for i in range(2)]
    tB = [alloc("tB%d" % i, [128, 512], F32, at=A0 + 8192 + i * 2048) for i in range(2)]
    usb = alloc("usb", [128, NSEQ, 6], F32, at=A0 + 24576)
    Wq = alloc("Wq", [128, DC, 1024], BF16, at=A0)
    Wo2 = alloc("Wo2", [128, DC, 1024], BF16, at=A0 + 16384)
    B0 = A0 + 32768
    vT = alloc("vT", [128, DC, NT], BF16, at=A0 + 29824)
    Wco = alloc("Wco", [128, DC, 1024], BF16, at=A0 + 29824 + 35968)
    ubuf = [alloc("ubuf%d" % i, [128, 516], F32, at=A0 + 25600 + i * 2112) for i in range(2)]
    assert A0 + 29824 + 35968 + 16384 <= cap
    KT = alloc("KT", [128, 2, NT], BF16, at=B0)
    Vb = alloc("Vb", [128, 19, 256], BF16, at=B0 + 8992)
    Vs_bf = alloc("Vs_bf", [NS, 256], BF16, at=B0 + 8992 + 9728)
    C0 = B0 + 8992 + 9728 + 8192
    cosT = alloc("cosT", [128, NT], F32, at=C0)
    sinT = alloc("sinT", [128, NT], F32, at=C0 + 8992)
    E0 = C0 + 2 * 8992
    wkv = alloc("wkv", [128, DC, 512], BF16, at=A0)
    kraw = [alloc("kraw%d" % i, [128, 512], BF16, at=A0 + 8192 + i * 1024) for i in range(2)]
    kt1 = [alloc("kt1_%d" % i, [128, 512], F32, at=A0 + 10240 + i * 2048) for i in range(2)]
    kcb = [alloc("kcb%d" % i, [128, 512], BF16, at=A0 + 10240 + i * 1024) for i in range(2)]
    kt2 = [alloc("kt2_%d" % i, [128, 512], F32, at=A0 + 14336 + i * 2048) for i in range(2)]
    kt3 = [alloc("kt3_%d" % i, [128, 512], F32, at=A0 + 18432 + i * 2048) for i in range(2)]
    KoutT = alloc("KoutT", [128, 2, 192], F32, at=A0 + 22528)
    kstage = alloc("kstage", [128, 256], F32, at=A0 + 24576)
    vstage = alloc("vstage", [128, 256], F32, at=A0 + 25600)
    ksstage = alloc("ksstage", [64, 256], F32, at=A0 + 26624)
    vsstage = alloc("vsstage", [64, 256], F32, at=A0 + 27648)
    ckst = [alloc("ckst%d" % i, [128, 256], F32, at=E0 + i * 1024) for i in range(2)]
    KcT = [alloc("KcT%d" % i, [128, 2, 128], BF16, at=E0 + 2048 + i * 512) for i in range(2)]
    Vc = [alloc("Vc%d" % i, [128, 256], BF16, at=E0 + 3072 + i * 512) for i in range(2)]
    ustage = alloc("ustage", [34, D], F32, at=A0 + 0)
    assert E0 + 4096 <= cap, (E0, cap)
    xnA = alloc("xnA", [128, DC, 512], BF16, at=XN_OFF)
    QT = alloc("QT", [128, DC, 512], BF16, at=XN_OFF + 8192)
    attnT = alloc("attnT", [128, DC, 512], BF16, at=XN_OFF + 16384)
    PT = [alloc("PT%d" % i, [128, 512], BF16, at=XN_OFF + 24576 + i * 1024) for i in range(2)]
    qraw = [alloc("qraw%d" % i, [128, 512], BF16, at=XN_OFF + 26624 + i * 1024) for i in range(2)]
    qt1 = [alloc("qt1_%d" % i, [128, 512], F32, at=XN_OFF + 28672 + i * 2048) for i in range(2)]
    qt2 = alloc("qt2", [128, 512], F32, at=XN_OFF + 32768)
    qcb = [alloc("qcb%d" % i, [128, 512], BF16, at=XN_OFF + 28672 + i * 1024) for i in range(2)]
    PTC = alloc("PTC", [128, 1024], BF16, at=XN_OFF + 24576)
    PT.append(alloc("PT2", [128, 512], BF16, at=XN_OFF + 26624))
    lnDall = alloc("lnDall", [128, 512], F32, at=XN_OFF + 30720)
    lnD = [alloc("lnD%d" % i, [128, 128], F32, at=XN_OFF + 34816 + i * 512) for i in range(2)]
    assert XN_OFF + 34816 + 1024 <= XN_OFF + DC * NT * 2
    yblk = [alloc("yblk%d" % i, [128, DC, 512], F32, at=A0 + i * 16384) for i in range(2)]
    ystage = [alloc("ystage%d" % i, [128, D], F32, at=A0 + 32768 + i * 4096) for i in range(4)]

    PSA = nc.alloc_psum_tensor("psa", [128, 8, 512], F32)

    class _Bank:
        def __init__(self, i):
            self.i = i

        def __getitem__(self, idx):
            return PSA[idx[0], self.i, idx[1]]

    PS = [_Bank(i) for i in range(8)]

    class Res(RR):
        pass

    r_xT = RR()
    r_xn = RR()
    r_ps = [RR() for _ in range(8)]
    for r_ in r_ps:
        r_.excl = True
    r_const = RR()
    res_cache = {}

    def R(name):
        if name not in res_cache:
            res_cache[name] = RR()
        return res_cache[name]

    def W_(res, a=0, b=1):
        return (res, a, b)

    def deps_for(eng, reads, writes, extra):
        deps = []
        for (r, a, b) in reads:
            for s in r.cover(a, b):
                for ev in s[2]:
                    deps.append(ev)
                if getattr(r, "excl", False):
                    for ev in s[3]:
                        if not (ev.kind == 'c' and ev.eng == eng):
                            deps.append(ev)
        for (r, a, b) in writes:
            for s in r.cover(a, b):
                if not s[3]:
                    for ev in s[2]:
                        if not (ev.kind == 'c' and ev.eng == eng):
                            deps.append(ev)
                for ev in s[3]:
                    if not (ev.kind == 'c' and ev.eng == eng):
                        deps.append(ev)
        deps += [e for e in extra if e is not None]
        if eng == "pe":
            deps = [e for e in deps if not (e.kind == 'c' and e.eng == "pe")]
        return deps

    def commit(ev, reads, writes):
        for (r, a, b) in reads:
            for s in r.cover(a, b):
                s[3].append(ev)
        for (r, a, b) in writes:
            for s in r.cover(a, b):
                s[2] = [ev]
                s[3] = []

    def OP(eng, fn, reads=(), writes=(), extra=()):
        ev = S.op(eng, fn, deps_for(eng, reads, writes, extra))
        commit(ev, reads, writes)
        return ev

    def DMA(eng, fn, key, reads=(), writes=(), extra=()):
        ev = S.dma(eng, fn, key, deps_for("dma_" + eng, reads, writes, extra))
        commit(ev, reads, writes)
        return ev

    out_events = []

    def MM(out, lhsT, rhs, start=True, stop=True):
        return lambda e: e.matmul(out, lhsT, rhs, start=start, stop=stop)

    def MMX(out, lhsT, rhs, start=True, stop=True):
        return lambda e: e.matmul(out, lhsT, rhs, start=start, stop=stop, skip_group_check=True)

    def TR(out, in_, idn):
        return lambda e: e.transpose(out, in_, idn)

    def ACT(out, in_, func, bias=None, scale=None):
        kw = {}
        if bias is not None:
            kw["bias"] = bias
        if scale is not None:
            kw["scale"] = scale
        return lambda e: e.activation(out, in_, func, **kw)

    def TT(out, in0, in1, op):
        return lambda e: e.tensor_tensor(out, in0, in1, op)

    def TS(out, in0, s1, s2, op0, op1=None):
        if op1 is None:
            return lambda e: e.tensor_scalar(out, in0, s1, None, op0)
        return lambda e: e.tensor_scalar(out, in0, s1, s2, op0, op1)

    def STT(out, in0, scalar, in1, op0, op1):
        return lambda e: e.scalar_tensor_tensor(out, in0, scalar, in1, op0, op1)

    def CP(out, in_):
        return lambda e: e.tensor_copy(out, in_)

    def MS(ap, val):
        return lambda e: e.memset(ap, val)

    def RCP(out, in_):
        return lambda e: e.reciprocal(out, in_)

    def DM(out, in_):
        return lambda e: e.dma_start(out=out, in_=in_)

    def cres(i):
        return W_(r_const, i, i + 1)
    CONST = [W_(r_const, 0, 16)]
    DMA("sp", DM(ident[:], ident_d), "const", writes=[cres(0)])
    DMA("sp", DM(gT[:], gT_d), "const", writes=[cres(1)])
    DMA("sp", DM(ckT[:], ckT_d), "const", writes=[cres(2)])
    DMA("sp", DM(esink[:], sinkT_d), "const", writes=[W_(R("esink"))])
    DMA("pool", DM(swapm[:], swap_d), "constp", writes=[cres(4)])
    DMA("pool", DM(identb[:], ident_d), "constp", writes=[cres(8)])
    DMA("pool", DM(mstd[:], mstd_d), "constp", writes=[cres(5)])
    DMA("pool", DM(mfirst[:], mfirst_d), "constp", writes=[cres(6)])
    DMA("pool", DM(msamp[:], msamp_d), "constp", writes=[cres(7)])
    OP("dve", MS(ones[:], 1.0), writes=[W_(R("ones"))])
    OP("dve", MS(epst[:], EPS), writes=[W_(R("eps"))])

    ld_ctr = [0]

    def load_block(src_ap, r0, n, dst_fn, rdst):
        s_ = ld_ctr[0] % NXS
        ld_ctr[0] += 1
        xr = W_(R("xs%d" % s_))
        DMA("sp", DM(xs[s_][0:n, :], src_ap[r0:r0 + n, :]), "xs" + str(s_), writes=[xr])
        for hlf in range(2):
            bank = 6 + hlf
            for cc in range(4):
                c = hlf * 4 + cc
                OP("pe", TR(PS[bank][:, cc * 128:cc * 128 + n], xs[s_][0:n, c * 128:(c + 1) * 128], ident[0:n, 0:n]),
                   reads=[xr] + CONST, writes=[W_(r_ps[bank], cc * 128, cc * 128 + 128)])
            src = PS[bank][:, :].rearrange("p (c n) -> p c n", c=4)[:, :, 0:n]
            if hlf == 0:
                OP("dve", CP(dst_fn(hlf * 4, r0, n), src), reads=[W_(r_ps[bank], 0, 512)], writes=[rdst(r0, n)])
            else:
                OP("act", ACT(dst_fn(hlf * 4, r0, n), src, AF.Copy), reads=[W_(r_ps[bank], 0, 512)],
                   writes=[rdst(r0, n)])

    NBLK_X = (NT + 127) // 128
    x_loaded = [0]

    def load_x_upto(col_end):
        while x_loaded[0] < NBLK_X and x_loaded[0] * 128 < col_end:
            r0 = x_loaded[0] * 128
            n = min(128, NT - r0)
            load_block(xin, r0, n, lambda c0, r0_, n_: xT[:, c0:c0 + 4, r0_:r0_ + n_], lambda r0_, n_: W_(r_xT, r0_, r0_ + n_))
            x_loaded[0] += 1

    def split_tiles(a, b, n):
        w = b - a
        base_w = (w // n) // 2 * 2
        rem = w - base_w * n
        out = []
        x = a
        for i in range(n):
            ww = base_w + (2 if i < rem // 2 else 0)
            if i == n - 1:
                ww = b - x
            out.append((x, x + ww))
            x += ww
        return out

    TILES0 = split_tiles(0, NT, 5)
    TILES1 = [(max(a, HALO), b) for (a, b) in TILES0]
    norm_ctr = [0]

    def norm_tile(nidx, a, b, dst, dst_res, dst_off, extra=(), defer=False):
        w = b - a
        p6 = W_(r_ps[6], 0, 512)
        for hf in range(4):
            OP("act", ACT(sq[:, 2 * hf:2 * hf + 2, 0:w], xT[:, 2 * hf:2 * hf + 2, a:b], AF.Square),
               reads=[W_(r_xT, a, b)], writes=[W_(R("sq"), hf, hf + 1)])
        for c in range(DC):
            OP("pe", MM(PS[6][:, 0:w], ones[:, :], sq[:, c, 0:w], c == 0, c == DC - 1),
               reads=[W_(R("sq"), c // 2, c // 2 + 1), W_(R("ones"))], writes=[p6])
        OP("act", ACT(PS[6][:, 0:w], PS[6][:, 0:w], AF.Ln, bias=epst[:, 0:1], scale=1.0 / D),
           reads=[p6, W_(R("eps"))], writes=[p6])
        OP("act", ACT(PS[6][:, 0:w], PS[6][:, 0:w], AF.Exp, scale=-0.5), reads=[p6], writes=[p6])

        def part2():
            for c in range(DC):
                OP("dve", STT(dst[:, c, dst_off:dst_off + w], xT[:, c, a:b], gT[:, nidx * 8 + c:nidx * 8 + c + 1],
                              PS[6][:, 0:w], ALU.mult, ALU.mult),
                   reads=[W_(r_xT, a, b), p6] + CONST, writes=[dst_res(a, b)], extra=extra)
        if defer:
            return part2
        part2()

    def xn_res(a, b):
        return W_(r_xn, a, b)

    grp_ctr = [0]

    def ffn(l, i, nidx, tiles, phase_extra=(), pre_tile=None, w0_extra=None):
        w_in = ffn_w_in[l, i]
        w_out = ffn_w_out[l, i]
        pending = [None]
        it = [0]

        def flush():
            if pending[0] is not None:
                for st_ in range(4):
                    pending[0](st_)
                pending[0] = None

        ngrp = FCH // 2
        for gi in range(ngrp):
            s = grp_ctr[0] % 2
            grp_ctr[0] += 1
            f0 = gi * 2
            wr = R("W%d" % s)
            wres = W_(wr, 0, 3)
            ex = phase_extra if gi < 2 else ()
            if gi == 0 and w0_extra is not None:
                ex = w0_extra
            DMA("pool", DM(Wg[s][:], w_in[:, f0 * 128:f0 * 128 + 256].rearrange("(c p) n -> p c n", p=128)),
                "W%d" % s, writes=[W_(wr, 0, 1)], extra=ex)
            DMA("pool", DM(Wu[s][:], w_in[:, DFF + f0 * 128:DFF + f0 * 128 + 256].rearrange("(c p) n -> p c n", p=128)),
                "W%d" % s, writes=[W_(wr, 1, 2)], extra=ex)
            DMA("pool", DM(Wo_[s][:], w_out[f0 * 128:f0 * 128 + 256, :].rearrange("(f p) n -> p f n", p=128)),
                "W%d" % s, writes=[W_(wr, 2, 3)], extra=ex)
            for ti, (a, b) in enumerate(tiles):
                if gi == 0:
                    if ti == 0:
                        if pre_tile is not None:
                            pre_tile(0)
                        norm_tile(nidx, a, b, xn, xn_res, a, extra=phase_extra)
                    if ti + 1 < len(tiles):
                        a2, b2 = tiles[ti + 1]
                        if pre_tile is not None:
                            pre_tile(ti + 1)
                        norm_tile(nidx, a2, b2, xn, xn_res, a2, extra=phase_extra)
                w = b - a
                par = it[0] % 2
                it[0] += 1
                prev = pending[0]
                pending[0] = None
                step = [0]

                def prev_pair():
                    if prev is not None:
                        prev(step[0])
                    step[0] += 1

                for fi in range(2):
                    gb, ub = 2 * fi, 2 * fi + 1
                    for c in range(DC):
                        OP("pe", MM(PS[gb][:, 0:w], Wg[s][:, c, fi * 128:(fi + 1) * 128], xn[:, c, a:b], c == 0, c == DC - 1),
                           reads=[wres, W_(r_xn, a, b)], writes=[W_(r_ps[gb], 0, 512)])
                    OP("act", ACT(sbuf_[fi][:, 0:w], PS[gb][:, 0:w], AF.Silu),
                       reads=[W_(r_ps[gb], 0, 512)], writes=[W_(R("s%d" % fi))])
                    prev_pair()
                    for c in range(DC):
                        OP("pe", MM(PS[ub][:, 0:w], Wu[s][:, c, fi * 128:(fi + 1) * 128], xn[:, c, a:b], c == 0, c == DC - 1),
                           reads=[wres, W_(r_xn, a, b)], writes=[W_(r_ps[ub], 0, 512)])
                    OP("dve", TT(hbuf[par][fi][:, 0:w], sbuf_[fi][:, 0:w], PS[ub][:, 0:w], ALU.mult),
                       reads=[W_(R("s%d" % fi)), W_(r_ps[ub], 0, 512)], writes=[W_(R("h%d%d" % (par, fi)))])
                    prev_pair()

                def wout(stepi, a=a, b=b, w=w, par=par, s=s, wres=wres):
                    for d_ in (2 * stepi, 2 * stepi + 1):
                        ob = (4, 5, 7)[d_ % 3]
                        for fi in range(2):
                            OP("pe", MM(PS[ob][:, 0:w], Wo_[s][:, fi, d_ * 128:(d_ + 1) * 128], hbuf[par][fi][:, 0:w],
                                        fi == 0, fi == 1),
                               reads=[wres, W_(R("h%d%d" % (par, fi)))], writes=[W_(r_ps[ob], 0, 512)])
                        OP("dve", STT(xT[:, d_, a:b], PS[ob][:, 0:w], 0.5, xT[:, d_, a:b], ALU.mult, ALU.add),
                           reads=[W_(r_ps[ob], 0, 512), W_(r_xT, a, b)], writes=[W_(r_xT, a, b)])
                pending[0] = wout
        flush()

    def gather(names):
        evs = []
        for n in names:
            if n in res_cache:
                evs += res_cache[n].all_events()
        return evs

    def dbg_dump(tag):
        if dbg_d is not None and DEBUG.get("xT") == tag:
            out_events.append(DMA("sp", DM(dbg_d, xT[:, :, :].rearrange("p c n -> p (c n)")),
                                  "dbg", reads=[W_(r_xT, 0, NT)]))
        if DEBUG.get("stop") == tag:
            raise _Stop()

    try:
        FFN_BUFS = ["W0", "W1", "h00", "h01", "h10", "h11", "s0", "s1"]
        dbg_dump("phase0")
        def pre0(ti):
            load_x_upto(TILES0[ti][1])
            if ti == len(TILES0) - 1:
                out_events.append(DMA("sp", DM(ks_d[:, 0:124, :], ck[:, 4:128, :]), "cachecp"))
                out_events.append(DMA("sp", DM(vs_d[:, 0:124, :], cv[:, 4:128, :]), "cachecp"))
                load_block(sconv, 0, 2 * NSEQ, lambda c0, r0_, n_: uprevT[:, c0:c0 + 4, r0_:r0_ + n_],
                           lambda r0_, n_: W_(R("uprevT")))
        ffn(0, 0, 0, TILES0, pre_tile=pre0)
        dbg_dump("ffn1")

        ffn_evs = gather(FFN_BUFS + ["xs%d" % i for i in range(8)])
        DMA("pool", DM(Wco[:], conv_w_out.rearrange("(c p) n -> p c n", p=128)), "Wco", writes=[W_(R("Wco"))])
        cw = conv_w_in.rearrange("(c p) (k m) -> p k c m", p=128, k=3)
        def conv_w_dma(fc_):
            s_ = fc_ % 2
            DMA("pool", DM(Wbcz[s_][:], cw[:, :, :, fc_ * 128:(fc_ + 1) * 128]), "Wbcz%d" % s_,
                writes=[W_(R("Wbcz%d" % s_))], extra=gather(["W1"]) if fc_ < 2 else ())
        conv_w_dma(0)
        for fc in range(DC):
            s = fc % 2
            wres = W_(R("Wbcz%d" % s))
            if fc + 1 < DC:
                conv_w_dma(fc + 1)
            OP("pool", MS(ubuf[0][:, 0:2], 0.0), writes=[W_(R("ubuf0"), 0, 2)])
            for ti, (a, b) in enumerate(TILES0):
                if fc == 0:
                    if ti == 0:
                        norm_tile(1, a, b, xn, xn_res, a)
                    if ti + 1 < len(TILES0):
                        norm_tile(1, TILES0[ti + 1][0], TILES0[ti + 1][1], xn, xn_res, TILES0[ti + 1][0])
                w = b - a
                par = ti % 2
                bo = 3 * ((fc * len(TILES0) + ti) % 2)
                pb = min(b, NP_)
                wp = pb - a
                has_s = b > NP_
                for k in range(3):
                    for c in range(DC):
                        OP("pe", MM(PS[bo + k][:, 0:w], Wbcz[s][:, k, c, :], xn[:, c, a:b], c == 0, c == DC - 1),
                           reads=[wres, W_(r_xn, a, b)], writes=[W_(r_ps[bo + k], 0, 512)])
                ex = ffn_evs if (fc == 0 and ti < 2) else ()
                cr = W_(R("c_sb%d" % par))
                tAr = W_(R("tA%d" % par))
                tBr = W_(R("tB%d" % par))
                OP("act", ACT(c_sb[par][:, 0:w], PS[bo + 1][:, 0:w], AF.Copy), reads=[W_(r_ps[bo + 1], 0, 512)], writes=[cr], extra=ex)
                ub = ubuf[par]
                ur = R("ubuf%d" % par)
                OP("dve", TT(ub[:, 2:2 + wp], c_sb[par][:, 0:wp], PS[bo + 2][:, 0:wp], ALU.mult),
                   reads=[cr, W_(r_ps[bo + 2], 0, 512)], writes=[W_(ur, 2, 516)], extra=ex)
                if ti + 1 < len(TILES0):
                    OP("pool", CP(ubuf[1 - par][:, 0:2], ub[:, wp:wp + 2]),
                       reads=[W_(ur, 2, 516)], writes=[W_(R("ubuf%d" % (1 - par)), 0, 2)])
                OP("dve", TS(tA[par][:, 0:wp], ub[:, 0:wp], ckT[:, fc:fc + 1], None, ALU.mult),
                   reads=[W_(ur, 0, 516)] + CONST, writes=[tAr], extra=ex)
                OP("dve", STT(tB[par][:, 0:wp], ub[:, 1:1 + wp], ckT[:, 8 + fc:9 + fc], tA[par][:, 0:wp], ALU.mult, ALU.add),
                   reads=[W_(ur, 0, 516), tAr] + CONST, writes=[tBr], extra=ex)
                OP("dve", STT(tA[par][:, 0:wp], ub[:, 2:2 + wp], ckT[:, 16 + fc:17 + fc], tB[par][:, 0:wp], ALU.mult, ALU.add),
                   reads=[W_(ur, 0, 516), tBr] + CONST, writes=[tAr])
                OP("dve", TT(vT[:, fc, a:a + wp], tA[par][:, 0:wp], PS[bo + 0][:, 0:wp], ALU.mult),
                   reads=[tAr, W_(r_ps[bo + 0], 0, 512)], writes=[W_(R("vT"), fc * NT + a, fc * NT + a + wp)])
                if has_s:
                    OP("pool", CP(uoT[:, fc, 0:2], ub[:, wp:wp + 2]), reads=[W_(ur, 2, 516)], writes=[W_(R("uoT"), 0, 1)])
                    pv = uprevT[:, fc, :].rearrange("p (b j) -> p b j", j=2)
                    OP("pool", CP(usb[:, :, 0:2], pv), reads=[W_(R("uprevT"))], writes=[W_(R("usb"), 0, 1)], extra=ex)
                    c3 = c_sb[par][:, wp:wp + NS].rearrange("p (b t) -> p b t", t=4)
                    z3 = PS[bo + 2][:, wp:wp + NS].rearrange("p (b t) -> p b t", t=4)
                    b3 = PS[bo + 0][:, wp:wp + NS].rearrange("p (b t) -> p b t", t=4)
                    OP("dve", TT(usb[:, :, 2:6], c3, z3, ALU.mult),
                       reads=[cr, W_(r_ps[bo + 2], 0, 512)], writes=[W_(R("usb"), 1, 2)])
                    t3a = tA[par][:, 0:NS].rearrange("p (b t) -> p b t", t=4)
                    t3b = tB[par][:, 0:NS].rearrange("p (b t) -> p b t", t=4)
                    OP("dve", TS(t3a, usb[:, :, 0:4], ckT[:, fc:fc + 1], None, ALU.mult),
                       reads=[W_(R("usb"), 0, 2)] + CONST, writes=[tAr])
                    OP("dve", STT(t3b, usb[:, :, 1:5], ckT[:, 8 + fc:9 + fc], t3a, ALU.mult, ALU.add),
                       reads=[W_(R("usb"), 0, 2), tAr] + CONST, writes=[tBr])
                    OP("dve", STT(t3a, usb[:, :, 2:6], ckT[:, 16 + fc:17 + fc], t3b, ALU.mult, ALU.add),
                       reads=[W_(R("usb"), 0, 2), tBr] + CONST, writes=[tAr])
                    v3 = vT[:, fc, NP_:NT].rearrange("p (b t) -> p b t", t=4)
                    OP("dve", TT(v3, t3a, b3, ALU.mult),
                       reads=[tAr, W_(r_ps[bo + 0], 0, 512)], writes=[W_(R("vT"), fc * NT + NP_, fc * NT + NT)])
                    uo3 = uoT[:, fc, 2:34].rearrange("p (b j) -> p b j", j=2)
                    OP("pool", CP(uo3, usb[:, :, 4:6]), reads=[W_(R("usb"), 1, 2)], writes=[W_(R("uoT"), 1, 2)])
        for (a, b) in TILES0:
            w = b - a
            for d_ in range(DC):
                ob = 4 + d_ % 2
                for fc in range(DC):
                    OP("pe", MM(PS[ob][:, 0:w], Wco[:, fc, d_ * 128:(d_ + 1) * 128], vT[:, fc, a:b], fc == 0, fc == DC - 1),
                       reads=[W_(R("Wco")), W_(R("vT"), fc * NT + a, fc * NT + b)], writes=[W_(r_ps[ob], 0, 512)])
                OP("dve", TT(xT[:, d_, a:b], PS[ob][:, 0:w], xT[:, d_, a:b], ALU.add),
                   reads=[W_(r_ps[ob], 0, 512), W_(r_xT, a, b)], writes=[W_(r_xT, a, b)])
        for hlf in range(2):
            bank = 6 + hlf
            for cc in range(4):
                c = hlf * 4 + cc
                OP("pe", TR(PS[bank][0:34, cc * 128:(cc + 1) * 128], uoT[:, c, :], ident[:, :]),
                   reads=[W_(R("uoT"), 0, 2)] + CONST, writes=[W_(r_ps[bank], 0, 512)])
            OP("act", ACT(ustage[:, hlf * 512:(hlf + 1) * 512], PS[bank][0:34, :], AF.Copy),
               reads=[W_(r_ps[bank], 0, 512)], writes=[W_(R("ustage"), hlf, hlf + 1)], extra=gather(["c_sb0", "c_sb1"]))
        out_events.append(DMA("sp", DM(up_d, ustage[0:2, :]), "uout", reads=[W_(R("ustage"), 0, 2)]))
        out_events.append(DMA("sp", DM(us_d, ustage[2:34, :]), "uout", reads=[W_(R("ustage"), 0, 2)]))
        dbg_dump("conv")

        conv_evs = gather(["Wbcz0", "Wbcz1", "c_sb0", "c_sb1", "tA0", "tA1", "tB0", "tB1", "usb", "vT", "Wco",
                           "ubuf0", "ubuf1", "ustage"])
        assert grp_ctr[0] % 2 == 1
        ffn(0, 1, 2, TILES0, phase_extra=conv_evs, w0_extra=gather(["Wbcz0", "Wbcz1", "W1"]))
        dbg_dump("ffn2")

        ffn_evs = gather(FFN_BUFS)
        conv2_evs = gather(["vT", "Wco", "ubuf0", "ubuf1"])
        DMA("pool", DM(wkv[:], w_kv.rearrange("(c p) n -> p c n", p=128)), "wkv", writes=[W_(R("wkv"))], extra=ffn_evs)
        DMA("sp", DM(cosT[:], cos_d), "tabs", writes=[W_(R("tabs"), 0, 1)], extra=conv2_evs)
        DMA("sp", DM(sinT[:], sin_d), "tabs", writes=[W_(R("tabs"), 1, 2)], extra=conv2_evs)
        TABS = W_(R("tabs"), 0, 2)

        def rope_dve(ps_raw, w, a, mbuf, mres, cbuf, cres_, extra=()):
            OP("dve", TT(mbuf[:, 0:w], PS[ps_raw][:, 0:w], sinT[:, a:a + w], ALU.mult),
               reads=[W_(r_ps[ps_raw], 0, 512), TABS], writes=[mres], extra=extra)
            OP("dve", TT(cbuf[:, 0:w], PS[ps_raw][:, 0:w], cosT[:, a:a + w], ALU.mult),
               reads=[W_(r_ps[ps_raw], 0, 512), TABS], writes=[cres_], extra=extra)

        def rope_pe(ps_rot, w, mbuf, mres, cbuf, cres_):
            OP("pe", MM(PS[ps_rot][:, 0:w], swapm[:, :], mbuf[:, 0:w], True, False),
               reads=[mres] + CONST, writes=[W_(r_ps[ps_rot], 0, 512)])
            OP("pe", MM(PS[ps_rot][:, 0:w], identb[:, :], cbuf[:, 0:w], False, True),
               reads=[cres_] + CONST, writes=[W_(r_ps[ps_rot], 0, 512)])

        def rope_chunk(ps_raw, ps_rot, w, a, mbuf, mres, cbuf, cres_, extra=()):
            rope_dve(ps_raw, w, a, mbuf, mres, cbuf, cres_, extra)
            rope_pe(ps_rot, w, mbuf, mres, cbuf, cres_)

        dbg_dump("kv_a")
        kv_ctr = 0
        norm_tile(3, TILES0[0][0], TILES0[0][1], xn, xn_res, TILES0[0][0])
        for ti, (a, b) in enumerate(TILES0):
            w = b - a
            pars = []
            for kc in range(2):
                par = kv_ctr % 2
                kv_ctr += 1
                pars.append(par)
                pr = 0 + par
                for c in range(DC):
                    OP("pe", MM(PS[pr][:, 0:w], wkv[:, c, kc * 128:(kc + 1) * 128], xn[:, c, a:b], c == 0, c == DC - 1),
                       reads=[W_(R("wkv")), W_(r_xn, a, b)], writes=[W_(r_ps[pr], 0, 512)])
            part2 = None
            if ti + 1 < len(TILES0):
                norm_tile(3, TILES0[ti + 1][0], TILES0[ti + 1][1], xn, xn_res, TILES0[ti + 1][0])
            for kc in range(2):
                par = pars[kc]
                pr, pt = 0 + par, 2 + par
                rope_chunk(pr, pt, w, a, kraw[par], W_(R("kraw%d" % par)), kcb[par], W_(R("kcb%d" % par)), extra=ffn_evs)
                OP("act", ACT(KT[:, kc, a:b], PS[pt][:, 0:w], AF.Copy),
                   reads=[W_(r_ps[pt], 0, 512)], writes=[W_(R("KT"), kc * NT + a, kc * NT + b)], extra=conv2_evs)
                lo, hi = max(a, NP_ - 128), min(b, NP_)
                if lo < hi:
                    OP("act", ACT(KoutT[:, kc, lo - (NP_ - 128):hi - (NP_ - 128)], PS[pt][:, lo - a:hi - a], AF.Copy),
                       reads=[W_(r_ps[pt], 0, 512)], writes=[W_(R("KoutT"), kc * 2, kc * 2 + 1)], extra=ffn_evs)
                if b > NP_:
                    OP("act", ACT(KoutT[:, kc, 128:192], PS[pt][:, NP_ - a:NT - a], AF.Copy),
                       reads=[W_(r_ps[pt], 0, 512)], writes=[W_(R("KoutT"), kc * 2 + 1, kc * 2 + 2)], extra=ffn_evs)
            if part2 is not None:
                part2()
        dbg_dump("kv_k")
        VBLK = [2] + [HALO + 128 * m for m in range(16)] + [HALO + 1796, HALO + 1924]
        for bi, c0 in enumerate(VBLK):
            bank = 4 + bi % 2
            for c in range(DC):
                OP("pe", MM(PS[bank][:, 0:256], xn[:, c, c0:c0 + 128], wkv[:, c, 256:512], c == 0, c == DC - 1),
                   reads=[W_(R("wkv")), W_(r_xn, c0, c0 + 128)], writes=[W_(r_ps[bank], 0, 512)])
            OP("act", ACT(Vb[:, bi, :], PS[bank][:, 0:256], AF.Copy),
               reads=[W_(r_ps[bank], 0, 512)], writes=[W_(R("Vb"), bi, bi + 1)], extra=conv2_evs)
            if bi == 18:
                OP("dve", CP(vstage[:, :], PS[bank][:, 0:256]),
                   reads=[W_(r_ps[bank], 0, 512)], writes=[W_(R("vstage"))], extra=ffn_evs)
                out_events.append(DMA("sp", DM(vp_d, vstage[:, :]), "vpo", reads=[W_(R("vstage"))]))
        dbg_dump("kv_v")
        for c in range(DC):
            OP("pe", MM(PS[4][0:NS, 0:256], xn[:, c, NP_:NT], wkv[:, c, 256:512], c == 0, c == DC - 1),
               reads=[W_(R("wkv")), W_(r_xn, NP_, NT)], writes=[W_(r_ps[4], 0, 512)])
        OP("dve", CP(vsstage[:, :], PS[4][0:NS, 0:256]),
           reads=[W_(r_ps[4], 0, 512)], writes=[W_(R("vsstage"))], extra=ffn_evs)
        for sb_ in range(NSEQ):
            out_events.append(DMA("sp", DM(vs_d[sb_, 124:128, :], vsstage[4 * sb_:4 * sb_ + 4, :]), "vso",
                                  reads=[W_(R("vsstage"))]))
        OP("act", ACT(Vs_bf[:, :], PS[4][0:NS, 0:256], AF.Copy),
           reads=[W_(r_ps[4], 0, 512)], writes=[W_(R("Vnew"))], extra=conv2_evs)
        dbg_dump("kv_vs")
        for kc in range(2):
            OP("pe", TR(PS[6][:, kc * 128:(kc + 1) * 128], KoutT[:, kc, 0:128], ident[:, :]),
               reads=[W_(R("KoutT"), 0, 4)] + CONST, writes=[W_(r_ps[6], 0, 512)])
        OP("dve", CP(kstage[:, :], PS[6][:, 0:256]), reads=[W_(r_ps[6], 0, 512)], writes=[W_(R("kstage"))], extra=ffn_evs)
        out_events.append(DMA("sp", DM(kp_d, kstage[:, :]), "kpo", reads=[W_(R("kstage"))]))
        for kc in range(2):
            OP("pe", TR(PS[7][0:NS, kc * 128:(kc + 1) * 128], KoutT[:, kc, 128:192], ident[:, :]),
               reads=[W_(R("KoutT"), 0, 4)] + CONST, writes=[W_(r_ps[7], 0, 512)])
        OP("dve", CP(ksstage[:, :], PS[7][0:NS, 0:256]), reads=[W_(r_ps[7], 0, 512)], writes=[W_(R("ksstage"))], extra=ffn_evs)
        for sb_ in range(NSEQ):
            out_events.append(DMA("sp", DM(ks_d[sb_, 124:128, :], ksstage[4 * sb_:4 * sb_ + 4, :]), "kso",
                                  reads=[W_(R("ksstage"))]))

        dbg_dump("kv")
        kv_evs = gather(["wkv", "kraw0", "kraw1", "kt1_0", "kt1_1", "kt2_0", "kt2_1", "kt3_0", "kt3_1", "kcb0", "kcb1", "KoutT",
                         "kstage", "vstage", "ksstage", "vsstage"])
        ffn(1, 0, 4, TILES1, phase_extra=kv_evs)
        dbg_dump("ffn3")

        ffn_evs = gather(FFN_BUFS)
        xn_evs = r_xn.all_events()
        for jp_ in range(DC):
            gc_, i4 = jp_ // 4, jp_ % 4
            for half in range(2):
                g = 2 * gc_ + half
                src = w_q[:, g * 256 + i4 * 64:g * 256 + (i4 + 1) * 64].rearrange("(c p) m -> p c m", p=128)
                c0_ = jp_ * 128 + half * 64
                DMA("pool", DM(Wq[:, :, c0_:c0_ + 64], src), "Wq%d" % jp_,
                    writes=[W_(R("Wq"), 2 * jp_ + half, 2 * jp_ + half + 1)], extra=ffn_evs)
        wi = 16
        for gc in range(2):
            for half in range(2):
                g = 2 * gc + half
                src2 = w_o[g * 256:(g + 1) * 256, :].rearrange("(i p) n -> p i n", p=64)
                dst2 = Wo2[half * 64:(half + 1) * 64, gc * 4:(gc + 1) * 4, :]
                DMA("pool", DM(dst2, src2), "Wo2", writes=[W_(R("Wq"), wi, wi + 1)], extra=ffn_evs)
                wi += 1
        WQR = W_(R("Wq"), 0, wi)
        WOR = W_(R("Wq"), 16, wi)
        ESK = W_(R("esink"))
        OP("act", ACT(esink[:, :], esink[:, :], AF.Exp), reads=[ESK], writes=[ESK])

        ATILES = [(HALO + 512 * i, HALO + 512 * (i + 1)) for i in range(4)] + [(NP_ - 128, NT)]
        QTR = W_(R("QT"))
        KTR = W_(R("KT"), 0, 2 * NT)
        VBR = W_(R("Vb"), 0, 19)
        u_ctr = [0]
        seq_ctr = [0]

        def unit_prompt(jp, qc0, qa, vprev_i, vcur_i, mask_ap):
            u = u_ctr[0]
            u_ctr[0] += 1
            par = u % 3
            zpar = u % 2
            gc = jp // 4
            xb = 2 * par
            zb = 6 + zpar
            ptr = W_(R("PT%d" % par)) if par < 2 else W_(R("qraw0"))
            lr = W_(R("lnD%d" % zpar))

            def A():
                for (bank, base) in ((xb, 0), (xb + 1, 64)):
                    OP("pe", MM(PS[bank][:, 0:128], KT[base:base + 64, gc, qa - 128:qa], QT[base:base + 64, jp, qc0:qc0 + 128]),
                       reads=[QTR, KTR], writes=[W_(r_ps[bank], 0, 512)])
                    OP("pe", MM(PS[bank][:, 128:256], KT[base:base + 64, gc, qa:qa + 128], QT[base:base + 64, jp, qc0:qc0 + 128]),
                       reads=[QTR, KTR], writes=[W_(r_ps[bank], 0, 512)])
                OP("act", ACT(PT[par][:, :].rearrange("p (b n) -> p b n", b=2), PSA[:, xb:xb + 2, 0:256], AF.Exp, scale=0.125),
                   reads=[W_(r_ps[xb], 0, 512), W_(r_ps[xb + 1], 0, 512)], writes=[ptr])
                OP("dve", TT(PT[par][:, :], PT[par][:, :], mask_ap, ALU.mult), reads=[ptr] + CONST, writes=[ptr])

            def B():
                for (base, off) in ((0, 0), (64, 256)):
                    g = 2 * gc + (base // 64)
                    OP("pe", MM(PS[zb][base:base + 64, 0:128], Vb[:, vprev_i, g * 64:(g + 1) * 64], PT[par][:, off:off + 128], True, False),
                       reads=[ptr, VBR], writes=[W_(r_ps[zb], 0, 512)])
                    OP("pe", MM(PS[zb][base:base + 64, 0:128], Vb[:, vcur_i, g * 64:(g + 1) * 64], PT[par][:, off + 128:off + 256], False, True),
                       reads=[ptr, VBR], writes=[W_(r_ps[zb], 0, 512)])
                    OP("pe", MM(PS[zb][base:base + 64, 128:256], ones[:, 0:64], PT[par][:, off:off + 128], True, False),
                       reads=[ptr, W_(R("ones"))], writes=[W_(r_ps[zb], 0, 512)])
                    OP("pe", MM(PS[zb][base:base + 64, 128:256], ones[:, 0:64], PT[par][:, off + 128:off + 256], False, True),
                       reads=[ptr, W_(R("ones"))], writes=[W_(r_ps[zb], 0, 512)])
                OP("act", ACT(lnD[zpar][:, 0:128], PS[zb][:, 128:256], AF.Ln, bias=esink[:, jp:jp + 1]),
                   reads=[W_(r_ps[zb], 0, 512), ESK], writes=[lr])
                OP("act", ACT(lnD[zpar][:, 0:128], lnD[zpar][:, 0:128], AF.Exp, scale=-1.0), reads=[lr], writes=[lr])
                OP("dve", TT(attnT[:, jp, qc0:qc0 + 128], PS[zb][:, 0:128], lnD[zpar][:, 0:128], ALU.mult),
                   reads=[W_(r_ps[zb], 0, 512), lr], writes=[W_(R("attnT"))])
            return A, B

        def sample_attention():
            P0, P1 = W_(R("PT0")), W_(R("PT1"))
            z4, z5 = W_(r_ps[4], 0, 512), W_(r_ps[5], 0, 512)
            for jp in range(DC):
                gc = jp // 4
                for (bank, base) in ((0, 0), (1, 64)):
                    OP("pe", MM(PS[bank][0:NS, jp * 64:(jp + 1) * 64], KT[base:base + 64, gc, NP_:NT],
                                QT[base:base + 64, jp, 128:192]),
                       reads=[QTR, KTR], writes=[W_(r_ps[bank], 0, 512)])
            OP("act", ACT(PTC[0:NS, :].rearrange("p (b n) -> p b n", b=2), PSA[0:NS, 0:2, :], AF.Exp, scale=0.125),
               reads=[W_(r_ps[0], 0, 512), W_(r_ps[1], 0, 512)], writes=[P0, P1])
            for k in range(16):
                OP("dve", TT(PTC[0:NS, k * 64:(k + 1) * 64], PTC[0:NS, k * 64:(k + 1) * 64], msamp[0:NS, 64:128], ALU.mult),
                   reads=[P0, P1] + CONST, writes=[P0, P1])
            for (base, hoff) in ((0, 0), (64, 512)):
                for jp in range(DC):
                    g = 2 * (jp // 4) + (base // 64)
                    c0 = hoff + jp * 64
                    OP("pe", MMX(PS[4][base:base + 64, jp * 64:(jp + 1) * 64], Vs_bf[:, g * 64:(g + 1) * 64],
                                 PTC[0:NS, c0:c0 + 64], jp == 0, False),
                       reads=[P0, P1, W_(R("Vnew"))], writes=[z4])
                for jp in range(DC):
                    c0 = hoff + jp * 64
                    OP("pe", MMX(PS[5][base:base + 64, jp * 64:(jp + 1) * 64], ones[0:NS, 0:64],
                                 PTC[0:NS, c0:c0 + 64], jp == 0, False),
                       reads=[P0, P1, W_(R("ones"))], writes=[z5])

            def seq_unit(sb_):
                par = sb_ % 2
                sp = sb_ % 2
                xb = 2 if par == 0 else 0
                ptr = W_(R("PT%d" % par))
                kcr = W_(R("KcT%d" % sp))
                vcr = W_(R("Vc%d" % sp))
                qc0 = 128 + 4 * sb_

                def A():
                    DMA("sp", DM(ckst[sp][:, :], ck[sb_]), "ckst%d" % sp, writes=[W_(R("ckst%d" % sp))])
                    DMA("pool", DM(Vc[sp][:, :], cv[sb_]), "Vc%d" % sp, writes=[vcr])
                    for kc in range(2):
                        OP("pe", TR(PS[6][:, kc * 128:(kc + 1) * 128], ckst[sp][:, kc * 128:(kc + 1) * 128], ident[:, :]),
                           reads=[W_(R("ckst%d" % sp))] + CONST, writes=[W_(r_ps[6], 0, 512)])
                    OP("act", ACT(KcT[sp][:, :, :], PS[6][:, 0:256].rearrange("p (c n) -> p c n", c=2), AF.Copy),
                       reads=[W_(r_ps[6], 0, 512)], writes=[kcr])
                    for (bank, base) in ((xb, 0), (xb + 1, 64)):
                        for jp in range(DC):
                            OP("pe", MM(PS[bank][:, jp * 4:jp * 4 + 4], KcT[sp][base:base + 64, jp // 4, :],
                                        QT[base:base + 64, jp, qc0:qc0 + 4]),
                               reads=[QTR, kcr], writes=[W_(r_ps[bank], 0, 512)])
                    OP("act", ACT(PT[par][:, 0:64].rearrange("p (b n) -> p b n", b=2), PSA[:, xb:xb + 2, 0:32], AF.Exp, scale=0.125),
                       reads=[W_(r_ps[xb], 0, 512), W_(r_ps[xb + 1], 0, 512)], writes=[ptr])
                    OP("dve", TT(PT[par][:, 0:64], PT[par][:, 0:64], msamp[:, 0:64], ALU.mult), reads=[ptr] + CONST, writes=[ptr])

                def B():
                    for (base, hoff) in ((0, 0), (64, 32)):
                        for jp in range(DC):
                            g = 2 * (jp // 4) + (base // 64)
                            col = jp * 64 + 4 * sb_
                            OP("pe", MMX(PS[4][base:base + 64, col:col + 4], Vc[sp][:, g * 64:(g + 1) * 64],
                                         PT[par][:, hoff + jp * 4:hoff + jp * 4 + 4], False, False),
                               reads=[ptr, vcr], writes=[z4])
                        for jp in range(DC):
                            col = jp * 64 + 4 * sb_
                            OP("pe", MMX(PS[5][base:base + 64, col:col + 4], ones[:, 0:64],
                                         PT[par][:, hoff + jp * 4:hoff + jp * 4 + 4], False, sb_ == NSEQ - 1),
                               reads=[ptr, W_(R("ones"))], writes=[z5])
                return A, B

            prevB_ = None
            for sb_ in range(NSEQ):
                A_, B_ = seq_unit(sb_)
                A_()
                if prevB_ is not None:
                    prevB_()
                prevB_ = B_
            prevB_()
            lr = W_(R("lnDall"))
            for jp in range(DC):
                OP("dve", TS(lnDall[:, jp * 64:(jp + 1) * 64], PS[5][:, jp * 64:(jp + 1) * 64], esink[:, jp:jp + 1], None, ALU.add),
                   reads=[z5, ESK], writes=[lr])
            OP("act", ACT(lnDall[:, :], lnDall[:, :], AF.Ln), reads=[lr], writes=[lr])
            OP("act", ACT(lnDall[:, :], lnDall[:, :], AF.Exp, scale=-1.0), reads=[lr], writes=[lr])
            OP("dve", TT(attnT[:, :, 128:192], PS[4][:, :].rearrange("p (j q) -> p j q", j=DC),
                         lnDall[:, :].rearrange("p (j q) -> p j q", j=DC), ALU.mult),
               reads=[z4, lr], writes=[W_(R("attnT"))])

        norm_tile(5, ATILES[0][0], ATILES[0][1], xnA, lambda a_, b_: W_(R("xnA")), 0, extra=xn_evs)
        for ti, (a, b) in enumerate(ATILES):
            w = b - a
            def q_tail(jp):
                par = jp % 2
                pt = 2 * (jp % 4) + 1
                rope_pe(pt, w, qraw[par], W_(R("qraw%d" % par)), qcb[par], W_(R("qcb%d" % par)))
                OP("act", ACT(QT[:, jp, 0:w], PS[pt][:, 0:w], AF.Copy), reads=[W_(r_ps[pt], 0, 512)], writes=[QTR])

            for jp in range(DC):
                par = jp % 2
                pr, pt = 2 * (jp % 4), 2 * (jp % 4) + 1
                for c in range(DC):
                    OP("pe", MM(PS[pr][:, 0:w], Wq[:, c, jp * 128:(jp + 1) * 128], xnA[:, c, 0:w], c == 0, c == DC - 1),
                       reads=[W_(R("Wq"), 2 * jp, 2 * jp + 2), W_(R("xnA"))], writes=[W_(r_ps[pr], 0, 512)])
                rope_dve(pr, w, a, qraw[par], W_(R("qraw%d" % par)), qcb[par], W_(R("qcb%d" % par)))
                if jp >= 1:
                    q_tail(jp - 1)
            q_tail(DC - 1)
            units = []
            if ti < 4:
                for bi in range(4):
                    m = ti * 4 + bi
                    for jp in range(DC):
                        units.append(unit_prompt(jp, 128 * bi, a + 128 * bi, m, m + 1, (mfirst if m == 0 else mstd)[:, :]))
            else:
                for jp in range(DC):
                    units.append(unit_prompt(jp, 0, a, 17, 18, mstd[:, :]))
            pend = []
            for ui, (A_, B_) in enumerate(units):
                A_()
                pend.append(B_)
                if len(pend) > 2:
                    pend.pop(0)()
                if ui == 4 and ti + 1 < len(ATILES):
                    norm_tile(5, ATILES[ti + 1][0], ATILES[ti + 1][1], xnA, lambda a_, b_: W_(R("xnA")), 0)
            for B_ in pend:
                B_()
            if ti == 4:
                sample_attention()
            lo = 0 if ti < 4 else 124
            for d_ in range(DC):
                ob = 4 + d_ % 4
                for jp in range(DC):
                    OP("pe", MM(PS[ob][:, 0:w], Wo2[:, jp, d_ * 128:(d_ + 1) * 128], attnT[:, jp, 0:w], jp == 0, jp == DC - 1),
                       reads=[WOR, W_(R("attnT"))], writes=[W_(r_ps[ob], 0, 512)])
                OP("dve", TT(xT[:, d_, a + lo:b], PS[ob][:, lo:w], xT[:, d_, a + lo:b], ALU.add),
                   reads=[W_(r_ps[ob], 0, 512), W_(r_xT, a + lo, b)], writes=[W_(r_xT, a + lo, b)])
        dbg_dump("attn")

        att_evs = gather(["Wq", "xnA", "QT", "attnT", "PT0", "PT1", "qraw0", "qraw1", "qcb0", "qcb1", "qt1_0", "qt1_1", "qt2",
                          "lnD0", "lnD1", "lnDall"])
        ffn(1, 1, 6, TILES1, phase_extra=att_evs)
        dbg_dump("ffn4")

        ffn_evs = gather(FFN_BUFS)
        FB = 512
        nblk = (OWN + NS + FB - 1) // FB
        blocks = []
        for bi in range(nblk):
            a = HALO + bi * FB
            b = min(a + FB, NT)
            blocks.append((bi, a, b))

        def fin_norm(bi, a, b):
            par = bi % 2
            yr = W_(R("yblk%d" % par))
            norm_tile(7, a, b, yblk[par], lambda a_, b_, yr=yr: yr, 0, extra=ffn_evs)

        sub_ctr = [0]

        def fin_out(bi, a, b):
            par = bi % 2
            yr = W_(R("yblk%d" % par))
            nsub = (b - a + 127) // 128
            for sub in range(nsub):
                a_s = a + sub * 128
                ws = min(128, b - a_s)
                sc = sub_ctr[0]
                sub_ctr[0] += 1
                ys = sc % 4
                for hlf in range(2):
                    bank = 2 * (sc % 2) + hlf
                    for cc in range(4):
                        c = hlf * 4 + cc
                        OP("pe", TR(PS[bank][0:ws, cc * 128:(cc + 1) * 128], yblk[par][:, c, sub * 128:sub * 128 + ws], ident[:, :]),
                           reads=[yr] + CONST, writes=[W_(r_ps[bank], 0, 512)])
                    OP("act", ACT(ystage[ys][0:ws, hlf * 512:(hlf + 1) * 512], PS[bank][0:ws, :], AF.Copy),
                       reads=[W_(r_ps[bank], 0, 512)], writes=[W_(R("ystage%d" % ys), hlf, hlf + 1)], extra=ffn_evs)
                out_events.append(DMA("sp", DM(y_d[a_s - HALO:a_s - HALO + ws, :], ystage[ys][0:ws, :]), "yo%d" % ys,
                                      reads=[W_(R("ystage%d" % ys), 0, 2)]))

        fin_norm(*blocks[0])
        for i, blk_ in enumerate(blocks):
            if i + 1 < len(blocks):
                fin_norm(*blocks[i + 1])
            fin_out(*blk_)
    except _Stop:
        pass
    S.op("sp", lambda e: None, waits=out_events)
    S.emit()
    return nc


_NC_CACHE = {}


def _host_tables(core):
    c = core % 4
    p0 = c * OWN
    pos = np.concatenate([np.arange(p0 - HALO, p0 + OWN), 8192 + np.tile(np.arange(4), NSEQ)]).astype(np.int64)
    posf = np.maximum(pos, 0).astype(np.float32)
    inv = (1.0 / (10000.0 ** (np.arange(0, 64, 2, dtype=np.float32) / np.float32(64)))).astype(np.float32)
    ang = (posf[:, None] * inv[None, :]).astype(np.float32)
    cos = np.cos(ang).astype(np.float32)
    sin = np.sin(ang).astype(np.float32)
    p = np.arange(128)
    j = p % 32
    half = (p % 64) // 32
    cosT = np.ascontiguousarray(cos[:, j].T)
    sinT = np.ascontiguousarray((sin[:, j] * np.where(half == 0, 1.0, -1.0)[None, :]).T.astype(np.float32))
    return cosT, sinT


def kernel(x_prompt, x_sample, state_conv, cache_k, cache_v, meta_tokens, norm_g, ffn_w_in, ffn_w_out,
           conv_w_in, conv_kernel, conv_w_out, kv_norm_g, w_kv, w_q, w_o, sinks, final_norm_g):
    f32 = np.float32
    x_prompt = np.asarray(x_prompt, f32)
    x_sample = np.asarray(x_sample, f32)
    B = x_prompt.shape[0]
    if "nc" not in _NC_CACHE:
        _NC_CACHE["nc"] = build_nc()
    nc = _NC_CACHE["nc"]

    def fm(v):
        return np.ascontiguousarray(np.asarray(v, f32).reshape(DC, 128).T)

    ng = np.asarray(norm_g, f32)
    gT = np.concatenate([fm(ng[0, 0]), fm(ng[0, 1]), fm(ng[0, 2]), fm(kv_norm_g),
                         fm(ng[1, 0]), fm(ng[1, 1]), fm(ng[1, 2]), fm(final_norm_g)], axis=1)
    ckn = np.asarray(conv_kernel, f32)[0]
    ckT = np.concatenate([fm(ckn[0]), fm(ckn[1]), fm(ckn[2])], axis=1)
    sk = np.asarray(sinks, f32)[0]
    sinkT = np.zeros((128, 40), f32)
    for jp in range(8):
        gc, i = jp // 4, jp % 4
        sinkT[0:64, jp] = sk[4 * (2 * gc) + i]
        sinkT[64:128, jp] = sk[4 * (2 * gc + 1) + i]
        sinkT[:, 8 + 4 * jp:12 + 4 * jp] = sinkT[:, jp:jp + 1]
    ident = np.eye(128, dtype=f32)
    swapm = np.zeros((128, 128), f32)
    pp = np.arange(128)
    swapm[pp, pp ^ 32] = 1.0
    kk = np.arange(128)[:, None]
    qq = np.arange(128)[None, :]
    prev = (kk >= qq).astype(f32)
    curm = (kk <= qq).astype(f32)
    mstd = np.concatenate([prev, curm, prev, curm], axis=1)
    mfirst0 = np.concatenate([np.zeros_like(prev), curm, np.zeros_like(prev), curm], axis=1)
    ii = np.arange(128)[:, None]
    tt = np.arange(4)[None, :]
    sprev = (ii >= tt).astype(f32)
    scur = ((ii <= tt) & (ii < 4)).astype(f32)
    msamp = np.zeros((128, 128), f32)
    msamp[:, 0:64] = np.tile(sprev, (1, 16))
    kq = np.arange(64)
    msamp[0:64, 64:128] = ((kq[:, None] // 4 == kq[None, :] // 4) & (kq[:, None] % 4 <= kq[None, :] % 4)).astype(f32)

    shared = dict(
        ffn_w_in=np.asarray(ffn_w_in, f32), ffn_w_out=np.asarray(ffn_w_out, f32),
        conv_w_in=np.asarray(conv_w_in, f32)[0], conv_w_out=np.asarray(conv_w_out, f32)[0],
        w_kv=np.asarray(w_kv, f32), w_q=np.asarray(w_q, f32)[0], w_o=np.asarray(w_o, f32)[0],
        gT=gT, ckT=ckT, sinkT=sinkT, ident=ident, swapm=swapm, mstd=mstd, msamp=msamp)
    meta = np.asarray(meta_tokens, f32)
    sc = np.asarray(state_conv, f32)[0]
    ckc = np.asarray(cache_k, f32).reshape(128, 128, 256)
    cvc = np.asarray(cache_v, f32).reshape(128, 128, 256)
    in_maps = []
    for core in range(8):
        bseq, c = core // 4, core % 4
        xfull = np.concatenate([meta, x_prompt[bseq]], axis=0)
        p0 = c * OWN
        lo = p0 - HALO
        rows = np.zeros((NP_, D), f32)
        s0 = max(lo, 0)
        rows[s0 - lo:] = xfull[s0:p0 + OWN]
        xin = np.concatenate([rows, x_sample[16 * core:16 * core + 16].reshape(NS, D)], axis=0)
        cosT, sinT = _host_tables(core)
        m = dict(shared)
        m.update(xin=np.ascontiguousarray(xin),
                 sconv=np.ascontiguousarray(sc[16 * core:16 * core + 16].reshape(2 * NSEQ, D)),
                 ck=np.ascontiguousarray(ckc[16 * core:16 * core + 16]),
                 cv=np.ascontiguousarray(cvc[16 * core:16 * core + 16]),
                 cosT=cosT, sinT=sinT, mfirst=(mfirst0 if c == 0 else mstd))
        in_maps.append(m)
    res = run_bass_kernel_spmd(nc, in_maps, core_ids=list(range(8)))
    R_ = res.results
    _NC_CACHE["last"] = R_
    y_prompt = np.zeros((B, 8192, D), f32)
    y_sample = np.zeros((128, 4, D), f32)
    ncp = np.zeros((1, B, 2, D), f32)
    ncs = np.zeros((1, 128, 2, D), f32)
    nkp = np.zeros((B, 128, 4, 64), f32)
    nvp = np.zeros((B, 128, 4, 64), f32)
    nks = np.zeros((128, 128, 4, 64), f32)
    nvs = np.zeros((128, 128, 4, 64), f32)
    for core in range(8):
        bseq, c = core // 4, core % 4
        r = R_[core]
        y = np.asarray(r["y"])
        yp = y[0:OWN]
        p0 = c * OWN
        if c == 0:
            y_prompt[bseq, 0:OWN - 16] = yp[16:]
        else:
            y_prompt[bseq, p0 - 16:p0 - 16 + OWN] = yp
        y_sample[16 * core:16 * core + 16] = y[OWN:].reshape(16, 4, D)
        ncs[0, 16 * core:16 * core + 16] = np.asarray(r["u_s"]).reshape(16, 2, D)
        nks[16 * core:16 * core + 16] = np.asarray(r["ks"]).reshape(16, 128, 4, 64)
        nvs[16 * core:16 * core + 16] = np.asarray(r["vs"]).reshape(16, 128, 4, 64)
        if c == 3:
            ncp[0, bseq] = np.asarray(r["u_p"])
            nkp[bseq] = np.asarray(r["kp"]).reshape(128, 4, 64)
            nvp[bseq] = np.asarray(r["vp"]).reshape(128, 4, 64)
    return (y_prompt, y_sample, ncp, ncs, nkp, nvp, nks, nvs)
```
